# Optimizing a Trainium2 kernel written in Bass

```python
import math
import jax, jax.numpy as jnp
from jax import lax
import numpy as np

D_MODEL = 1024
BATCH = 16
SEQ = 256
DEPTH = 2
DEC_BATCH = 2
DEC_SEQ = 2048
PAST_LEN = 512

GRID_W = 64
ATTN_DIM = D_MODEL // 2
SC_DIM = D_MODEL // 4
SSM_DIM = D_MODEL // 4
MIX_DIM = ATTN_DIM + SC_DIM + SSM_DIM
HEAD_DIM = 64
N_HEADS = ATTN_DIM // HEAD_DIM
KV_HEADS = 2
Q_PER_KV = N_HEADS // KV_HEADS
KV_DIM = KV_HEADS * HEAD_DIM
WINDOW = 128
Q_BLOCK = 128
BAND = Q_BLOCK + 2 * WINDOW
ATTN_SCALE = 1.0 / math.sqrt(HEAD_DIM)
ROPE_BASE = 10000.0
ROPE_FREQS = HEAD_DIM // 4
CONV_W = 3
SSM_CH = 16
SSM_GROUPS = SSM_DIM // SSM_CH
SSM_STATE = 64
N_DIR = 2
DT_MIN = 1e-3
DT_MAX = 1e-1
IN_DIM = ATTN_DIM + 2 * KV_DIM + 3 * SC_DIM + SSM_DIM
IN_SPLITS = (ATTN_DIM, ATTN_DIM + KV_DIM, ATTN_DIM + 2 * KV_DIM,
             ATTN_DIM + 2 * KV_DIM + SC_DIM, ATTN_DIM + 2 * KV_DIM + 2 * SC_DIM,
             ATTN_DIM + 2 * KV_DIM + 3 * SC_DIM)
D_FF = 11 * D_MODEL // 4
RMS_EPS = 1e-6

kernel_name = 'hybrid_prefix_diffusion_step'


def rms_norm(x, g):
    xf = x.astype(jnp.float32)
    y = xf * lax.rsqrt(jnp.mean(xf * xf, axis=-1, keepdims=True) + RMS_EPS)
    return (y * g.astype(jnp.float32)).astype(x.dtype)


def dwconv3(x, w):
    ch = x.shape[-1]
    rhs = jnp.transpose(w)[:, None, :].astype(x.dtype)
    return lax.conv_general_dilated(x, rhs, window_strides=(1,), padding=((1, 1),),
                                    dimension_numbers=('NWC', 'WIO', 'NWC'),
                                    feature_group_count=ch)


def axial_rope(seq_len):
    rows = seq_len // GRID_W
    row = jnp.repeat(jnp.arange(rows, dtype=jnp.float32), GRID_W)
    col = jnp.tile(jnp.arange(GRID_W, dtype=jnp.float32), rows)
    freqs = ROPE_BASE ** (-jnp.arange(ROPE_FREQS, dtype=jnp.float32) / ROPE_FREQS)
    ang = jnp.stack([row[:, None] * freqs, col[:, None] * freqs], axis=1)
    return jnp.cos(ang), jnp.sin(ang)


def apply_rope(x, cos, sin):
    b, l, h, _ = x.shape
    xr = x.reshape(b, l, h, 2, 2, ROPE_FREQS)
    x1, x2 = xr[..., 0, :], xr[..., 1, :]
    c, s = cos[None, :, None], sin[None, :, None]
    out = jnp.stack([x1 * c - x2 * s, x2 * c + x1 * s], axis=-2)
    return out.reshape(b, l, h, HEAD_DIM).astype(x.dtype)


def softmax_with_sink(s, sink):
    sk = sink.astype(jnp.float32).reshape(1, KV_HEADS, Q_PER_KV, 1, 1)
    m = jnp.maximum(jnp.max(s, axis=-1, keepdims=True), sk)
    e = jnp.exp(s - m)
    return e / (jnp.sum(e, axis=-1, keepdims=True) + jnp.exp(sk - m))


def context_attention(q, k, v, sink):
    b, lq = q.shape[0], q.shape[1]
    nb = lq // Q_BLOCK
    qb = q.reshape(b, nb, Q_BLOCK, KV_HEADS, Q_PER_KV, HEAD_DIM).transpose(1, 0, 2, 3, 4, 5)

    def one_block(qblk):
        s = jnp.einsum('bqkrd,bskd->bkrqs', qblk, k).astype(jnp.float32) * ATTN_SCALE
        p = softmax_with_sink(s, sink).astype(v.dtype)
        return jnp.einsum('bkrqs,bskd->bqkrd', p, v)

    o = lax.map(one_block, qb)
    return o.transpose(1, 0, 2, 3, 4, 5).reshape(b, lq, ATTN_DIM)


def latent_attention(q, k, v, ck, cv, sink):
    b, l = q.shape[0], q.shape[1]
    nb = l // Q_BLOCK
    qr = q.reshape(b, l, KV_HEADS, Q_PER_KV, HEAD_DIM)
    pad = ((0, 0), (WINDOW, WINDOW), (0, 0), (0, 0))
    kp, vp = jnp.pad(k, pad), jnp.pad(v, pad)
    r = jnp.arange(Q_BLOCK)[:, None]
    j = jnp.arange(BAND)[None, :]

    def one_block(i):
        start = i * Q_BLOCK
        qblk = lax.dynamic_slice_in_dim(qr, start, Q_BLOCK, axis=1)
        kb = lax.dynamic_slice_in_dim(kp, start, BAND, axis=1)
        vb = lax.dynamic_slice_in_dim(vp, start, BAND, axis=1)
        qpos = start + r
        kpos = start - WINDOW + j
        mask = (jnp.abs(qpos - kpos) <= WINDOW) & (kpos >= 0) & (kpos < l)
        s_band = jnp.einsum('bqkrd,bskd->bkrqs', qblk, kb).astype(jnp.float32) * ATTN_SCALE
        s_band = jnp.where(mask, s_band, -jnp.inf)
        s_ctx = jnp.einsum('bqkrd,bskd->bkrqs', qblk, ck).astype(jnp.float32) * ATTN_SCALE
        p = softmax_with_sink(jnp.concatenate([s_band, s_ctx], axis=-1), sink).astype(v.dtype)
        return (jnp.einsum('bkrqs,bskd->bqkrd', p[..., :BAND], vb)
                + jnp.einsum('bkrqs,bskd->bqkrd', p[..., BAND:], cv))

    o = lax.map(one_block, jnp.arange(nb))
    return o.transpose(1, 0, 2, 3, 4, 5).reshape(b, l, ATTN_DIM)


def _complex_affine_combine(e1, e2):
    a1r, a1i, b1r, b1i = e1
    a2r, a2i, b2r, b2i = e2
    return (a2r * a1r - a2i * a1i, a2r * a1i + a2i * a1r,
            a2r * b1r - a2i * b1i + b2r, a2r * b1i + a2i * b1r + b2i)


def zoh(lam_re, lam_im, log_dt, b_re, b_im):
    dt = jnp.exp(log_dt)[:, None]
    mag = jnp.exp(lam_re * dt)
    ar, ai = mag * jnp.cos(lam_im * dt), mag * jnp.sin(lam_im * dt)
    den = lam_re * lam_re + lam_im * lam_im
    fr = ((ar - 1.0) * lam_re + ai * lam_im) / den
    fi = (ai * lam_re - (ar - 1.0) * lam_im) / den
    bb_re = fr[..., None] * b_re - fi[..., None] * b_im
    bb_im = fr[..., None] * b_im + fi[..., None] * b_re
    return ar, ai, bb_re, bb_im


def ssm_direction(u, lam_re, lam_im, log_dt, b_re, b_im, c_re, c_im, h0, reverse):
    f32 = jnp.float32
    ar, ai, bb_re, bb_im = zoh(lam_re.astype(f32), lam_im.astype(f32), log_dt.astype(f32),
                               b_re.astype(f32), b_im.astype(f32))
    bu_re = jnp.einsum('blgc,gnc->blgn', u, bb_re)
    bu_im = jnp.einsum('blgc,gnc->blgn', u, bb_im)
    if reverse:
        bu_re, bu_im = jnp.flip(bu_re, 1), jnp.flip(bu_im, 1)
    a_re = jnp.broadcast_to(ar, bu_re.shape)
    a_im = jnp.broadcast_to(ai, bu_im.shape)
    acc_re, acc_im, h_re, h_im = lax.associative_scan(
        _complex_affine_combine, (a_re, a_im, bu_re, bu_im), axis=1)
    if h0 is not None:
        h0r, h0i = h0[0][:, None], h0[1][:, None]
        h_re, h_im = (h_re + acc_re * h0r - acc_im * h0i,
                      h_im + acc_re * h0i + acc_im * h0r)
    if reverse:
        h_re, h_im = jnp.flip(h_re, 1), jnp.flip(h_im, 1)
    y = (jnp.einsum('blgn,gcn->blgc', h_re, c_re.astype(f32))
         - jnp.einsum('blgn,gcn->blgc', h_im, c_im.astype(f32)))
    return y, h_re, h_im


def ssm_mixer(su, lp, h0_re=None, h0_im=None):
    f32 = jnp.float32
    b, l, _ = su.shape
    u = su.astype(f32).reshape(b, l, SSM_GROUPS, SSM_CH)
    y = lp['ssm_d'].astype(f32) * u
    fin_re, fin_im = [], []
    for d in range(N_DIR):
        h0 = None if h0_re is None else (h0_re[:, d].astype(f32), h0_im[:, d].astype(f32))
        yd, h_re, h_im = ssm_direction(u, lp['ssm_lam_re'][d], lp['ssm_lam_im'][d],
                                       lp['ssm_log_dt'][d], lp['ssm_b_re'][d], lp['ssm_b_im'][d],
                                       lp['ssm_c_re'][d], lp['ssm_c_im'][d], h0, d == 1)
        y = y + yd
        t = -1 if d == 0 else 0
        fin_re.append(h_re[:, t])
        fin_im.append(h_im[:, t])
    z = jax.nn.gelu(y.reshape(b, l, SSM_DIM))
    out = (z * jax.nn.sigmoid(z @ lp['ssm_w_glu'].astype(f32))).astype(su.dtype)
    if h0_re is None:
        return out, jnp.stack(fin_re, axis=1), jnp.stack(fin_im, axis=1)
    return out


def conv_ffn(h, lp):
    u = dwconv3(h @ lp['ffn_w_up'], lp['ffn_conv'])
    a, g = jnp.split(u, 2, axis=-1)
    return (a * jax.nn.silu(g)) @ lp['ffn_w_down']


def adaln(cvec, lp):
    return jax.nn.silu(cvec) @ lp['w_ada'] + lp['b_ada']


def block(x, mod, lp, ctx=None, rope=None):
    b, l, _ = x.shape
    sh1, sc1, g1, sh2, sc2, g2 = jnp.split(mod, 6, axis=-1)
    h = rms_norm(x, lp['norm_mix']) * (1.0 + sc1) + sh1
    q, k, v, gb, gc, gh, su = jnp.split(h @ lp['w_in'], IN_SPLITS, axis=-1)
    q = q.reshape(b, l, N_HEADS, HEAD_DIM)
    k = k.reshape(b, l, KV_HEADS, HEAD_DIM)
    v = v.reshape(b, l, KV_HEADS, HEAD_DIM)
    if ctx is None:
        attn = context_attention(q, k, v, lp['attn_sink'])
        ssm, st_re, st_im = ssm_mixer(su, lp)
    else:
        ck, cv, h0_re, h0_im = ctx
        cos, sin = rope
        attn = latent_attention(apply_rope(q, cos, sin), apply_rope(k, cos, sin), v,
                                ck, cv, lp['attn_sink'])
        ssm = ssm_mixer(su, lp, h0_re, h0_im)
    conv = gb * dwconv3(gc * gh, lp['sc_conv'])
    mix = jnp.concatenate([attn, conv.astype(attn.dtype), ssm.astype(attn.dtype)], axis=-1) @ lp['w_out']
    x = x + g1 * mix
    h2 = rms_norm(x, lp['norm_ffn']) * (1.0 + sc2) + sh2
    x = x + g2 * conv_ffn(h2, lp)
    if ctx is None:
        return x, k, v, st_re, st_im
    return x


def setup_inputs(seed: int = 0) -> dict:
    key = jax.random.key(seed)
    ks = jax.random.split(key, 32)
    f32 = jnp.float32

    def nrm(k, shape, scale=1.0):
        return jax.random.normal(k, shape, f32) * scale

    gsn = (DEPTH, N_DIR, SSM_GROUPS, SSM_STATE)
    return {
        'x_prompt': nrm(ks[0], (BATCH, SEQ, D_MODEL)),
        'x_sample': nrm(ks[1], (DEC_BATCH, DEC_SEQ, D_MODEL)),
        'cache_k': nrm(ks[2], (DEC_BATCH, DEPTH, PAST_LEN, KV_HEADS, HEAD_DIM)),
        'cache_v': nrm(ks[3], (DEC_BATCH, DEPTH, PAST_LEN, KV_HEADS, HEAD_DIM)),
        'state_ssm_re': nrm(ks[4], (DEC_BATCH, DEPTH, N_DIR, SSM_GROUPS, SSM_STATE), 0.3),
        'state_ssm_im': nrm(ks[5], (DEC_BATCH, DEPTH, N_DIR, SSM_GROUPS, SSM_STATE), 0.3),
        'c': nrm(ks[6], (DEC_BATCH, D_MODEL)),
        'c_ctx': nrm(ks[7], (D_MODEL,)),
        'norm_mix': 1.0 + nrm(ks[8], (DEPTH, D_MODEL), 0.02),
        'norm_ffn': 1.0 + nrm(ks[9], (DEPTH, D_MODEL), 0.02),
        'norm_final': 1.0 + nrm(ks[10], (D_MODEL,), 0.02),
        'w_ada': nrm(ks[11], (DEPTH, D_MODEL, 6 * D_MODEL), 0.5 * D_MODEL ** -0.5),
        'b_ada': nrm(ks[12], (DEPTH, 6 * D_MODEL), 0.02),
        'w_in': nrm(ks[13], (DEPTH, D_MODEL, IN_DIM), D_MODEL ** -0.5),
        'w_out': nrm(ks[14], (DEPTH, MIX_DIM, D_MODEL), MIX_DIM ** -0.5),
        'attn_sink': nrm(ks[15], (DEPTH, N_HEADS), 0.5),
        'sc_conv': nrm(ks[16], (DEPTH, SC_DIM, CONV_W), CONV_W ** -0.5),
        'ssm_lam_re': -0.5 + nrm(ks[17], gsn, 0.01),
        'ssm_lam_im': jnp.pi * jnp.arange(SSM_STATE, dtype=f32) + nrm(ks[18], gsn, 0.01),
        'ssm_log_dt': jax.random.uniform(ks[19], (DEPTH, N_DIR, SSM_GROUPS), f32,
                                         minval=math.log(DT_MIN), maxval=math.log(DT_MAX)),
        'ssm_b_re': nrm(ks[20], gsn + (SSM_CH,), (2 * SSM_CH) ** -0.5),
        'ssm_b_im': nrm(ks[21], gsn + (SSM_CH,), (2 * SSM_CH) ** -0.5),
        'ssm_c_re': nrm(ks[22], (DEPTH, N_DIR, SSM_GROUPS, SSM_CH, SSM_STATE), SSM_STATE ** -0.5),
        'ssm_c_im': nrm(ks[23], (DEPTH, N_DIR, SSM_GROUPS, SSM_CH, SSM_STATE), SSM_STATE ** -0.5),
        'ssm_d': nrm(ks[24], (DEPTH, SSM_GROUPS, SSM_CH)),
        'ssm_w_glu': nrm(ks[25], (DEPTH, SSM_DIM, SSM_DIM), SSM_DIM ** -0.5),
        'ffn_w_up': nrm(ks[26], (DEPTH, D_MODEL, 2 * D_FF), D_MODEL ** -0.5),
        'ffn_conv': nrm(ks[27], (DEPTH, 2 * D_FF, CONV_W), CONV_W ** -0.5),
        'ffn_w_down': nrm(ks[28], (DEPTH, D_FF, D_MODEL), D_FF ** -0.5),
    }


def reference(x_prompt, x_sample, cache_k, cache_v, state_ssm_re, state_ssm_im, c, c_ctx,
              norm_mix, norm_ffn, norm_final, w_ada, b_ada, w_in, w_out, attn_sink, sc_conv,
              ssm_lam_re, ssm_lam_im, ssm_log_dt, ssm_b_re, ssm_b_im, ssm_c_re, ssm_c_im,
              ssm_d, ssm_w_glu, ffn_w_up, ffn_conv, ffn_w_down):
    stacked = (('norm_mix', norm_mix), ('norm_ffn', norm_ffn), ('w_ada', w_ada),
               ('b_ada', b_ada), ('w_in', w_in), ('w_out', w_out), ('attn_sink', attn_sink),
               ('sc_conv', sc_conv), ('ssm_lam_re', ssm_lam_re), ('ssm_lam_im', ssm_lam_im),
               ('ssm_log_dt', ssm_log_dt), ('ssm_b_re', ssm_b_re), ('ssm_b_im', ssm_b_im),
               ('ssm_c_re', ssm_c_re), ('ssm_c_im', ssm_c_im), ('ssm_d', ssm_d),
               ('ssm_w_glu', ssm_w_glu), ('ffn_w_up', ffn_w_up), ('ffn_conv', ffn_conv),
               ('ffn_w_down', ffn_w_down))
    rope = axial_rope(x_sample.shape[1])
    xp, xs = x_prompt, x_sample
    ks_out, vs_out, sre_out, sim_out = [], [], [], []
    for l in range(DEPTH):
        lp = {name: arr[l] for name, arr in stacked}
        mod_ctx = adaln(c_ctx, lp)[None, None, :]
        xp, k_l, v_l, sre_l, sim_l = block(xp, mod_ctx, lp)
        ks_out.append(k_l)
        vs_out.append(v_l)
        sre_out.append(sre_l)
        sim_out.append(sim_l)
        mod_lat = adaln(c, lp)[:, None, :]
        ctx = (cache_k[:, l], cache_v[:, l], state_ssm_re[:, l], state_ssm_im[:, l])
        xs = block(xs, mod_lat, lp, ctx=ctx, rope=rope)
    y_prompt = rms_norm(xp, norm_final)
    y_sample = rms_norm(xs, norm_final)
    new_cache_k = jnp.stack(ks_out, axis=1)
    new_cache_v = jnp.stack(vs_out, axis=1)
    new_state_ssm_re = jnp.stack(sre_out, axis=1)
    new_state_ssm_im = jnp.stack(sim_out, axis=1)
    return (y_prompt, y_sample, new_cache_k, new_cache_v, new_state_ssm_re, new_state_ssm_im)
```

```python
import math
from contextlib import ExitStack

import numpy as np
import ml_dtypes

import concourse.bass as bass
import concourse.mybir as mybir
from concourse.bass_utils import run_bass_kernel_spmd

F32 = mybir.dt.float32
BF16 = mybir.dt.bfloat16
U8 = mybir.dt.uint8
I32 = mybir.dt.int32
ALU = mybir.AluOpType
AF = mybir.ActivationFunctionType
AX = mybir.AxisListType

D = 1024
DEPTH = 2
NQH = 8
HD = 64
DFF = 2816
IN_DIM = 1792
PAST = 512
EPS = 1e-6
NEG = -30000.0
TWO_PI = 2.0 * math.pi

ARENA_BYTES = 207 * 1024


class Tile:
    def __init__(self, name, ap, lo, hi):
        self.name, self.ap, self.lo, self.hi = name, ap, lo, hi

    def __getitem__(self, k):
        return self.ap[k]


class Op:
    __slots__ = ("eng", "calls", "waits", "sem", "val", "isdma")


class _Rec:
    def __init__(self):
        self.calls = []

    def __getattr__(self, name):
        def f(*a, **k):
            self.calls.append((name, a, k))
            return None
        return f


class Prog:
    ENG = ("pe", "act", "dve", "pool", "sp")
    CAP_C = 30000
    CAP_D = 1800

    def __init__(self, nc, es):
        self.nc, self.es = nc, es
        self.ops = {e: [] for e in self.ENG}
        self.state = {}
        self.seen = {e: {} for e in self.ENG}
        self.sems = {}
        self.cnt = {}
        self.tile_init = {}
        self.tiles_live = []
        self.freed = []
        self.tile_keys = {}
        self.arena = es.enter_context(nc.sbuf_tensor("arena", [128, ARENA_BYTES], U8))
        self.top = 0
        self.nsem = 0
        self.bank_rr = 0
        self.reserved_banks = set()
        self.psum = es.enter_context(nc.psum_tensor("psum", [128, 8 * 512], F32))
        self.n_ops = 0
        self.final = {}

    def alloc(self, name, shape, dtype):
        esz = 4 if dtype in (F32, I32) else 2
        n = int(np.prod(shape)) * esz
        n = (n + 63) // 64 * 64
        lo = self.top
        hi = lo + n
        assert hi <= ARENA_BYTES, f"arena overflow allocating {name}: {hi}"
        self.top = hi
        ap = self.arena[:, lo:hi].bitcast(dtype)
        used = int(np.prod(shape))
        ap = ap[:, 0:used]
        if len(shape) == 2:
            ap = ap.rearrange("p (a b) -> p a b", b=shape[1])
        elif len(shape) == 3:
            ap = ap.rearrange("p (a b c) -> p a b c", b=shape[1], c=shape[2])
        elif len(shape) == 4:
            ap = ap.rearrange("p (a b c d) -> p a b c d", b=shape[1], c=shape[2], d=shape[3])
        name = f"{name}#{len(self.tile_keys)}"
        t = Tile(name, ap, lo, hi)
        inh = []
        for (flo, fhi, toks) in self.freed:
            if flo < hi and lo < fhi:
                inh.extend(toks)
        self.tile_init[name] = inh
        self.tile_keys[name] = set()
        self.tiles_live.append(t)
        return t

    def mark(self):
        return (self.top, len(self.tiles_live))

    def release(self, mk):
        top, nlive = mk
        for t in self.tiles_live[nlive:]:
            toks = list(self.tile_init[t.name])
            for k in self.tile_keys[t.name]:
                st = self.state.get(k)
                if st:
                    toks.extend(st[0].items()); toks.extend(st[1].items())
            best = {}
            for (s, v) in toks:
                if v > best.get(s, -1):
                    best[s] = v
            self.freed.append((t.lo, t.hi, list(best.items())))
        del self.tiles_live[nlive:]
        self.top = top

    def bank(self, i):
        return self.psum[:, i * 512:(i + 1) * 512]

    def next_bank(self):
        while True:
            b = self.bank_rr % 8
            self.bank_rr += 1
            if b not in self.reserved_banks:
                return b

    def _key(self, k):
        assert isinstance(k, (Tile, tuple, str)), f"bad dependency key {type(k)}"
        if isinstance(k, Tile):
            k = (k.name, None)
        elif isinstance(k, tuple) and isinstance(k[0], Tile):
            k = (k[0].name,) + tuple(k[1:])
        if isinstance(k, tuple) and k[0] in self.tile_keys:
            self.tile_keys[k[0]].add(k)
        return k

    def _getstate(self, k):
        st = self.state.get(k)
        if st is None:
            inh = self.tile_init.get(k[0], []) if isinstance(k, tuple) else []
            d = {}
            for (s_, v_) in inh:
                if v_ > d.get(s_, -1):
                    d[s_] = v_
            st = [d, {}]
            self.state[k] = st
        return st

    DMA_K = {"sp": 16, "pool": 12, "act": 4, "dve": 2, "pe": 2}

    def _token(self, eng, isdma):
        if isdma:
            K = self.DMA_K[eng]
            c = self.cnt.get((eng, "d"), 0)
            self.cnt[(eng, "d")] = c + 1
            j, m = c % K, c // K
            sk = (eng, "d", j)
            if sk not in self.sems:
                self.sems[sk] = self.es.enter_context(self.nc.semaphore(f"s_{eng}_d{j}"))
                self.nsem += 1
            forced = (sk, 16 * m) if m > 0 else None
            self.final[sk] = 16 * (m + 1)
            return (sk, 16 * (m + 1)), forced
        c = self.cnt.get((eng, "c"), 0)
        epoch, idx = divmod(c, self.CAP_C)
        self.cnt[(eng, "c")] = c + 1
        sk = (eng, "c", epoch)
        if sk not in self.sems:
            self.sems[sk] = self.es.enter_context(self.nc.semaphore(f"s_{eng}_c{epoch}"))
            self.nsem += 1
        self.final[sk] = idx + 1
        return (sk, idx + 1), None

    def op(self, eng, fn, r=(), w=(), dma=False):
        o = Op()
        rec = _Rec()
        fn(rec)
        assert len(rec.calls) >= 1
        o.eng, o.calls, o.isdma = eng, rec.calls, dma
        need = {}
        rk = [self._key(k) for k in r]
        wk = [self._key(k) for k in w]
        for k in rk:
            st = self._getstate(k)
            for (s, v) in st[0].items():
                if v > need.get(s, -1):
                    need[s] = v
            if isinstance(k, tuple) and k[0] == "ps":
                for (s, v) in st[1].items():
                    if s[0] != eng and v > need.get(s, -1):
                        need[s] = v
        for k in wk:
            st = self._getstate(k)
            for (s, v) in list(st[0].items()) + list(st[1].items()):
                if v > need.get(s, -1):
                    need[s] = v
        tok, forced = self._token(eng, dma)
        if forced is not None and forced[1] > need.get(forced[0], -1):
            need[forced[0]] = forced[1]
        waits = []
        seen = self.seen[eng]
        for s, v in need.items():
            if s[0] == "pe" and eng == "pe" and s[1] == "c":
                continue
            if seen.get(s, -1) >= v:
                continue
            seen[s] = v
            waits.append((s, v))
        o.waits = waits
        o.sem, o.val = tok
        for k in rk:
            self.state[k][1][tok[0]] = tok[1]
        for k in wk:
            self.state[k] = [{tok[0]: tok[1]}, {}]
        self.ops[eng].append(o)
        self.n_ops += 1
        return o

    def dma(self, eng, out, in_, r=(), w=(), **kw):
        return self.op(eng, lambda e: e.dma_start(out=out, in_=in_, **kw), r=r, w=w, dma=True)

    def emit(self):
        nc = self.nc
        with nc.Block() as block:
            def run(engname):
                def f(e):
                    for o in self.ops[engname]:
                        for (s, v) in o.waits:
                            e.wait_ge(self.sems[s], v)
                        ins = None
                        for (nm_, a_, k_) in o.calls:
                            ins = getattr(e, nm_)(*a_, **k_)
                        ins.then_inc(self.sems[o.sem], 16 if o.isdma else 1)
                    if engname == "sp":
                        for sk, v in self.final.items():
                            e.wait_ge(self.sems[sk], v)
                return f
            block.tensor(run("pe"))
            block.scalar(run("act"))
            block.vector(run("dve"))
            block.gpsimd(run("pool"))
            block.sync(run("sp"))


def mk(ap, dims, off=0):
    return bass.AP(ap.tensor, ap.offset + off, [list(ap.ap[0])] + [list(d) for d in dims])


def make_consts():
    c = {}
    c["ident"] = np.eye(128, dtype=np.float32)
    c["identbf"] = np.eye(128, dtype=np.float32).astype(ml_dtypes.bfloat16)
    t = np.arange(2048)
    row = (t // 64).astype(np.float32)
    col = (t % 64).astype(np.float32)
    freqs = (np.float32(10000.0) ** (-np.arange(16, dtype=np.float32) / np.float32(16))).astype(np.float32)
    rope = np.zeros((128, 2, 2048), np.float32)
    for p in range(128):
        d = p % 64
        blk, i = divmod(d, 16)
        pos = row if blk < 2 else col
        ang = (pos * freqs[i]).astype(np.float32)
        rope[p, 0] = np.cos(ang)
        rope[p, 1] = -np.sin(ang) if blk in (0, 2) else np.sin(ang)
    c["rope"] = rope
    kl = np.arange(128)[:, None]
    ql = np.arange(128)[None, :]
    mb = np.zeros((128, 2, 512), np.float32)
    lo = np.where(kl >= ql, 0.0, NEG)
    hi = np.where(kl <= ql, 0.0, NEG)
    for hq in range(4):
        mb[:, 0, hq * 128:(hq + 1) * 128] = lo
        mb[:, 1, hq * 128:(hq + 1) * 128] = hi
    c["maskb"] = mb.astype(ml_dtypes.bfloat16)
    xs = np.zeros((128, 8, 240), np.float32)
    for g in range(8):
        for ci in range(16):
            xs[g * 16 + ci, g, 112 + ci] = 1.0
    c["xsel"] = xs.astype(ml_dtypes.bfloat16)
    ys = np.zeros((128, 8, 128), np.float32)
    for g in range(8):
        for j in range(8):
            for co in range(16):
                ys[j * 16 + co, g, g * 16 + co] = 1.0
    c["ysel"] = ys.astype(ml_dtypes.bfloat16)
    mj = np.zeros((128, 8), np.float32)
    for j in range(8):
        mj[j * 16:(j + 1) * 16, j] = 1.0
    c["maskj"] = mj
    tm = np.zeros((128, 2, 128), np.float32)
    s_idx = (np.arange(128) // 16)[:, None]
    j_idx = (np.arange(128) // 16)[None, :]
    tm[:, 0, :] = (j_idx >= s_idx)
    tm[:, 1, :] = (j_idx <= s_idx)
    c["toepm"] = tm
    vs = np.zeros((1, 128), np.float32)
    vs[0, 64:] = 1.0
    c["vsink"] = vs.astype(ml_dtypes.bfloat16)
    return c


CONST_SPECS = [("ident", [128, 128], F32), ("identbf", [128, 128], BF16), ("rope", [128, 2, 2048], F32),
               ("maskb", [128, 2, 512], BF16), ("xsel", [128, 8, 240], BF16), ("ysel", [128, 8, 128], BF16),
               ("maskj", [128, 8], F32), ("toepm", [128, 2, 128], F32), ("vsink", [1, 128], BF16)]

IN_SPECS = [("xp", [512, D]), ("xs", [2048, D]), ("ck", [2, PAST, 128]), ("cv", [2, PAST, 128]),
            ("sre", [2, 2, 16, 64]), ("sim", [2, 2, 16, 64]), ("cvec", [2, D]),
            ("norm_mix", [2, D]), ("norm_ffn", [2, D]), ("norm_final", [D]),
            ("w_ada", [2, D, 6 * D]), ("b_ada", [2, 6 * D]), ("w_in", [2, D, IN_DIM]), ("w_out", [2, D, D]),
            ("attn_sink", [2, 8]), ("sc_conv", [2, 256, 3]),
            ("ssm_lam_re", [2, 2, 16, 64]), ("ssm_lam_im", [2, 2, 16, 64]), ("ssm_log_dt", [2, 2, 16]),
            ("ssm_b_re", [2, 2, 16, 64, 16]), ("ssm_b_im", [2, 2, 16, 64, 16]),
            ("ssm_c_re", [2, 2, 16, 16, 64]), ("ssm_c_im", [2, 2, 16, 16, 64]),
            ("ssm_d", [2, 16, 16]), ("ssm_w_glu", [2, 256, 256]),
            ("ffn_w_up", [2, D, 2 * DFF]), ("ffn_conv", [2, 2 * DFF, 3]), ("ffn_w_down", [2, DFF, D])]

OUT_SPECS = [("yp", [512, D]), ("ys", [2048, D]), ("nk", [2, 2, 256, 128]), ("nv", [2, 2, 256, 128]),
             ("nsr", [2, 2, 2, 16, 64]), ("nsi", [2, 2, 2, 16, 64])]


class Builder:
    def __init__(self, stages=("prep", "P", "S"), dbg=()):
        self.stages = stages
        self.dbg_specs = list(dbg)
        self.nc = nc = bass.Bass("TRN2", target_bir_lowering=False)
        self.es = ExitStack()
        class _Lazy(dict):
            def __init__(s_, specs, prefix):
                super().__init__()
                s_.specs, s_.prefix = specs, prefix

            def __missing__(s_, name):
                shape, dt = s_.specs[name]
                ap = nc.dram_tensor(s_.prefix + name, shape, dt, kind="ExternalInput").ap()
                s_[name] = ap
                return ap
        self.I = _Lazy({n: (sh, F32) for n, sh in IN_SPECS}, "")
        self.C = _Lazy({n: (sh, dt) for n, sh, dt in CONST_SPECS}, "c_")
        self.O = {}
        for name, shape in OUT_SPECS:
            self.O[name] = nc.dram_tensor(name, shape, F32, kind="ExternalOutput").ap()
        for name, shape, dt_ in self.dbg_specs:
            self.O[name] = nc.dram_tensor(name, shape, dt_, kind="ExternalOutput").ap()
        self.W = {}
        for name, kc, n in [("win", 8, IN_DIM), ("wout", 8, D), ("wup", 8, 2 * DFF), ("wdn", 22, D), ("wglu", 2, 256)]:
            self.W[name] = [nc.dram_tensor(f"s_{name}{l}", [128, kc, n], BF16, kind="Internal").ap() for l in range(2)]
        self.S_kt = [nc.dram_tensor(f"s_kt{l}", [128, 16, 128], BF16, kind="Internal").ap() for l in range(2)]
        self.S_ws = [nc.dram_tensor(f"s_ws{l}", [128, 2, 16, 128], BF16, kind="Internal").ap() for l in range(2)]
        self.S_wo = [nc.dram_tensor(f"s_wo{l}", [128, 2, 8, 2, 128], BF16, kind="Internal").ap() for l in range(2)]
        self.S_e = [nc.dram_tensor(f"s_e{l}", [128, 2, 2, 8, 256], F32, kind="Internal").ap() for l in range(2)]
        self.P = Prog(nc, self.es)

    def wkeys(self, name, l, c0, c1, kcs=None):
        nkc = {"win": 8, "wout": 8, "wup": 8, "wdn": 22, "wglu": 2}[name]
        ks = []
        for kc in (range(nkc) if kcs is None else kcs):
            for b in range(c0 // 2048, (c1 - 1) // 2048 + 1):
                ks.append(("W", name, l, kc, b))
        return ks

    def persistent(self):
        P = self.P
        self.ident = P.alloc("ident", [128], F32)
        self.identbf = P.alloc("identbf", [128], BF16)
        self.onesbf = P.alloc("onesbf", [128], BF16)
        self.modT = P.alloc("modT", [2, 48, 2], F32)
        self.gsc1 = P.alloc("gsc1", [2, 8, 2], F32)
        self.gsc2 = P.alloc("gsc2", [2, 8, 2], F32)
        self.nfT = P.alloc("nfT", [8], F32)
        self.a8mag = P.alloc("a8mag", [4, 8], F32)
        self.e1 = P.alloc("e1", [2, 4, 8], F32)
        self.esrow = P.alloc("esrow", [2, 8, 128], BF16)
        self.vsink = P.alloc("vsink", [128], BF16)
        self.scw = P.alloc("scw", [2, 2, 3], F32)
        self.fcw = P.alloc("fcw", [2, 44, 3], F32)
        self.maskj = P.alloc("maskj", [8], F32)
        P.dma("sp", self.ident[:], self.C["ident"][:, :], w=[self.ident])
        P.dma("sp", self.identbf[:], self.C["identbf"][:, :], w=[self.identbf])
        P.dma("sp", self.maskj[:], self.C["maskj"][:, :], w=[self.maskj])
        P.dma("sp", self.vsink[0:1, :], self.C["vsink"][:, :], w=[self.vsink])
        P.op("dve", lambda e: e.memset(self.onesbf[:], 1.0), w=[self.onesbf])
        I = self.I
        P.dma("sp", self.nfT[:], I["norm_final"].rearrange("(c p) -> p c", p=128), w=[self.nfT],
              allow_slow_non_contiguous=True)
        for l in range(2):
            P.dma("sp", self.scw[:, l], I["sc_conv"][l].rearrange("(c p) k -> p c k", p=128), w=[self.scw])
            P.dma("sp", self.fcw[:, l], I["ffn_conv"][l].rearrange("(c p) k -> p c k", p=128), w=[self.fcw])

    def prep_casts(self):
        P, I = self.P, self.I
        mk0 = P.mark()
        stage = [P.alloc(f"cst{i}", [2048], F32) for i in range(3)]
        outb = [P.alloc(f"cob{i}", [2048], BF16) for i in range(3)]
        engs = ["act", "dve", "pool"]
        i = 0
        for l in range(2):
            for name, src, K, N in [("win", "w_in", D, IN_DIM), ("wglu", "ssm_w_glu", 256, 256), ("wout", "w_out", D, D),
                                    ("wup", "ffn_w_up", D, 2 * DFF), ("wdn", "ffn_w_down", DFF, D)]:
                if self.cast_only is not None and name not in self.cast_only:
                    continue
                for kc in range(K // 128):
                    for b, c0 in enumerate(range(0, N, 2048)):
                        cw = min(2048, N - c0)
                        st, ob = stage[i % 3], outb[i % 3]
                        eng = engs[i % 3]
                        i += 1
                        P.dma("sp", st[:, 0:cw], I[src][l, kc * 128:(kc + 1) * 128, c0:c0 + cw], w=[st])
                        if eng == "act":
                            P.op("act", lambda e, st=st, ob=ob, cw=cw: e.copy(out=ob[:, 0:cw], in_=st[:, 0:cw]), r=[st], w=[ob])
                        else:
                            P.op(eng, lambda e, st=st, ob=ob, cw=cw: e.tensor_copy(out=ob[:, 0:cw], in_=st[:, 0:cw]), r=[st], w=[ob])
                        P.dma("pool", self.W[name][l][:, kc, c0:c0 + cw], ob[:, 0:cw], r=[ob], w=[("W", name, l, kc, b)])
        P.release(mk0)

    def prep_adaln(self):
        P, I = self.P, self.I
        mk0 = P.mark()
        craw = P.alloc("craw", [8, 2], F32)
        scT = P.alloc("scT", [8, 2], F32)
        bT = P.alloc("bT", [2, 48], F32)
        nmT = P.alloc("nmT", [2, 2, 8], F32)
        wa = [P.alloc(f"wa{i}", [8, 512], F32) for i in range(2)]
        for v in range(2):
            P.dma("sp", craw[:, :, v], I["cvec"][v].rearrange("(c p) -> p c", p=128), w=[craw], allow_slow_non_contiguous=True)
        P.op("act", lambda e: e.activation(out=scT[:], in_=craw[:], func=AF.Silu), r=[craw], w=[scT])
        for l in range(2):
            P.dma("sp", bT[:, l], I["b_ada"][l].rearrange("(c p) -> p c", p=128), w=[bT], allow_slow_non_contiguous=True)
            P.dma("sp", nmT[:, 0, l], I["norm_mix"][l].rearrange("(c p) -> p c", p=128), w=[nmT], allow_slow_non_contiguous=True)
            P.dma("sp", nmT[:, 1, l], I["norm_ffn"][l].rearrange("(c p) -> p c", p=128), w=[nmT], allow_slow_non_contiguous=True)
        for l in range(2):
            bk = P.next_bank()
            ps = P.bank(bk)
            for j in range(12):
                w = wa[j % 2]
                P.dma("sp", w[:], I["w_ada"][l, :, j * 512:(j + 1) * 512].rearrange("(kc p) n -> p kc n", p=128), w=[w])
                for oc in range(4):
                    col = (j * 4 + oc) * 2
                    for kc in range(8):
                        P.op("pe", lambda e, w=w, oc=oc, kc=kc, col=col, ps=ps: e.matmul(
                            ps[:, col:col + 2], lhsT=w[:, kc, oc * 128:(oc + 1) * 128], rhs=scT[:, kc, :],
                            start=(kc == 0), stop=(kc == 7)), r=[w, scT], w=[("ps", bk)])
            P.op("dve", lambda e, l=l, ps=ps: e.tensor_tensor(
                out=self.modT[:, l], in0=ps[:, 0:96].rearrange("p (a b) -> p a b", b=2),
                in1=mk(bT[:, l], [[1, 48], [0, 2]]), op=ALU.add), r=[("ps", bk), bT], w=[self.modT])
            P.op("dve", lambda e, l=l: e.scalar_tensor_tensor(
                out=self.gsc1[:, l], in0=self.modT[:, l, 8:16, :], scalar=1.0,
                in1=mk(nmT[:, 0, l], [[1, 8], [0, 2]]), op0=ALU.add, op1=ALU.mult), r=[self.modT, nmT], w=[self.gsc1])
            P.op("dve", lambda e, l=l: e.scalar_tensor_tensor(
                out=self.gsc2[:, l], in0=self.modT[:, l, 32:40, :], scalar=1.0,
                in1=mk(nmT[:, 1, l], [[1, 8], [0, 2]]), op0=ALU.add, op1=ALU.mult), r=[self.modT, nmT], w=[self.gsc2])
        P.release(mk0)

    def prep_ssm(self):
        P, I, C = self.P, self.I, self.C
        mk0 = P.mark()
        V = lambda fn, r, w: P.op("dve", fn, r=r, w=w)
        A = lambda fn, r, w: P.op("act", fn, r=r, w=w)
        LD = [(l, d) for l in range(2) for d in range(2)]
        lamr = P.alloc("lamr", [4, 8], F32); lami = P.alloc("lami", [4, 8], F32); ldt = P.alloc("ldt", [4, 8], F32)
        for i, (l, d) in enumerate(LD):
            for h in range(2):
                for tl, nm in ((lamr, "ssm_lam_re"), (lami, "ssm_lam_im")):
                    src = I[nm][l, d]
                    P.dma("sp", tl[64 * h:64 * h + 64, i, :], bass.AP(src.tensor, src.offset + h * 64, [[1, 64], [128, 8]]),
                          w=[tl], allow_slow_non_contiguous=True)
                src = I["ssm_log_dt"][l, d]
                P.dma("sp", ldt[64 * h:64 * h + 64, i, :], bass.AP(src.tensor, src.offset + h, [[0, 64], [2, 8]]),
                      w=[ldt], allow_slow_non_contiguous=True)
        names = ["dt", "lrdt", "mag", "ang", "rs", "rc", "sn", "cs", "ar", "ai", "den", "rden", "am1", "t1", "t2", "t3", "t4",
                 "fr", "fi", "mag2", "rm", "ivr", "ivi", "inv8"]
        T = {n: P.alloc(n, [4, 8], F32) for n in names}
        negpi = P.alloc("negpi", [1], F32)
        V(lambda e: e.memset(negpi[:], -math.pi), [], [negpi])
        qi = P.alloc("qi", [4, 8], I32)

        def taylor_exp(out_t, x_t, deg, tmp):
            V(lambda e: e.tensor_scalar(out=out_t[:], in0=x_t[:], scalar1=1.0 / deg, scalar2=1.0, op0=ALU.mult, op1=ALU.add), [x_t], [out_t])
            for k in range(deg - 1, 0, -1):
                V(lambda e: e.tensor_tensor(out=tmp[:], in0=out_t[:], in1=x_t[:], op=ALU.mult), [out_t, x_t], [tmp])
                V(lambda e, k=k: e.tensor_scalar(out=out_t[:], in0=tmp[:], scalar1=1.0 / k, scalar2=1.0, op0=ALU.mult, op1=ALU.add), [tmp], [out_t])
        V(lambda e: e.tensor_copy(out=qi[:], in_=ldt[:]), [ldt], [qi])
        V(lambda e: e.tensor_copy(out=T["t1"][:], in_=qi[:]), [qi], [T["t1"]])
        V(lambda e: e.tensor_tensor(out=T["t2"][:], in0=ldt[:], in1=T["t1"][:], op=ALU.subtract), [ldt, T["t1"]], [T["t2"]])
        taylor_exp(T["t3"], T["t2"], 12, T["t4"])
        V(lambda e: e.memset(T["dt"][:], 0.0), [], [T["dt"]])
        for j in range(-10, 1):
            V(lambda e, j=j: e.tensor_scalar(out=T["t4"][:], in0=T["t1"][:], scalar1=float(j), scalar2=math.exp(j), op0=ALU.is_equal, op1=ALU.mult),
              [T["t1"]], [T["t4"]])
            V(lambda e: e.tensor_tensor(out=T["dt"][:], in0=T["dt"][:], in1=T["t4"][:], op=ALU.add), [T["dt"], T["t4"]], [T["dt"]])
        V(lambda e: e.tensor_tensor(out=T["dt"][:], in0=T["dt"][:], in1=T["t3"][:], op=ALU.mult), [T["dt"], T["t3"]], [T["dt"]])
        V(lambda e: e.tensor_tensor(out=T["lrdt"][:], in0=lamr[:], in1=T["dt"][:], op=ALU.mult), [lamr, T["dt"]], [T["lrdt"]])
        taylor_exp(T["mag"], T["lrdt"], 7, T["t4"])
        V(lambda e: e.tensor_tensor(out=T["t1"][:], in0=T["mag"][:], in1=T["mag"][:], op=ALU.mult), [T["mag"]], [T["t1"]])
        V(lambda e: e.tensor_tensor(out=T["t2"][:], in0=T["t1"][:], in1=T["t1"][:], op=ALU.mult), [T["t1"]], [T["t2"]])
        V(lambda e: e.tensor_tensor(out=self.a8mag[:], in0=T["t2"][:], in1=T["t2"][:], op=ALU.mult), [T["t2"]], [self.a8mag])
        V(lambda e: e.reciprocal(out=T["inv8"][:], in_=self.a8mag[:]), [self.a8mag], [T["inv8"]])
        V(lambda e: e.tensor_tensor(out=T["ang"][:], in0=lami[:], in1=T["dt"][:], op=ALU.mult), [lami, T["dt"]], [T["ang"]])
        def range_reduce(out_t, add):
            V(lambda e: e.tensor_scalar(out=T["t1"][:], in0=T["ang"][:], scalar1=add, scalar2=1.0 / TWO_PI, op0=ALU.add, op1=ALU.mult),
              [T["ang"]], [T["t1"]])
            V(lambda e: e.tensor_copy(out=qi[:], in_=T["t1"][:]), [T["t1"]], [qi])
            V(lambda e: e.tensor_copy(out=T["t2"][:], in_=qi[:]), [qi], [T["t2"]])
            V(lambda e: e.scalar_tensor_tensor(out=T["t3"][:], in0=T["t2"][:], scalar=-TWO_PI, in1=T["ang"][:], op0=ALU.mult, op1=ALU.add),
              [T["t2"], T["ang"]], [T["t3"]])
            V(lambda e: e.tensor_scalar_add(out=T["t3"][:], in0=T["t3"][:], scalar1=add), [T["t3"]], [T["t3"]])
            V(lambda e: e.tensor_scalar(out=T["t4"][:], in0=T["t3"][:], scalar1=math.pi, scalar2=-TWO_PI, op0=ALU.is_gt, op1=ALU.mult),
              [T["t3"]], [T["t4"]])
            V(lambda e: e.tensor_tensor(out=T["t3"][:], in0=T["t3"][:], in1=T["t4"][:], op=ALU.add), [T["t3"], T["t4"]], [T["t3"]])
            V(lambda e: e.tensor_scalar(out=T["t4"][:], in0=T["t3"][:], scalar1=-math.pi, scalar2=TWO_PI, op0=ALU.is_lt, op1=ALU.mult),
              [T["t3"]], [T["t4"]])
            V(lambda e: e.tensor_tensor(out=out_t[:], in0=T["t3"][:], in1=T["t4"][:], op=ALU.add), [T["t3"], T["t4"]], [out_t])
        range_reduce(T["rs"], 0.0)
        xx, x2, ps_, pc_ = T["t1"], T["t2"], T["t3"], T["t4"]
        V(lambda e: e.tensor_scalar_mul(out=xx[:], in0=T["rs"][:], scalar1=0.25), [T["rs"]], [xx])
        V(lambda e: e.tensor_tensor(out=x2[:], in0=xx[:], in1=xx[:], op=ALU.mult), [xx], [x2])

        def horner(p, coefs):
            V(lambda e: e.tensor_scalar(out=p[:], in0=x2[:], scalar1=coefs[0], scalar2=1.0, op0=ALU.mult, op1=ALU.add), [x2], [p])
            for cf in coefs[1:]:
                V(lambda e: e.tensor_tensor(out=p[:], in0=p[:], in1=x2[:], op=ALU.mult), [p, x2], [p])
                V(lambda e, cf=cf: e.tensor_scalar(out=p[:], in0=p[:], scalar1=cf, scalar2=1.0, op0=ALU.mult, op1=ALU.add), [p], [p])
        horner(ps_, [-1.0 / 110.0, -1.0 / 72.0, -1.0 / 42.0, -1.0 / 20.0, -1.0 / 6.0])
        V(lambda e: e.tensor_tensor(out=ps_[:], in0=ps_[:], in1=xx[:], op=ALU.mult), [ps_, xx], [ps_])
        horner(pc_, [-1.0 / 90.0, -1.0 / 56.0, -1.0 / 30.0, -1.0 / 12.0, -1.0 / 2.0])
        sA, cA = T["sn"], T["cs"]
        for it in range(2):
            V(lambda e: e.scalar_tensor_tensor(out=sA[:], in0=ps_[:], scalar=2.0, in1=pc_[:], op0=ALU.mult, op1=ALU.mult), [ps_, pc_], [sA])
            V(lambda e: e.tensor_tensor(out=cA[:], in0=ps_[:], in1=ps_[:], op=ALU.mult), [ps_], [cA])
            V(lambda e: e.tensor_scalar(out=cA[:], in0=cA[:], scalar1=-2.0, scalar2=1.0, op0=ALU.mult, op1=ALU.add), [cA], [cA])
            if it == 0:
                V(lambda e: e.tensor_copy(out=ps_[:], in_=sA[:]), [sA], [ps_])
                V(lambda e: e.tensor_copy(out=pc_[:], in_=cA[:]), [cA], [pc_])

        def tt(o, a, b, op):
            V(lambda e: e.tensor_tensor(out=o[:], in0=a[:], in1=b[:], op=op), [a, b], [o])
        tt(T["ar"], T["mag"], T["cs"], ALU.mult)
        tt(T["ai"], T["mag"], T["sn"], ALU.mult)
        tt(T["t1"], lamr, lamr, ALU.mult)
        tt(T["t2"], lami, lami, ALU.mult)
        tt(T["den"], T["t1"], T["t2"], ALU.add)
        V(lambda e: e.reciprocal(out=T["rden"][:], in_=T["den"][:]), [T["den"]], [T["rden"]])
        V(lambda e: e.tensor_scalar_add(out=T["am1"][:], in0=T["ar"][:], scalar1=-1.0), [T["ar"]], [T["am1"]])
        tt(T["t1"], T["am1"], lamr, ALU.mult)
        tt(T["t2"], T["ai"], lami, ALU.mult)
        tt(T["t3"], T["t1"], T["t2"], ALU.add)
        tt(T["fr"], T["t3"], T["rden"], ALU.mult)
        tt(T["t1"], T["ai"], lamr, ALU.mult)
        tt(T["t2"], T["am1"], lami, ALU.mult)
        tt(T["t3"], T["t1"], T["t2"], ALU.subtract)
        tt(T["fi"], T["t3"], T["rden"], ALU.mult)
        tt(T["t1"], T["ar"], T["ar"], ALU.mult)
        tt(T["t2"], T["ai"], T["ai"], ALU.mult)
        tt(T["mag2"], T["t1"], T["t2"], ALU.add)
        V(lambda e: e.reciprocal(out=T["rm"][:], in_=T["mag2"][:]), [T["mag2"]], [T["rm"]])
        tt(T["ivr"], T["ar"], T["rm"], ALU.mult)
        V(lambda e: e.scalar_tensor_tensor(out=T["ivi"][:], in0=T["ai"][:], scalar=-1.0, in1=T["rm"][:], op0=ALU.mult, op1=ALU.mult),
          [T["ai"], T["rm"]], [T["ivi"]])
        if self.cut == 1:
            P.release(mk0); return
        apr = P.alloc("apr", [4, 17, 8], F32); api = P.alloc("api", [4, 17, 8], F32)
        V(lambda e: e.memset(apr[:, :, 8, :], 1.0), [], [(apr, 8)])
        V(lambda e: e.memset(api[:, :, 8, :], 0.0), [], [(api, 8)])
        V(lambda e: e.tensor_copy(out=apr[:, :, 9, :], in_=T["ar"][:]), [T["ar"]], [(apr, 9)])
        V(lambda e: e.tensor_copy(out=api[:, :, 9, :], in_=T["ai"][:]), [T["ai"]], [(api, 9)])
        V(lambda e: e.tensor_copy(out=apr[:, :, 7, :], in_=T["ivr"][:]), [T["ivr"]], [(apr, 7)])
        V(lambda e: e.tensor_copy(out=api[:, :, 7, :], in_=T["ivi"][:]), [T["ivi"]], [(api, 7)])

        def cmul_small(k_out, k_in, br, bi):
            xr, xi = apr[:, :, k_in, :], api[:, :, k_in, :]
            V(lambda e: e.tensor_tensor(out=T["t1"][:], in0=xr, in1=br[:], op=ALU.mult), [(apr, k_in), br], [T["t1"]])
            V(lambda e: e.tensor_tensor(out=T["t2"][:], in0=xi, in1=bi[:], op=ALU.mult), [(api, k_in), bi], [T["t2"]])
            V(lambda e: e.tensor_tensor(out=apr[:, :, k_out, :], in0=T["t1"][:], in1=T["t2"][:], op=ALU.subtract), [T["t1"], T["t2"]], [(apr, k_out)])
            V(lambda e: e.tensor_tensor(out=T["t3"][:], in0=xr, in1=bi[:], op=ALU.mult), [(apr, k_in), bi], [T["t3"]])
            V(lambda e: e.tensor_tensor(out=T["t4"][:], in0=xi, in1=br[:], op=ALU.mult), [(api, k_in), br], [T["t4"]])
            V(lambda e: e.tensor_tensor(out=api[:, :, k_out, :], in0=T["t3"][:], in1=T["t4"][:], op=ALU.add), [T["t3"], T["t4"]], [(api, k_out)])
        for k in range(9, 16):
            cmul_small(k + 1, k, T["ar"], T["ai"])
        for k in range(7, 0, -1):
            cmul_small(k - 1, k, T["ivr"], T["ivi"])
        APW_R = [(apr, k) for k in range(17)]
        APW_I = [(api, k) for k in range(17)]
        V(lambda e: e.tensor_tensor(out=self.e1[:, 0], in0=apr[:, :, 16, :], in1=T["inv8"][:], op=ALU.mult), [(apr, 16), T["inv8"]], [self.e1])
        V(lambda e: e.tensor_tensor(out=self.e1[:, 1], in0=api[:, :, 16, :], in1=T["inv8"][:], op=ALU.mult), [(api, 16), T["inv8"]], [self.e1])

        if self.cut == 2:
            P.release(mk0); return
        Et = P.alloc("Et", [2, 8, 256], F32)
        wk = P.alloc("wk", [2, 2, 8], F32)
        et1 = P.alloc("et1", [8, 128], F32); et2 = P.alloc("et2", [8, 128], F32)
        Br = P.alloc("Br", [8, 16], F32); Bi = P.alloc("Bi", [8, 16], F32)
        bbr = P.alloc("bbr", [8, 16], F32); bbi = P.alloc("bbi", [8, 16], F32)
        Cr = P.alloc("Cr", [8, 16], F32); Ci = P.alloc("Ci", [8, 16], F32)
        cn = P.alloc("cn", [2, 64], F32)
        pbr = P.alloc("pbr", [8, 8, 16], F32); pbi = P.alloc("pbi", [8, 8, 16], F32)
        pcr = P.alloc("pcr", [8, 8, 16], F32); pci = P.alloc("pci", [8, 8, 16], F32)
        q1 = P.alloc("q1", [8, 8, 16], F32); q2 = P.alloc("q2", [8, 8, 16], F32)
        wsb = P.alloc("wsb", [16, 128], BF16)
        wob = P.alloc("wob", [8, 2, 128], BF16)
        ktacc = P.alloc("ktacc", [16, 128], F32)
        ktb = P.alloc("ktb", [16, 128], BF16)
        toep = P.alloc("toep", [2, 128], F32)
        dtab = P.alloc("dtab", [16], F32)
        ktmp = P.alloc("ktmp", [128], F32)
        P.dma("sp", toep[:], C["toepm"][:, :, :], w=[toep])

        def bc_last(ap2, n):
            return mk(ap2, [list(ap2.ap[1]), [0, n]])

        def cprod(outr, outi, kstart, kstep, Xr, Xi, neg_im, xkeys):
            a_r = apr[:, ld, kstart, :]
            a_i = api[:, ld, kstart, :]
            AR = mk(a_r, [[1, 8], [8 * kstep, 8], [0, 16]])
            AI = mk(a_i, [[1, 8], [8 * kstep, 8], [0, 16]])
            XR = mk(Xr[:], [[16, 8], [0, 8], [1, 16]])
            XI = mk(Xi[:], [[16, 8], [0, 8], [1, 16]])
            V(lambda e: e.tensor_tensor(out=q1[:], in0=AR, in1=XR, op=ALU.mult), APW_R + xkeys, [q1])
            V(lambda e: e.tensor_tensor(out=q2[:], in0=AI, in1=XI, op=ALU.mult), APW_I + xkeys, [q2])
            V(lambda e: e.tensor_tensor(out=outr[:], in0=q1[:], in1=q2[:], op=ALU.subtract), [q1, q2], [outr])
            V(lambda e: e.tensor_tensor(out=q1[:], in0=AR, in1=XI, op=ALU.mult), APW_R + xkeys, [q1])
            V(lambda e: e.tensor_tensor(out=q2[:], in0=AI, in1=XR, op=ALU.mult), APW_I + xkeys, [q2])
            if neg_im:
                V(lambda e: e.scalar_tensor_tensor(out=outi[:], in0=q1[:], scalar=-1.0, in1=q2[:], op0=ALU.mult, op1=ALU.subtract),
                  [q1, q2], [outi])
            else:
                V(lambda e: e.tensor_tensor(out=outi[:], in0=q1[:], in1=q2[:], op=ALU.add), [q1, q2], [outi])

        for ld, (l, d) in enumerate(LD):
            V(lambda e: e.memset(Et[:, 0, :, 0:1], 1.0), [], [Et])
            V(lambda e: e.memset(Et[:, 1, :, 0:1], 0.0), [], [Et])
            V(lambda e, ld=ld: e.tensor_copy(out=wk[:, 0, 0, :], in_=self.e1[:, 0, ld, :]), [self.e1], [wk])
            V(lambda e, ld=ld: e.tensor_copy(out=wk[:, 0, 1, :], in_=self.e1[:, 1, ld, :]), [self.e1], [wk])
            for k in range(8):
                n = 1 << k
                pp, qq = k % 2, (k + 1) % 2
                wr = mk(wk[:, pp, 0, :], [[1, 8], [0, n]])
                wi = mk(wk[:, pp, 1, :], [[1, 8], [0, n]])
                t1v = et1[:, :, 0:n]; t2v = et2[:, :, 0:n]
                V(lambda e, n=n, wr=wr, t1v=t1v: e.tensor_tensor(out=t1v, in0=Et[:, 0, :, 0:n], in1=wr, op=ALU.mult), [Et, wk], [et1])
                V(lambda e, n=n, wi=wi, t2v=t2v: e.tensor_tensor(out=t2v, in0=Et[:, 1, :, 0:n], in1=wi, op=ALU.mult), [Et, wk], [et2])
                V(lambda e, n=n, t1v=t1v, t2v=t2v: e.tensor_tensor(out=Et[:, 0, :, n:2 * n], in0=t1v, in1=t2v, op=ALU.subtract), [et1, et2], [Et])
                V(lambda e, n=n, wi=wi, t1v=t1v: e.tensor_tensor(out=t1v, in0=Et[:, 0, :, 0:n], in1=wi, op=ALU.mult), [Et, wk], [et1])
                V(lambda e, n=n, wr=wr, t2v=t2v: e.tensor_tensor(out=t2v, in0=Et[:, 1, :, 0:n], in1=wr, op=ALU.mult), [Et, wk], [et2])
                V(lambda e, n=n, t1v=t1v, t2v=t2v: e.tensor_tensor(out=Et[:, 1, :, n:2 * n], in0=t1v, in1=t2v, op=ALU.add), [et1, et2], [Et])
                if k < 7:
                    a_r, a_i = wk[:, pp, 0, :], wk[:, pp, 1, :]
                    s1 = et1[:, :, 0]; s2 = et2[:, :, 0]
                    V(lambda e, a_r=a_r, s1=s1: e.tensor_tensor(out=s1, in0=a_r, in1=a_r, op=ALU.mult), [wk], [et1])
                    V(lambda e, a_i=a_i, s2=s2: e.tensor_tensor(out=s2, in0=a_i, in1=a_i, op=ALU.mult), [wk], [et2])
                    V(lambda e, qq=qq, s1=s1, s2=s2: e.tensor_tensor(out=wk[:, qq, 0, :], in0=s1, in1=s2, op=ALU.subtract), [et1, et2], [wk])
                    V(lambda e, a_r=a_r, a_i=a_i, s1=s1: e.tensor_tensor(out=s1, in0=a_r, in1=a_i, op=ALU.mult), [wk], [et1])
                    V(lambda e, qq=qq, s1=s1: e.tensor_scalar_mul(out=wk[:, qq, 1, :], in0=s1, scalar1=2.0), [et1], [wk])
            P.dma("pool", self.S_e[l][:, d], Et[:], r=[Et], w=[("S_e", l, d)])
            if self.cut == 3:
                continue
            for h in range(2):
                for tl, nm in ((Br, "ssm_b_re"), (Bi, "ssm_b_im")):
                    src = I[nm][l, d]
                    P.dma("sp", tl[64 * h:64 * h + 64, :, :], bass.AP(src.tensor, src.offset + h * 1024, [[16, 64], [2048, 8], [1, 16]]), w=[tl])
            FR = bc_last(T["fr"][:, ld, :], 16); FI = bc_last(T["fi"][:, ld, :], 16)
            V(lambda e, FR=FR: e.tensor_tensor(out=q1[:, :, 0, :], in0=Br[:], in1=FR, op=ALU.mult), [Br, T["fr"]], [q1])
            V(lambda e, FI=FI: e.tensor_tensor(out=q2[:, :, 0, :], in0=Bi[:], in1=FI, op=ALU.mult), [Bi, T["fi"]], [q2])
            V(lambda e: e.tensor_tensor(out=bbr[:], in0=q1[:, :, 0, :], in1=q2[:, :, 0, :], op=ALU.subtract), [q1, q2], [bbr])
            V(lambda e, FR=FR: e.tensor_tensor(out=q1[:, :, 0, :], in0=Bi[:], in1=FR, op=ALU.mult), [Bi, T["fr"]], [q1])
            V(lambda e, FI=FI: e.tensor_tensor(out=q2[:, :, 0, :], in0=Br[:], in1=FI, op=ALU.mult), [Br, T["fi"]], [q2])
            V(lambda e: e.tensor_tensor(out=bbi[:], in0=q1[:, :, 0, :], in1=q2[:, :, 0, :], op=ALU.add), [q1, q2], [bbi])
            for Cx, nm in ((Cr, "ssm_c_re"), (Ci, "ssm_c_im")):
                P.dma("sp", cn[:], I[nm][l, d].rearrange("(t g) c n -> (g c) t n", t=2), w=[cn])
                for t in range(2):
                    bk = P.next_bank(); ps = P.bank(bk)
                    P.op("pe", lambda e, t=t, ps=ps: e.transpose(out=ps[0:64, 0:128], in_=cn[:, t, :], identity=self.ident[:]),
                         r=[cn, self.ident], w=[("ps", bk)])
                    for par in range(2):
                        src_ = mk(ps[0:64, 0:128], [[32, 4], [1, 16]], off=par * 16)
                        V(lambda e, Cx=Cx, t=t, par=par, src_=src_: e.tensor_copy(out=Cx[64 * par:64 * par + 64, 4 * t:4 * t + 4, :], in_=src_),
                          [("ps", bk)], [Cx])
            if self.cut == 4:
                continue
            if d == 0:
                cprod(pbr, pbi, 15, -1, bbr, bbi, False, [bbr, bbi])
                cprod(pcr, pci, 1, 1, Cr, Ci, True, [Cr, Ci])
            else:
                cprod(pbr, pbi, 8, 1, bbr, bbi, False, [bbr, bbi])
                cprod(pcr, pci, 8, -1, Cr, Ci, True, [Cr, Ci])
            if self.cut == 5:
                continue
            for ggp in range(4):
                bk = P.next_bank(); ps = P.bank(bk)
                for ggl in range(2):
                    gg = 2 * ggp + ggl
                    for comp, pb in enumerate((pbr, pbi)):
                        col = (ggl * 2 + comp) * 128
                        P.op("pe", lambda e, pb=pb, gg=gg, col=col, ps=ps: e.transpose(
                            out=ps[:, col:col + 128], in_=pb[:, gg, :, :].rearrange("p s c -> p (s c)"),
                            identity=self.ident[:]), r=[pb, self.ident], w=[("ps", bk)])
                for comp in range(2):
                    src_ = mk(ps[:, comp * 128:comp * 128 + 1], [[256, 2], [64, 2], [1, 64]])
                    dst_ = mk(wsb[:, 4 * ggp, comp * 64:comp * 64 + 1], [[256, 2], [128, 2], [1, 64]])
                    V(lambda e, src_=src_, dst_=dst_: e.tensor_copy(out=dst_, in_=src_), [("ps", bk)], [wsb])
            P.dma("pool", self.S_ws[l][:, d], wsb[:], r=[wsb], w=[("S_ws", l, d)])
            if self.cut == 6:
                continue
            if d == 0:
                for s8 in range(8):
                    src = I["ssm_d"][l]
                    P.dma("sp", dtab[16 * s8:16 * s8 + 16, :], bass.AP(src.tensor, src.offset, [[1, 16], [16, 16]]), w=[dtab],
                          allow_slow_non_contiguous=True)
            for g in range(16):
                h, gg = g % 2, g // 2
                bk = P.next_bank(); ps = P.bank(bk)
                P.op("pe", lambda e, h=h, gg=gg, ps=ps: e.matmul(
                    ps[:, 0:128], lhsT=pbr[64 * h:64 * h + 64, gg, :, :].rearrange("p s c -> p (s c)"),
                    rhs=pcr[64 * h:64 * h + 64, gg, :, :].rearrange("p s c -> p (s c)"), start=True, stop=False),
                    r=[pbr, pcr], w=[("ps", bk)])
                P.op("pe", lambda e, h=h, gg=gg, ps=ps: e.matmul(
                    ps[:, 0:128], lhsT=pbi[64 * h:64 * h + 64, gg, :, :].rearrange("p s c -> p (s c)"),
                    rhs=pci[64 * h:64 * h + 64, gg, :, :].rearrange("p s c -> p (s c)"), start=False, stop=True),
                    r=[pbi, pci], w=[("ps", bk)])
                if d == 0:
                    V(lambda e, g=g, ps=ps: e.tensor_tensor(out=ktacc[:, g, :], in0=ps[:, 0:128], in1=toep[:, 0, :], op=ALU.mult),
                      [("ps", bk), toep], [(ktacc, g)])
                else:
                    V(lambda e, ps=ps: e.tensor_tensor(out=ktmp[:], in0=ps[:, 0:128], in1=toep[:, 1, :], op=ALU.mult),
                      [("ps", bk), toep], [ktmp])
                    V(lambda e, g=g: e.tensor_tensor(out=ktacc[:, g, :], in0=ktacc[:, g, :], in1=ktmp[:], op=ALU.add),
                      [ktmp, (ktacc, g)], [(ktacc, g)])
                    V(lambda e, g=g: e.scalar_tensor_tensor(out=ktb[:, g, :], in0=self.ident[:], scalar=dtab[:, g:g + 1],
                                                              in1=ktacc[:, g, :], op0=ALU.mult, op1=ALU.add),
                      [self.ident, dtab, (ktacc, g)], [ktb])
            if d == 1:
                P.dma("pool", self.S_kt[l], ktb[:], r=[ktb], w=[("S_kt", l)])
            if self.cut == 7:
                continue
            if d == 0:
                cprod(pcr, pci, 9, 1, Cr, Ci, True, [Cr, Ci])
            else:
                cprod(pcr, pci, 16, -1, Cr, Ci, True, [Cr, Ci])
            V(lambda e: e.tensor_copy(out=wob[:, :, 0, :], in_=pcr[:].rearrange("p g j c -> p g (j c)")), [pcr], [wob])
            V(lambda e: e.tensor_copy(out=wob[:, :, 1, :], in_=pci[:].rearrange("p g j c -> p g (j c)")), [pci], [wob])
            P.dma("pool", self.S_wo[l][:, d], wob[:], r=[wob], w=[("S_wo", l, d)])
        sk = P.alloc("sk", [16], F32); ske = P.alloc("ske", [16], F32)
        P.dma("sp", sk[0:1, :], I["attn_sink"].rearrange("l h -> (l h)").rearrange("(o n) -> o n", o=1), w=[sk])
        A(lambda e: e.activation(out=ske[0:1, :], in_=sk[0:1, :], func=AF.Exp), [sk], [ske])
        V(lambda e: e.tensor_copy(out=self.esrow[0:1, :, :, :].rearrange("p l h q -> p (l h) q"),
                                  in_=mk(ske[0:1, :], [[1, 16], [0, 128]])), [ske], [self.esrow])
        P.release(mk0)

    def dbg_out(self, name, tile_ap, keys):
        if name in self.O:
            self.P.dma("pool", self.O[name], tile_ap, r=keys, w=[("dbg", name)])

    def load_x(self, U):
        P = self.P
        T = U["T"]
        mk0 = P.mark()
        xst = [P.alloc(f"xst{i}", [4, D], F32) for i in range(2)]
        for blk in range(T // 512):
            st = xst[blk % 2]
            P.dma("sp", st[:], U["x"][blk * 512:(blk + 1) * 512, :].rearrange("(i p) f -> p i f", p=128), w=[st])
            for c in range(8):
                bk = P.next_bank(); ps = P.bank(bk)
                for i in range(4):
                    P.op("pe", lambda e, st=st, i=i, c=c, ps=ps: e.transpose(
                        out=ps[:, i * 128:(i + 1) * 128], in_=st[:, i, c * 128:(c + 1) * 128], identity=self.ident[:]),
                        r=[st, self.ident], w=[("ps", bk)])
                eng = "act" if c % 2 else "dve"
                dst = self.xT[:, c, blk * 512:(blk + 1) * 512]
                if eng == "act":
                    P.op("act", lambda e, dst=dst, ps=ps: e.copy(out=dst, in_=ps[:, :]), r=[("ps", bk)], w=[(self.xT, c, blk)])
                else:
                    P.op("dve", lambda e, dst=dst, ps=ps: e.tensor_copy(out=dst, in_=ps[:, :]), r=[("ps", bk)], w=[(self.xT, c, blk)])
        P.release(mk0)

    def xkeys(self, t0, n, cs=range(8)):
        return [(self.xT, c, b) for c in cs for b in range(t0 // 512, (t0 + n - 1) // 512 + 1)]

    def norm_cols(self, t0, n):
        P = self.P
        rt, rstd = self.n_rt, self.n_rstd
        bk = P.next_bank(); ps = P.bank(bk)
        for c in range(8):
            sqb = self.n_sqb[c % 2]
            P.op("act", lambda e, c=c, sqb=sqb: e.activation(out=sqb[:, 0:n], in_=self.xT[:, c, t0:t0 + n], func=AF.Square),
                 r=self.xkeys(t0, n, [c]), w=[sqb])
            P.op("pe", lambda e, c=c, ps=ps, sqb=sqb: e.matmul(ps[:, 0:n], lhsT=self.onesbf[:], rhs=sqb[:, 0:n], start=(c == 0), stop=(c == 7)),
                 r=[sqb, self.onesbf], w=[("ps", bk)])
        P.op("act", lambda e, ps=ps: e.activation(out=rt[:, 0:n], in_=ps[:, 0:n], func=AF.Sqrt, scale=1.0 / D, bias=self.epsT[:, 0:1]),
             r=[("ps", bk), self.epsT], w=[rt])
        P.op("dve", lambda e: e.reciprocal(out=rstd[:, 0:n], in_=rt[:, 0:n]), r=[rt], w=[rstd])
        return rstd

    def norm_mod(self, U, l, which, t0, n, dst):
        P = self.P
        v = U["v"]
        rstd = self.norm_cols(t0, n)
        gsc = self.gsc1 if which == 1 else self.gsc2
        shb = 0 if which == 1 else 24
        for c in range(8):
            tmp = self.n_tmp[c % 2]
            P.op("dve", lambda e, c=c, tmp=tmp: e.tensor_tensor(out=tmp[:, 0:n], in0=self.xT[:, c, t0:t0 + n], in1=rstd[:, 0:n], op=ALU.mult),
                 r=self.xkeys(t0, n, [c]) + [rstd], w=[tmp])
            d_ap, d_keys = dst(c)
            P.op("act", lambda e, c=c, tmp=tmp, d_ap=d_ap: e.activation(
                out=d_ap, in_=tmp[:, 0:n], func=AF.Identity, scale=gsc[:, l, c, v:v + 1], bias=self.modT[:, l, shb + c, v:v + 1]),
                r=[tmp, gsc, self.modT], w=d_keys)

    def proj(self, wt, wcols, hT, n, kc_n=8, wk=None):
        P = self.P
        bk = P.next_bank(); ps = P.bank(bk)
        w0, w1 = wcols
        for kc in range(kc_n):
            P.op("pe", lambda e, kc=kc, ps=ps: e.matmul(ps[0:(w1 - w0), 0:n], lhsT=wt[:, kc, w0:w1], rhs=hT[:, kc, 0:n],
                                                          start=(kc == 0), stop=(kc == kc_n - 1)),
                 r=(wk if wk is not None else [wt]) + [hT], w=[("ps", bk)])
        return bk, ps

    def ssm_pass(self, U, l, ssmT):
        P, C, I = self.P, self.C, self.I
        T, NSEQ, L = U["T"], U["NSEQ"], U["L"]
        CT, Cq = T // 8, L // 8
        mk0 = P.mark()
        uT = P.alloc("uT", [2, T], BF16)
        mk1 = P.mark()
        wu = P.alloc("wu", [8, 256], BF16)
        hT = P.alloc("hT", [8, 512], BF16)
        P.dma("sp", wu[:], self.W["win"][l][:, :, 1536:1792], r=self.wkeys("win", l, 1536, 1792), w=[wu])
        for blk in range(T // 512):
            self.norm_mod(U, l, 1, blk * 512, 512, lambda c: (hT[:, c, :], [hT]))
            for oc in range(2):
                bk, ps = self.proj(wu, (oc * 128, oc * 128 + 128), hT, 512)
                P.op("dve", lambda e, oc=oc, blk=blk, ps=ps: e.tensor_copy(out=uT[:, oc, blk * 512:(blk + 1) * 512], in_=ps[:, :]),
                     r=[("ps", bk)], w=[(uT, oc, blk)])
        P.release(mk1)
        UK = [(uT, oc, b) for oc in range(2) for b in range(T // 512)]
        kt = P.alloc("kt", [16, 128], BF16); ws = P.alloc("ws", [2, 16, 128], BF16); wo = P.alloc("wo", [2, 8, 2, 128], BF16)
        xsel = P.alloc("xsel", [8, 240], BF16); ysel = P.alloc("ysel", [8, 128], BF16)
        P.dma("sp", kt[:], self.S_kt[l], r=[("S_kt", l)], w=[kt])
        P.dma("sp", ws[:], self.S_ws[l], r=[("S_ws", l, 0), ("S_ws", l, 1)], w=[ws])
        P.dma("sp", wo[:], self.S_wo[l], r=[("S_wo", l, 0), ("S_wo", l, 1)], w=[wo])
        P.dma("sp", xsel[:], C["xsel"], w=[xsel])
        P.dma("sp", ysel[:], C["ysel"], w=[ysel])
        X = P.alloc("X", [16, CT], BF16)
        for g in range(16):
            bk = P.next_bank(); ps = P.bank(bk)
            for s in range(8):
                rhs = mk(uT[:, g // 8, s:s + 1], [[8, CT]])
                P.op("pe", lambda e, g=g, s=s, rhs=rhs, ps=ps: e.matmul(
                    ps[:, 0:CT], lhsT=xsel[:, g % 8, (7 - s) * 16:(7 - s) * 16 + 128], rhs=rhs, start=(s == 0), stop=(s == 7)),
                    r=UK + [xsel], w=[("ps", bk)])
            P.op("act", lambda e, g=g, ps=ps: e.copy(out=X[:, g, :], in_=ps[:, 0:CT]), r=[("ps", bk)], w=[(X, g)])
        S1 = Cq + 1
        Hb = P.alloc("Hb", [2, 2, 8, NSEQ * S1], BF16)
        fin = P.alloc("fin", [NSEQ, 2, 2, 8], F32)
        mk2 = P.mark()
        Et = P.alloc("Et2", [2, 8, 256], F32)
        tt_ = [P.alloc(f"l2t{i}", [Cq], F32) for i in range(6)]
        h0 = P.alloc("h0", [2, 2, 8], F32)
        gi0 = P.alloc("gi0", [2, 2, 8], F32)
        hq = P.alloc("hq", [4, 8], F32)
        if U["lat"]:
            for d in range(2):
                for comp, nm in enumerate(("sre", "sim")):
                    for h in range(2):
                        src = I[nm][l, d]
                        P.dma("sp", h0[64 * h:64 * h + 64, d, comp, :], bass.AP(src.tensor, src.offset + h * 64, [[1, 64], [128, 8]]),
                              w=[h0], allow_slow_non_contiguous=True)
            for d in range(2):
                ld = l * 2 + d
                er, ei = self.e1[:, 0, ld, :], self.e1[:, 1, ld, :]
                hr, hi = h0[:, d, 0, :], h0[:, d, 1, :]
                ops = [(hq[:, 0, :], er, hr, ALU.mult), (hq[:, 1, :], ei, hi, ALU.mult), (gi0[:, d, 0, :], hq[:, 0, :], hq[:, 1, :], ALU.subtract),
                       (hq[:, 2, :], er, hi, ALU.mult), (hq[:, 3, :], ei, hr, ALU.mult), (gi0[:, d, 1, :], hq[:, 2, :], hq[:, 3, :], ALU.add)]
                for (o_, a_, b_, op_) in ops:
                    P.op("dve", lambda e, o_=o_, a_=a_, b_=b_, op_=op_: e.tensor_tensor(out=o_, in0=a_, in1=b_, op=op_),
                         r=[h0, hq, self.e1, gi0], w=[hq, gi0])
        else:
            P.op("dve", lambda e: e.memset(h0[:], 0.0), w=[h0])
            P.op("dve", lambda e: e.memset(gi0[:], 0.0), w=[gi0])
        for d in range(2):
            ld = l * 2 + d
            P.dma("sp", Et[:], self.S_e[l][:, d], r=[("S_e", l, d)], w=[Et])
            for gg in range(8):
                bk = P.next_bank(); ps = P.bank(bk)
                for h in range(2):
                    g = 2 * gg + h
                    for comp in range(2):
                        P.op("pe", lambda e, g=g, h=h, comp=comp, d=d, ps=ps: e.matmul(
                            ps[64 * h:64 * h + 64, comp * CT:(comp + 1) * CT], lhsT=ws[:, d, g, comp * 64:(comp + 1) * 64], rhs=X[:, g, :],
                            start=True, stop=True, tile_position=(0, 64 * h)), r=[ws, (X, g)], w=[("ps", bk)])
                for sq in range(NSEQ):
                    def seqview(base):
                        a = ps[:, base + sq * Cq: base + (sq + 1) * Cq]
                        if d == 0:
                            return a
                        return mk(ps[:, base + (sq + 1) * Cq - 1: base + (sq + 1) * Cq], [[-1, Cq]])
                    Sr, Si = seqview(0), seqview(CT)
                    Ec, Es = Et[:, 0, gg, 0:Cq], Et[:, 1, gg, 0:Cq]
                    t1, t2, t3, t4, t5, t6 = [t[:, 0:Cq] for t in tt_]
                    TK = lambda i: [tt_[i]]
                    def vop(o_, a_, b_, op_, r, w):
                        P.op("dve", lambda e: e.tensor_tensor(out=o_, in0=a_, in1=b_, op=op_), r=r, w=w)
                    vop(t1, Sr, Ec, ALU.mult, [("ps", bk), Et], TK(0))
                    vop(t2, Si, Es, ALU.mult, [("ps", bk), Et], TK(1))
                    vop(t5, t1, t2, ALU.add, TK(0) + TK(1), TK(4))
                    vop(t3, Si, Ec, ALU.mult, [("ps", bk), Et], TK(2))
                    vop(t4, Sr, Es, ALU.mult, [("ps", bk), Et], TK(3))
                    vop(t6, t3, t4, ALU.subtract, TK(2) + TK(3), TK(5))
                    rr = mk(self.a8mag[:, ld, gg:gg + 1], [[0, Cq]])
                    P.op("dve", lambda e, rr=rr, t5=t5, t1=t1, d=d, gg=gg: e.tensor_tensor_scan(
                        out=t1, data0=rr, data1=t5, initial=gi0[:, d, 0, gg:gg + 1], op0=ALU.mult, op1=ALU.add),
                        r=TK(4) + [self.a8mag, gi0], w=TK(0))
                    P.op("dve", lambda e, rr=rr, t6=t6, t2=t2, d=d, gg=gg: e.tensor_tensor_scan(
                        out=t2, data0=rr, data1=t6, initial=gi0[:, d, 1, gg:gg + 1], op0=ALU.mult, op1=ALU.add),
                        r=TK(5) + [self.a8mag, gi0], w=TK(1))
                    vop(t3, t1, Ec, ALU.mult, TK(0) + [Et], TK(2))
                    vop(t4, t2, Es, ALU.mult, TK(1) + [Et], TK(3))
                    vop(t5, t3, t4, ALU.subtract, TK(2) + TK(3), TK(4))
                    vop(t3, t2, Ec, ALU.mult, TK(1) + [Et], TK(2))
                    vop(t4, t1, Es, ALU.mult, TK(0) + [Et], TK(3))
                    vop(t6, t3, t4, ALU.add, TK(2) + TK(3), TK(5))
                    for comp, tH in ((0, t5), (1, t6)):
                        base = Hb[:, d, comp, gg, :]
                        if d == 0:
                            dstv = base[:, sq * S1 + 1: sq * S1 + 1 + Cq]
                            init_slot = base[:, sq * S1: sq * S1 + 1]
                        else:
                            dstv = mk(base[:, sq * S1 + Cq - 1: sq * S1 + Cq], [[-1, Cq]])
                            init_slot = base[:, sq * S1 + Cq: sq * S1 + Cq + 1]
                        P.op("act", lambda e, dstv=dstv, tH=tH: e.copy(out=dstv, in_=tH), r=[tt_[4 + comp]], w=[(Hb, d, gg)])
                        P.op("act", lambda e, init_slot=init_slot, d=d, comp=comp, gg=gg: e.copy(out=init_slot, in_=h0[:, d, comp, gg:gg + 1]),
                             r=[h0], w=[(Hb, d, gg)])
                        if not U["lat"]:
                            P.op("act", lambda e, sq=sq, d=d, comp=comp, gg=gg, tH=tH: e.copy(
                                out=fin[:, sq, d, comp, gg:gg + 1], in_=tH[:, Cq - 1:Cq]), r=[tt_[4 + comp]], w=[fin])
        if not U["lat"]:
            for sq in range(NSEQ):
                for d in range(2):
                    for comp, nm in enumerate(("nsr", "nsi")):
                        dst = self.O[nm][sq, l, d]
                        P.dma("pool", bass.AP(dst.tensor, dst.offset, [[1, 128], [128, 8]]), fin[:, sq, d, comp, :], r=[fin],
                              w=[("out", nm, sq, l, d)], allow_slow_non_contiguous=True)
        P.release(mk2)
        HK = [(Hb, d, gg) for d in range(2) for gg in range(8)]
        NB = T // 512
        zT = uT
        yexp = [P.alloc(f"yexp{i}", [T], BF16) for i in range(2)]
        wglu = P.alloc("wglu", [2, 256], BF16)
        sg = [P.alloc(f"sg{i}", [512], BF16) for i in range(2)]
        P.dma("sp", wglu[:], self.W["wglu"][l], r=self.wkeys("wglu", l, 0, 256), w=[wglu])
        acc = []
        for tb in range(NB):
            b = P.next_bank(); P.reserved_banks.add(b); acc.append(b)
        for chunk in range(2):
            for gi in range(8):
                g = chunk * 8 + gi
                h, gg = g % 2, g // 2
                bk = P.next_bank(); ps = P.bank(bk)
                P.op("pe", lambda e, g=g, ps=ps: e.matmul(ps[:, 0:CT], lhsT=kt[:, g, :], rhs=X[:, g, :], start=True, stop=False),
                     r=[kt, (X, g)], w=[("ps", bk)])
                k = 0
                for d in range(2):
                    for comp in range(2):
                        off = 0 if d == 0 else 1
                        rhs = mk(Hb[64 * h:64 * h + 64, d, comp, gg, off:off + 1], [[S1, NSEQ], [1, Cq]])
                        outv = ps[:, 0:CT].rearrange("p (a b) -> p a b", b=Cq)
                        k += 1
                        P.op("pe", lambda e, h=h, d=d, comp=comp, gg=gg, rhs=rhs, outv=outv, k=k: e.matmul(
                            outv, lhsT=wo[64 * h:64 * h + 64, d, gg, comp, :], rhs=rhs, start=False, stop=(k == 4)),
                            r=[wo] + HK, w=[("ps", bk)])
                ye = yexp[g % 2]
                in0 = mk(ps[:, 0:1], [[1, CT], [0, 8]])
                in1 = mk(self.maskj[:, 0:1], [[0, CT], [1, 8]])
                P.op("dve", lambda e, ye=ye, in0=in0, in1=in1: e.tensor_tensor(
                    out=ye[:].rearrange("p (a b) -> p a b", b=8), in0=in0, in1=in1, op=ALU.mult),
                    r=[("ps", bk), self.maskj], w=[ye])
                for tb in range(NB):
                    P.op("pe", lambda e, gi=gi, tb=tb, ye=ye: e.matmul(
                        P.bank(acc[tb])[:, :], lhsT=ysel[:, gi, :], rhs=ye[:, tb * 512:(tb + 1) * 512], start=(gi == 0), stop=(gi == 7)),
                        r=[ysel, ye], w=[("ps", acc[tb])])
            for tb in range(NB):
                P.op("act", lambda e, chunk=chunk, tb=tb: e.activation(
                    out=zT[:, chunk, tb * 512:(tb + 1) * 512], in_=P.bank(acc[tb])[:, :], func=AF.Gelu),
                    r=[("ps", acc[tb])], w=[(uT, chunk, tb)])
        for b in acc:
            P.reserved_banks.discard(b)
        for tb in range(NB):
            for oc in range(2):
                bk = P.next_bank(); ps = P.bank(bk)
                for kc in range(2):
                    P.op("pe", lambda e, kc=kc, oc=oc, tb=tb, ps=ps: e.matmul(
                        ps[:, :], lhsT=wglu[:, kc, oc * 128:(oc + 1) * 128], rhs=zT[:, kc, tb * 512:(tb + 1) * 512],
                        start=(kc == 0), stop=(kc == 1)), r=[wglu, (uT, kc, tb)], w=[("ps", bk)])
                s_ = sg[(tb * 2 + oc) % 2]
                P.op("act", lambda e, s_=s_, ps=ps: e.activation(out=s_[:], in_=ps[:, :], func=AF.Sigmoid), r=[("ps", bk)], w=[s_])
                P.op("dve", lambda e, s_=s_, oc=oc, tb=tb: e.tensor_tensor(
                    out=ssmT[:, oc, tb * 512:(tb + 1) * 512], in0=zT[:, oc, tb * 512:(tb + 1) * 512], in1=s_[:], op=ALU.mult),
                    r=[s_, (uT, oc, tb)], w=[(ssmT, oc, tb)])
        P.release(mk0)

    def kv_pass(self, U, l, krT, vaug, gbT, pT):
        P, C = self.P, self.C
        T, NSEQ, L, lat = U["T"], U["NSEQ"], U["L"], U["lat"]
        mk0 = P.mark()
        win = self.W["win"][l]
        hT = P.alloc("hT", [8, 512], BF16)
        wkd = P.alloc("wkd", [8, 2, 128], BF16)
        wkv = P.alloc("wkv", [8, 256], BF16)
        wg = P.alloc("wg", [8, 768], BF16)
        for kv in range(2):
            for hh in range(2):
                P.dma("sp", wkd[:, :, kv, hh * 64:(hh + 1) * 64], win[:, :, 512 + kv * 64:512 + (kv + 1) * 64],
                      r=self.wkeys("win", l, 512, 640), w=[wkd])
        P.dma("sp", wkv[:], win[:, :, 512:768], r=self.wkeys("win", l, 512, 768), w=[wkv])
        P.dma("sp", wg[:], win[:, :, 768:1536], r=self.wkeys("win", l, 768, 1536), w=[wg])
        if lat:
            wkp = P.alloc("wkp", [8, 2, 128], BF16)
            rope = P.alloc("rope", [2, 512], F32)
            r1 = P.alloc("r1", [512], F32); r2 = P.alloc("r2", [512], F32)
            for b_ in range(2):
                srcv = mk(wkd[:, 0, 0, 0:1], [[64, 32], [32, 2], [1, 16]], off=(1 - b_) * 16)
                dstv = mk(wkp[:, 0, 0, 0:1], [[64, 32], [32, 2], [1, 16]], off=b_ * 16)
                P.op("pool", lambda e, srcv=srcv, dstv=dstv: e.tensor_copy(out=dstv, in_=srcv), r=[wkd], w=[wkp])
        else:
            kvst = [P.alloc(f"kvst{i}", [256], F32) for i in range(2)]
        gct = P.alloc("gct", [512], F32)
        P.op("dve", lambda e: e.memset(vaug[:, :, :, 64:128], 1.0), w=[vaug])
        for blk in range(T // 512):
            t0 = blk * 512
            self.norm_mod(U, l, 1, t0, 512, lambda c: (hT[:, c, :], [hT]))
            if lat:
                P.dma("sp", rope[:], C["rope"][:, :, t0:t0 + 512], w=[rope])
            for kv in range(2):
                bk, ps = self.proj(wkd[:, :, kv, :], (0, 128), hT, 512, wk=[wkd])
                if lat:
                    bk2, ps2 = self.proj(wkp[:, :, kv, :], (0, 128), hT, 512, wk=[wkp])
                    P.op("dve", lambda e, ps=ps: e.tensor_tensor(out=r1[:], in0=ps[:, :], in1=rope[:, 0, :], op=ALU.mult), r=[("ps", bk), rope], w=[r1])
                    P.op("dve", lambda e, ps2=ps2: e.tensor_tensor(out=r2[:], in0=ps2[:, :], in1=rope[:, 1, :], op=ALU.mult), r=[("ps", bk2), rope], w=[r2])
                    P.op("dve", lambda e, kv=kv, t0=t0: e.tensor_tensor(out=krT[:, kv, t0:t0 + 512], in0=r1[:], in1=r2[:], op=ALU.add),
                         r=[r1, r2], w=[(krT, kv, blk)])
                else:
                    P.op("act", lambda e, kv=kv, t0=t0, ps=ps: e.copy(out=krT[:, kv, t0:t0 + 512], in_=ps[:, :]), r=[("ps", bk)], w=[(krT, kv, blk)])
            for i in range(4):
                if self.cut == 12:
                    break
                tile_i = blk * 4 + i
                bk = P.next_bank(); ps = P.bank(bk)
                c0 = 128 if lat else 0
                for kc in range(8):
                    P.op("pe", lambda e, kc=kc, i=i, ps=ps, c0=c0: e.matmul(ps[:, c0:256], lhsT=hT[:, kc, i * 128:(i + 1) * 128], rhs=wkv[:, kc, c0:256],
                                                                              start=(kc == 0), stop=(kc == 7)), r=[hT, wkv], w=[("ps", bk)])
                P.op("dve", lambda e, tile_i=tile_i, ps=ps: e.tensor_copy(out=vaug[:, tile_i, :, 0:64], in_=ps[:, 128:256].rearrange("p (a b) -> p a b", b=64)),
                     r=[("ps", bk)], w=[(vaug, tile_i)])
                if not lat and self.cut != 15:
                    st = kvst[tile_i % 2]
                    P.op("act", lambda e, st=st, ps=ps: e.copy(out=st[:], in_=ps[:, 0:256]), r=[("ps", bk)], w=[st])
                    sq, tl = divmod(tile_i, L // 128)
                    P.dma("sp", self.O["nk"][sq, l, tl * 128:(tl + 1) * 128, :], st[:, 0:128], r=[st], w=[("out", "nk", tile_i, l)])
                    P.dma("sp", self.O["nv"][sq, l, tl * 128:(tl + 1) * 128, :], st[:, 128:256], r=[st], w=[("out", "nv", tile_i, l)])
            for oc in range(6):
                if self.cut in (12, 13):
                    break
                bk, ps = self.proj(wg, (oc * 128, oc * 128 + 128), hT, 512)
                which, c = divmod(oc, 2)
                if which == 0:
                    P.op("act", lambda e, c=c, t0=t0, ps=ps: e.copy(out=gbT[:, c, t0:t0 + 512], in_=ps[:, :]), r=[("ps", bk)], w=[(gbT, c, blk)])
                elif which == 1:
                    P.op("act", lambda e, c=c, t0=t0, ps=ps: e.copy(out=pT[:, c, t0:t0 + 512], in_=ps[:, :]), r=[("ps", bk)], w=[(pT, c, blk)])
                else:
                    P.op("dve", lambda e, c=c, t0=t0, ps=ps: e.tensor_tensor(out=pT[:, c, t0:t0 + 512], in0=ps[:, :], in1=pT[:, c, t0:t0 + 512], op=ALU.mult),
                         r=[("ps", bk), (pT, c, blk)], w=[(pT, c, blk)])
        P.release(mk0)

    def mix_pass(self, U, l, krT, vaug, gbT, pT, ssmT):
        P, C, I = self.P, self.C, self.I
        T, NSEQ, L, lat, v = U["T"], U["NSEQ"], U["L"], U["lat"], U["v"]
        mk0 = P.mark()
        win = self.W["win"][l]
        hT = P.alloc("hT", [8, 512], BF16)
        wq = P.alloc("wq", [8, 512], BF16)
        wo_ = P.alloc("wo_", [8, D], BF16)
        qT = P.alloc("qT", [4, 512], BF16)
        atT = P.alloc("atT", [4, 512], BF16)
        cvT = P.alloc("cvT", [2, 512], BF16)
        cacc = P.alloc("cacc", [512], F32)
        PT = [P.alloc(f"PT{i}", [512], BF16) for i in range(3)]
        Rt = P.alloc("Rt", [512], F32)
        P.dma("sp", wq[:], win[:, :, 0:512], r=self.wkeys("win", l, 0, 512), w=[wq])
        P.dma("sp", wo_[:], self.W["wout"][l], r=self.wkeys("wout", l, 0, D), w=[wo_])
        if lat:
            wqp = P.alloc("wqp", [8, 512], BF16)
            rope = P.alloc("rope", [2, 512], F32)
            r1 = P.alloc("r1", [512], F32); r2 = P.alloc("r2", [512], F32)
            maskb = P.alloc("maskb", [2, 512], BF16)
            ckd = P.alloc("ckd", [2, PAST], BF16)
            cva = P.alloc("cva", [4, 2, 128], BF16)
            cst = P.alloc("cst", [4, 2, 64], F32)
            for b_ in range(2):
                srcv = mk(wq[:, 0, 0:1], [[64, 64], [32, 2], [1, 16]], off=(1 - b_) * 16)
                dstv = mk(wqp[:, 0, 0:1], [[64, 64], [32, 2], [1, 16]], off=b_ * 16)
                P.op("pool", lambda e, srcv=srcv, dstv=dstv: e.tensor_copy(out=dstv, in_=srcv), r=[wq], w=[wqp])
            P.dma("sp", maskb[:], C["maskb"], w=[maskb])
            m01 = P.alloc("m01", [2, 256], BF16)
            P.op("dve", lambda e: e.tensor_single_scalar(out=m01[:], in_=maskb[:, :, 0:256], scalar=0.0, op=ALU.is_equal), r=[maskb], w=[m01])
            P.op("dve", lambda e: e.memset(cva[:, :, :, 64:128], 1.0), w=[cva])
            P.dma("sp", cst[:], I["cv"][l].rearrange("(i p) (k d) -> p i k d", p=128, d=64), w=[cst])
            P.op("dve", lambda e: e.tensor_copy(out=cva[:, :, :, 0:64], in_=cst[:]), r=[cst], w=[cva])
            for kv in range(2):
                for hh in range(2):
                    P.dma("sp", cst[:, :, hh, :], I["ck"][l][:, kv * 64:(kv + 1) * 64].rearrange("(i p) d -> p i d", p=128), r=[cva], w=[cst])
                bk = P.next_bank(); ps = P.bank(bk)
                for i in range(4):
                    P.op("pe", lambda e, i=i, ps=ps: e.transpose(out=ps[:, i * 128:(i + 1) * 128], in_=cst[:, i, :, :].rearrange("p a b -> p (a b)"),
                                                                 identity=self.ident[:]), r=[cst, self.ident], w=[("ps", bk)])
                P.op("act", lambda e, kv=kv, ps=ps: e.copy(out=ckd[:, kv, :], in_=ps[:, :]), r=[("ps", bk)], w=[ckd])
        po_banks = []
        for i in range(2):
            b = P.next_bank(); P.reserved_banks.add(b); po_banks.append(b)
        npo = 0
        TPS = L // 128
        for blk in range(T // 512):
            t0 = blk * 512
            self.norm_mod(U, l, 1, t0, 512, lambda c: (hT[:, c, :], [hT]))
            if lat:
                P.dma("sp", rope[:], C["rope"][:, :, t0:t0 + 512], w=[rope])
            for hc in range(4):
                bk, ps = self.proj(wq, (hc * 128, hc * 128 + 128), hT, 512)
                if lat:
                    bk2, ps2 = self.proj(wqp, (hc * 128, hc * 128 + 128), hT, 512)
                    P.op("dve", lambda e, ps=ps: e.tensor_tensor(out=r1[:], in0=ps[:, :], in1=rope[:, 0, :], op=ALU.mult), r=[("ps", bk), rope], w=[r1])
                    P.op("dve", lambda e, ps2=ps2: e.tensor_tensor(out=r2[:], in0=ps2[:, :], in1=rope[:, 1, :], op=ALU.mult), r=[("ps", bk2), rope], w=[r2])
                    P.op("dve", lambda e, hc=hc: e.tensor_tensor(out=qT[:, hc, :], in0=r1[:], in1=r2[:], op=ALU.add), r=[r1, r2], w=[(qT, hc)])
                else:
                    P.op("act", lambda e, hc=hc, ps=ps: e.copy(out=qT[:, hc, :], in_=ps[:, :]), r=[("ps", bk)], w=[(qT, hc)])
            if lat:
                pieces = [(t0, t0 + 512, t0 > 0, t0 + 512 < T)]
            else:
                pieces = [(t0 + i * L, t0 + (i + 1) * L, False, False) for i in range(512 // L)]
            for c in range(2):
                for (a0, a1, hl, hr) in pieces:
                    n = a1 - a0
                    o0 = a0 - t0
                    pk = [(pT, c, b) for b in range(max(0, blk - 1), min(T // 512, blk + 2))]
                    P.op("dve", lambda e, c=c, a0=a0, a1=a1, o0=o0, n=n: e.tensor_scalar_mul(
                        out=cacc[:, o0:o0 + n], in0=pT[:, c, a0:a1], scalar1=self.scw[:, l, c, 1:2]), r=pk + [self.scw], w=[cacc])
                    a = 0 if hl else 1
                    P.op("dve", lambda e, c=c, a0=a0, a1=a1, o0=o0, n=n, a=a: e.scalar_tensor_tensor(
                        out=cacc[:, o0 + a:o0 + n], in0=pT[:, c, a0 + a - 1:a1 - 1], scalar=self.scw[:, l, c, 0:1],
                        in1=cacc[:, o0 + a:o0 + n], op0=ALU.mult, op1=ALU.add), r=pk + [self.scw, cacc], w=[cacc])
                    b_ = 0 if hr else 1
                    P.op("dve", lambda e, c=c, a0=a0, a1=a1, o0=o0, n=n, b_=b_: e.scalar_tensor_tensor(
                        out=cacc[:, o0:o0 + n - b_], in0=pT[:, c, a0 + 1:a1 + 1 - b_], scalar=self.scw[:, l, c, 2:3],
                        in1=cacc[:, o0:o0 + n - b_], op0=ALU.mult, op1=ALU.add), r=pk + [self.scw, cacc], w=[cacc])
                P.op("dve", lambda e, c=c, t0=t0: e.tensor_tensor(out=cvT[:, c, :], in0=cacc[:], in1=gbT[:, c, t0:t0 + 512], op=ALU.mult),
                     r=[cacc, (gbT, c, blk)], w=[(cvT, c)])
            for qi in range(4):
                qt = blk * 4 + qi
                sq, ql = divmod(qt, TPS)
                for kv in range(2):
                    srcs = []
                    if lat:
                        for kt_, m in ((ql - 1, 0), (ql, None), (ql + 1, 1)):
                            if 0 <= kt_ < TPS:
                                srcs.append((krT[:, kv, kt_ * 128:(kt_ + 1) * 128], [(krT, kv, kt_ // 4)], vaug[:, kt_, kv, :], [(vaug, kt_)], m))
                        for i in range(4):
                            srcs.append((ckd[:, kv, i * 128:(i + 1) * 128], [ckd], cva[:, i, kv, :], [cva], None))
                    else:
                        for kt_ in range(TPS):
                            gt = sq * TPS + kt_
                            srcs.append((krT[:, kv, gt * 128:(gt + 1) * 128], [(krT, kv, gt // 4)], vaug[:, gt, kv, :], [(vaug, gt)], None))
                    pob = po_banks[npo % 2]; npo += 1
                    po = P.bank(pob)
                    ns = len(srcs)

                    def S(i):
                        kT_ap, kkeys, _, _, m = srcs[i]
                        pt = PT[i % 3]
                        for hh in range(2):
                            bk = P.next_bank(); ps = P.bank(bk)
                            for j in range(2):
                                hq = 2 * j + hh
                                h = kv * 4 + hq
                                P.op("pe", lambda e, j=j, h=h, hh=hh, ps=ps, kT_ap=kT_ap: e.matmul(
                                    ps[:, j * 128:(j + 1) * 128], lhsT=kT_ap[64 * hh:64 * hh + 64, :],
                                    rhs=qT[64 * hh:64 * hh + 64, h // 2, qi * 128:(qi + 1) * 128], start=True, stop=True),
                                    r=kkeys + [(qT, h // 2)], w=[("ps", bk)])
                            P.op("act", lambda e, pt=pt, ps=ps, hh=hh: e.activation(out=pt[:, hh * 256:(hh + 1) * 256], in_=ps[:, 0:256], func=AF.Exp, scale=0.125),
                                 r=[("ps", bk)], w=[pt])
                            if m is not None:
                                P.op("pool", lambda e, pt=pt, hh=hh, m=m: e.tensor_tensor(out=pt[:, hh * 256:(hh + 1) * 256], in0=pt[:, hh * 256:(hh + 1) * 256],
                                                                                           in1=m01[:, m, :], op=ALU.mult), r=[pt, m01], w=[pt])

                    def PV(i):
                        _, _, v_ap, vkeys, _ = srcs[i]
                        pt = PT[i % 3]
                        P.op("pe", lambda e, v_ap=v_ap, pt=pt, i=i: e.matmul(po[:, :], lhsT=v_ap, rhs=pt[:], start=(i == 0), stop=False),
                             r=vkeys + [pt], w=[("ps", pob)])
                    S(0)
                    for i in range(ns):
                        if i + 1 < ns:
                            S(i + 1)
                        PV(i)
                    es_rhs = mk(self.esrow[0:1, l, kv * 4, 0:1], [[128, 2], [256, 2], [1, 128]])
                    P.op("pe", lambda e, es_rhs=es_rhs: e.matmul(po[:, :].rearrange("p (a b c) -> p a b c", a=2, b=2), lhsT=self.vsink[0:1, :], rhs=es_rhs,
                                                                  start=False, stop=True), r=[self.vsink, self.esrow], w=[("ps", pob)])
                    P.op("dve", lambda e: e.reciprocal(out=Rt[0:64, :], in_=po[64:128, :]), r=[("ps", pob)], w=[Rt])
                    for par in range(2):
                        in0 = po[0:64, par * 256:(par + 1) * 256].rearrange("p (a b) -> p a b", b=128)
                        in1 = Rt[0:64, par * 256:(par + 1) * 256].rearrange("p (a b) -> p a b", b=128)
                        outv = atT[64 * par:64 * par + 64, kv * 2:kv * 2 + 2, qi * 128:(qi + 1) * 128]
                        P.op("dve", lambda e, in0=in0, in1=in1, outv=outv: e.tensor_tensor(out=outv, in0=in0, in1=in1, op=ALU.mult),
                             r=[("ps", pob), Rt], w=[(atT, kv)])
            rhs_list = [(atT[:, i, :], [(atT, 0), (atT, 1)]) for i in range(4)] + [(cvT[:, i, :], [(cvT, i)]) for i in range(2)] + \
                       [(ssmT[:, i, t0:t0 + 512], [(ssmT, i, blk)]) for i in range(2)]
            for oc in range(8):
                bk = P.next_bank(); ps = P.bank(bk)
                for kc, (rap, rkeys) in enumerate(rhs_list):
                    P.op("pe", lambda e, kc=kc, oc=oc, rap=rap, ps=ps: e.matmul(ps[:, :], lhsT=wo_[:, kc, oc * 128:(oc + 1) * 128], rhs=rap,
                                                                                  start=(kc == 0), stop=(kc == 7)), r=[wo_] + rkeys, w=[("ps", bk)])
                P.op("dve", lambda e, oc=oc, t0=t0, ps=ps: e.scalar_tensor_tensor(
                    out=self.xT[:, oc, t0:t0 + 512], in0=ps[:, :], scalar=self.modT[:, l, 16 + oc, v:v + 1], in1=self.xT[:, oc, t0:t0 + 512],
                    op0=ALU.mult, op1=ALU.add), r=[("ps", bk), self.modT, (self.xT, oc, blk)], w=[(self.xT, oc, blk)])
        for b in po_banks:
            P.reserved_banks.discard(b)
        P.release(mk0)

    def ffn_pass(self, U, l):
        P = self.P
        T, NSEQ, L, lat, v = U["T"], U["NSEQ"], U["L"], U["lat"], U["v"]
        mk0 = P.mark()
        wdn = P.alloc("wdn", [22, D], BF16)
        for c in range(22):
            P.dma("sp", wdn[:, c, :], self.W["wdn"][l][:, c, :], r=self.wkeys("wdn", l, 0, D, [c]), w=[(wdn, c)])
        h2T = P.alloc("h2T", [8, 2, 258], BF16)
        halo = P.alloc("halo", [8, 2], BF16)
        gated = P.alloc("gated", [22, 2, 256], BF16)
        wus = [P.alloc(f"wus{i}", [8, 256], BF16) for i in range(3)]
        ca = [P.alloc(f"ca{i}", [256], F32) for i in range(2)]
        cg = [P.alloc(f"cg{i}", [256], F32) for i in range(2)]
        sgt = [P.alloc(f"sgt{i}", [256], F32) for i in range(2)]
        pieces = []
        for t0 in range(0, T, 256):
            sq_start = (t0 % L) == 0
            sq_end = ((t0 + 256) % L) == 0
            pieces.append((t0, t0 + 256, not sq_start, not sq_end))
        nwu = 0
        for sb in range(len(pieces) // 2):
            pcs = pieces[2 * sb:2 * sb + 2]
            for pi, (a0, a1, hl, hr) in enumerate(pcs):
                if pi == 0 and hl:
                    n = (a1 - a0) + int(hr)
                    P.op("pool", lambda e: e.tensor_copy(out=h2T[:, :, 0, 0:1], in_=halo[:, :, 0:1]), r=[halo], w=[(h2T, 0)])
                    self.norm_mod(U, l, 2, a0, n, lambda c, n=n: (h2T[:, c, 0, 1:1 + n], [(h2T, 0)]))
                else:
                    n = (a1 - a0) + int(hl) + int(hr)
                    self.norm_mod(U, l, 2, a0 - int(hl), n, lambda c, pi=pi, n=n: (h2T[:, c, pi, 0:n], [(h2T, pi)]))
            a0_, a1_, hl_l, hr_l = pcs[1]
            lastcol = int(hl_l) + (a1_ - a0_) - 1
            P.op("pool", lambda e, lastcol=lastcol: e.tensor_copy(out=halo[:, :, 0:1], in_=h2T[:, :, 1, lastcol:lastcol + 1]), r=[(h2T, 1)], w=[halo])
            for c in range(22):
                wu = wus[nwu % 3]; nwu += 1
                P.dma("sp", wu[:, :, 0:128], self.W["wup"][l][:, :, c * 128:(c + 1) * 128], r=self.wkeys("wup", l, c * 128, (c + 1) * 128), w=[wu])
                P.dma("sp", wu[:, :, 128:256], self.W["wup"][l][:, :, DFF + c * 128:DFF + (c + 1) * 128],
                      r=self.wkeys("wup", l, DFF + c * 128, DFF + (c + 1) * 128), w=[wu])
                for pi, (a0, a1, hl, hr) in enumerate(pcs):
                    m = a1 - a0
                    hl_, hr_ = int(hl), int(hr)
                    n = m + hl_ + hr_
                    res = []
                    for half, (acc_t, ch) in enumerate(((ca[pi], c), (cg[pi], 22 + c))):
                        bk = P.next_bank(); ps = P.bank(bk)
                        for kc in range(8):
                            P.op("pe", lambda e, kc=kc, half=half, ps=ps, pi=pi, n=n: e.matmul(
                                ps[:, 0:n], lhsT=wu[:, kc, half * 128:(half + 1) * 128], rhs=h2T[:, kc, pi, 0:n], start=(kc == 0), stop=(kc == 7)),
                                r=[wu, (h2T, pi)], w=[("ps", bk)])
                        w_ = self.fcw[:, l, ch, :]
                        P.op("act", lambda e, acc_t=acc_t, ps=ps, w_=w_: e.activation(
                            out=acc_t[:, 0:m], in_=ps[:, hl_:hl_ + m], func=AF.Identity, scale=w_[:, 1:2]), r=[("ps", bk), self.fcw], w=[acc_t])
                        a = 0 if hl else 1
                        P.op("dve", lambda e, acc_t=acc_t, ps=ps, w_=w_, a=a: e.scalar_tensor_tensor(
                            out=acc_t[:, a:m], in0=ps[:, hl_ + a - 1:hl_ + m - 1], scalar=w_[:, 0:1], in1=acc_t[:, a:m], op0=ALU.mult, op1=ALU.add),
                            r=[("ps", bk), self.fcw, acc_t], w=[acc_t])
                        b_ = 0 if hr else 1
                        P.op("dve", lambda e, acc_t=acc_t, ps=ps, w_=w_, b_=b_: e.scalar_tensor_tensor(
                            out=acc_t[:, 0:m - b_], in0=ps[:, hl_ + 1:hl_ + m + 1 - b_], scalar=w_[:, 2:3], in1=acc_t[:, 0:m - b_], op0=ALU.mult, op1=ALU.add),
                            r=[("ps", bk), self.fcw, acc_t], w=[acc_t])
                    sg_ = sgt[pi]
                    P.op("act", lambda e, sg_=sg_, pi=pi: e.activation(out=sg_[:, 0:m], in_=cg[pi][:, 0:m], func=AF.Silu), r=[cg[pi]], w=[sg_])
                    P.op("dve", lambda e, sg_=sg_, pi=pi, c=c: e.tensor_tensor(out=gated[:, c, pi, 0:m], in0=ca[pi][:, 0:m], in1=sg_[:, 0:m], op=ALU.mult),
                         r=[ca[pi], sg_], w=[(gated, c, pi)])
            for pi, (a0, a1, hl, hr) in enumerate(pcs):
                m = a1 - a0
                for oc in range(8):
                    bk = P.next_bank(); ps = P.bank(bk)
                    for c in range(22):
                        P.op("pe", lambda e, c=c, oc=oc, pi=pi, ps=ps: e.matmul(ps[:, 0:m], lhsT=wdn[:, c, oc * 128:(oc + 1) * 128], rhs=gated[:, c, pi, 0:m],
                                                                                  start=(c == 0), stop=(c == 21)), r=[(wdn, c), (gated, c, pi)], w=[("ps", bk)])
                    P.op("dve", lambda e, oc=oc, a0=a0, a1=a1, ps=ps: e.scalar_tensor_tensor(
                        out=self.xT[:, oc, a0:a1], in0=ps[:, 0:m], scalar=self.modT[:, l, 40 + oc, v:v + 1], in1=self.xT[:, oc, a0:a1],
                        op0=ALU.mult, op1=ALU.add), r=[("ps", bk), self.modT] + self.xkeys(a0, m, [oc]), w=self.xkeys(a0, m, [oc]))
        P.release(mk0)

    def final_pass(self, U):
        P = self.P
        T = U["T"]
        mk0 = P.mark()
        yT = P.alloc("yT", [8, 512], F32)
        yst = [P.alloc(f"yst{i}", [D], F32) for i in range(2)]
        nst = 0
        for blk in range(T // 512):
            t0 = blk * 512
            rstd = self.norm_cols(t0, 512)
            for c in range(8):
                tmp = self.n_tmp[c % 2]
                P.op("dve", lambda e, c=c, tmp=tmp, t0=t0: e.tensor_tensor(out=tmp[:], in0=self.xT[:, c, t0:t0 + 512], in1=rstd[:], op=ALU.mult),
                     r=self.xkeys(t0, 512, [c]) + [rstd], w=[tmp])
                P.op("act", lambda e, c=c, tmp=tmp: e.activation(out=yT[:, c, :], in_=tmp[:], func=AF.Identity, scale=self.nfT[:, c:c + 1]),
                     r=[tmp, self.nfT], w=[(yT, c)])
            for i in range(4):
                st = yst[nst % 2]; nst += 1
                for hf in range(2):
                    bk = P.next_bank(); ps = P.bank(bk)
                    for cc in range(4):
                        c = hf * 4 + cc
                        P.op("pe", lambda e, c=c, cc=cc, i=i, ps=ps: e.transpose(out=ps[:, cc * 128:(cc + 1) * 128], in_=yT[:, c, i * 128:(i + 1) * 128],
                                                                                 identity=self.ident[:]), r=[(yT, c), self.ident], w=[("ps", bk)])
                    if hf == 0:
                        P.op("act", lambda e, st=st, ps=ps: e.copy(out=st[:, 0:512], in_=ps[:, :]), r=[("ps", bk)], w=[(st, 0)])
                    else:
                        P.op("dve", lambda e, st=st, ps=ps: e.tensor_copy(out=st[:, 512:1024], in_=ps[:, :]), r=[("ps", bk)], w=[(st, 1)])
                P.dma("pool", U["y"][t0 + i * 128:t0 + (i + 1) * 128, :], st[:], r=[(st, 0), (st, 1)], w=[("out", "y", U["name"], blk, i)])
        P.release(mk0)

    def run_unit(self, U, layers=(0, 1), passes=("ssm", "kv", "mix", "ffn", "final")):
        P = self.P
        T = U["T"]
        mk0 = P.mark()
        self.n_sqb = [P.alloc(f"n_sqb{i}", [512], BF16) for i in range(2)]
        self.n_rt = P.alloc("n_rt", [512], F32)
        self.n_rstd = P.alloc("n_rstd", [512], F32)
        self.n_tmp = [P.alloc(f"n_tmp{i}", [512], F32) for i in range(2)]
        self.load_x(U)
        for l in layers:
            mk1 = P.mark()
            ssmT = P.alloc("ssmT", [2, T], BF16)
            if "ssm" in passes:
                self.ssm_pass(U, l, ssmT)
            if "kv" in passes:
                krT = P.alloc("krT", [2, T], BF16)
                vaug = P.alloc("vaug", [T // 128, 2, 128], BF16)
                gbT = P.alloc("gbT", [2, T], BF16)
                pT = P.alloc("pT", [2, T], F32 if False else BF16)
                self.kv_pass(U, l, krT, vaug, gbT, pT)
                if "mix" in passes:
                    self.mix_pass(U, l, krT, vaug, gbT, pT, ssmT)
            P.release(mk1)
            if "ffn" in passes:
                self.ffn_pass(U, l)
        if "final" in passes:
            self.final_pass(U)
        P.release(mk0)

    def build(self):
        P = self.P
        self.xT = P.alloc("xT", [8, 2048], F32)
        self.epsT = P.alloc("epsT", [1], F32)
        P.op("dve", lambda e: e.memset(self.epsT[:], EPS), w=[self.epsT])
        self.persistent()
        if "prep" in self.stages or "casts" in self.stages:
            self.prep_casts()
        if "prep" in self.stages or "adaln" in self.stages:
            self.prep_adaln()
        if "prep" in self.stages or "ssmt" in self.stages:
            self.prep_ssm()
        UP = dict(name="P", T=512, NSEQ=2, L=256, v=0, lat=False, x=self.I["xp"], y=self.O["yp"])
        US = dict(name="S", T=2048, NSEQ=1, L=2048, v=1, lat=True, x=self.I["xs"], y=self.O["ys"])
        if "P" in self.stages:
            self.run_unit(UP, **self.unit_kw.get("P", {}))
        if "S" in self.stages:
            self.run_unit(US, **self.unit_kw.get("S", {}))
        if self.post is not None:
            self.post(self)
        P.emit()
        self.es.close()
        return self.nc

    unit_kw = {}
    cast_only = None
    post = None
    cut = 0


def make_in_maps(inputs, consts, B=None):
    f = lambda a: np.ascontiguousarray(np.asarray(a, dtype=np.float32))
    xp = f(inputs["x_prompt"]); xs = f(inputs["x_sample"])
    maps = []
    for c in range(8):
        b = c // 4
        m = {
            "xp": xp[2 * c:2 * c + 2].reshape(512, D),
            "xs": xs[b],
            "ck": f(inputs["cache_k"])[b].reshape(2, PAST, 128),
            "cv": f(inputs["cache_v"])[b].reshape(2, PAST, 128),
            "sre": f(inputs["state_ssm_re"])[b],
            "sim": f(inputs["state_ssm_im"])[b],
            "cvec": np.stack([f(inputs["c_ctx"]), f(inputs["c"])[b]], 0),
        }
        for name, _ in IN_SPECS[7:]:
            m[name] = f(inputs[name])
        for k, a in consts.items():
            m["c_" + k] = a
        if B is not None:
            used = set(B.I.keys()) | set("c_" + k for k in B.C.keys())
            m = {k: a for k, a in m.items() if k in used}
        maps.append({k: np.ascontiguousarray(a) for k, a in m.items()})
    return maps


_CACHE = {}


def kernel(**inputs):
    consts = make_consts()
    if "nc" not in _CACHE:
        B = Builder()
        _CACHE["nc"] = B.build()
        _CACHE["B"] = B
    nc = _CACHE["nc"]
    in_maps = make_in_maps(inputs, consts, _CACHE["B"])
    res = run_bass_kernel_spmd(nc, in_maps, core_ids=list(range(8)))
    R = res.results
    y_prompt = np.concatenate([R[c]["yp"].reshape(2, 256, D) for c in range(8)], 0)
    y_sample = np.stack([R[0]["ys"], R[4]["ys"]], 0)
    nk = np.concatenate([R[c]["nk"].reshape(2, 2, 256, 2, 64) for c in range(8)], 0)
    nv = np.concatenate([R[c]["nv"].reshape(2, 2, 256, 2, 64) for c in range(8)], 0)
    nsr = np.concatenate([R[c]["nsr"] for c in range(8)], 0)
    nsi = np.concatenate([R[c]["nsi"] for c in range(8)], 0)
    return (y_prompt.astype(np.float32), y_sample.astype(np.float32), nk.astype(np.float32), nv.astype(np.float32),
            nsr.astype(np.float32), nsi.astype(np.float32))
```

```python
import math
from contextlib import ExitStack

import numpy as np
import ml_dtypes

import concourse.bass as bass
import concourse.mybir as mybir
from concourse.bass_utils import run_bass_kernel_spmd

F32 = mybir.dt.float32
BF16 = mybir.dt.bfloat16
U8 = mybir.dt.uint8
I32 = mybir.dt.int32
ALU = mybir.AluOpType
AF = mybir.ActivationFunctionType
AX = mybir.AxisListType

D = 1024
DEPTH = 2
NQH = 8
HD = 64
DFF = 2816
IN_DIM = 1792
PAST = 512
EPS = 1e-6
NEG = -30000.0
TWO_PI = 2.0 * math.pi

ARENA_BYTES = 207 * 1024
CASTW = 1024


class Tile:
    def __init__(self, name, ap, lo, hi):
        self.name, self.ap, self.lo, self.hi = name, ap, lo, hi

    def __getitem__(self, k):
        return self.ap[k]


class Op:
    __slots__ = ("eng", "calls", "waits", "sem", "val", "isdma")


class _Rec:
    def __init__(self):
        self.calls = []

    def __getattr__(self, name):
        def f(*a, **k):
            self.calls.append((name, a, k))
            return None
        return f


class Prog:
    ENG = ("pe", "act", "dve", "pool", "sp")
    CAP_C = 30000
    CAP_D = 1800

    def __init__(self, nc, es):
        self.nc, self.es = nc, es
        self.ops = {e: [] for e in self.ENG}
        self.state = {}
        self.seen = {e: {} for e in self.ENG}
        self.sems = {}
        self.cnt = {}
        self.tile_init = {}
        self.tiles_live = []
        self.freed = []
        self.tile_keys = {}
        self.arena = es.enter_context(nc.sbuf_tensor("arena", [128, ARENA_BYTES], U8))
        self.top = 0
        self.nsem = 0
        self.bank_rr = 0
        self.reserved_banks = set()
        self.psum = es.enter_context(nc.psum_tensor("psum", [128, 8 * 512], F32))
        self.n_ops = 0
        self.final = {}

    def alloc(self, name, shape, dtype):
        esz = 4 if dtype in (F32, I32) else 2
        n = int(np.prod(shape)) * esz
        n = (n + 63) // 64 * 64
        lo = self.top
        hi = lo + n
        assert hi <= ARENA_BYTES, f"arena overflow allocating {name}: {hi}"
        self.top = hi
        ap = self.arena[:, lo:hi].bitcast(dtype)
        used = int(np.prod(shape))
        ap = ap[:, 0:used]
        if len(shape) == 2:
            ap = ap.rearrange("p (a b) -> p a b", b=shape[1])
        elif len(shape) == 3:
            ap = ap.rearrange("p (a b c) -> p a b c", b=shape[1], c=shape[2])
        elif len(shape) == 4:
            ap = ap.rearrange("p (a b c d) -> p a b c d", b=shape[1], c=shape[2], d=shape[3])
        name = f"{name}#{len(self.tile_keys)}"
        t = Tile(name, ap, lo, hi)
        inh = []
        for (flo, fhi, toks) in self.freed:
            if flo < hi and lo < fhi:
                inh.extend(toks)
        self.tile_init[name] = inh
        self.tile_keys[name] = set()
        self.tiles_live.append(t)
        return t

    def mark(self):
        return (self.top, len(self.tiles_live))

    def release(self, mk):
        top, nlive = mk
        for t in self.tiles_live[nlive:]:
            toks = list(self.tile_init[t.name])
            for k in self.tile_keys[t.name]:
                st = self.state.get(k)
                if st:
                    toks.extend(st[0].items()); toks.extend(st[1].items())
            best = {}
            for (s, v) in toks:
                if v > best.get(s, -1):
                    best[s] = v
            self.freed.append((t.lo, t.hi, list(best.items())))
        del self.tiles_live[nlive:]
        self.top = top

    def bank(self, i):
        return self.psum[:, i * 512:(i + 1) * 512]

    def next_bank(self):
        while True:
            b = self.bank_rr % 8
            self.bank_rr += 1
            if b not in self.reserved_banks:
                return b

    def _key(self, k):
        assert isinstance(k, (Tile, tuple, str)), f"bad dependency key {type(k)}"
        if isinstance(k, Tile):
            k = (k.name, None)
        elif isinstance(k, tuple) and isinstance(k[0], Tile):
            k = (k[0].name,) + tuple(k[1:])
        if isinstance(k, tuple) and k[0] in self.tile_keys:
            self.tile_keys[k[0]].add(k)
        return k

    def _getstate(self, k):
        st = self.state.get(k)
        if st is None:
            inh = self.tile_init.get(k[0], []) if isinstance(k, tuple) else []
            d = {}
            for (s_, v_) in inh:
                if v_ > d.get(s_, -1):
                    d[s_] = v_
            st = [d, {}]
            self.state[k] = st
        return st

    DMA_K = {"sp": 16, "pool": 12, "act": 4, "dve": 2, "pe": 2}

    def _token(self, eng, isdma):
        if isdma:
            K = self.DMA_K[eng]
            c = self.cnt.get((eng, "d"), 0)
            self.cnt[(eng, "d")] = c + 1
            j, m = c % K, c // K
            sk = (eng, "d", j)
            if sk not in self.sems:
                self.sems[sk] = self.es.enter_context(self.nc.semaphore(f"s_{eng}_d{j}"))
                self.nsem += 1
            forced = (sk, 16 * m) if m > 0 else None
            self.final[sk] = 16 * (m + 1)
            return (sk, 16 * (m + 1)), forced
        c = self.cnt.get((eng, "c"), 0)
        epoch, idx = divmod(c, self.CAP_C)
        self.cnt[(eng, "c")] = c + 1
        sk = (eng, "c", epoch)
        if sk not in self.sems:
            self.sems[sk] = self.es.enter_context(self.nc.semaphore(f"s_{eng}_c{epoch}"))
            self.nsem += 1
        self.final[sk] = idx + 1
        return (sk, idx + 1), None

    def op(self, eng, fn, r=(), w=(), dma=False):
        o = Op()
        rec = _Rec()
        fn(rec)
        assert len(rec.calls) >= 1
        o.eng, o.calls, o.isdma = eng, rec.calls, dma
        need = {}
        rk = [self._key(k) for k in r]
        wk = [self._key(k) for k in w]
        for k in rk:
            st = self._getstate(k)
            for (s, v) in st[0].items():
                if v > need.get(s, -1):
                    need[s] = v
            if isinstance(k, tuple) and k[0] == "ps":
                for (s, v) in st[1].items():
                    if s[0] != eng and v > need.get(s, -1):
                        need[s] = v
        for k in wk:
            st = self._getstate(k)
            for (s, v) in list(st[0].items()) + list(st[1].items()):
                if s[0] == eng and s[1] == "c":
                    continue
                if v > need.get(s, -1):
                    need[s] = v
        tok, forced = self._token(eng, dma)
        if forced is not None and forced[1] > need.get(forced[0], -1):
            need[forced[0]] = forced[1]
        waits = []
        seen = self.seen[eng]
        for s, v in need.items():
            if s[0] == "pe" and eng == "pe" and s[1] == "c":
                continue
            if seen.get(s, -1) >= v:
                continue
            seen[s] = v
            waits.append((s, v))
        o.waits = waits
        o.sem, o.val = tok
        for k in rk:
            self.state[k][1][tok[0]] = tok[1]
        for k in wk:
            self.state[k] = [{tok[0]: tok[1]}, {}]
        self.ops[eng].append(o)
        self.n_ops += 1
        return o

    def dma(self, eng, out, in_, r=(), w=(), **kw):
        return self.op(eng, lambda e: e.dma_start(out=out, in_=in_, **kw), r=r, w=w, dma=True)

    def emit(self):
        nc = self.nc
        with nc.Block() as block:
            def run(engname):
                def f(e):
                    for o in self.ops[engname]:
                        for (s, v) in o.waits:
                            e.wait_ge(self.sems[s], v)
                        ins = None
                        for (nm_, a_, k_) in o.calls:
                            ins = getattr(e, nm_)(*a_, **k_)
                        ins.then_inc(self.sems[o.sem], 16 if o.isdma else 1)
                    if engname == "sp":
                        for sk, v in self.final.items():
                            e.wait_ge(self.sems[sk], v)
                return f
            block.tensor(run("pe"))
            block.scalar(run("act"))
            block.vector(run("dve"))
            block.gpsimd(run("pool"))
            block.sync(run("sp"))


def mk(ap, dims, off=0):
    return bass.AP(ap.tensor, ap.offset + off, [list(ap.ap[0])] + [list(d) for d in dims])


def make_consts():
    c = {}
    c["ident"] = np.eye(128, dtype=np.float32)
    c["identbf"] = np.eye(128, dtype=np.float32).astype(ml_dtypes.bfloat16)
    t = np.arange(2048)
    row = (t // 64).astype(np.float32)
    col = (t % 64).astype(np.float32)
    freqs = (np.float32(10000.0) ** (-np.arange(16, dtype=np.float32) / np.float32(16))).astype(np.float32)
    rope = np.zeros((128, 2, 2048), np.float32)
    for p in range(128):
        d = p % 64
        blk, i = divmod(d, 16)
        pos = row if blk < 2 else col
        ang = (pos * freqs[i]).astype(np.float32)
        rope[p, 0] = np.cos(ang)
        rope[p, 1] = -np.sin(ang) if blk in (0, 2) else np.sin(ang)
    c["rope"] = rope
    kl = np.arange(128)[:, None]
    ql = np.arange(128)[None, :]
    mb = np.zeros((128, 2, 512), np.float32)
    lo = np.where(kl >= ql, 0.0, NEG)
    hi = np.where(kl <= ql, 0.0, NEG)
    for hq in range(4):
        mb[:, 0, hq * 128:(hq + 1) * 128] = lo
        mb[:, 1, hq * 128:(hq + 1) * 128] = hi
    c["maskb"] = mb.astype(ml_dtypes.bfloat16)
    xs = np.zeros((128, 8, 240), np.float32)
    for g in range(8):
        for ci in range(16):
            xs[g * 16 + ci, g, 112 + ci] = 1.0
    c["xsel"] = xs.astype(ml_dtypes.bfloat16)
    ys = np.zeros((128, 8, 128), np.float32)
    for g in range(8):
        for j in range(8):
            for co in range(16):
                ys[j * 16 + co, g, g * 16 + co] = 1.0
    c["ysel"] = ys.astype(ml_dtypes.bfloat16)
    mj = np.zeros((128, 8), np.float32)
    for j in range(8):
        mj[j * 16:(j + 1) * 16, j] = 1.0
    c["maskj"] = mj
    tm = np.zeros((128, 2, 128), np.float32)
    s_idx = (np.arange(128) // 16)[:, None]
    j_idx = (np.arange(128) // 16)[None, :]
    tm[:, 0, :] = (j_idx >= s_idx)
    tm[:, 1, :] = (j_idx <= s_idx)
    c["toepm"] = tm
    vs = np.zeros((1, 128), np.float32)
    vs[0, 64:] = 1.0
    c["vsink"] = vs.astype(ml_dtypes.bfloat16)
    return c


CONST_SPECS = [("ident", [128, 128], F32), ("identbf", [128, 128], BF16), ("rope", [128, 2, 2048], F32),
               ("maskb", [128, 2, 512], BF16), ("xsel", [128, 8, 240], BF16), ("ysel", [128, 8, 128], BF16),
               ("maskj", [128, 8], F32), ("toepm", [128, 2, 128], F32), ("vsink", [1, 128], BF16)]

IN_SPECS = [("xp", [512, D]), ("xs", [2048, D]), ("ck", [2, PAST, 128]), ("cv", [2, PAST, 128]),
            ("sre", [2, 2, 16, 64]), ("sim", [2, 2, 16, 64]), ("cvec", [2, D]),
            ("norm_mix", [2, D]), ("norm_ffn", [2, D]), ("norm_final", [D]),
            ("w_ada", [2, D, 6 * D]), ("b_ada", [2, 6 * D]), ("w_in", [2, D, IN_DIM]), ("w_out", [2, D, D]),
            ("attn_sink", [2, 8]), ("sc_conv", [2, 256, 3]),
            ("ssm_lam_re", [2, 2, 16, 64]), ("ssm_lam_im", [2, 2, 16, 64]), ("ssm_log_dt", [2, 2, 16]),
            ("ssm_b_re", [2, 2, 16, 64, 16]), ("ssm_b_im", [2, 2, 16, 64, 16]),
            ("ssm_c_re", [2, 2, 16, 16, 64]), ("ssm_c_im", [2, 2, 16, 16, 64]),
            ("ssm_d", [2, 16, 16]), ("ssm_w_glu", [2, 256, 256]),
            ("ffn_w_up", [2, D, 2 * DFF]), ("ffn_conv", [2, 2 * DFF, 3]), ("ffn_w_down", [2, DFF, D])]

OUT_SPECS = [("yp", [512, D]), ("ys", [2048, D]), ("nk", [2, 2, 256, 128]), ("nv", [2, 2, 256, 128]),
             ("nsr", [2, 2, 2, 16, 64]), ("nsi", [2, 2, 2, 16, 64])]


class Builder:
    def __init__(self, stages=("prep", "P", "S"), dbg=()):
        self.stages = stages
        self.dbg_specs = list(dbg)
        self.nc = nc = bass.Bass("TRN2", target_bir_lowering=False)
        self.es = ExitStack()
        class _Lazy(dict):
            def __init__(s_, specs, prefix):
                super().__init__()
                s_.specs, s_.prefix = specs, prefix

            def __missing__(s_, name):
                shape, dt = s_.specs[name]
                ap = nc.dram_tensor(s_.prefix + name, shape, dt, kind="ExternalInput").ap()
                s_[name] = ap
                return ap
        self.I = _Lazy({n: (sh, F32) for n, sh in IN_SPECS}, "")
        self.C = _Lazy({n: (sh, dt) for n, sh, dt in CONST_SPECS}, "c_")
        self.O = {}
        for name, shape in OUT_SPECS:
            self.O[name] = nc.dram_tensor(name, shape, F32, kind="ExternalOutput").ap()
        for name, shape, dt_ in self.dbg_specs:
            self.O[name] = nc.dram_tensor(name, shape, dt_, kind="ExternalOutput").ap()
        self.W = {}
        for name, kc, n in [("win", 8, IN_DIM), ("wout", 8, D), ("wup", 8, 2 * DFF), ("wdn", 22, D), ("wglu", 2, 256)]:
            self.W[name] = [nc.dram_tensor(f"s_{name}{l}", [128, kc, n], BF16, kind="Internal").ap() for l in range(2)]
        self.S_kt = [nc.dram_tensor(f"s_kt{l}", [128, 16, 128], BF16, kind="Internal").ap() for l in range(2)]
        self.S_ws = [nc.dram_tensor(f"s_ws{l}", [128, 2, 16, 128], BF16, kind="Internal").ap() for l in range(2)]
        self.S_wo = [nc.dram_tensor(f"s_wo{l}", [128, 2, 8, 2, 128], BF16, kind="Internal").ap() for l in range(2)]
        self.S_e = [nc.dram_tensor(f"s_e{l}", [128, 2, 2, 8, 256], F32, kind="Internal").ap() for l in range(2)]
        self.P = Prog(nc, self.es)

    def wkeys(self, name, l, c0, c1, kcs=None):
        nkc = {"win": 8, "wout": 8, "wup": 8, "wdn": 22, "wglu": 2}[name]
        ks = []
        for kc in (range(nkc) if kcs is None else kcs):
            for b in range(c0 // CASTW, (c1 - 1) // CASTW + 1):
                ks.append(("W", name, l, kc, b))
        return ks

    def persistent(self):
        P = self.P
        self.ident = P.alloc("ident", [128], F32)
        self.identbf = P.alloc("identbf", [128], BF16)
        self.onesbf = P.alloc("onesbf", [128], BF16)
        self.modT = P.alloc("modT", [2, 48, 2], F32)
        self.gsc1 = P.alloc("gsc1", [2, 8, 2], F32)
        self.gsc2 = P.alloc("gsc2", [2, 8, 2], F32)
        self.nfT = P.alloc("nfT", [8], F32)
        self.a8mag = P.alloc("a8mag", [4, 8], F32)
        self.e1 = P.alloc("e1", [2, 4, 8], F32)
        self.esrow = P.alloc("esrow", [2, 8, 128], BF16)
        self.vsink = P.alloc("vsink", [128], BF16)
        self.scw = P.alloc("scw", [2, 2, 3], F32)
        self.fcw = P.alloc("fcw", [2, 44, 3], F32)
        self.maskj = P.alloc("maskj", [8], F32)
        P.dma("sp", self.ident[:], self.C["ident"][:, :], w=[self.ident])
        P.dma("sp", self.identbf[:], self.C["identbf"][:, :], w=[self.identbf])
        P.dma("sp", self.maskj[:], self.C["maskj"][:, :], w=[self.maskj])
        P.dma("sp", self.vsink[0:1, :], self.C["vsink"][:, :], w=[self.vsink])
        P.op("dve", lambda e: e.memset(self.onesbf[:], 1.0), w=[self.onesbf])
        I = self.I
        P.dma("sp", self.nfT[:], I["norm_final"].rearrange("(c p) -> p c", p=128), w=[self.nfT],
              allow_slow_non_contiguous=True)
        for l in range(2):
            P.dma("sp", self.scw[:, l], I["sc_conv"][l].rearrange("(c p) k -> p c k", p=128), w=[self.scw])
            P.dma("sp", self.fcw[:, l], I["ffn_conv"][l].rearrange("(c p) k -> p c k", p=128), w=[self.fcw])

    def prep_casts(self):
        P, I = self.P, self.I
        mk0 = P.mark()
        NB = 4
        stage = [P.alloc(f"cst{i}", [CASTW], F32) for i in range(NB)]
        outb = [P.alloc(f"cob{i}", [CASTW], BF16) for i in range(NB)]
        pieces = []
        for l in range(2):
            for name, src, K, N in [("win", "w_in", D, IN_DIM), ("wglu", "ssm_w_glu", 256, 256), ("wout", "w_out", D, D),
                                    ("wup", "ffn_w_up", D, 2 * DFF), ("wdn", "ffn_w_down", DFF, D)]:
                if self.cast_only is not None and name not in self.cast_only:
                    continue
                for kc in range(K // 128):
                    for b, c0 in enumerate(range(0, N, CASTW)):
                        pieces.append((l, name, src, kc, b, c0, min(CASTW, N - c0)))

        def load(i):
            l, name, src, kc, b, c0, cw = pieces[i]
            st = stage[i % NB]
            P.dma("pool", st[:, 0:cw], I[src][l, kc * 128:(kc + 1) * 128, c0:c0 + cw], w=[st])
        PRE = 3
        for i in range(min(PRE, len(pieces))):
            load(i)
        for i, (l, name, src, kc, b, c0, cw) in enumerate(pieces):
            st, ob = stage[i % NB], outb[i % NB]
            P.op("pool", lambda e, st=st, ob=ob, cw=cw: e.tensor_copy(out=ob[:, 0:cw], in_=st[:, 0:cw]), r=[st], w=[ob])
            P.dma("pool", self.W[name][l][:, kc, c0:c0 + cw], ob[:, 0:cw], r=[ob], w=[("W", name, l, kc, b)])
            if i + PRE < len(pieces):
                load(i + PRE)
        self.cast_mark = mk0

    def prep_adaln_setup(self):
        P, I = self.P, self.I
        self.scT = P.alloc("scT", [8, 2], F32)
        self.bT = P.alloc("bT", [2, 48], F32)
        self.nmT = P.alloc("nmT", [2, 2, 8], F32)
        mk0 = P.mark()
        craw = P.alloc("craw", [8, 2], F32)
        for v in range(2):
            P.dma("sp", craw[:, :, v], I["cvec"][v].rearrange("(c p) -> p c", p=128), w=[craw], allow_slow_non_contiguous=True)
        P.op("act", lambda e: e.activation(out=self.scT[:], in_=craw[:], func=AF.Silu), r=[craw], w=[self.scT])
        for l in range(2):
            P.dma("sp", self.bT[:, l], I["b_ada"][l].rearrange("(c p) -> p c", p=128), w=[self.bT], allow_slow_non_contiguous=True)
            P.dma("sp", self.nmT[:, 0, l], I["norm_mix"][l].rearrange("(c p) -> p c", p=128), w=[self.nmT], allow_slow_non_contiguous=True)
            P.dma("sp", self.nmT[:, 1, l], I["norm_ffn"][l].rearrange("(c p) -> p c", p=128), w=[self.nmT], allow_slow_non_contiguous=True)
        P.release(mk0)
        self.adaln_done = set()

    def prep_adaln(self, l):
        if l in self.adaln_done:
            return
        self.adaln_done.add(l)
        P, I = self.P, self.I
        scT, bT, nmT = self.scT, self.bT, self.nmT
        mk0 = P.mark()
        wa = [P.alloc(f"wa{i}", [8, 512], F32) for i in range(2)]
        bk = P.next_bank()
        P.reserved_banks.add(bk)
        ps = P.bank(bk)
        for j in range(12):
            w = wa[j % 2]
            P.dma("sp", w[:], I["w_ada"][l, :, j * 512:(j + 1) * 512].rearrange("(kc p) n -> p kc n", p=128), w=[w])
            for oc in range(4):
                col = (j * 4 + oc) * 2
                for kc in range(8):
                    P.op("pe", lambda e, w=w, oc=oc, kc=kc, col=col, ps=ps: e.matmul(
                        ps[:, col:col + 2], lhsT=w[:, kc, oc * 128:(oc + 1) * 128], rhs=scT[:, kc, :],
                        start=(kc == 0), stop=(kc == 7)), r=[w, scT], w=[("ps", bk)])
        P.op("dve", lambda e, l=l, ps=ps: e.tensor_tensor(
            out=self.modT[:, l], in0=ps[:, 0:96].rearrange("p (a b) -> p a b", b=2),
            in1=mk(bT[:, l], [[1, 48], [0, 2]]), op=ALU.add), r=[("ps", bk), bT], w=[(self.modT, l)])
        P.op("dve", lambda e, l=l: e.scalar_tensor_tensor(
            out=self.gsc1[:, l], in0=self.modT[:, l, 8:16, :], scalar=1.0,
            in1=mk(nmT[:, 0, l], [[1, 8], [0, 2]]), op0=ALU.add, op1=ALU.mult), r=[(self.modT, l), nmT], w=[(self.gsc1, l)])
        P.op("dve", lambda e, l=l: e.scalar_tensor_tensor(
            out=self.gsc2[:, l], in0=self.modT[:, l, 32:40, :], scalar=1.0,
            in1=mk(nmT[:, 1, l], [[1, 8], [0, 2]]), op0=ALU.add, op1=ALU.mult), r=[(self.modT, l), nmT], w=[(self.gsc2, l)])
        P.reserved_banks.discard(bk)
        P.release(mk0)

    def prep_ssm(self):
        P, I, C = self.P, self.I, self.C
        mk0 = P.mark()
        V = lambda fn, r, w: P.op("dve", fn, r=r, w=w)
        A = lambda fn, r, w: P.op("act", fn, r=r, w=w)
        LD = [(l, d) for l in range(2) for d in range(2)]
        lamr = P.alloc("lamr", [4, 8], F32); lami = P.alloc("lami", [4, 8], F32); ldt = P.alloc("ldt", [4, 8], F32)
        for i, (l, d) in enumerate(LD):
            for h in range(2):
                for tl, nm in ((lamr, "ssm_lam_re"), (lami, "ssm_lam_im")):
                    src = I[nm][l, d]
                    P.dma("sp", tl[64 * h:64 * h + 64, i, :], bass.AP(src.tensor, src.offset + h * 64, [[1, 64], [128, 8]]),
                          w=[tl], allow_slow_non_contiguous=True)
                src = I["ssm_log_dt"][l, d]
                P.dma("sp", ldt[64 * h:64 * h + 64, i, :], bass.AP(src.tensor, src.offset + h, [[0, 64], [2, 8]]),
                      w=[ldt], allow_slow_non_contiguous=True)
        names = ["dt", "lrdt", "mag", "ang", "rs", "rc", "sn", "cs", "ar", "ai", "den", "rden", "am1", "t1", "t2", "t3", "t4",
                 "fr", "fi", "mag2", "rm", "ivr", "ivi", "inv8"]
        T = {n: P.alloc(n, [4, 8], F32) for n in names}
        negpi = P.alloc("negpi", [1], F32)
        V(lambda e: e.memset(negpi[:], -math.pi), [], [negpi])
        qi = P.alloc("qi", [4, 8], I32)

        def taylor_exp(out_t, x_t, deg, tmp):
            V(lambda e: e.tensor_scalar(out=out_t[:], in0=x_t[:], scalar1=1.0 / deg, scalar2=1.0, op0=ALU.mult, op1=ALU.add), [x_t], [out_t])
            for k in range(deg - 1, 0, -1):
                V(lambda e: e.tensor_tensor(out=tmp[:], in0=out_t[:], in1=x_t[:], op=ALU.mult), [out_t, x_t], [tmp])
                V(lambda e, k=k: e.tensor_scalar(out=out_t[:], in0=tmp[:], scalar1=1.0 / k, scalar2=1.0, op0=ALU.mult, op1=ALU.add), [tmp], [out_t])
        V(lambda e: e.tensor_copy(out=qi[:], in_=ldt[:]), [ldt], [qi])
        V(lambda e: e.tensor_copy(out=T["t1"][:], in_=qi[:]), [qi], [T["t1"]])
        V(lambda e: e.tensor_tensor(out=T["t2"][:], in0=ldt[:], in1=T["t1"][:], op=ALU.subtract), [ldt, T["t1"]], [T["t2"]])
        taylor_exp(T["t3"], T["t2"], 12, T["t4"])
        V(lambda e: e.memset(T["dt"][:], 0.0), [], [T["dt"]])
        for j in range(-10, 1):
            V(lambda e, j=j: e.tensor_scalar(out=T["t4"][:], in0=T["t1"][:], scalar1=float(j), scalar2=math.exp(j), op0=ALU.is_equal, op1=ALU.mult),
              [T["t1"]], [T["t4"]])
            V(lambda e: e.tensor_tensor(out=T["dt"][:], in0=T["dt"][:], in1=T["t4"][:], op=ALU.add), [T["dt"], T["t4"]], [T["dt"]])
        V(lambda e: e.tensor_tensor(out=T["dt"][:], in0=T["dt"][:], in1=T["t3"][:], op=ALU.mult), [T["dt"], T["t3"]], [T["dt"]])
        V(lambda e: e.tensor_tensor(out=T["lrdt"][:], in0=lamr[:], in1=T["dt"][:], op=ALU.mult), [lamr, T["dt"]], [T["lrdt"]])
        taylor_exp(T["mag"], T["lrdt"], 7, T["t4"])
        V(lambda e: e.tensor_tensor(out=T["t1"][:], in0=T["mag"][:], in1=T["mag"][:], op=ALU.mult), [T["mag"]], [T["t1"]])
        V(lambda e: e.tensor_tensor(out=T["t2"][:], in0=T["t1"][:], in1=T["t1"][:], op=ALU.mult), [T["t1"]], [T["t2"]])
        V(lambda e: e.tensor_tensor(out=self.a8mag[:], in0=T["t2"][:], in1=T["t2"][:], op=ALU.mult), [T["t2"]], [self.a8mag])
        V(lambda e: e.reciprocal(out=T["inv8"][:], in_=self.a8mag[:]), [self.a8mag], [T["inv8"]])
        V(lambda e: e.tensor_tensor(out=T["ang"][:], in0=lami[:], in1=T["dt"][:], op=ALU.mult), [lami, T["dt"]], [T["ang"]])
        def range_reduce(out_t, add):
            V(lambda e: e.tensor_scalar(out=T["t1"][:], in0=T["ang"][:], scalar1=add, scalar2=1.0 / TWO_PI, op0=ALU.add, op1=ALU.mult),
              [T["ang"]], [T["t1"]])
            V(lambda e: e.tensor_copy(out=qi[:], in_=T["t1"][:]), [T["t1"]], [qi])
            V(lambda e: e.tensor_copy(out=T["t2"][:], in_=qi[:]), [qi], [T["t2"]])
            V(lambda e: e.scalar_tensor_tensor(out=T["t3"][:], in0=T["t2"][:], scalar=-TWO_PI, in1=T["ang"][:], op0=ALU.mult, op1=ALU.add),
              [T["t2"], T["ang"]], [T["t3"]])
            V(lambda e: e.tensor_scalar_add(out=T["t3"][:], in0=T["t3"][:], scalar1=add), [T["t3"]], [T["t3"]])
            V(lambda e: e.tensor_scalar(out=T["t4"][:], in0=T["t3"][:], scalar1=math.pi, scalar2=-TWO_PI, op0=ALU.is_gt, op1=ALU.mult),
              [T["t3"]], [T["t4"]])
            V(lambda e: e.tensor_tensor(out=T["t3"][:], in0=T["t3"][:], in1=T["t4"][:], op=ALU.add), [T["t3"], T["t4"]], [T["t3"]])
            V(lambda e: e.tensor_scalar(out=T["t4"][:], in0=T["t3"][:], scalar1=-math.pi, scalar2=TWO_PI, op0=ALU.is_lt, op1=ALU.mult),
              [T["t3"]], [T["t4"]])
            V(lambda e: e.tensor_tensor(out=out_t[:], in0=T["t3"][:], in1=T["t4"][:], op=ALU.add), [T["t3"], T["t4"]], [out_t])
        range_reduce(T["rs"], 0.0)
        xx, x2, ps_, pc_ = T["t1"], T["t2"], T["t3"], T["t4"]
        V(lambda e: e.tensor_scalar_mul(out=xx[:], in0=T["rs"][:], scalar1=0.25), [T["rs"]], [xx])
        V(lambda e: e.tensor_tensor(out=x2[:], in0=xx[:], in1=xx[:], op=ALU.mult), [xx], [x2])

        def horner(p, coefs):
            V(lambda e: e.tensor_scalar(out=p[:], in0=x2[:], scalar1=coefs[0], scalar2=1.0, op0=ALU.mult, op1=ALU.add), [x2], [p])
            for cf in coefs[1:]:
                V(lambda e: e.tensor_tensor(out=p[:], in0=p[:], in1=x2[:], op=ALU.mult), [p, x2], [p])
                V(lambda e, cf=cf: e.tensor_scalar(out=p[:], in0=p[:], scalar1=cf, scalar2=1.0, op0=ALU.mult, op1=ALU.add), [p], [p])
        horner(ps_, [-1.0 / 110.0, -1.0 / 72.0, -1.0 / 42.0, -1.0 / 20.0, -1.0 / 6.0])
        V(lambda e: e.tensor_tensor(out=ps_[:], in0=ps_[:], in1=xx[:], op=ALU.mult), [ps_, xx], [ps_])
        horner(pc_, [-1.0 / 90.0, -1.0 / 56.0, -1.0 / 30.0, -1.0 / 12.0, -1.0 / 2.0])
        sA, cA = T["sn"], T["cs"]
        for it in range(2):
            V(lambda e: e.scalar_tensor_tensor(out=sA[:], in0=ps_[:], scalar=2.0, in1=pc_[:], op0=ALU.mult, op1=ALU.mult), [ps_, pc_], [sA])
            V(lambda e: e.tensor_tensor(out=cA[:], in0=ps_[:], in1=ps_[:], op=ALU.mult), [ps_], [cA])
            V(lambda e: e.tensor_scalar(out=cA[:], in0=cA[:], scalar1=-2.0, scalar2=1.0, op0=ALU.mult, op1=ALU.add), [cA], [cA])
            if it == 0:
                V(lambda e: e.tensor_copy(out=ps_[:], in_=sA[:]), [sA], [ps_])
                V(lambda e: e.tensor_copy(out=pc_[:], in_=cA[:]), [cA], [pc_])

        def tt(o, a, b, op):
            V(lambda e: e.tensor_tensor(out=o[:], in0=a[:], in1=b[:], op=op), [a, b], [o])
        tt(T["ar"], T["mag"], T["cs"], ALU.mult)
        tt(T["ai"], T["mag"], T["sn"], ALU.mult)
        tt(T["t1"], lamr, lamr, ALU.mult)
        tt(T["t2"], lami, lami, ALU.mult)
        tt(T["den"], T["t1"], T["t2"], ALU.add)
        V(lambda e: e.reciprocal(out=T["rden"][:], in_=T["den"][:]), [T["den"]], [T["rden"]])
        V(lambda e: e.tensor_scalar_add(out=T["am1"][:], in0=T["ar"][:], scalar1=-1.0), [T["ar"]], [T["am1"]])
        tt(T["t1"], T["am1"], lamr, ALU.mult)
        tt(T["t2"], T["ai"], lami, ALU.mult)
        tt(T["t3"], T["t1"], T["t2"], ALU.add)
        tt(T["fr"], T["t3"], T["rden"], ALU.mult)
        tt(T["t1"], T["ai"], lamr, ALU.mult)
        tt(T["t2"], T["am1"], lami, ALU.mult)
        tt(T["t3"], T["t1"], T["t2"], ALU.subtract)
        tt(T["fi"], T["t3"], T["rden"], ALU.mult)
        tt(T["t1"], T["ar"], T["ar"], ALU.mult)
        tt(T["t2"], T["ai"], T["ai"], ALU.mult)
        tt(T["mag2"], T["t1"], T["t2"], ALU.add)
        V(lambda e: e.reciprocal(out=T["rm"][:], in_=T["mag2"][:]), [T["mag2"]], [T["rm"]])
        tt(T["ivr"], T["ar"], T["rm"], ALU.mult)
        V(lambda e: e.scalar_tensor_tensor(out=T["ivi"][:], in0=T["ai"][:], scalar=-1.0, in1=T["rm"][:], op0=ALU.mult, op1=ALU.mult),
          [T["ai"], T["rm"]], [T["ivi"]])
        if self.cut == 1:
            P.release(mk0); return
        apr = P.alloc("apr", [4, 17, 8], F32); api = P.alloc("api", [4, 17, 8], F32)
        V(lambda e: e.memset(apr[:, :, 8, :], 1.0), [], [(apr, 8)])
        V(lambda e: e.memset(api[:, :, 8, :], 0.0), [], [(api, 8)])
        V(lambda e: e.tensor_copy(out=apr[:, :, 9, :], in_=T["ar"][:]), [T["ar"]], [(apr, 9)])
        V(lambda e: e.tensor_copy(out=api[:, :, 9, :], in_=T["ai"][:]), [T["ai"]], [(api, 9)])
        V(lambda e: e.tensor_copy(out=apr[:, :, 7, :], in_=T["ivr"][:]), [T["ivr"]], [(apr, 7)])
        V(lambda e: e.tensor_copy(out=api[:, :, 7, :], in_=T["ivi"][:]), [T["ivi"]], [(api, 7)])

        def cmul_small(k_out, k_in, br, bi):
            xr, xi = apr[:, :, k_in, :], api[:, :, k_in, :]
            V(lambda e: e.tensor_tensor(out=T["t1"][:], in0=xr, in1=br[:], op=ALU.mult), [(apr, k_in), br], [T["t1"]])
            V(lambda e: e.tensor_tensor(out=T["t2"][:], in0=xi, in1=bi[:], op=ALU.mult), [(api, k_in), bi], [T["t2"]])
            V(lambda e: e.tensor_tensor(out=apr[:, :, k_out, :], in0=T["t1"][:], in1=T["t2"][:], op=ALU.subtract), [T["t1"], T["t2"]], [(apr, k_out)])
            V(lambda e: e.tensor_tensor(out=T["t3"][:], in0=xr, in1=bi[:], op=ALU.mult), [(apr, k_in), bi], [T["t3"]])
            V(lambda e: e.tensor_tensor(out=T["t4"][:], in0=xi, in1=br[:], op=ALU.mult), [(api, k_in), br], [T["t4"]])
            V(lambda e: e.tensor_tensor(out=api[:, :, k_out, :], in0=T["t3"][:], in1=T["t4"][:], op=ALU.add), [T["t3"], T["t4"]], [(api, k_out)])
        for k in range(9, 16):
            cmul_small(k + 1, k, T["ar"], T["ai"])
        for k in range(7, 0, -1):
            cmul_small(k - 1, k, T["ivr"], T["ivi"])
        APW_R = [(apr, k) for k in range(17)]
        APW_I = [(api, k) for k in range(17)]
        V(lambda e: e.tensor_tensor(out=self.e1[:, 0], in0=apr[:, :, 16, :], in1=T["inv8"][:], op=ALU.mult), [(apr, 16), T["inv8"]], [self.e1])
        V(lambda e: e.tensor_tensor(out=self.e1[:, 1], in0=api[:, :, 16, :], in1=T["inv8"][:], op=ALU.mult), [(api, 16), T["inv8"]], [self.e1])

        if self.cut == 2:
            P.release(mk0); return
        Et = P.alloc("Et", [2, 8, 256], F32)
        wk = P.alloc("wk", [2, 2, 8], F32)
        et1 = P.alloc("et1", [8, 128], F32); et2 = P.alloc("et2", [8, 128], F32)
        Br = P.alloc("Br", [8, 16], F32); Bi = P.alloc("Bi", [8, 16], F32)
        bbr = P.alloc("bbr", [8, 16], F32); bbi = P.alloc("bbi", [8, 16], F32)
        Cr = P.alloc("Cr", [8, 16], F32); Ci = P.alloc("Ci", [8, 16], F32)
        cn = P.alloc("cn", [2, 64], F32)
        pbr = P.alloc("pbr", [8, 8, 16], F32); pbi = P.alloc("pbi", [8, 8, 16], F32)
        pcr = P.alloc("pcr", [8, 8, 16], F32); pci = P.alloc("pci", [8, 8, 16], F32)
        q1 = P.alloc("q1", [8, 8, 16], F32); q2 = P.alloc("q2", [8, 8, 16], F32)
        wsb = P.alloc("wsb", [16, 128], BF16)
        wob = P.alloc("wob", [8, 2, 128], BF16)
        ktacc = P.alloc("ktacc", [16, 128], F32)
        ktb = P.alloc("ktb", [16, 128], BF16)
        toep = P.alloc("toep", [2, 128], F32)
        dtab = P.alloc("dtab", [16], F32)
        ktmp = P.alloc("ktmp", [128], F32)
        P.dma("sp", toep[:], C["toepm"][:, :, :], w=[toep])

        def bc_last(ap2, n):
            return mk(ap2, [list(ap2.ap[1]), [0, n]])

        def cprod(outr, outi, kstart, kstep, Xr, Xi, neg_im, xkeys):
            a_r = apr[:, ld, kstart, :]
            a_i = api[:, ld, kstart, :]
            AR = mk(a_r, [[1, 8], [8 * kstep, 8], [0, 16]])
            AI = mk(a_i, [[1, 8], [8 * kstep, 8], [0, 16]])
            XR = mk(Xr[:], [[16, 8], [0, 8], [1, 16]])
            XI = mk(Xi[:], [[16, 8], [0, 8], [1, 16]])
            V(lambda e: e.tensor_tensor(out=q1[:], in0=AR, in1=XR, op=ALU.mult), APW_R + xkeys, [q1])
            V(lambda e: e.tensor_tensor(out=q2[:], in0=AI, in1=XI, op=ALU.mult), APW_I + xkeys, [q2])
            V(lambda e: e.tensor_tensor(out=outr[:], in0=q1[:], in1=q2[:], op=ALU.subtract), [q1, q2], [outr])
            V(lambda e: e.tensor_tensor(out=q1[:], in0=AR, in1=XI, op=ALU.mult), APW_R + xkeys, [q1])
            V(lambda e: e.tensor_tensor(out=q2[:], in0=AI, in1=XR, op=ALU.mult), APW_I + xkeys, [q2])
            if neg_im:
                V(lambda e: e.scalar_tensor_tensor(out=outi[:], in0=q1[:], scalar=-1.0, in1=q2[:], op0=ALU.mult, op1=ALU.subtract),
                  [q1, q2], [outi])
            else:
                V(lambda e: e.tensor_tensor(out=outi[:], in0=q1[:], in1=q2[:], op=ALU.add), [q1, q2], [outi])

        for ld, (l, d) in enumerate(LD):
            V(lambda e: e.memset(Et[:, 0, :, 0:1], 1.0), [], [Et])
            V(lambda e: e.memset(Et[:, 1, :, 0:1], 0.0), [], [Et])
            V(lambda e, ld=ld: e.tensor_copy(out=wk[:, 0, 0, :], in_=self.e1[:, 0, ld, :]), [self.e1], [wk])
            V(lambda e, ld=ld: e.tensor_copy(out=wk[:, 0, 1, :], in_=self.e1[:, 1, ld, :]), [self.e1], [wk])
            for k in range(8):
                n = 1 << k
                pp, qq = k % 2, (k + 1) % 2
                wr = mk(wk[:, pp, 0, :], [[1, 8], [0, n]])
                wi = mk(wk[:, pp, 1, :], [[1, 8], [0, n]])
                t1v = et1[:, :, 0:n]; t2v = et2[:, :, 0:n]
                V(lambda e, n=n, wr=wr, t1v=t1v: e.tensor_tensor(out=t1v, in0=Et[:, 0, :, 0:n], in1=wr, op=ALU.mult), [Et, wk], [et1])
                V(lambda e, n=n, wi=wi, t2v=t2v: e.tensor_tensor(out=t2v, in0=Et[:, 1, :, 0:n], in1=wi, op=ALU.mult), [Et, wk], [et2])
                V(lambda e, n=n, t1v=t1v, t2v=t2v: e.tensor_tensor(out=Et[:, 0, :, n:2 * n], in0=t1v, in1=t2v, op=ALU.subtract), [et1, et2], [Et])
                V(lambda e, n=n, wi=wi, t1v=t1v: e.tensor_tensor(out=t1v, in0=Et[:, 0, :, 0:n], in1=wi, op=ALU.mult), [Et, wk], [et1])
                V(lambda e, n=n, wr=wr, t2v=t2v: e.tensor_tensor(out=t2v, in0=Et[:, 1, :, 0:n], in1=wr, op=ALU.mult), [Et, wk], [et2])
                V(lambda e, n=n, t1v=t1v, t2v=t2v: e.tensor_tensor(out=Et[:, 1, :, n:2 * n], in0=t1v, in1=t2v, op=ALU.add), [et1, et2], [Et])
                if k < 7:
                    a_r, a_i = wk[:, pp, 0, :], wk[:, pp, 1, :]
                    s1 = et1[:, :, 0]; s2 = et2[:, :, 0]
                    V(lambda e, a_r=a_r, s1=s1: e.tensor_tensor(out=s1, in0=a_r, in1=a_r, op=ALU.mult), [wk], [et1])
                    V(lambda e, a_i=a_i, s2=s2: e.tensor_tensor(out=s2, in0=a_i, in1=a_i, op=ALU.mult), [wk], [et2])
                    V(lambda e, qq=qq, s1=s1, s2=s2: e.tensor_tensor(out=wk[:, qq, 0, :], in0=s1, in1=s2, op=ALU.subtract), [et1, et2], [wk])
                    V(lambda e, a_r=a_r, a_i=a_i, s1=s1: e.tensor_tensor(out=s1, in0=a_r, in1=a_i, op=ALU.mult), [wk], [et1])
                    V(lambda e, qq=qq, s1=s1: e.tensor_scalar_mul(out=wk[:, qq, 1, :], in0=s1, scalar1=2.0), [et1], [wk])
            P.dma("sp", self.S_e[l][:, d], Et[:], r=[Et], w=[("S_e", l, d)])
            if self.cut == 3:
                continue
            for h in range(2):
                for tl, nm in ((Br, "ssm_b_re"), (Bi, "ssm_b_im")):
                    src = I[nm][l, d]
                    P.dma("sp", tl[64 * h:64 * h + 64, :, :], bass.AP(src.tensor, src.offset + h * 1024, [[16, 64], [2048, 8], [1, 16]]), w=[tl])
            FR = bc_last(T["fr"][:, ld, :], 16); FI = bc_last(T["fi"][:, ld, :], 16)
            V(lambda e, FR=FR: e.tensor_tensor(out=q1[:, :, 0, :], in0=Br[:], in1=FR, op=ALU.mult), [Br, T["fr"]], [q1])
            V(lambda e, FI=FI: e.tensor_tensor(out=q2[:, :, 0, :], in0=Bi[:], in1=FI, op=ALU.mult), [Bi, T["fi"]], [q2])
            V(lambda e: e.tensor_tensor(out=bbr[:], in0=q1[:, :, 0, :], in1=q2[:, :, 0, :], op=ALU.subtract), [q1, q2], [bbr])
            V(lambda e, FR=FR: e.tensor_tensor(out=q1[:, :, 0, :], in0=Bi[:], in1=FR, op=ALU.mult), [Bi, T["fr"]], [q1])
            V(lambda e, FI=FI: e.tensor_tensor(out=q2[:, :, 0, :], in0=Br[:], in1=FI, op=ALU.mult), [Br, T["fi"]], [q2])
            V(lambda e: e.tensor_tensor(out=bbi[:], in0=q1[:, :, 0, :], in1=q2[:, :, 0, :], op=ALU.add), [q1, q2], [bbi])
            for Cx, nm in ((Cr, "ssm_c_re"), (Ci, "ssm_c_im")):
                P.dma("sp", cn[:], I[nm][l, d].rearrange("(t g) c n -> (g c) t n", t=2), w=[cn])
                for t in range(2):
                    bk = P.next_bank(); ps = P.bank(bk)
                    P.op("pe", lambda e, t=t, ps=ps: e.transpose(out=ps[0:64, 0:128], in_=cn[:, t, :], identity=self.ident[:]),
                         r=[cn, self.ident], w=[("ps", bk)])
                    for par in range(2):
                        src_ = mk(ps[0:64, 0:128], [[32, 4], [1, 16]], off=par * 16)
                        V(lambda e, Cx=Cx, t=t, par=par, src_=src_: e.tensor_copy(out=Cx[64 * par:64 * par + 64, 4 * t:4 * t + 4, :], in_=src_),
                          [("ps", bk)], [Cx])
            if self.cut == 4:
                continue
            if d == 0:
                cprod(pbr, pbi, 15, -1, bbr, bbi, False, [bbr, bbi])
                cprod(pcr, pci, 1, 1, Cr, Ci, True, [Cr, Ci])
            else:
                cprod(pbr, pbi, 8, 1, bbr, bbi, False, [bbr, bbi])
                cprod(pcr, pci, 8, -1, Cr, Ci, True, [Cr, Ci])
            if self.cut == 5:
                continue
            for ggp in range(4):
                bk = P.next_bank(); ps = P.bank(bk)
                for ggl in range(2):
                    gg = 2 * ggp + ggl
                    for comp, pb in enumerate((pbr, pbi)):
                        col = (ggl * 2 + comp) * 128
                        P.op("pe", lambda e, pb=pb, gg=gg, col=col, ps=ps: e.transpose(
                            out=ps[:, col:col + 128], in_=pb[:, gg, :, :].rearrange("p s c -> p (s c)"),
                            identity=self.ident[:]), r=[pb, self.ident], w=[("ps", bk)])
                for comp in range(2):
                    src_ = mk(ps[:, comp * 128:comp * 128 + 1], [[256, 2], [64, 2], [1, 64]])
                    dst_ = mk(wsb[:, 4 * ggp, comp * 64:comp * 64 + 1], [[256, 2], [128, 2], [1, 64]])
                    V(lambda e, src_=src_, dst_=dst_: e.tensor_copy(out=dst_, in_=src_), [("ps", bk)], [wsb])
            P.dma("sp", self.S_ws[l][:, d], wsb[:], r=[wsb], w=[("S_ws", l, d)])
            if self.cut == 6:
                continue
            if d == 0:
                for s8 in range(8):
                    src = I["ssm_d"][l]
                    P.dma("sp", dtab[16 * s8:16 * s8 + 16, :], bass.AP(src.tensor, src.offset, [[1, 16], [16, 16]]), w=[dtab],
                          allow_slow_non_contiguous=True)
            for g in range(16):
                h, gg = g % 2, g // 2
                bk = P.next_bank(); ps = P.bank(bk)
                P.op("pe", lambda e, h=h, gg=gg, ps=ps: e.matmul(
                    ps[:, 0:128], lhsT=pbr[64 * h:64 * h + 64, gg, :, :].rearrange("p s c -> p (s c)"),
                    rhs=pcr[64 * h:64 * h + 64, gg, :, :].rearrange("p s c -> p (s c)"), start=True, stop=False),
                    r=[pbr, pcr], w=[("ps", bk)])
                P.op("pe", lambda e, h=h, gg=gg, ps=ps: e.matmul(
                    ps[:, 0:128], lhsT=pbi[64 * h:64 * h + 64, gg, :, :].rearrange("p s c -> p (s c)"),
                    rhs=pci[64 * h:64 * h + 64, gg, :, :].rearrange("p s c -> p (s c)"), start=False, stop=True),
                    r=[pbi, pci], w=[("ps", bk)])
                if d == 0:
                    V(lambda e, g=g, ps=ps: e.tensor_tensor(out=ktacc[:, g, :], in0=ps[:, 0:128], in1=toep[:, 0, :], op=ALU.mult),
                      [("ps", bk), toep], [(ktacc, g)])
                else:
                    V(lambda e, ps=ps: e.tensor_tensor(out=ktmp[:], in0=ps[:, 0:128], in1=toep[:, 1, :], op=ALU.mult),
                      [("ps", bk), toep], [ktmp])
                    V(lambda e, g=g: e.tensor_tensor(out=ktacc[:, g, :], in0=ktacc[:, g, :], in1=ktmp[:], op=ALU.add),
                      [ktmp, (ktacc, g)], [(ktacc, g)])
                    V(lambda e, g=g: e.scalar_tensor_tensor(out=ktb[:, g, :], in0=self.ident[:], scalar=dtab[:, g:g + 1],
                                                              in1=ktacc[:, g, :], op0=ALU.mult, op1=ALU.add),
                      [self.ident, dtab, (ktacc, g)], [ktb])
            if d == 1:
                P.dma("sp", self.S_kt[l], ktb[:], r=[ktb], w=[("S_kt", l)])
            if self.cut == 7:
                continue
            if d == 0:
                cprod(pcr, pci, 9, 1, Cr, Ci, True, [Cr, Ci])
            else:
                cprod(pcr, pci, 16, -1, Cr, Ci, True, [Cr, Ci])
            V(lambda e: e.tensor_copy(out=wob[:, :, 0, :], in_=pcr[:].rearrange("p g j c -> p g (j c)")), [pcr], [wob])
            V(lambda e: e.tensor_copy(out=wob[:, :, 1, :], in_=pci[:].rearrange("p g j c -> p g (j c)")), [pci], [wob])
            P.dma("sp", self.S_wo[l][:, d], wob[:], r=[wob], w=[("S_wo", l, d)])
        sk = P.alloc("sk", [16], F32); ske = P.alloc("ske", [16], F32)
        P.dma("sp", sk[0:1, :], I["attn_sink"].rearrange("l h -> (l h)").rearrange("(o n) -> o n", o=1), w=[sk])
        A(lambda e: e.activation(out=ske[0:1, :], in_=sk[0:1, :], func=AF.Exp), [sk], [ske])
        V(lambda e: e.tensor_copy(out=self.esrow[0:1, :, :, :].rearrange("p l h q -> p (l h) q"),
                                  in_=mk(ske[0:1, :], [[1, 16], [0, 128]])), [ske], [self.esrow])
        P.release(mk0)

    def dbg_out(self, name, tile_ap, keys):
        if name in self.O:
            self.P.dma("sp", self.O[name], tile_ap, r=keys, w=[("dbg", name)])

    def load_x(self, U):
        P = self.P
        T = U["T"]
        mk0 = P.mark()
        xst = [P.alloc(f"xst{i}", [4, D], F32) for i in range(2)]
        for blk in range(T // 512):
            st = xst[blk % 2]
            P.dma("sp", st[:], U["x"][blk * 512:(blk + 1) * 512, :].rearrange("(i p) f -> p i f", p=128), w=[st])
            for c in range(8):
                bk = P.next_bank(); ps = P.bank(bk)
                for i in range(4):
                    P.op("pe", lambda e, st=st, i=i, c=c, ps=ps: e.transpose(
                        out=ps[:, i * 128:(i + 1) * 128], in_=st[:, i, c * 128:(c + 1) * 128], identity=self.ident[:]),
                        r=[st, self.ident], w=[("ps", bk)])
                eng = "act" if c % 2 else "dve"
                dst = self.xT[:, c, blk * 512:(blk + 1) * 512]
                if eng == "act":
                    P.op("act", lambda e, dst=dst, ps=ps: e.copy(out=dst, in_=ps[:, :]), r=[("ps", bk)], w=[(self.xT, c, blk)])
                else:
                    P.op("dve", lambda e, dst=dst, ps=ps: e.tensor_copy(out=dst, in_=ps[:, :]), r=[("ps", bk)], w=[(self.xT, c, blk)])
        P.release(mk0)

    def xkeys(self, t0, n, cs=range(8)):
        return [(self.xT, c, b) for c in cs for b in range(t0 // 512, (t0 + n - 1) // 512 + 1)]

    def norm_cols(self, t0, n):
        P = self.P
        rt, rstd = self.n_rt, self.n_rstd
        bk = P.next_bank(); ps = P.bank(bk)
        for c in range(8):
            sqb = self.n_sqb[c % 2]
            P.op("act", lambda e, c=c, sqb=sqb: e.activation(out=sqb[:, 0:n], in_=self.xT[:, c, t0:t0 + n], func=AF.Square),
                 r=self.xkeys(t0, n, [c]), w=[sqb])
            P.op("pe", lambda e, c=c, ps=ps, sqb=sqb: e.matmul(ps[:, 0:n], lhsT=self.onesbf[:], rhs=sqb[:, 0:n], start=(c == 0), stop=(c == 7)),
                 r=[sqb, self.onesbf], w=[("ps", bk)])
        P.op("act", lambda e, ps=ps: e.activation(out=rt[:, 0:n], in_=ps[:, 0:n], func=AF.Sqrt, scale=1.0 / D, bias=self.epsT[:, 0:1]),
             r=[("ps", bk), self.epsT], w=[rt])
        P.op("dve", lambda e: e.reciprocal(out=rstd[:, 0:n], in_=rt[:, 0:n]), r=[rt], w=[rstd])
        return rstd

    def norm_mod(self, U, l, which, t0, n, dst):
        P = self.P
        v = U["v"]
        rstd = self.norm_cols(t0, n)
        gsc = self.gsc1 if which == 1 else self.gsc2
        shb = 0 if which == 1 else 24
        for c in range(8):
            tmp = self.n_tmp[c % 2]
            P.op("dve", lambda e, c=c, tmp=tmp: e.tensor_tensor(out=tmp[:, 0:n], in0=self.xT[:, c, t0:t0 + n], in1=rstd[:, 0:n], op=ALU.mult),
                 r=self.xkeys(t0, n, [c]) + [rstd], w=[tmp])
            d_ap, d_keys = dst(c)
            P.op("act", lambda e, c=c, tmp=tmp, d_ap=d_ap: e.activation(
                out=d_ap, in_=tmp[:, 0:n], func=AF.Identity, scale=gsc[:, l, c, v:v + 1], bias=self.modT[:, l, shb + c, v:v + 1]),
                r=[tmp, (gsc, l), (self.modT, l)], w=d_keys)

    def proj(self, wt, wcols, hT, n, kc_n=8, wk=None):
        P = self.P
        bk = P.next_bank(); ps = P.bank(bk)
        w0, w1 = wcols
        for kc in range(kc_n):
            P.op("pe", lambda e, kc=kc, ps=ps: e.matmul(ps[0:(w1 - w0), 0:n], lhsT=wt[:, kc, w0:w1], rhs=hT[:, kc, 0:n],
                                                          start=(kc == 0), stop=(kc == kc_n - 1)),
                 r=(wk if wk is not None else [wt]) + [hT], w=[("ps", bk)])
        return bk, ps

    def ssm_pass(self, U, l, ssmT):
        P, C, I = self.P, self.C, self.I
        T, NSEQ, L = U["T"], U["NSEQ"], U["L"]
        CT, Cq = T // 8, L // 8
        mk0 = P.mark()
        uT = P.alloc("uT", [2, T], BF16)
        mk1 = P.mark()
        wu = P.alloc("wu", [8, 256], BF16)
        hT = P.alloc("hT", [8, 512], BF16)
        P.dma("sp", wu[:], self.W["win"][l][:, :, 1536:1792], r=self.wkeys("win", l, 1536, 1792), w=[wu])
        for blk in range(T // 512):
            self.norm_mod(U, l, 1, blk * 512, 512, lambda c: (hT[:, c, :], [hT]))
            for oc in range(2):
                bk, ps = self.proj(wu, (oc * 128, oc * 128 + 128), hT, 512)
                P.op("dve", lambda e, oc=oc, blk=blk, ps=ps: e.tensor_copy(out=uT[:, oc, blk * 512:(blk + 1) * 512], in_=ps[:, :]),
                     r=[("ps", bk)], w=[(uT, oc, blk)])
        P.release(mk1)
        UK = [(uT, oc, b) for oc in range(2) for b in range(T // 512)]
        kt = P.alloc("kt", [16, 128], BF16); ws = P.alloc("ws", [2, 16, 128], BF16); wo = P.alloc("wo", [2, 8, 2, 128], BF16)
        xsel = P.alloc("xsel", [8, 240], BF16); ysel = P.alloc("ysel", [8, 128], BF16)
        P.dma("sp", kt[:], self.S_kt[l], r=[("S_kt", l)], w=[kt])
        P.dma("sp", ws[:], self.S_ws[l], r=[("S_ws", l, 0), ("S_ws", l, 1)], w=[ws])
        P.dma("sp", wo[:], self.S_wo[l], r=[("S_wo", l, 0), ("S_wo", l, 1)], w=[wo])
        P.dma("sp", xsel[:], C["xsel"], w=[xsel])
        P.dma("sp", ysel[:], C["ysel"], w=[ysel])
        X = P.alloc("X", [16, CT], BF16)
        for g in range(16):
            bk = P.next_bank(); ps = P.bank(bk)
            for s in range(8):
                rhs = mk(uT[:, g // 8, s:s + 1], [[8, CT]])
                P.op("pe", lambda e, g=g, s=s, rhs=rhs, ps=ps: e.matmul(
                    ps[:, 0:CT], lhsT=xsel[:, g % 8, (7 - s) * 16:(7 - s) * 16 + 128], rhs=rhs, start=(s == 0), stop=(s == 7)),
                    r=UK + [xsel], w=[("ps", bk)])
            P.op("act", lambda e, g=g, ps=ps: e.copy(out=X[:, g, :], in_=ps[:, 0:CT]), r=[("ps", bk)], w=[(X, g)])
        S1 = Cq + 1
        Hb = P.alloc("Hb", [2, 2, 8, NSEQ * S1], BF16)
        fin = P.alloc("fin", [NSEQ, 2, 2, 8], F32)
        mk2 = P.mark()
        Et = P.alloc("Et2", [2, 8, 256], F32)
        tt_ = [P.alloc(f"l2t{i}", [Cq], F32) for i in range(6)]
        h0 = P.alloc("h0", [2, 2, 8], F32)
        gi0 = P.alloc("gi0", [2, 2, 8], F32)
        hq = P.alloc("hq", [4, 8], F32)
        if U["lat"]:
            for d in range(2):
                for comp, nm in enumerate(("sre", "sim")):
                    for h in range(2):
                        src = I[nm][l, d]
                        P.dma("sp", h0[64 * h:64 * h + 64, d, comp, :], bass.AP(src.tensor, src.offset + h * 64, [[1, 64], [128, 8]]),
                              w=[h0], allow_slow_non_contiguous=True)
            for d in range(2):
                ld = l * 2 + d
                er, ei = self.e1[:, 0, ld, :], self.e1[:, 1, ld, :]
                hr, hi = h0[:, d, 0, :], h0[:, d, 1, :]
                ops = [(hq[:, 0, :], er, hr, ALU.mult), (hq[:, 1, :], ei, hi, ALU.mult), (gi0[:, d, 0, :], hq[:, 0, :], hq[:, 1, :], ALU.subtract),
                       (hq[:, 2, :], er, hi, ALU.mult), (hq[:, 3, :], ei, hr, ALU.mult), (gi0[:, d, 1, :], hq[:, 2, :], hq[:, 3, :], ALU.add)]
                for (o_, a_, b_, op_) in ops:
                    P.op("dve", lambda e, o_=o_, a_=a_, b_=b_, op_=op_: e.tensor_tensor(out=o_, in0=a_, in1=b_, op=op_),
                         r=[h0, hq, self.e1, gi0], w=[hq, gi0])
        else:
            P.op("dve", lambda e: e.memset(h0[:], 0.0), w=[h0])
            P.op("dve", lambda e: e.memset(gi0[:], 0.0), w=[gi0])
        for d in range(2):
            ld = l * 2 + d
            P.dma("sp", Et[:], self.S_e[l][:, d], r=[("S_e", l, d)], w=[Et])
            for gg in range(8):
                bk = P.next_bank(); ps = P.bank(bk)
                for h in range(2):
                    g = 2 * gg + h
                    for comp in range(2):
                        P.op("pe", lambda e, g=g, h=h, comp=comp, d=d, ps=ps: e.matmul(
                            ps[64 * h:64 * h + 64, comp * CT:(comp + 1) * CT], lhsT=ws[:, d, g, comp * 64:(comp + 1) * 64], rhs=X[:, g, :],
                            start=True, stop=True, tile_position=(0, 64 * h)), r=[ws, (X, g)], w=[("ps", bk)])
                for sq in range(NSEQ):
                    def seqview(base):
                        a = ps[:, base + sq * Cq: base + (sq + 1) * Cq]
                        if d == 0:
                            return a
                        return mk(ps[:, base + (sq + 1) * Cq - 1: base + (sq + 1) * Cq], [[-1, Cq]])
                    Sr, Si = seqview(0), seqview(CT)
                    Ec, Es = Et[:, 0, gg, 0:Cq], Et[:, 1, gg, 0:Cq]
                    t1, t2, t3, t4, t5, t6 = [t[:, 0:Cq] for t in tt_]
                    TK = lambda i: [tt_[i]]
                    def vop(o_, a_, b_, op_, r, w):
                        P.op("dve", lambda e: e.tensor_tensor(out=o_, in0=a_, in1=b_, op=op_), r=r, w=w)
                    vop(t1, Sr, Ec, ALU.mult, [("ps", bk), Et], TK(0))
                    vop(t2, Si, Es, ALU.mult, [("ps", bk), Et], TK(1))
                    vop(t5, t1, t2, ALU.add, TK(0) + TK(1), TK(4))
                    vop(t3, Si, Ec, ALU.mult, [("ps", bk), Et], TK(2))
                    vop(t4, Sr, Es, ALU.mult, [("ps", bk), Et], TK(3))
                    vop(t6, t3, t4, ALU.subtract, TK(2) + TK(3), TK(5))
                    rr = mk(self.a8mag[:, ld, gg:gg + 1], [[0, Cq]])
                    P.op("dve", lambda e, rr=rr, t5=t5, t1=t1, d=d, gg=gg: e.tensor_tensor_scan(
                        out=t1, data0=rr, data1=t5, initial=gi0[:, d, 0, gg:gg + 1], op0=ALU.mult, op1=ALU.add),
                        r=TK(4) + [self.a8mag, gi0], w=TK(0))
                    P.op("dve", lambda e, rr=rr, t6=t6, t2=t2, d=d, gg=gg: e.tensor_tensor_scan(
                        out=t2, data0=rr, data1=t6, initial=gi0[:, d, 1, gg:gg + 1], op0=ALU.mult, op1=ALU.add),
                        r=TK(5) + [self.a8mag, gi0], w=TK(1))
                    vop(t3, t1, Ec, ALU.mult, TK(0) + [Et], TK(2))
                    vop(t4, t2, Es, ALU.mult, TK(1) + [Et], TK(3))
                    vop(t5, t3, t4, ALU.subtract, TK(2) + TK(3), TK(4))
                    vop(t3, t2, Ec, ALU.mult, TK(1) + [Et], TK(2))
                    vop(t4, t1, Es, ALU.mult, TK(0) + [Et], TK(3))
                    vop(t6, t3, t4, ALU.add, TK(2) + TK(3), TK(5))
                    for comp, tH in ((0, t5), (1, t6)):
                        base = Hb[:, d, comp, gg, :]
                        if d == 0:
                            dstv = base[:, sq * S1 + 1: sq * S1 + 1 + Cq]
                            init_slot = base[:, sq * S1: sq * S1 + 1]
                        else:
                            dstv = mk(base[:, sq * S1 + Cq - 1: sq * S1 + Cq], [[-1, Cq]])
                            init_slot = base[:, sq * S1 + Cq: sq * S1 + Cq + 1]
                        P.op("act", lambda e, dstv=dstv, tH=tH: e.copy(out=dstv, in_=tH), r=[tt_[4 + comp]], w=[(Hb, d, gg)])
                        P.op("act", lambda e, init_slot=init_slot, d=d, comp=comp, gg=gg: e.copy(out=init_slot, in_=h0[:, d, comp, gg:gg + 1]),
                             r=[h0], w=[(Hb, d, gg)])
                        if not U["lat"]:
                            P.op("act", lambda e, sq=sq, d=d, comp=comp, gg=gg, tH=tH: e.copy(
                                out=fin[:, sq, d, comp, gg:gg + 1], in_=tH[:, Cq - 1:Cq]), r=[tt_[4 + comp]], w=[fin])
        if not U["lat"]:
            for sq in range(NSEQ):
                for d in range(2):
                    for comp, nm in enumerate(("nsr", "nsi")):
                        dst = self.O[nm][sq, l, d]
                        P.dma("sp", bass.AP(dst.tensor, dst.offset, [[1, 128], [128, 8]]), fin[:, sq, d, comp, :], r=[fin],
                              w=[("out", nm, sq, l, d)], allow_slow_non_contiguous=True)
        P.release(mk2)
        HK = [(Hb, d, gg) for d in range(2) for gg in range(8)]
        NB = T // 512
        zT = uT
        yexp = [P.alloc(f"yexp{i}", [T], BF16) for i in range(2)]
        wglu = P.alloc("wglu", [2, 256], BF16)
        sg = [P.alloc(f"sg{i}", [512], BF16) for i in range(2)]
        P.dma("sp", wglu[:], self.W["wglu"][l], r=self.wkeys("wglu", l, 0, 256), w=[wglu])
        acc = []
        for tb in range(NB):
            b = P.next_bank(); P.reserved_banks.add(b); acc.append(b)
        for chunk in range(2):
            for gi in range(8):
                g = chunk * 8 + gi
                h, gg = g % 2, g // 2
                bk = P.next_bank(); ps = P.bank(bk)
                P.op("pe", lambda e, g=g, ps=ps: e.matmul(ps[:, 0:CT], lhsT=kt[:, g, :], rhs=X[:, g, :], start=True, stop=False),
                     r=[kt, (X, g)], w=[("ps", bk)])
                k = 0
                for d in range(2):
                    for comp in range(2):
                        off = 0 if d == 0 else 1
                        rhs = mk(Hb[64 * h:64 * h + 64, d, comp, gg, off:off + 1], [[S1, NSEQ], [1, Cq]])
                        outv = ps[:, 0:CT].rearrange("p (a b) -> p a b", b=Cq)
                        k += 1
                        P.op("pe", lambda e, h=h, d=d, comp=comp, gg=gg, rhs=rhs, outv=outv, k=k: e.matmul(
                            outv, lhsT=wo[64 * h:64 * h + 64, d, gg, comp, :], rhs=rhs, start=False, stop=(k == 4)),
                            r=[wo] + HK, w=[("ps", bk)])
                ye = yexp[g % 2]
                in0 = mk(ps[:, 0:1], [[1, CT], [0, 8]])
                in1 = mk(self.maskj[:, 0:1], [[0, CT], [1, 8]])
                P.op("dve", lambda e, ye=ye, in0=in0, in1=in1: e.tensor_tensor(
                    out=ye[:].rearrange("p (a b) -> p a b", b=8), in0=in0, in1=in1, op=ALU.mult),
                    r=[("ps", bk), self.maskj], w=[ye])
                for tb in range(NB):
                    P.op("pe", lambda e, gi=gi, tb=tb, ye=ye: e.matmul(
                        P.bank(acc[tb])[:, :], lhsT=ysel[:, gi, :], rhs=ye[:, tb * 512:(tb + 1) * 512], start=(gi == 0), stop=(gi == 7)),
                        r=[ysel, ye], w=[("ps", acc[tb])])
            for tb in range(NB):
                P.op("act", lambda e, chunk=chunk, tb=tb: e.activation(
                    out=zT[:, chunk, tb * 512:(tb + 1) * 512], in_=P.bank(acc[tb])[:, :], func=AF.Gelu),
                    r=[("ps", acc[tb])], w=[(uT, chunk, tb)])
        for b in acc:
            P.reserved_banks.discard(b)
        for tb in range(NB):
            for oc in range(2):
                bk = P.next_bank(); ps = P.bank(bk)
                for kc in range(2):
                    P.op("pe", lambda e, kc=kc, oc=oc, tb=tb, ps=ps: e.matmul(
                        ps[:, :], lhsT=wglu[:, kc, oc * 128:(oc + 1) * 128], rhs=zT[:, kc, tb * 512:(tb + 1) * 512],
                        start=(kc == 0), stop=(kc == 1)), r=[wglu, (uT, kc, tb)], w=[("ps", bk)])
                s_ = sg[(tb * 2 + oc) % 2]
                P.op("act", lambda e, s_=s_, ps=ps: e.activation(out=s_[:], in_=ps[:, :], func=AF.Sigmoid), r=[("ps", bk)], w=[s_])
                P.op("dve", lambda e, s_=s_, oc=oc, tb=tb: e.tensor_tensor(
                    out=ssmT[:, oc, tb * 512:(tb + 1) * 512], in0=zT[:, oc, tb * 512:(tb + 1) * 512], in1=s_[:], op=ALU.mult),
                    r=[s_, (uT, oc, tb)], w=[(ssmT, oc, tb)])
        P.release(mk0)

    def kv_pass(self, U, l, krT, vaug, gbT, pT):
        P, C = self.P, self.C
        T, NSEQ, L, lat = U["T"], U["NSEQ"], U["L"], U["lat"]
        mk0 = P.mark()
        win = self.W["win"][l]
        hT = P.alloc("hT", [8, 512], BF16)
        wkd = P.alloc("wkd", [8, 2, 128], BF16)
        wkv = P.alloc("wkv", [8, 256], BF16)
        wg = P.alloc("wg", [8, 768], BF16)
        for kv in range(2):
            for hh in range(2):
                P.dma("sp", wkd[:, :, kv, hh * 64:(hh + 1) * 64], win[:, :, 512 + kv * 64:512 + (kv + 1) * 64],
                      r=self.wkeys("win", l, 512, 640), w=[wkd])
        P.dma("sp", wkv[:], win[:, :, 512:768], r=self.wkeys("win", l, 512, 768), w=[wkv])
        P.dma("sp", wg[:], win[:, :, 768:1536], r=self.wkeys("win", l, 768, 1536), w=[wg])
        if lat:
            wkp = P.alloc("wkp", [8, 2, 128], BF16)
            rope = P.alloc("rope", [2, 512], F32)
            r1 = P.alloc("r1", [512], F32); r2 = P.alloc("r2", [512], F32)
            for b_ in range(2):
                srcv = mk(wkd[:, 0, 0, 0:1], [[64, 32], [32, 2], [1, 16]], off=(1 - b_) * 16)
                dstv = mk(wkp[:, 0, 0, 0:1], [[64, 32], [32, 2], [1, 16]], off=b_ * 16)
                P.op("pool", lambda e, srcv=srcv, dstv=dstv: e.tensor_copy(out=dstv, in_=srcv), r=[wkd], w=[wkp])
        else:
            kvst = [P.alloc(f"kvst{i}", [256], F32) for i in range(2)]
        gct = P.alloc("gct", [512], F32)
        P.op("dve", lambda e: e.memset(vaug[:, :, :, 64:128], 1.0), w=[vaug])
        for blk in range(T // 512):
            t0 = blk * 512
            self.norm_mod(U, l, 1, t0, 512, lambda c: (hT[:, c, :], [hT]))
            if lat:
                P.dma("sp", rope[:], C["rope"][:, :, t0:t0 + 512], w=[rope])
            for kv in range(2):
                bk, ps = self.proj(wkd[:, :, kv, :], (0, 128), hT, 512, wk=[wkd])
                if lat:
                    bk2, ps2 = self.proj(wkp[:, :, kv, :], (0, 128), hT, 512, wk=[wkp])
                    P.op("dve", lambda e, ps=ps: e.tensor_tensor(out=r1[:], in0=ps[:, :], in1=rope[:, 0, :], op=ALU.mult), r=[("ps", bk), rope], w=[r1])
                    P.op("dve", lambda e, ps2=ps2: e.tensor_tensor(out=r2[:], in0=ps2[:, :], in1=rope[:, 1, :], op=ALU.mult), r=[("ps", bk2), rope], w=[r2])
                    P.op("dve", lambda e, kv=kv, t0=t0: e.tensor_tensor(out=krT[:, kv, t0:t0 + 512], in0=r1[:], in1=r2[:], op=ALU.add),
                         r=[r1, r2], w=[(krT, kv, blk)])
                else:
                    P.op("act", lambda e, kv=kv, t0=t0, ps=ps: e.copy(out=krT[:, kv, t0:t0 + 512], in_=ps[:, :]), r=[("ps", bk)], w=[(krT, kv, blk)])
            for i in range(4):
                if self.cut == 12:
                    break
                tile_i = blk * 4 + i
                bk = P.next_bank(); ps = P.bank(bk)
                c0 = 128 if lat else 0
                for kc in range(8):
                    P.op("pe", lambda e, kc=kc, i=i, ps=ps, c0=c0: e.matmul(ps[:, c0:256], lhsT=hT[:, kc, i * 128:(i + 1) * 128], rhs=wkv[:, kc, c0:256],
                                                                              start=(kc == 0), stop=(kc == 7)), r=[hT, wkv], w=[("ps", bk)])
                P.op("dve", lambda e, tile_i=tile_i, ps=ps: e.tensor_copy(out=vaug[:, tile_i, :, 0:64], in_=ps[:, 128:256].rearrange("p (a b) -> p a b", b=64)),
                     r=[("ps", bk)], w=[(vaug, tile_i)])
                if not lat and self.cut != 15:
                    st = kvst[tile_i % 2]
                    P.op("act", lambda e, st=st, ps=ps: e.copy(out=st[:], in_=ps[:, 0:256]), r=[("ps", bk)], w=[st])
                    sq, tl = divmod(tile_i, L // 128)
                    P.dma("sp", self.O["nk"][sq, l, tl * 128:(tl + 1) * 128, :], st[:, 0:128], r=[st], w=[("out", "nk", tile_i, l)])
                    P.dma("sp", self.O["nv"][sq, l, tl * 128:(tl + 1) * 128, :], st[:, 128:256], r=[st], w=[("out", "nv", tile_i, l)])
            for oc in range(6):
                if self.cut in (12, 13):
                    break
                bk, ps = self.proj(wg, (oc * 128, oc * 128 + 128), hT, 512)
                which, c = divmod(oc, 2)
                if which == 0:
                    P.op("act", lambda e, c=c, t0=t0, ps=ps: e.copy(out=gbT[:, c, t0:t0 + 512], in_=ps[:, :]), r=[("ps", bk)], w=[(gbT, c, blk)])
                elif which == 1:
                    P.op("act", lambda e, c=c, t0=t0, ps=ps: e.copy(out=pT[:, c, t0:t0 + 512], in_=ps[:, :]), r=[("ps", bk)], w=[(pT, c, blk)])
                else:
                    P.op("dve", lambda e, c=c, t0=t0, ps=ps: e.tensor_tensor(out=pT[:, c, t0:t0 + 512], in0=ps[:, :], in1=pT[:, c, t0:t0 + 512], op=ALU.mult),
                         r=[("ps", bk), (pT, c, blk)], w=[(pT, c, blk)])
        P.release(mk0)

    def mix_pass(self, U, l, krT, vaug, gbT, pT, ssmT):
        P, C, I = self.P, self.C, self.I
        T, NSEQ, L, lat, v = U["T"], U["NSEQ"], U["L"], U["lat"], U["v"]
        mk0 = P.mark()
        win = self.W["win"][l]
        hT = P.alloc("hT", [8, 512], BF16)
        wq = P.alloc("wq", [8, 512], BF16)
        wo_ = P.alloc("wo_", [8, D], BF16)
        qT = P.alloc("qT", [4, 512], BF16)
        atT = P.alloc("atT", [4, 512], BF16)
        cvT = P.alloc("cvT", [2, 512], BF16)
        cacc = P.alloc("cacc", [512], F32)
        PT = [P.alloc(f"PT{i}", [512], BF16) for i in range(3)]
        Rt = P.alloc("Rt", [512], F32)
        P.dma("sp", wq[:], win[:, :, 0:512], r=self.wkeys("win", l, 0, 512), w=[wq])
        P.dma("sp", wo_[:], self.W["wout"][l], r=self.wkeys("wout", l, 0, D), w=[wo_])
        if lat:
            wqp = P.alloc("wqp", [8, 512], BF16)
            rope = P.alloc("rope", [2, 512], F32)
            r1 = P.alloc("r1", [512], F32); r2 = P.alloc("r2", [512], F32)
            maskb = P.alloc("maskb", [2, 512], BF16)
            ckd = P.alloc("ckd", [2, PAST], BF16)
            cva = P.alloc("cva", [4, 2, 128], BF16)
            cst = P.alloc("cst", [4, 2, 64], F32)
            for b_ in range(2):
                srcv = mk(wq[:, 0, 0:1], [[64, 64], [32, 2], [1, 16]], off=(1 - b_) * 16)
                dstv = mk(wqp[:, 0, 0:1], [[64, 64], [32, 2], [1, 16]], off=b_ * 16)
                P.op("pool", lambda e, srcv=srcv, dstv=dstv: e.tensor_copy(out=dstv, in_=srcv), r=[wq], w=[wqp])
            P.dma("sp", maskb[:], C["maskb"], w=[maskb])
            m01 = P.alloc("m01", [2, 256], BF16)
            P.op("dve", lambda e: e.tensor_single_scalar(out=m01[:], in_=maskb[:, :, 0:256], scalar=0.0, op=ALU.is_equal), r=[maskb], w=[m01])
            P.op("dve", lambda e: e.memset(cva[:, :, :, 64:128], 1.0), w=[cva])
            P.dma("sp", cst[:], I["cv"][l].rearrange("(i p) (k d) -> p i k d", p=128, d=64), w=[cst])
            P.op("dve", lambda e: e.tensor_copy(out=cva[:, :, :, 0:64], in_=cst[:]), r=[cst], w=[cva])
            for kv in range(2):
                for hh in range(2):
                    P.dma("sp", cst[:, :, hh, :], I["ck"][l][:, kv * 64:(kv + 1) * 64].rearrange("(i p) d -> p i d", p=128), r=[cva], w=[cst])
                bk = P.next_bank(); ps = P.bank(bk)
                for i in range(4):
                    P.op("pe", lambda e, i=i, ps=ps: e.transpose(out=ps[:, i * 128:(i + 1) * 128], in_=cst[:, i, :, :].rearrange("p a b -> p (a b)"),
                                                                 identity=self.ident[:]), r=[cst, self.ident], w=[("ps", bk)])
                P.op("act", lambda e, kv=kv, ps=ps: e.copy(out=ckd[:, kv, :], in_=ps[:, :]), r=[("ps", bk)], w=[ckd])
        po_banks = []
        for i in range(2):
            b = P.next_bank(); P.reserved_banks.add(b); po_banks.append(b)
        npo = 0
        TPS = L // 128
        for blk in range(T // 512):
            t0 = blk * 512
            self.norm_mod(U, l, 1, t0, 512, lambda c: (hT[:, c, :], [hT]))
            if lat:
                P.dma("sp", rope[:], C["rope"][:, :, t0:t0 + 512], w=[rope])
            for hc in range(4):
                bk, ps = self.proj(wq, (hc * 128, hc * 128 + 128), hT, 512)
                if lat:
                    bk2, ps2 = self.proj(wqp, (hc * 128, hc * 128 + 128), hT, 512)
                    P.op("dve", lambda e, ps=ps: e.tensor_tensor(out=r1[:], in0=ps[:, :], in1=rope[:, 0, :], op=ALU.mult), r=[("ps", bk), rope], w=[r1])
                    P.op("dve", lambda e, ps2=ps2: e.tensor_tensor(out=r2[:], in0=ps2[:, :], in1=rope[:, 1, :], op=ALU.mult), r=[("ps", bk2), rope], w=[r2])
                    P.op("dve", lambda e, hc=hc: e.tensor_tensor(out=qT[:, hc, :], in0=r1[:], in1=r2[:], op=ALU.add), r=[r1, r2], w=[(qT, hc)])
                else:
                    P.op("act", lambda e, hc=hc, ps=ps: e.copy(out=qT[:, hc, :], in_=ps[:, :]), r=[("ps", bk)], w=[(qT, hc)])
            if lat:
                pieces = [(t0, t0 + 512, t0 > 0, t0 + 512 < T)]
            else:
                pieces = [(t0 + i * L, t0 + (i + 1) * L, False, False) for i in range(512 // L)]
            for c in range(2):
                for (a0, a1, hl, hr) in pieces:
                    n = a1 - a0
                    o0 = a0 - t0
                    pk = [(pT, c, b) for b in range(max(0, blk - 1), min(T // 512, blk + 2))]
                    P.op("dve", lambda e, c=c, a0=a0, a1=a1, o0=o0, n=n: e.tensor_scalar_mul(
                        out=cacc[:, o0:o0 + n], in0=pT[:, c, a0:a1], scalar1=self.scw[:, l, c, 1:2]), r=pk + [self.scw], w=[cacc])
                    a = 0 if hl else 1
                    P.op("dve", lambda e, c=c, a0=a0, a1=a1, o0=o0, n=n, a=a: e.scalar_tensor_tensor(
                        out=cacc[:, o0 + a:o0 + n], in0=pT[:, c, a0 + a - 1:a1 - 1], scalar=self.scw[:, l, c, 0:1],
                        in1=cacc[:, o0 + a:o0 + n], op0=ALU.mult, op1=ALU.add), r=pk + [self.scw, cacc], w=[cacc])
                    b_ = 0 if hr else 1
                    P.op("dve", lambda e, c=c, a0=a0, a1=a1, o0=o0, n=n, b_=b_: e.scalar_tensor_tensor(
                        out=cacc[:, o0:o0 + n - b_], in0=pT[:, c, a0 + 1:a1 + 1 - b_], scalar=self.scw[:, l, c, 2:3],
                        in1=cacc[:, o0:o0 + n - b_], op0=ALU.mult, op1=ALU.add), r=pk + [self.scw, cacc], w=[cacc])
                P.op("dve", lambda e, c=c, t0=t0: e.tensor_tensor(out=cvT[:, c, :], in0=cacc[:], in1=gbT[:, c, t0:t0 + 512], op=ALU.mult),
                     r=[cacc, (gbT, c, blk)], w=[(cvT, c)])
            for qi in range(4):
                qt = blk * 4 + qi
                sq, ql = divmod(qt, TPS)
                for kv in range(2):
                    srcs = []
                    if lat:
                        for kt_, m in ((ql - 1, 0), (ql, None), (ql + 1, 1)):
                            if 0 <= kt_ < TPS:
                                srcs.append((krT[:, kv, kt_ * 128:(kt_ + 1) * 128], [(krT, kv, kt_ // 4)], vaug[:, kt_, kv, :], [(vaug, kt_)], m))
                        for i in range(4):
                            srcs.append((ckd[:, kv, i * 128:(i + 1) * 128], [ckd], cva[:, i, kv, :], [cva], None))
                    else:
                        for kt_ in range(TPS):
                            gt = sq * TPS + kt_
                            srcs.append((krT[:, kv, gt * 128:(gt + 1) * 128], [(krT, kv, gt // 4)], vaug[:, gt, kv, :], [(vaug, gt)], None))
                    pob = po_banks[npo % 2]; npo += 1
                    po = P.bank(pob)
                    ns = len(srcs)

                    def S(i):
                        kT_ap, kkeys, _, _, m = srcs[i]
                        pt = PT[i % 3]
                        for hh in range(2):
                            bk = P.next_bank(); ps = P.bank(bk)
                            for j in range(2):
                                hq = 2 * j + hh
                                h = kv * 4 + hq
                                P.op("pe", lambda e, j=j, h=h, hh=hh, ps=ps, kT_ap=kT_ap: e.matmul(
                                    ps[:, j * 128:(j + 1) * 128], lhsT=kT_ap[64 * hh:64 * hh + 64, :],
                                    rhs=qT[64 * hh:64 * hh + 64, h // 2, qi * 128:(qi + 1) * 128], start=True, stop=True),
                                    r=kkeys + [(qT, h // 2)], w=[("ps", bk)])
                            P.op("act", lambda e, pt=pt, ps=ps, hh=hh: e.activation(out=pt[:, hh * 256:(hh + 1) * 256], in_=ps[:, 0:256], func=AF.Exp, scale=0.125),
                                 r=[("ps", bk)], w=[pt])
                            if m is not None:
                                P.op("pool", lambda e, pt=pt, hh=hh, m=m: e.tensor_tensor(out=pt[:, hh * 256:(hh + 1) * 256], in0=pt[:, hh * 256:(hh + 1) * 256],
                                                                                           in1=m01[:, m, :], op=ALU.mult), r=[pt, m01], w=[pt])

                    def PV(i):
                        _, _, v_ap, vkeys, _ = srcs[i]
                        pt = PT[i % 3]
                        P.op("pe", lambda e, v_ap=v_ap, pt=pt, i=i: e.matmul(po[:, :], lhsT=v_ap, rhs=pt[:], start=(i == 0), stop=False),
                             r=vkeys + [pt], w=[("ps", pob)])
                    S(0)
                    for i in range(ns):
                        if i + 1 < ns:
                            S(i + 1)
                        PV(i)
                    es_rhs = mk(self.esrow[0:1, l, kv * 4, 0:1], [[128, 2], [256, 2], [1, 128]])
                    P.op("pe", lambda e, es_rhs=es_rhs: e.matmul(po[:, :].rearrange("p (a b c) -> p a b c", a=2, b=2), lhsT=self.vsink[0:1, :], rhs=es_rhs,
                                                                  start=False, stop=True), r=[self.vsink, self.esrow], w=[("ps", pob)])
                    P.op("dve", lambda e: e.reciprocal(out=Rt[0:64, :], in_=po[64:128, :]), r=[("ps", pob)], w=[Rt])
                    for par in range(2):
                        in0 = po[0:64, par * 256:(par + 1) * 256].rearrange("p (a b) -> p a b", b=128)
                        in1 = Rt[0:64, par * 256:(par + 1) * 256].rearrange("p (a b) -> p a b", b=128)
                        outv = atT[64 * par:64 * par + 64, kv * 2:kv * 2 + 2, qi * 128:(qi + 1) * 128]
                        P.op("dve", lambda e, in0=in0, in1=in1, outv=outv: e.tensor_tensor(out=outv, in0=in0, in1=in1, op=ALU.mult),
                             r=[("ps", pob), Rt], w=[(atT, kv)])
            rhs_list = [(atT[:, i, :], [(atT, 0), (atT, 1)]) for i in range(4)] + [(cvT[:, i, :], [(cvT, i)]) for i in range(2)] + \
                       [(ssmT[:, i, t0:t0 + 512], [(ssmT, i, blk)]) for i in range(2)]
            for oc in range(8):
                bk = P.next_bank(); ps = P.bank(bk)
                for kc, (rap, rkeys) in enumerate(rhs_list):
                    P.op("pe", lambda e, kc=kc, oc=oc, rap=rap, ps=ps: e.matmul(ps[:, :], lhsT=wo_[:, kc, oc * 128:(oc + 1) * 128], rhs=rap,
                                                                                  start=(kc == 0), stop=(kc == 7)), r=[wo_] + rkeys, w=[("ps", bk)])
                P.op("dve", lambda e, oc=oc, t0=t0, ps=ps: e.scalar_tensor_tensor(
                    out=self.xT[:, oc, t0:t0 + 512], in0=ps[:, :], scalar=self.modT[:, l, 16 + oc, v:v + 1], in1=self.xT[:, oc, t0:t0 + 512],
                    op0=ALU.mult, op1=ALU.add), r=[("ps", bk), (self.modT, l), (self.xT, oc, blk)], w=[(self.xT, oc, blk)])
        for b in po_banks:
            P.reserved_banks.discard(b)
        P.release(mk0)

    def ffn_pass(self, U, l):
        P = self.P
        T, NSEQ, L, lat, v = U["T"], U["NSEQ"], U["L"], U["lat"], U["v"]
        mk0 = P.mark()
        wdn = P.alloc("wdn", [22, D], BF16)
        for c in range(22):
            P.dma("sp", wdn[:, c, :], self.W["wdn"][l][:, c, :], r=self.wkeys("wdn", l, 0, D, [c]), w=[(wdn, c)])
        h2T = P.alloc("h2T", [8, 2, 258], BF16)
        halo = P.alloc("halo", [8, 2], BF16)
        gated = P.alloc("gated", [22, 2, 256], BF16)
        wus = [P.alloc(f"wus{i}", [8, 256], BF16) for i in range(3)]
        ca = [P.alloc(f"ca{i}", [256], F32) for i in range(2)]
        cg = [P.alloc(f"cg{i}", [256], F32) for i in range(2)]
        sgt = [P.alloc(f"sgt{i}", [256], F32) for i in range(2)]
        pieces = []
        for t0 in range(0, T, 256):
            sq_start = (t0 % L) == 0
            sq_end = ((t0 + 256) % L) == 0
            pieces.append((t0, t0 + 256, not sq_start, not sq_end))
        nwu = 0
        for sb in range(len(pieces) // 2):
            pcs = pieces[2 * sb:2 * sb + 2]
            for pi, (a0, a1, hl, hr) in enumerate(pcs):
                if pi == 0 and hl:
                    n = (a1 - a0) + int(hr)
                    P.op("pool", lambda e: e.tensor_copy(out=h2T[:, :, 0, 0:1], in_=halo[:, :, 0:1]), r=[halo], w=[(h2T, 0)])
                    self.norm_mod(U, l, 2, a0, n, lambda c, n=n: (h2T[:, c, 0, 1:1 + n], [(h2T, 0)]))
                else:
                    n = (a1 - a0) + int(hl) + int(hr)
                    self.norm_mod(U, l, 2, a0 - int(hl), n, lambda c, pi=pi, n=n: (h2T[:, c, pi, 0:n], [(h2T, pi)]))
            a0_, a1_, hl_l, hr_l = pcs[1]
            lastcol = int(hl_l) + (a1_ - a0_) - 1
            P.op("pool", lambda e, lastcol=lastcol: e.tensor_copy(out=halo[:, :, 0:1], in_=h2T[:, :, 1, lastcol:lastcol + 1]), r=[(h2T, 1)], w=[halo])
            for c in range(22):
                wu = wus[nwu % 3]; nwu += 1
                P.dma("sp", wu[:, :, 0:128], self.W["wup"][l][:, :, c * 128:(c + 1) * 128], r=self.wkeys("wup", l, c * 128, (c + 1) * 128), w=[wu])
                P.dma("sp", wu[:, :, 128:256], self.W["wup"][l][:, :, DFF + c * 128:DFF + (c + 1) * 128],
                      r=self.wkeys("wup", l, DFF + c * 128, DFF + (c + 1) * 128), w=[wu])
                for pi, (a0, a1, hl, hr) in enumerate(pcs):
                    m = a1 - a0
                    hl_, hr_ = int(hl), int(hr)
                    n = m + hl_ + hr_
                    res = []
                    for half, (acc_t, ch) in enumerate(((ca[pi], c), (cg[pi], 22 + c))):
                        bk = P.next_bank(); ps = P.bank(bk)
                        for kc in range(8):
                            P.op("pe", lambda e, kc=kc, half=half, ps=ps, pi=pi, n=n: e.matmul(
                                ps[:, 0:n], lhsT=wu[:, kc, half * 128:(half + 1) * 128], rhs=h2T[:, kc, pi, 0:n], start=(kc == 0), stop=(kc == 7)),
                                r=[wu, (h2T, pi)], w=[("ps", bk)])
                        w_ = self.fcw[:, l, ch, :]
                        P.op("act", lambda e, acc_t=acc_t, ps=ps, w_=w_: e.activation(
                            out=acc_t[:, 0:m], in_=ps[:, hl_:hl_ + m], func=AF.Identity, scale=w_[:, 1:2]), r=[("ps", bk), self.fcw], w=[acc_t])
                        a = 0 if hl else 1
                        P.op("dve", lambda e, acc_t=acc_t, ps=ps, w_=w_, a=a: e.scalar_tensor_tensor(
                            out=acc_t[:, a:m], in0=ps[:, hl_ + a - 1:hl_ + m - 1], scalar=w_[:, 0:1], in1=acc_t[:, a:m], op0=ALU.mult, op1=ALU.add),
                            r=[("ps", bk), self.fcw, acc_t], w=[acc_t])
                        b_ = 0 if hr else 1
                        P.op("dve", lambda e, acc_t=acc_t, ps=ps, w_=w_, b_=b_: e.scalar_tensor_tensor(
                            out=acc_t[:, 0:m - b_], in0=ps[:, hl_ + 1:hl_ + m + 1 - b_], scalar=w_[:, 2:3], in1=acc_t[:, 0:m - b_], op0=ALU.mult, op1=ALU.add),
                            r=[("ps", bk), self.fcw, acc_t], w=[acc_t])
                    sg_ = sgt[pi]
                    P.op("act", lambda e, sg_=sg_, pi=pi: e.activation(out=sg_[:, 0:m], in_=cg[pi][:, 0:m], func=AF.Silu), r=[cg[pi]], w=[sg_])
                    P.op("dve", lambda e, sg_=sg_, pi=pi, c=c: e.tensor_tensor(out=gated[:, c, pi, 0:m], in0=ca[pi][:, 0:m], in1=sg_[:, 0:m], op=ALU.mult),
                         r=[ca[pi], sg_], w=[(gated, c, pi)])
            for pi, (a0, a1, hl, hr) in enumerate(pcs):
                m = a1 - a0
                for oc in range(8):
                    bk = P.next_bank(); ps = P.bank(bk)
                    for c in range(22):
                        P.op("pe", lambda e, c=c, oc=oc, pi=pi, ps=ps: e.matmul(ps[:, 0:m], lhsT=wdn[:, c, oc * 128:(oc + 1) * 128], rhs=gated[:, c, pi, 0:m],
                                                                                  start=(c == 0), stop=(c == 21)), r=[(wdn, c), (gated, c, pi)], w=[("ps", bk)])
                    P.op("dve", lambda e, oc=oc, a0=a0, a1=a1, ps=ps: e.scalar_tensor_tensor(
                        out=self.xT[:, oc, a0:a1], in0=ps[:, 0:m], scalar=self.modT[:, l, 40 + oc, v:v + 1], in1=self.xT[:, oc, a0:a1],
                        op0=ALU.mult, op1=ALU.add), r=[("ps", bk), (self.modT, l)] + self.xkeys(a0, m, [oc]), w=self.xkeys(a0, m, [oc]))
        P.release(mk0)

    def final_pass(self, U):
        P = self.P
        T = U["T"]
        mk0 = P.mark()
        yT = P.alloc("yT", [8, 512], F32)
        yst = [P.alloc(f"yst{i}", [D], F32) for i in range(2)]
        nst = 0
        for blk in range(T // 512):
            t0 = blk * 512
            rstd = self.norm_cols(t0, 512)
            for c in range(8):
                tmp = self.n_tmp[c % 2]
                P.op("dve", lambda e, c=c, tmp=tmp, t0=t0: e.tensor_tensor(out=tmp[:], in0=self.xT[:, c, t0:t0 + 512], in1=rstd[:], op=ALU.mult),
                     r=self.xkeys(t0, 512, [c]) + [rstd], w=[tmp])
                P.op("act", lambda e, c=c, tmp=tmp: e.activation(out=yT[:, c, :], in_=tmp[:], func=AF.Identity, scale=self.nfT[:, c:c + 1]),
                     r=[tmp, self.nfT], w=[(yT, c)])
            for i in range(4):
                st = yst[nst % 2]; nst += 1
                for hf in range(2):
                    bk = P.next_bank(); ps = P.bank(bk)
                    for cc in range(4):
                        c = hf * 4 + cc
                        P.op("pe", lambda e, c=c, cc=cc, i=i, ps=ps: e.transpose(out=ps[:, cc * 128:(cc + 1) * 128], in_=yT[:, c, i * 128:(i + 1) * 128],
                                                                                 identity=self.ident[:]), r=[(yT, c), self.ident], w=[("ps", bk)])
                    if hf == 0:
                        P.op("act", lambda e, st=st, ps=ps: e.copy(out=st[:, 0:512], in_=ps[:, :]), r=[("ps", bk)], w=[(st, 0)])
                    else:
                        P.op("dve", lambda e, st=st, ps=ps: e.tensor_copy(out=st[:, 512:1024], in_=ps[:, :]), r=[("ps", bk)], w=[(st, 1)])
                P.dma("sp", U["y"][t0 + i * 128:t0 + (i + 1) * 128, :], st[:], r=[(st, 0), (st, 1)], w=[("out", "y", U["name"], blk, i)])
        P.release(mk0)

    def run_unit(self, U, layers=(0, 1), passes=("ssm", "kv", "mix", "ffn", "final")):
        P = self.P
        T = U["T"]
        mk0 = P.mark()
        self.n_sqb = [P.alloc(f"n_sqb{i}", [512], BF16) for i in range(2)]
        self.n_rt = P.alloc("n_rt", [512], F32)
        self.n_rstd = P.alloc("n_rstd", [512], F32)
        self.n_tmp = [P.alloc(f"n_tmp{i}", [512], F32) for i in range(2)]
        self.load_x(U)
        for l in layers:
            self.prep_adaln(l)
            mk1 = P.mark()
            ssmT = P.alloc("ssmT", [2, T], BF16)
            if "ssm" in passes:
                self.ssm_pass(U, l, ssmT)
            if "kv" in passes:
                krT = P.alloc("krT", [2, T], BF16)
                vaug = P.alloc("vaug", [T // 128, 2, 128], BF16)
                gbT = P.alloc("gbT", [2, T], BF16)
                pT = P.alloc("pT", [2, T], F32 if False else BF16)
                self.kv_pass(U, l, krT, vaug, gbT, pT)
                if "mix" in passes:
                    self.mix_pass(U, l, krT, vaug, gbT, pT, ssmT)
            P.release(mk1)
            if "ffn" in passes:
                self.ffn_pass(U, l)
        if "final" in passes:
            self.final_pass(U)
        P.release(mk0)

    def build(self):
        P = self.P
        self.xT = P.alloc("xT", [8, 2048], F32)
        self.epsT = P.alloc("epsT", [1], F32)
        P.op("dve", lambda e: e.memset(self.epsT[:], EPS), w=[self.epsT])
        self.persistent()
        self.prep_adaln_setup()
        if "prep" in self.stages or "adaln" in self.stages:
            self.prep_adaln(0)
        if "prep" in self.stages or "casts" in self.stages:
            self.prep_casts()
        if "prep" in self.stages or "ssmt" in self.stages:
            self.prep_ssm()
        UP = dict(name="P", T=512, NSEQ=2, L=256, v=0, lat=False, x=self.I["xp"], y=self.O["yp"])
        US = dict(name="S", T=2048, NSEQ=1, L=2048, v=1, lat=True, x=self.I["xs"], y=self.O["ys"])
        if "P" in self.stages:
            self.run_unit(UP, **self.unit_kw.get("P", {}))
        if self.cast_mark is not None:
            P.release(self.cast_mark)
        if "S" in self.stages:
            self.run_unit(US, **self.unit_kw.get("S", {}))
        if self.post is not None:
            self.post(self)
        P.emit()
        self.es.close()
        return self.nc

    unit_kw = {}
    cast_mark = None
    cast_only = None
    post = None
    cut = 0


def make_in_maps(inputs, consts, B=None):
    f = lambda a: np.ascontiguousarray(np.asarray(a, dtype=np.float32))
    xp = f(inputs["x_prompt"]); xs = f(inputs["x_sample"])
    maps = []
    for c in range(8):
        b = c // 4
        m = {
            "xp": xp[2 * c:2 * c + 2].reshape(512, D),
            "xs": xs[b],
            "ck": f(inputs["cache_k"])[b].reshape(2, PAST, 128),
            "cv": f(inputs["cache_v"])[b].reshape(2, PAST, 128),
            "sre": f(inputs["state_ssm_re"])[b],
            "sim": f(inputs["state_ssm_im"])[b],
            "cvec": np.stack([f(inputs["c_ctx"]), f(inputs["c"])[b]], 0),
        }
        for name, _ in IN_SPECS[7:]:
            m[name] = f(inputs[name])
        for k, a in consts.items():
            m["c_" + k] = a
        if B is not None:
            used = set(B.I.keys()) | set("c_" + k for k in B.C.keys())
            m = {k: a for k, a in m.items() if k in used}
        maps.append({k: np.ascontiguousarray(a) for k, a in m.items()})
    return maps


_CACHE = {}


def kernel(**inputs):
    consts = make_consts()
    if "nc" not in _CACHE:
        B = Builder()
        _CACHE["nc"] = B.build()
        _CACHE["B"] = B
    nc = _CACHE["nc"]
    in_maps = make_in_maps(inputs, consts, _CACHE["B"])
    res = run_bass_kernel_spmd(nc, in_maps, core_ids=list(range(8)))
    R = res.results
    y_prompt = np.concatenate([R[c]["yp"].reshape(2, 256, D) for c in range(8)], 0)
    y_sample = np.stack([R[0]["ys"], R[4]["ys"]], 0)
    nk = np.concatenate([R[c]["nk"].reshape(2, 2, 256, 2, 64) for c in range(8)], 0)
    nv = np.concatenate([R[c]["nv"].reshape(2, 2, 256, 2, 64) for c in range(8)], 0)
    nsr = np.concatenate([R[c]["nsr"] for c in range(8)], 0)
    nsi = np.concatenate([R[c]["nsi"] for c in range(8)], 0)
    return (y_prompt.astype(np.float32), y_sample.astype(np.float32), nk.astype(np.float32), nv.astype(np.float32),
            nsr.astype(np.float32), nsi.astype(np.float32))
```

```python
import math
from contextlib import ExitStack

import numpy as np
import ml_dtypes

import concourse.bass as bass
import concourse.mybir as mybir
from concourse.bass_utils import run_bass_kernel_spmd

F32 = mybir.dt.float32
BF16 = mybir.dt.bfloat16
U8 = mybir.dt.uint8
I32 = mybir.dt.int32
ALU = mybir.AluOpType
AF = mybir.ActivationFunctionType
AX = mybir.AxisListType

D = 1024
DEPTH = 2
NQH = 8
HD = 64
DFF = 2816
IN_DIM = 1792
PAST = 512
EPS = 1e-6
NEG = -30000.0
TWO_PI = 2.0 * math.pi

ARENA_BYTES = 207 * 1024
CASTW = 1024


class Tile:
    def __init__(self, name, ap, lo, hi):
        self.name, self.ap, self.lo, self.hi = name, ap, lo, hi

    def __getitem__(self, k):
        return self.ap[k]


class Op:
    __slots__ = ("eng", "calls", "waits", "sem", "val", "isdma")


class _Rec:
    def __init__(self):
        self.calls = []

    def __getattr__(self, name):
        def f(*a, **k):
            self.calls.append((name, a, k))
            return None
        return f


class Prog:
    ENG = ("pe", "act", "dve", "pool", "sp")
    CAP_C = 30000
    CAP_D = 1800

    def __init__(self, nc, es):
        self.nc, self.es = nc, es
        self.ops = {e: [] for e in self.ENG}
        self.state = {}
        self.seen = {e: {} for e in self.ENG}
        self.sems = {}
        self.cnt = {}
        self.tile_init = {}
        self.tiles_live = []
        self.freed = []
        self.tile_keys = {}
        self.arena = es.enter_context(nc.sbuf_tensor("arena", [128, ARENA_BYTES], U8))
        self.top = 0
        self.nsem = 0
        self.bank_rr = 0
        self.reserved_banks = set()
        self.psum = es.enter_context(nc.psum_tensor("psum", [128, 8 * 512], F32))
        self.n_ops = 0
        self.final = {}

    def alloc(self, name, shape, dtype):
        esz = 4 if dtype in (F32, I32) else 2
        n = int(np.prod(shape)) * esz
        n = (n + 63) // 64 * 64
        lo = self.top
        hi = lo + n
        assert hi <= ARENA_BYTES, f"arena overflow allocating {name}: {hi}"
        self.top = hi
        ap = self.arena[:, lo:hi].bitcast(dtype)
        used = int(np.prod(shape))
        ap = ap[:, 0:used]
        if len(shape) == 2:
            ap = ap.rearrange("p (a b) -> p a b", b=shape[1])
        elif len(shape) == 3:
            ap = ap.rearrange("p (a b c) -> p a b c", b=shape[1], c=shape[2])
        elif len(shape) == 4:
            ap = ap.rearrange("p (a b c d) -> p a b c d", b=shape[1], c=shape[2], d=shape[3])
        name = f"{name}#{len(self.tile_keys)}"
        t = Tile(name, ap, lo, hi)
        inh = []
        for (flo, fhi, toks) in self.freed:
            if flo < hi and lo < fhi:
                inh.extend(toks)
        self.tile_init[name] = inh
        self.tile_keys[name] = set()
        self.tiles_live.append(t)
        return t

    def mark(self):
        return (self.top, len(self.tiles_live))

    def release(self, mk):
        top, nlive = mk
        for t in self.tiles_live[nlive:]:
            toks = list(self.tile_init[t.name])
            for k in self.tile_keys[t.name]:
                st = self.state.get(k)
                if st:
                    toks.extend(st[0].items()); toks.extend(st[1].items())
            best = {}
            for (s, v) in toks:
                if v > best.get(s, -1):
                    best[s] = v
            self.freed.append((t.lo, t.hi, list(best.items())))
        del self.tiles_live[nlive:]
        self.top = top

    def bank(self, i):
        return self.psum[:, i * 512:(i + 1) * 512]

    def next_bank(self):
        while True:
            b = self.bank_rr % 8
            self.bank_rr += 1
            if b not in self.reserved_banks:
                return b

    def _key(self, k):
        assert isinstance(k, (Tile, tuple, str)), f"bad dependency key {type(k)}"
        if isinstance(k, Tile):
            k = (k.name, None)
        elif isinstance(k, tuple) and isinstance(k[0], Tile):
            k = (k[0].name,) + tuple(k[1:])
        if isinstance(k, tuple) and k[0] in self.tile_keys:
            self.tile_keys[k[0]].add(k)
        return k

    def _getstate(self, k):
        st = self.state.get(k)
        if st is None:
            inh = self.tile_init.get(k[0], []) if isinstance(k, tuple) else []
            d = {}
            for (s_, v_) in inh:
                if v_ > d.get(s_, -1):
                    d[s_] = v_
            st = [d, {}]
            self.state[k] = st
        return st

    DMA_K = {"sp": 16, "pool": 12, "act": 8, "dve": 2, "pe": 2}

    def _token(self, eng, isdma):
        if isdma:
            K = self.DMA_K[eng]
            c = self.cnt.get((eng, "d"), 0)
            self.cnt[(eng, "d")] = c + 1
            j, m = c % K, c // K
            sk = (eng, "d", j)
            if sk not in self.sems:
                self.sems[sk] = self.es.enter_context(self.nc.semaphore(f"s_{eng}_d{j}"))
                self.nsem += 1
            forced = (sk, 16 * m) if m > 0 else None
            self.final[sk] = 16 * (m + 1)
            return (sk, 16 * (m + 1)), forced
        c = self.cnt.get((eng, "c"), 0)
        epoch, idx = divmod(c, self.CAP_C)
        self.cnt[(eng, "c")] = c + 1
        sk = (eng, "c", epoch)
        if sk not in self.sems:
            self.sems[sk] = self.es.enter_context(self.nc.semaphore(f"s_{eng}_c{epoch}"))
            self.nsem += 1
        self.final[sk] = idx + 1
        return (sk, idx + 1), None

    def op(self, eng, fn, r=(), w=(), dma=False):
        o = Op()
        rec = _Rec()
        fn(rec)
        assert len(rec.calls) >= 1
        o.eng, o.calls, o.isdma = eng, rec.calls, dma
        need = {}
        rk = [self._key(k) for k in r]
        wk = [self._key(k) for k in w]
        for k in rk:
            st = self._getstate(k)
            for (s, v) in st[0].items():
                if v > need.get(s, -1):
                    need[s] = v
            if isinstance(k, tuple) and k[0] == "ps":
                for (s, v) in st[1].items():
                    if s[0] != eng and v > need.get(s, -1):
                        need[s] = v
        for k in wk:
            st = self._getstate(k)
            for (s, v) in list(st[0].items()) + list(st[1].items()):
                if s[0] == eng and s[1] == "c":
                    continue
                if v > need.get(s, -1):
                    need[s] = v
        tok, forced = self._token(eng, dma)
        if forced is not None and forced[1] > need.get(forced[0], -1):
            need[forced[0]] = forced[1]
        waits = []
        seen = self.seen[eng]
        for s, v in need.items():
            if s[0] == "pe" and eng == "pe" and s[1] == "c":
                continue
            if seen.get(s, -1) >= v:
                continue
            seen[s] = v
            waits.append((s, v))
        o.waits = waits
        o.sem, o.val = tok
        for k in rk:
            self.state[k][1][tok[0]] = tok[1]
        for k in wk:
            self.state[k] = [{tok[0]: tok[1]}, {}]
        self.ops[eng].append(o)
        self.n_ops += 1
        return o

    def dma(self, eng, out, in_, r=(), w=(), **kw):
        return self.op(eng, lambda e: e.dma_start(out=out, in_=in_, **kw), r=r, w=w, dma=True)

    def emit(self):
        nc = self.nc
        with nc.Block() as block:
            def run(engname):
                def f(e):
                    for o in self.ops[engname]:
                        for (s, v) in o.waits:
                            e.wait_ge(self.sems[s], v)
                        ins = None
                        for (nm_, a_, k_) in o.calls:
                            ins = getattr(e, nm_)(*a_, **k_)
                        ins.then_inc(self.sems[o.sem], 16 if o.isdma else 1)
                    if engname == "sp":
                        for sk, v in self.final.items():
                            e.wait_ge(self.sems[sk], v)
                return f
            block.tensor(run("pe"))
            block.scalar(run("act"))
            block.vector(run("dve"))
            block.gpsimd(run("pool"))
            block.sync(run("sp"))


def mk(ap, dims, off=0):
    return bass.AP(ap.tensor, ap.offset + off, [list(ap.ap[0])] + [list(d) for d in dims])


def make_consts():
    c = {}
    c["ident"] = np.eye(128, dtype=np.float32)
    c["identbf"] = np.eye(128, dtype=np.float32).astype(ml_dtypes.bfloat16)
    t = np.arange(2048)
    row = (t // 64).astype(np.float32)
    col = (t % 64).astype(np.float32)
    freqs = (np.float32(10000.0) ** (-np.arange(16, dtype=np.float32) / np.float32(16))).astype(np.float32)
    rope = np.zeros((128, 2, 2048), np.float32)
    for p in range(128):
        d = p % 64
        blk, i = divmod(d, 16)
        pos = row if blk < 2 else col
        ang = (pos * freqs[i]).astype(np.float32)
        rope[p, 0] = np.cos(ang)
        rope[p, 1] = -np.sin(ang) if blk in (0, 2) else np.sin(ang)
    c["rope"] = rope
    kl = np.arange(128)[:, None]
    ql = np.arange(128)[None, :]
    mb = np.zeros((128, 2, 512), np.float32)
    lo = np.where(kl >= ql, 0.0, NEG)
    hi = np.where(kl <= ql, 0.0, NEG)
    for hq in range(4):
        mb[:, 0, hq * 128:(hq + 1) * 128] = lo
        mb[:, 1, hq * 128:(hq + 1) * 128] = hi
    c["maskb"] = mb.astype(ml_dtypes.bfloat16)
    xs = np.zeros((128, 8, 240), np.float32)
    for g in range(8):
        for ci in range(16):
            xs[g * 16 + ci, g, 112 + ci] = 1.0
    c["xsel"] = xs.astype(ml_dtypes.bfloat16)
    ys = np.zeros((128, 8, 128), np.float32)
    for g in range(8):
        for j in range(8):
            for co in range(16):
                ys[j * 16 + co, g, g * 16 + co] = 1.0
    c["ysel"] = ys.astype(ml_dtypes.bfloat16)
    mj = np.zeros((128, 8), np.float32)
    for j in range(8):
        mj[j * 16:(j + 1) * 16, j] = 1.0
    c["maskj"] = mj
    tm = np.zeros((128, 2, 128), np.float32)
    s_idx = (np.arange(128) // 16)[:, None]
    j_idx = (np.arange(128) // 16)[None, :]
    tm[:, 0, :] = (j_idx >= s_idx)
    tm[:, 1, :] = (j_idx <= s_idx)
    c["toepm"] = tm
    vs = np.zeros((1, 128), np.float32)
    vs[0, 64:] = 1.0
    c["vsink"] = vs.astype(ml_dtypes.bfloat16)
    return c


CONST_SPECS = [("ident", [128, 128], F32), ("identbf", [128, 128], BF16), ("rope", [128, 2, 2048], F32),
               ("maskb", [128, 2, 512], BF16), ("xsel", [128, 8, 240], BF16), ("ysel", [128, 8, 128], BF16),
               ("maskj", [128, 8], F32), ("toepm", [128, 2, 128], F32), ("vsink", [1, 128], BF16)]

IN_SPECS = [("xp", [512, D]), ("xs", [2048, D]), ("ck", [2, PAST, 128]), ("cv", [2, PAST, 128]),
            ("sre", [2, 2, 16, 64]), ("sim", [2, 2, 16, 64]), ("cvec", [2, D]),
            ("norm_mix", [2, D]), ("norm_ffn", [2, D]), ("norm_final", [D]),
            ("w_ada", [2, D, 6 * D]), ("b_ada", [2, 6 * D]), ("w_in", [2, D, IN_DIM]), ("w_out", [2, D, D]),
            ("attn_sink", [2, 8]), ("sc_conv", [2, 256, 3]),
            ("ssm_lam_re", [2, 2, 16, 64]), ("ssm_lam_im", [2, 2, 16, 64]), ("ssm_log_dt", [2, 2, 16]),
            ("ssm_b_re", [2, 2, 16, 64, 16]), ("ssm_b_im", [2, 2, 16, 64, 16]),
            ("ssm_c_re", [2, 2, 16, 16, 64]), ("ssm_c_im", [2, 2, 16, 16, 64]),
            ("ssm_d", [2, 16, 16]), ("ssm_w_glu", [2, 256, 256]),
            ("ffn_w_up", [2, D, 2 * DFF]), ("ffn_conv", [2, 2 * DFF, 3]), ("ffn_w_down", [2, DFF, D])]

OUT_SPECS = [("yp", [512, D]), ("ys", [2048, D]), ("nk", [2, 2, 256, 128]), ("nv", [2, 2, 256, 128]),
             ("nsr", [2, 2, 2, 16, 64]), ("nsi", [2, 2, 2, 16, 64])]


class Builder:
    def __init__(self, stages=("prep", "P", "S"), dbg=()):
        self.stages = stages
        self.dbg_specs = list(dbg)
        self.nc = nc = bass.Bass("TRN2", target_bir_lowering=False)
        self.es = ExitStack()
        class _Lazy(dict):
            def __init__(s_, specs, prefix):
                super().__init__()
                s_.specs, s_.prefix = specs, prefix

            def __missing__(s_, name):
                shape, dt = s_.specs[name]
                ap = nc.dram_tensor(s_.prefix + name, shape, dt, kind="ExternalInput").ap()
                s_[name] = ap
                return ap
        self.I = _Lazy({n: (sh, F32) for n, sh in IN_SPECS}, "")
        self.C = _Lazy({n: (sh, dt) for n, sh, dt in CONST_SPECS}, "c_")
        self.O = {}
        for name, shape in OUT_SPECS:
            self.O[name] = nc.dram_tensor(name, shape, F32, kind="ExternalOutput").ap()
        for name, shape, dt_ in self.dbg_specs:
            self.O[name] = nc.dram_tensor(name, shape, dt_, kind="ExternalOutput").ap()
        self.W = {}
        for name, kc, n in [("win", 8, IN_DIM), ("wout", 8, D), ("wup", 8, 2 * DFF), ("wdn", 22, D), ("wglu", 2, 256)]:
            self.W[name] = [nc.dram_tensor(f"s_{name}{l}", [128, kc, n], BF16, kind="Internal").ap() for l in range(2)]
        self.S_kt = [nc.dram_tensor(f"s_kt{l}", [128, 16, 128], BF16, kind="Internal").ap() for l in range(2)]
        self.S_ws = [nc.dram_tensor(f"s_ws{l}", [128, 2, 16, 128], BF16, kind="Internal").ap() for l in range(2)]
        self.S_wo = [nc.dram_tensor(f"s_wo{l}", [128, 2, 8, 2, 128], BF16, kind="Internal").ap() for l in range(2)]
        self.S_e = [nc.dram_tensor(f"s_e{l}", [128, 2, 2, 8, 256], F32, kind="Internal").ap() for l in range(2)]
        self.P = Prog(nc, self.es)

    def wkeys(self, name, l, c0, c1, kcs=None):
        nkc = {"win": 8, "wout": 8, "wup": 8, "wdn": 22, "wglu": 2}[name]
        ks = []
        for kc in (range(nkc) if kcs is None else kcs):
            for b in range(c0 // CASTW, (c1 - 1) // CASTW + 1):
                ks.append(("W", name, l, kc, b))
        return ks

    def persistent(self):
        P = self.P
        self.ident = P.alloc("ident", [128], F32)
        self.identbf = P.alloc("identbf", [128], BF16)
        self.onesbf = P.alloc("onesbf", [128], BF16)
        self.modT = P.alloc("modT", [2, 48, 2], F32)
        self.gsc1 = P.alloc("gsc1", [2, 8, 2], F32)
        self.gsc2 = P.alloc("gsc2", [2, 8, 2], F32)
        self.nfT = P.alloc("nfT", [8], F32)
        self.a8mag = P.alloc("a8mag", [4, 8], F32)
        self.e1 = P.alloc("e1", [2, 4, 8], F32)
        self.esrow = P.alloc("esrow", [2, 8, 128], BF16)
        self.vsink = P.alloc("vsink", [128], BF16)
        self.scw = P.alloc("scw", [2, 2, 3], F32)
        self.fcw = P.alloc("fcw", [2, 44, 3], F32)
        self.maskj = P.alloc("maskj", [8], F32)
        P.dma("sp", self.ident[:], self.C["ident"][:, :], w=[self.ident])
        P.dma("sp", self.identbf[:], self.C["identbf"][:, :], w=[self.identbf])
        P.dma("sp", self.maskj[:], self.C["maskj"][:, :], w=[self.maskj])
        P.dma("sp", self.vsink[0:1, :], self.C["vsink"][:, :], w=[self.vsink])
        P.op("dve", lambda e: e.memset(self.onesbf[:], 1.0), w=[self.onesbf])
        I = self.I
        P.dma("sp", self.nfT[:], I["norm_final"].rearrange("(c p) -> p c", p=128), w=[self.nfT],
              allow_slow_non_contiguous=True)
        for l in range(2):
            P.dma("sp", self.scw[:, l], I["sc_conv"][l].rearrange("(c p) k -> p c k", p=128), w=[self.scw])
            P.dma("sp", self.fcw[:, l], I["ffn_conv"][l].rearrange("(c p) k -> p c k", p=128), w=[self.fcw])

    def alloc_cast_bufs(self, tag, NB=3):
        P = self.P
        return ([P.alloc(f"cst{tag}{i}", [CASTW], F32) for i in range(NB)], [P.alloc(f"cob{tag}{i}", [CASTW], BF16) for i in range(NB)])

    def prep_casts(self, layers, eng, bufs, first_dep=()):
        P, I = self.P, self.I
        stage, outb = bufs
        NB = len(stage)
        pieces = []
        for l in layers:
            for name, src, K, N in [("win", "w_in", D, IN_DIM), ("wglu", "ssm_w_glu", 256, 256), ("wout", "w_out", D, D),
                                    ("wup", "ffn_w_up", D, 2 * DFF), ("wdn", "ffn_w_down", DFF, D)]:
                if self.cast_only is not None and name not in self.cast_only:
                    continue
                for kc in range(K // 128):
                    for b, c0 in enumerate(range(0, N, CASTW)):
                        pieces.append((l, name, src, kc, b, c0, min(CASTW, N - c0)))

        def load(i):
            l, name, src, kc, b, c0, cw = pieces[i]
            st = stage[i % NB]
            P.dma(eng, st[:, 0:cw], I[src][l, kc * 128:(kc + 1) * 128, c0:c0 + cw], r=list(first_dep) if i == 0 else [], w=[st])
        PRE = NB - 1
        for i in range(min(PRE, len(pieces))):
            load(i)
        for i, (l, name, src, kc, b, c0, cw) in enumerate(pieces):
            st, ob = stage[i % NB], outb[i % NB]
            if eng == "act":
                P.op("act", lambda e, st=st, ob=ob, cw=cw: e.copy(out=ob[:, 0:cw], in_=st[:, 0:cw]), r=[st], w=[ob])
            else:
                P.op(eng, lambda e, st=st, ob=ob, cw=cw: e.tensor_copy(out=ob[:, 0:cw], in_=st[:, 0:cw]), r=[st], w=[ob])
            P.dma(eng, self.W[name][l][:, kc, c0:c0 + cw], ob[:, 0:cw], r=[ob], w=[("W", name, l, kc, b)])
            if i + PRE < len(pieces):
                load(i + PRE)

    def prep_adaln_setup(self):
        P, I = self.P, self.I
        self.scT = P.alloc("scT", [8, 2], F32)
        self.bT = P.alloc("bT", [2, 48], F32)
        self.nmT = P.alloc("nmT", [2, 2, 8], F32)
        mk0 = P.mark()
        craw = P.alloc("craw", [8, 2], F32)
        for v in range(2):
            P.dma("sp", craw[:, :, v], I["cvec"][v].rearrange("(c p) -> p c", p=128), w=[craw], allow_slow_non_contiguous=True)
        P.op("act", lambda e: e.activation(out=self.scT[:], in_=craw[:], func=AF.Silu), r=[craw], w=[self.scT])
        for l in range(2):
            P.dma("sp", self.bT[:, l], I["b_ada"][l].rearrange("(c p) -> p c", p=128), w=[self.bT], allow_slow_non_contiguous=True)
            P.dma("sp", self.nmT[:, 0, l], I["norm_mix"][l].rearrange("(c p) -> p c", p=128), w=[self.nmT], allow_slow_non_contiguous=True)
            P.dma("sp", self.nmT[:, 1, l], I["norm_ffn"][l].rearrange("(c p) -> p c", p=128), w=[self.nmT], allow_slow_non_contiguous=True)
        P.release(mk0)
        self.adaln_done = set()

    def prep_adaln(self, l):
        if l in self.adaln_done:
            return
        self.adaln_done.add(l)
        P, I = self.P, self.I
        scT, bT, nmT = self.scT, self.bT, self.nmT
        mk0 = P.mark()
        wa = [P.alloc(f"wa{i}", [8, 512], F32) for i in range(2)]
        bk = P.next_bank()
        P.reserved_banks.add(bk)
        ps = P.bank(bk)
        for j in range(12):
            w = wa[j % 2]
            P.dma("sp", w[:], I["w_ada"][l, :, j * 512:(j + 1) * 512].rearrange("(kc p) n -> p kc n", p=128), w=[w])
            for oc in range(4):
                col = (j * 4 + oc) * 2
                for kc in range(8):
                    P.op("pe", lambda e, w=w, oc=oc, kc=kc, col=col, ps=ps: e.matmul(
                        ps[:, col:col + 2], lhsT=w[:, kc, oc * 128:(oc + 1) * 128], rhs=scT[:, kc, :],
                        start=(kc == 0), stop=(kc == 7)), r=[w, scT], w=[("ps", bk)])
        P.op("dve", lambda e, l=l, ps=ps: e.tensor_tensor(
            out=self.modT[:, l], in0=ps[:, 0:96].rearrange("p (a b) -> p a b", b=2),
            in1=mk(bT[:, l], [[1, 48], [0, 2]]), op=ALU.add), r=[("ps", bk), bT], w=[(self.modT, l)])
        P.op("dve", lambda e, l=l: e.scalar_tensor_tensor(
            out=self.gsc1[:, l], in0=self.modT[:, l, 8:16, :], scalar=1.0,
            in1=mk(nmT[:, 0, l], [[1, 8], [0, 2]]), op0=ALU.add, op1=ALU.mult), r=[(self.modT, l), nmT], w=[(self.gsc1, l)])
        P.op("dve", lambda e, l=l: e.scalar_tensor_tensor(
            out=self.gsc2[:, l], in0=self.modT[:, l, 32:40, :], scalar=1.0,
            in1=mk(nmT[:, 1, l], [[1, 8], [0, 2]]), op0=ALU.add, op1=ALU.mult), r=[(self.modT, l), nmT], w=[(self.gsc2, l)])
        P.reserved_banks.discard(bk)
        P.release(mk0)

    def prep_ssm(self):
        P, I, C = self.P, self.I, self.C
        mk0 = P.mark()
        V = lambda fn, r, w: P.op("dve", fn, r=r, w=w)
        A = lambda fn, r, w: P.op("act", fn, r=r, w=w)
        LD = [(l, d) for l in range(2) for d in range(2)]
        lamr = P.alloc("lamr", [4, 8], F32); lami = P.alloc("lami", [4, 8], F32); ldt = P.alloc("ldt", [4, 8], F32)
        for i, (l, d) in enumerate(LD):
            for h in range(2):
                for tl, nm in ((lamr, "ssm_lam_re"), (lami, "ssm_lam_im")):
                    src = I[nm][l, d]
                    P.dma("sp", tl[64 * h:64 * h + 64, i, :], bass.AP(src.tensor, src.offset + h * 64, [[1, 64], [128, 8]]),
                          w=[tl], allow_slow_non_contiguous=True)
                src = I["ssm_log_dt"][l, d]
                P.dma("sp", ldt[64 * h:64 * h + 64, i, :], bass.AP(src.tensor, src.offset + h, [[0, 64], [2, 8]]),
                      w=[ldt], allow_slow_non_contiguous=True)
        names = ["dt", "lrdt", "mag", "ang", "rs", "rc", "sn", "cs", "ar", "ai", "den", "rden", "am1", "t1", "t2", "t3", "t4",
                 "fr", "fi", "mag2", "rm", "ivr", "ivi", "inv8"]
        T = {n: P.alloc(n, [4, 8], F32) for n in names}
        negpi = P.alloc("negpi", [1], F32)
        V(lambda e: e.memset(negpi[:], -math.pi), [], [negpi])
        qi = P.alloc("qi", [4, 8], I32)

        def taylor_exp(out_t, x_t, deg, tmp):
            V(lambda e: e.tensor_scalar(out=out_t[:], in0=x_t[:], scalar1=1.0 / deg, scalar2=1.0, op0=ALU.mult, op1=ALU.add), [x_t], [out_t])
            for k in range(deg - 1, 0, -1):
                V(lambda e: e.tensor_tensor(out=tmp[:], in0=out_t[:], in1=x_t[:], op=ALU.mult), [out_t, x_t], [tmp])
                V(lambda e, k=k: e.tensor_scalar(out=out_t[:], in0=tmp[:], scalar1=1.0 / k, scalar2=1.0, op0=ALU.mult, op1=ALU.add), [tmp], [out_t])
        V(lambda e: e.tensor_copy(out=qi[:], in_=ldt[:]), [ldt], [qi])
        V(lambda e: e.tensor_copy(out=T["t1"][:], in_=qi[:]), [qi], [T["t1"]])
        V(lambda e: e.tensor_tensor(out=T["t2"][:], in0=ldt[:], in1=T["t1"][:], op=ALU.subtract), [ldt, T["t1"]], [T["t2"]])
        taylor_exp(T["t3"], T["t2"], 12, T["t4"])
        V(lambda e: e.memset(T["dt"][:], 0.0), [], [T["dt"]])
        for j in range(-10, 1):
            V(lambda e, j=j: e.tensor_scalar(out=T["t4"][:], in0=T["t1"][:], scalar1=float(j), scalar2=math.exp(j), op0=ALU.is_equal, op1=ALU.mult),
              [T["t1"]], [T["t4"]])
            V(lambda e: e.tensor_tensor(out=T["dt"][:], in0=T["dt"][:], in1=T["t4"][:], op=ALU.add), [T["dt"], T["t4"]], [T["dt"]])
        V(lambda e: e.tensor_tensor(out=T["dt"][:], in0=T["dt"][:], in1=T["t3"][:], op=ALU.mult), [T["dt"], T["t3"]], [T["dt"]])
        V(lambda e: e.tensor_tensor(out=T["lrdt"][:], in0=lamr[:], in1=T["dt"][:], op=ALU.mult), [lamr, T["dt"]], [T["lrdt"]])
        taylor_exp(T["mag"], T["lrdt"], 7, T["t4"])
        V(lambda e: e.tensor_tensor(out=T["t1"][:], in0=T["mag"][:], in1=T["mag"][:], op=ALU.mult), [T["mag"]], [T["t1"]])
        V(lambda e: e.tensor_tensor(out=T["t2"][:], in0=T["t1"][:], in1=T["t1"][:], op=ALU.mult), [T["t1"]], [T["t2"]])
        V(lambda e: e.tensor_tensor(out=self.a8mag[:], in0=T["t2"][:], in1=T["t2"][:], op=ALU.mult), [T["t2"]], [self.a8mag])
        V(lambda e: e.reciprocal(out=T["inv8"][:], in_=self.a8mag[:]), [self.a8mag], [T["inv8"]])
        V(lambda e: e.tensor_tensor(out=T["ang"][:], in0=lami[:], in1=T["dt"][:], op=ALU.mult), [lami, T["dt"]], [T["ang"]])
        def range_reduce(out_t, add):
            V(lambda e: e.tensor_scalar(out=T["t1"][:], in0=T["ang"][:], scalar1=add, scalar2=1.0 / TWO_PI, op0=ALU.add, op1=ALU.mult),
              [T["ang"]], [T["t1"]])
            V(lambda e: e.tensor_copy(out=qi[:], in_=T["t1"][:]), [T["t1"]], [qi])
            V(lambda e: e.tensor_copy(out=T["t2"][:], in_=qi[:]), [qi], [T["t2"]])
            V(lambda e: e.scalar_tensor_tensor(out=T["t3"][:], in0=T["t2"][:], scalar=-TWO_PI, in1=T["ang"][:], op0=ALU.mult, op1=ALU.add),
              [T["t2"], T["ang"]], [T["t3"]])
            V(lambda e: e.tensor_scalar_add(out=T["t3"][:], in0=T["t3"][:], scalar1=add), [T["t3"]], [T["t3"]])
            V(lambda e: e.tensor_scalar(out=T["t4"][:], in0=T["t3"][:], scalar1=math.pi, scalar2=-TWO_PI, op0=ALU.is_gt, op1=ALU.mult),
              [T["t3"]], [T["t4"]])
            V(lambda e: e.tensor_tensor(out=T["t3"][:], in0=T["t3"][:], in1=T["t4"][:], op=ALU.add), [T["t3"], T["t4"]], [T["t3"]])
            V(lambda e: e.tensor_scalar(out=T["t4"][:], in0=T["t3"][:], scalar1=-math.pi, scalar2=TWO_PI, op0=ALU.is_lt, op1=ALU.mult),
              [T["t3"]], [T["t4"]])
            V(lambda e: e.tensor_tensor(out=out_t[:], in0=T["t3"][:], in1=T["t4"][:], op=ALU.add), [T["t3"], T["t4"]], [out_t])
        range_reduce(T["rs"], 0.0)
        xx, x2, ps_, pc_ = T["t1"], T["t2"], T["t3"], T["t4"]
        V(lambda e: e.tensor_scalar_mul(out=xx[:], in0=T["rs"][:], scalar1=0.25), [T["rs"]], [xx])
        V(lambda e: e.tensor_tensor(out=x2[:], in0=xx[:], in1=xx[:], op=ALU.mult), [xx], [x2])

        def horner(p, coefs):
            V(lambda e: e.tensor_scalar(out=p[:], in0=x2[:], scalar1=coefs[0], scalar2=1.0, op0=ALU.mult, op1=ALU.add), [x2], [p])
            for cf in coefs[1:]:
                V(lambda e: e.tensor_tensor(out=p[:], in0=p[:], in1=x2[:], op=ALU.mult), [p, x2], [p])
                V(lambda e, cf=cf: e.tensor_scalar(out=p[:], in0=p[:], scalar1=cf, scalar2=1.0, op0=ALU.mult, op1=ALU.add), [p], [p])
        horner(ps_, [-1.0 / 110.0, -1.0 / 72.0, -1.0 / 42.0, -1.0 / 20.0, -1.0 / 6.0])
        V(lambda e: e.tensor_tensor(out=ps_[:], in0=ps_[:], in1=xx[:], op=ALU.mult), [ps_, xx], [ps_])
        horner(pc_, [-1.0 / 90.0, -1.0 / 56.0, -1.0 / 30.0, -1.0 / 12.0, -1.0 / 2.0])
        sA, cA = T["sn"], T["cs"]
        for it in range(2):
            V(lambda e: e.scalar_tensor_tensor(out=sA[:], in0=ps_[:], scalar=2.0, in1=pc_[:], op0=ALU.mult, op1=ALU.mult), [ps_, pc_], [sA])
            V(lambda e: e.tensor_tensor(out=cA[:], in0=ps_[:], in1=ps_[:], op=ALU.mult), [ps_], [cA])
            V(lambda e: e.tensor_scalar(out=cA[:], in0=cA[:], scalar1=-2.0, scalar2=1.0, op0=ALU.mult, op1=ALU.add), [cA], [cA])
            if it == 0:
                V(lambda e: e.tensor_copy(out=ps_[:], in_=sA[:]), [sA], [ps_])
                V(lambda e: e.tensor_copy(out=pc_[:], in_=cA[:]), [cA], [pc_])

        def tt(o, a, b, op):
            V(lambda e: e.tensor_tensor(out=o[:], in0=a[:], in1=b[:], op=op), [a, b], [o])
        tt(T["ar"], T["mag"], T["cs"], ALU.mult)
        tt(T["ai"], T["mag"], T["sn"], ALU.mult)
        tt(T["t1"], lamr, lamr, ALU.mult)
        tt(T["t2"], lami, lami, ALU.mult)
        tt(T["den"], T["t1"], T["t2"], ALU.add)
        V(lambda e: e.reciprocal(out=T["rden"][:], in_=T["den"][:]), [T["den"]], [T["rden"]])
        V(lambda e: e.tensor_scalar_add(out=T["am1"][:], in0=T["ar"][:], scalar1=-1.0), [T["ar"]], [T["am1"]])
        tt(T["t1"], T["am1"], lamr, ALU.mult)
        tt(T["t2"], T["ai"], lami, ALU.mult)
        tt(T["t3"], T["t1"], T["t2"], ALU.add)
        tt(T["fr"], T["t3"], T["rden"], ALU.mult)
        tt(T["t1"], T["ai"], lamr, ALU.mult)
        tt(T["t2"], T["am1"], lami, ALU.mult)
        tt(T["t3"], T["t1"], T["t2"], ALU.subtract)
        tt(T["fi"], T["t3"], T["rden"], ALU.mult)
        tt(T["t1"], T["ar"], T["ar"], ALU.mult)
        tt(T["t2"], T["ai"], T["ai"], ALU.mult)
        tt(T["mag2"], T["t1"], T["t2"], ALU.add)
        V(lambda e: e.reciprocal(out=T["rm"][:], in_=T["mag2"][:]), [T["mag2"]], [T["rm"]])
        tt(T["ivr"], T["ar"], T["rm"], ALU.mult)
        V(lambda e: e.scalar_tensor_tensor(out=T["ivi"][:], in0=T["ai"][:], scalar=-1.0, in1=T["rm"][:], op0=ALU.mult, op1=ALU.mult),
          [T["ai"], T["rm"]], [T["ivi"]])
        if self.cut == 1:
            P.release(mk0); return
        apr = P.alloc("apr", [4, 17, 8], F32); api = P.alloc("api", [4, 17, 8], F32)
        V(lambda e: e.memset(apr[:, :, 8, :], 1.0), [], [(apr, 8)])
        V(lambda e: e.memset(api[:, :, 8, :], 0.0), [], [(api, 8)])
        V(lambda e: e.tensor_copy(out=apr[:, :, 9, :], in_=T["ar"][:]), [T["ar"]], [(apr, 9)])
        V(lambda e: e.tensor_copy(out=api[:, :, 9, :], in_=T["ai"][:]), [T["ai"]], [(api, 9)])
        V(lambda e: e.tensor_copy(out=apr[:, :, 7, :], in_=T["ivr"][:]), [T["ivr"]], [(apr, 7)])
        V(lambda e: e.tensor_copy(out=api[:, :, 7, :], in_=T["ivi"][:]), [T["ivi"]], [(api, 7)])

        def cmul_small(k_out, k_in, br, bi):
            xr, xi = apr[:, :, k_in, :], api[:, :, k_in, :]
            V(lambda e: e.tensor_tensor(out=T["t1"][:], in0=xr, in1=br[:], op=ALU.mult), [(apr, k_in), br], [T["t1"]])
            V(lambda e: e.tensor_tensor(out=T["t2"][:], in0=xi, in1=bi[:], op=ALU.mult), [(api, k_in), bi], [T["t2"]])
            V(lambda e: e.tensor_tensor(out=apr[:, :, k_out, :], in0=T["t1"][:], in1=T["t2"][:], op=ALU.subtract), [T["t1"], T["t2"]], [(apr, k_out)])
            V(lambda e: e.tensor_tensor(out=T["t3"][:], in0=xr, in1=bi[:], op=ALU.mult), [(apr, k_in), bi], [T["t3"]])
            V(lambda e: e.tensor_tensor(out=T["t4"][:], in0=xi, in1=br[:], op=ALU.mult), [(api, k_in), br], [T["t4"]])
            V(lambda e: e.tensor_tensor(out=api[:, :, k_out, :], in0=T["t3"][:], in1=T["t4"][:], op=ALU.add), [T["t3"], T["t4"]], [(api, k_out)])
        for k in range(9, 16):
            cmul_small(k + 1, k, T["ar"], T["ai"])
        for k in range(7, 0, -1):
            cmul_small(k - 1, k, T["ivr"], T["ivi"])
        APW_R = [(apr, k) for k in range(17)]
        APW_I = [(api, k) for k in range(17)]
        V(lambda e: e.tensor_tensor(out=self.e1[:, 0], in0=apr[:, :, 16, :], in1=T["inv8"][:], op=ALU.mult), [(apr, 16), T["inv8"]], [self.e1])
        V(lambda e: e.tensor_tensor(out=self.e1[:, 1], in0=api[:, :, 16, :], in1=T["inv8"][:], op=ALU.mult), [(api, 16), T["inv8"]], [self.e1])

        if self.cut == 2:
            P.release(mk0); return
        Et = P.alloc("Et", [2, 8, 256], F32)
        wk = P.alloc("wk", [2, 2, 8], F32)
        et1 = P.alloc("et1", [8, 128], F32); et2 = P.alloc("et2", [8, 128], F32)
        Br = P.alloc("Br", [8, 16], F32); Bi = P.alloc("Bi", [8, 16], F32)
        bbr = P.alloc("bbr", [8, 16], F32); bbi = P.alloc("bbi", [8, 16], F32)
        Cr = P.alloc("Cr", [8, 16], F32); Ci = P.alloc("Ci", [8, 16], F32)
        cn = P.alloc("cn", [2, 64], F32)
        pbr = P.alloc("pbr", [8, 8, 16], F32); pbi = P.alloc("pbi", [8, 8, 16], F32)
        pcr = P.alloc("pcr", [8, 8, 16], F32); pci = P.alloc("pci", [8, 8, 16], F32)
        q1 = P.alloc("q1", [8, 8, 16], F32); q2 = P.alloc("q2", [8, 8, 16], F32)
        wsb = P.alloc("wsb", [16, 128], BF16)
        wob = P.alloc("wob", [8, 2, 128], BF16)
        ktacc = P.alloc("ktacc", [16, 128], F32)
        ktb = P.alloc("ktb", [16, 128], BF16)
        toep = P.alloc("toep", [2, 128], F32)
        dtab = P.alloc("dtab", [16], F32)
        ktmp = P.alloc("ktmp", [128], F32)
        P.dma("sp", toep[:], C["toepm"][:, :, :], w=[toep])

        def bc_last(ap2, n):
            return mk(ap2, [list(ap2.ap[1]), [0, n]])

        def cprod(outr, outi, kstart, kstep, Xr, Xi, neg_im, xkeys):
            a_r = apr[:, ld, kstart, :]
            a_i = api[:, ld, kstart, :]
            AR = mk(a_r, [[1, 8], [8 * kstep, 8], [0, 16]])
            AI = mk(a_i, [[1, 8], [8 * kstep, 8], [0, 16]])
            XR = mk(Xr[:], [[16, 8], [0, 8], [1, 16]])
            XI = mk(Xi[:], [[16, 8], [0, 8], [1, 16]])
            V(lambda e: e.tensor_tensor(out=q1[:], in0=AR, in1=XR, op=ALU.mult), APW_R + xkeys, [q1])
            V(lambda e: e.tensor_tensor(out=q2[:], in0=AI, in1=XI, op=ALU.mult), APW_I + xkeys, [q2])
            V(lambda e: e.tensor_tensor(out=outr[:], in0=q1[:], in1=q2[:], op=ALU.subtract), [q1, q2], [outr])
            V(lambda e: e.tensor_tensor(out=q1[:], in0=AR, in1=XI, op=ALU.mult), APW_R + xkeys, [q1])
            V(lambda e: e.tensor_tensor(out=q2[:], in0=AI, in1=XR, op=ALU.mult), APW_I + xkeys, [q2])
            if neg_im:
                V(lambda e: e.scalar_tensor_tensor(out=outi[:], in0=q1[:], scalar=-1.0, in1=q2[:], op0=ALU.mult, op1=ALU.subtract),
                  [q1, q2], [outi])
            else:
                V(lambda e: e.tensor_tensor(out=outi[:], in0=q1[:], in1=q2[:], op=ALU.add), [q1, q2], [outi])

        for ld, (l, d) in enumerate(LD):
            V(lambda e: e.memset(Et[:, 0, :, 0:1], 1.0), [], [Et])
            V(lambda e: e.memset(Et[:, 1, :, 0:1], 0.0), [], [Et])
            V(lambda e, ld=ld: e.tensor_copy(out=wk[:, 0, 0, :], in_=self.e1[:, 0, ld, :]), [self.e1], [wk])
            V(lambda e, ld=ld: e.tensor_copy(out=wk[:, 0, 1, :], in_=self.e1[:, 1, ld, :]), [self.e1], [wk])
            for k in range(8):
                n = 1 << k
                pp, qq = k % 2, (k + 1) % 2
                wr = mk(wk[:, pp, 0, :], [[1, 8], [0, n]])
                wi = mk(wk[:, pp, 1, :], [[1, 8], [0, n]])
                t1v = et1[:, :, 0:n]; t2v = et2[:, :, 0:n]
                V(lambda e, n=n, wr=wr, t1v=t1v: e.tensor_tensor(out=t1v, in0=Et[:, 0, :, 0:n], in1=wr, op=ALU.mult), [Et, wk], [et1])
                V(lambda e, n=n, wi=wi, t2v=t2v: e.tensor_tensor(out=t2v, in0=Et[:, 1, :, 0:n], in1=wi, op=ALU.mult), [Et, wk], [et2])
                V(lambda e, n=n, t1v=t1v, t2v=t2v: e.tensor_tensor(out=Et[:, 0, :, n:2 * n], in0=t1v, in1=t2v, op=ALU.subtract), [et1, et2], [Et])
                V(lambda e, n=n, wi=wi, t1v=t1v: e.tensor_tensor(out=t1v, in0=Et[:, 0, :, 0:n], in1=wi, op=ALU.mult), [Et, wk], [et1])
                V(lambda e, n=n, wr=wr, t2v=t2v: e.tensor_tensor(out=t2v, in0=Et[:, 1, :, 0:n], in1=wr, op=ALU.mult), [Et, wk], [et2])
                V(lambda e, n=n, t1v=t1v, t2v=t2v: e.tensor_tensor(out=Et[:, 1, :, n:2 * n], in0=t1v, in1=t2v, op=ALU.add), [et1, et2], [Et])
                if k < 7:
                    a_r, a_i = wk[:, pp, 0, :], wk[:, pp, 1, :]
                    s1 = et1[:, :, 0]; s2 = et2[:, :, 0]
                    V(lambda e, a_r=a_r, s1=s1: e.tensor_tensor(out=s1, in0=a_r, in1=a_r, op=ALU.mult), [wk], [et1])
                    V(lambda e, a_i=a_i, s2=s2: e.tensor_tensor(out=s2, in0=a_i, in1=a_i, op=ALU.mult), [wk], [et2])
                    V(lambda e, qq=qq, s1=s1, s2=s2: e.tensor_tensor(out=wk[:, qq, 0, :], in0=s1, in1=s2, op=ALU.subtract), [et1, et2], [wk])
                    V(lambda e, a_r=a_r, a_i=a_i, s1=s1: e.tensor_tensor(out=s1, in0=a_r, in1=a_i, op=ALU.mult), [wk], [et1])
                    V(lambda e, qq=qq, s1=s1: e.tensor_scalar_mul(out=wk[:, qq, 1, :], in0=s1, scalar1=2.0), [et1], [wk])
            P.dma("sp", self.S_e[l][:, d], Et[:], r=[Et], w=[("S_e", l, d)])
            if self.cut == 3:
                continue
            for h in range(2):
                for tl, nm in ((Br, "ssm_b_re"), (Bi, "ssm_b_im")):
                    src = I[nm][l, d]
                    P.dma("sp", tl[64 * h:64 * h + 64, :, :], bass.AP(src.tensor, src.offset + h * 1024, [[16, 64], [2048, 8], [1, 16]]), w=[tl])
            FR = bc_last(T["fr"][:, ld, :], 16); FI = bc_last(T["fi"][:, ld, :], 16)
            V(lambda e, FR=FR: e.tensor_tensor(out=q1[:, :, 0, :], in0=Br[:], in1=FR, op=ALU.mult), [Br, T["fr"]], [q1])
            V(lambda e, FI=FI: e.tensor_tensor(out=q2[:, :, 0, :], in0=Bi[:], in1=FI, op=ALU.mult), [Bi, T["fi"]], [q2])
            V(lambda e: e.tensor_tensor(out=bbr[:], in0=q1[:, :, 0, :], in1=q2[:, :, 0, :], op=ALU.subtract), [q1, q2], [bbr])
            V(lambda e, FR=FR: e.tensor_tensor(out=q1[:, :, 0, :], in0=Bi[:], in1=FR, op=ALU.mult), [Bi, T["fr"]], [q1])
            V(lambda e, FI=FI: e.tensor_tensor(out=q2[:, :, 0, :], in0=Br[:], in1=FI, op=ALU.mult), [Br, T["fi"]], [q2])
            V(lambda e: e.tensor_tensor(out=bbi[:], in0=q1[:, :, 0, :], in1=q2[:, :, 0, :], op=ALU.add), [q1, q2], [bbi])
            for Cx, nm in ((Cr, "ssm_c_re"), (Ci, "ssm_c_im")):
                P.dma("sp", cn[:], I[nm][l, d].rearrange("(t g) c n -> (g c) t n", t=2), w=[cn])
                for t in range(2):
                    bk = P.next_bank(); ps = P.bank(bk)
                    P.op("pe", lambda e, t=t, ps=ps: e.transpose(out=ps[0:64, 0:128], in_=cn[:, t, :], identity=self.ident[:]),
                         r=[cn, self.ident], w=[("ps", bk)])
                    for par in range(2):
                        src_ = mk(ps[0:64, 0:128], [[32, 4], [1, 16]], off=par * 16)
                        V(lambda e, Cx=Cx, t=t, par=par, src_=src_: e.tensor_copy(out=Cx[64 * par:64 * par + 64, 4 * t:4 * t + 4, :], in_=src_),
                          [("ps", bk)], [Cx])
            if self.cut == 4:
                continue
            if d == 0:
                cprod(pbr, pbi, 15, -1, bbr, bbi, False, [bbr, bbi])
                cprod(pcr, pci, 1, 1, Cr, Ci, True, [Cr, Ci])
            else:
                cprod(pbr, pbi, 8, 1, bbr, bbi, False, [bbr, bbi])
                cprod(pcr, pci, 8, -1, Cr, Ci, True, [Cr, Ci])
            if self.cut == 5:
                continue
            for ggp in range(4):
                bk = P.next_bank(); ps = P.bank(bk)
                for ggl in range(2):
                    gg = 2 * ggp + ggl
                    for comp, pb in enumerate((pbr, pbi)):
                        col = (ggl * 2 + comp) * 128
                        P.op("pe", lambda e, pb=pb, gg=gg, col=col, ps=ps: e.transpose(
                            out=ps[:, col:col + 128], in_=pb[:, gg, :, :].rearrange("p s c -> p (s c)"),
                            identity=self.ident[:]), r=[pb, self.ident], w=[("ps", bk)])
                for comp in range(2):
                    src_ = mk(ps[:, comp * 128:comp * 128 + 1], [[256, 2], [64, 2], [1, 64]])
                    dst_ = mk(wsb[:, 4 * ggp, comp * 64:comp * 64 + 1], [[256, 2], [128, 2], [1, 64]])
                    V(lambda e, src_=src_, dst_=dst_: e.tensor_copy(out=dst_, in_=src_), [("ps", bk)], [wsb])
            P.dma("sp", self.S_ws[l][:, d], wsb[:], r=[wsb], w=[("S_ws", l, d)])
            if self.cut == 6:
                continue
            if d == 0:
                for s8 in range(8):
                    src = I["ssm_d"][l]
                    P.dma("sp", dtab[16 * s8:16 * s8 + 16, :], bass.AP(src.tensor, src.offset, [[1, 16], [16, 16]]), w=[dtab],
                          allow_slow_non_contiguous=True)
            for g in range(16):
                h, gg = g % 2, g // 2
                bk = P.next_bank(); ps = P.bank(bk)
                P.op("pe", lambda e, h=h, gg=gg, ps=ps: e.matmul(
                    ps[:, 0:128], lhsT=pbr[64 * h:64 * h + 64, gg, :, :].rearrange("p s c -> p (s c)"),
                    rhs=pcr[64 * h:64 * h + 64, gg, :, :].rearrange("p s c -> p (s c)"), start=True, stop=False),
                    r=[pbr, pcr], w=[("ps", bk)])
                P.op("pe", lambda e, h=h, gg=gg, ps=ps: e.matmul(
                    ps[:, 0:128], lhsT=pbi[64 * h:64 * h + 64, gg, :, :].rearrange("p s c -> p (s c)"),
                    rhs=pci[64 * h:64 * h + 64, gg, :, :].rearrange("p s c -> p (s c)"), start=False, stop=True),
                    r=[pbi, pci], w=[("ps", bk)])
                if d == 0:
                    V(lambda e, g=g, ps=ps: e.tensor_tensor(out=ktacc[:, g, :], in0=ps[:, 0:128], in1=toep[:, 0, :], op=ALU.mult),
                      [("ps", bk), toep], [(ktacc, g)])
                else:
                    V(lambda e, ps=ps: e.tensor_tensor(out=ktmp[:], in0=ps[:, 0:128], in1=toep[:, 1, :], op=ALU.mult),
                      [("ps", bk), toep], [ktmp])
                    V(lambda e, g=g: e.tensor_tensor(out=ktacc[:, g, :], in0=ktacc[:, g, :], in1=ktmp[:], op=ALU.add),
                      [ktmp, (ktacc, g)], [(ktacc, g)])
                    V(lambda e, g=g: e.scalar_tensor_tensor(out=ktb[:, g, :], in0=self.ident[:], scalar=dtab[:, g:g + 1],
                                                              in1=ktacc[:, g, :], op0=ALU.mult, op1=ALU.add),
                      [self.ident, dtab, (ktacc, g)], [ktb])
            if d == 1:
                P.dma("sp", self.S_kt[l], ktb[:], r=[ktb], w=[("S_kt", l)])
            if self.cut == 7:
                continue
            if d == 0:
                cprod(pcr, pci, 9, 1, Cr, Ci, True, [Cr, Ci])
            else:
                cprod(pcr, pci, 16, -1, Cr, Ci, True, [Cr, Ci])
            V(lambda e: e.tensor_copy(out=wob[:, :, 0, :], in_=pcr[:].rearrange("p g j c -> p g (j c)")), [pcr], [wob])
            V(lambda e: e.tensor_copy(out=wob[:, :, 1, :], in_=pci[:].rearrange("p g j c -> p g (j c)")), [pci], [wob])
            P.dma("sp", self.S_wo[l][:, d], wob[:], r=[wob], w=[("S_wo", l, d)])
        sk = P.alloc("sk", [16], F32); ske = P.alloc("ske", [16], F32)
        P.dma("sp", sk[0:1, :], I["attn_sink"].rearrange("l h -> (l h)").rearrange("(o n) -> o n", o=1), w=[sk])
        A(lambda e: e.activation(out=ske[0:1, :], in_=sk[0:1, :], func=AF.Exp), [sk], [ske])
        V(lambda e: e.tensor_copy(out=self.esrow[0:1, :, :, :].rearrange("p l h q -> p (l h) q"),
                                  in_=mk(ske[0:1, :], [[1, 16], [0, 128]])), [ske], [self.esrow, "ssm_done"])
        P.release(mk0)

    def dbg_out(self, name, tile_ap, keys):
        if name in self.O:
            self.P.dma("sp", self.O[name], tile_ap, r=keys, w=[("dbg", name)])

    def load_x(self, U):
        P = self.P
        T = U["T"]
        mk0 = P.mark()
        xst = [P.alloc(f"xst{i}", [4, D], F32) for i in range(2)]
        for blk in range(T // 512):
            st = xst[blk % 2]
            P.dma("sp", st[:], U["x"][blk * 512:(blk + 1) * 512, :].rearrange("(i p) f -> p i f", p=128), w=[st])
            for c in range(8):
                bk = P.next_bank(); ps = P.bank(bk)
                for i in range(4):
                    P.op("pe", lambda e, st=st, i=i, c=c, ps=ps: e.transpose(
                        out=ps[:, i * 128:(i + 1) * 128], in_=st[:, i, c * 128:(c + 1) * 128], identity=self.ident[:]),
                        r=[st, self.ident], w=[("ps", bk)])
                eng = "act" if c % 2 else "dve"
                dst = self.xT[:, c, blk * 512:(blk + 1) * 512]
                if eng == "act":
                    P.op("act", lambda e, dst=dst, ps=ps: e.copy(out=dst, in_=ps[:, :]), r=[("ps", bk)], w=[(self.xT, c, blk)])
                else:
                    P.op("dve", lambda e, dst=dst, ps=ps: e.tensor_copy(out=dst, in_=ps[:, :]), r=[("ps", bk)], w=[(self.xT, c, blk)])
        P.release(mk0)

    def xkeys(self, t0, n, cs=range(8)):
        return [(self.xT, c, b) for c in cs for b in range(t0 // 512, (t0 + n - 1) // 512 + 1)]

    def norm_cols(self, t0, n):
        P = self.P
        rt, rstd = self.n_rt, self.n_rstd
        bk = P.next_bank(); ps = P.bank(bk)
        for c in range(8):
            sqb = self.n_sqb[c % 2]
            P.op("act", lambda e, c=c, sqb=sqb: e.activation(out=sqb[:, 0:n], in_=self.xT[:, c, t0:t0 + n], func=AF.Square),
                 r=self.xkeys(t0, n, [c]), w=[sqb])
            P.op("pe", lambda e, c=c, ps=ps, sqb=sqb: e.matmul(ps[:, 0:n], lhsT=self.onesbf[:], rhs=sqb[:, 0:n], start=(c == 0), stop=(c == 7)),
                 r=[sqb, self.onesbf], w=[("ps", bk)])
        P.op("act", lambda e, ps=ps: e.activation(out=rt[:, 0:n], in_=ps[:, 0:n], func=AF.Sqrt, scale=1.0 / D, bias=self.epsT[:, 0:1]),
             r=[("ps", bk), self.epsT], w=[rt])
        P.op("dve", lambda e: e.reciprocal(out=rstd[:, 0:n], in_=rt[:, 0:n]), r=[rt], w=[rstd])
        return rstd

    def norm_mod(self, U, l, which, t0, n, dst):
        P = self.P
        v = U["v"]
        rstd = self.norm_cols(t0, n)
        gsc = self.gsc1 if which == 1 else self.gsc2
        shb = 0 if which == 1 else 24
        for c in range(8):
            tmp = self.n_tmp[c % 2]
            P.op("dve", lambda e, c=c, tmp=tmp: e.tensor_tensor(out=tmp[:, 0:n], in0=self.xT[:, c, t0:t0 + n], in1=rstd[:, 0:n], op=ALU.mult),
                 r=self.xkeys(t0, n, [c]) + [rstd], w=[tmp])
            d_ap, d_keys = dst(c)
            P.op("act", lambda e, c=c, tmp=tmp, d_ap=d_ap: e.activation(
                out=d_ap, in_=tmp[:, 0:n], func=AF.Identity, scale=gsc[:, l, c, v:v + 1], bias=self.modT[:, l, shb + c, v:v + 1]),
                r=[tmp, (gsc, l), (self.modT, l)], w=d_keys)

    def proj(self, wt, wcols, hT, n, kc_n=8, wk=None):
        P = self.P
        bk = P.next_bank(); ps = P.bank(bk)
        w0, w1 = wcols
        for kc in range(kc_n):
            P.op("pe", lambda e, kc=kc, ps=ps: e.matmul(ps[0:(w1 - w0), 0:n], lhsT=wt[:, kc, w0:w1], rhs=hT[:, kc, 0:n],
                                                          start=(kc == 0), stop=(kc == kc_n - 1)),
                 r=(wk if wk is not None else [wt]) + [hT], w=[("ps", bk)])
        return bk, ps

    def ssm_pass(self, U, l, ssmT):
        P, C, I = self.P, self.C, self.I
        T, NSEQ, L = U["T"], U["NSEQ"], U["L"]
        CT, Cq = T // 8, L // 8
        mk0 = P.mark()
        uT = P.alloc("uT", [2, T], BF16)
        mk1 = P.mark()
        wu = P.alloc("wu", [8, 256], BF16)
        hT = P.alloc("hT", [8, 512], BF16)
        P.dma("sp", wu[:], self.W["win"][l][:, :, 1536:1792], r=self.wkeys("win", l, 1536, 1792), w=[wu])
        for blk in range(T // 512):
            self.norm_mod(U, l, 1, blk * 512, 512, lambda c: (hT[:, c, :], [hT]))
            for oc in range(2):
                bk, ps = self.proj(wu, (oc * 128, oc * 128 + 128), hT, 512)
                P.op("dve", lambda e, oc=oc, blk=blk, ps=ps: e.tensor_copy(out=uT[:, oc, blk * 512:(blk + 1) * 512], in_=ps[:, :]),
                     r=[("ps", bk)], w=[(uT, oc, blk)])
        P.release(mk1)
        UK = [(uT, oc, b) for oc in range(2) for b in range(T // 512)]
        kt = P.alloc("kt", [16, 128], BF16); ws = P.alloc("ws", [2, 16, 128], BF16); wo = P.alloc("wo", [2, 8, 2, 128], BF16)
        xsel = P.alloc("xsel", [8, 240], BF16); ysel = P.alloc("ysel", [8, 128], BF16)
        P.dma("sp", kt[:], self.S_kt[l], r=[("S_kt", l)], w=[kt])
        P.dma("sp", ws[:], self.S_ws[l], r=[("S_ws", l, 0), ("S_ws", l, 1)], w=[ws])
        P.dma("sp", wo[:], self.S_wo[l], r=[("S_wo", l, 0), ("S_wo", l, 1)], w=[wo])
        P.dma("sp", xsel[:], C["xsel"], w=[xsel])
        P.dma("sp", ysel[:], C["ysel"], w=[ysel])
        X = P.alloc("X", [16, CT], BF16)
        for g in range(16):
            bk = P.next_bank(); ps = P.bank(bk)
            for s in range(8):
                rhs = mk(uT[:, g // 8, s:s + 1], [[8, CT]])
                P.op("pe", lambda e, g=g, s=s, rhs=rhs, ps=ps: e.matmul(
                    ps[:, 0:CT], lhsT=xsel[:, g % 8, (7 - s) * 16:(7 - s) * 16 + 128], rhs=rhs, start=(s == 0), stop=(s == 7)),
                    r=UK + [xsel], w=[("ps", bk)])
            P.op("act", lambda e, g=g, ps=ps: e.copy(out=X[:, g, :], in_=ps[:, 0:CT]), r=[("ps", bk)], w=[(X, g)])
        S1 = Cq + 1
        Hb = P.alloc("Hb", [2, 2, 8, NSEQ * S1], BF16)
        fin = P.alloc("fin", [NSEQ, 2, 2, 8], F32)
        mk2 = P.mark()
        Et = P.alloc("Et2", [2, 8, 256], F32)
        tt_ = [P.alloc(f"l2t{i}", [Cq], F32) for i in range(6)]
        h0 = P.alloc("h0", [2, 2, 8], F32)
        gi0 = P.alloc("gi0", [2, 2, 8], F32)
        hq = P.alloc("hq", [4, 8], F32)
        if U["lat"]:
            for d in range(2):
                for comp, nm in enumerate(("sre", "sim")):
                    for h in range(2):
                        src = I[nm][l, d]
                        P.dma("sp", h0[64 * h:64 * h + 64, d, comp, :], bass.AP(src.tensor, src.offset + h * 64, [[1, 64], [128, 8]]),
                              w=[h0], allow_slow_non_contiguous=True)
            for d in range(2):
                ld = l * 2 + d
                er, ei = self.e1[:, 0, ld, :], self.e1[:, 1, ld, :]
                hr, hi = h0[:, d, 0, :], h0[:, d, 1, :]
                ops = [(hq[:, 0, :], er, hr, ALU.mult), (hq[:, 1, :], ei, hi, ALU.mult), (gi0[:, d, 0, :], hq[:, 0, :], hq[:, 1, :], ALU.subtract),
                       (hq[:, 2, :], er, hi, ALU.mult), (hq[:, 3, :], ei, hr, ALU.mult), (gi0[:, d, 1, :], hq[:, 2, :], hq[:, 3, :], ALU.add)]
                for (o_, a_, b_, op_) in ops:
                    P.op("dve", lambda e, o_=o_, a_=a_, b_=b_, op_=op_: e.tensor_tensor(out=o_, in0=a_, in1=b_, op=op_),
                         r=[h0, hq, self.e1, gi0], w=[hq, gi0])
        else:
            P.op("dve", lambda e: e.memset(h0[:], 0.0), w=[h0])
            P.op("dve", lambda e: e.memset(gi0[:], 0.0), w=[gi0])
        for d in range(2):
            ld = l * 2 + d
            P.dma("sp", Et[:], self.S_e[l][:, d], r=[("S_e", l, d)], w=[Et])
            for gg in range(8):
                bk = P.next_bank(); ps = P.bank(bk)
                for h in range(2):
                    g = 2 * gg + h
                    for comp in range(2):
                        P.op("pe", lambda e, g=g, h=h, comp=comp, d=d, ps=ps: e.matmul(
                            ps[64 * h:64 * h + 64, comp * CT:(comp + 1) * CT], lhsT=ws[:, d, g, comp * 64:(comp + 1) * 64], rhs=X[:, g, :],
                            start=True, stop=True, tile_position=(0, 64 * h)), r=[ws, (X, g)], w=[("ps", bk)])
                for sq in range(NSEQ):
                    def seqview(base):
                        a = ps[:, base + sq * Cq: base + (sq + 1) * Cq]
                        if d == 0:
                            return a
                        return mk(ps[:, base + (sq + 1) * Cq - 1: base + (sq + 1) * Cq], [[-1, Cq]])
                    Sr, Si = seqview(0), seqview(CT)
                    Ec, Es = Et[:, 0, gg, 0:Cq], Et[:, 1, gg, 0:Cq]
                    t1, t2, t3, t4, t5, t6 = [t[:, 0:Cq] for t in tt_]
                    TK = lambda i: [tt_[i]]
                    def vop(o_, a_, b_, op_, r, w):
                        P.op("dve", lambda e: e.tensor_tensor(out=o_, in0=a_, in1=b_, op=op_), r=r, w=w)
                    vop(t1, Sr, Ec, ALU.mult, [("ps", bk), Et], TK(0))
                    vop(t2, Si, Es, ALU.mult, [("ps", bk), Et], TK(1))
                    vop(t5, t1, t2, ALU.add, TK(0) + TK(1), TK(4))
                    vop(t3, Si, Ec, ALU.mult, [("ps", bk), Et], TK(2))
                    vop(t4, Sr, Es, ALU.mult, [("ps", bk), Et], TK(3))
                    vop(t6, t3, t4, ALU.subtract, TK(2) + TK(3), TK(5))
                    rr = mk(self.a8mag[:, ld, gg:gg + 1], [[0, Cq]])
                    P.op("dve", lambda e, rr=rr, t5=t5, t1=t1, d=d, gg=gg: e.tensor_tensor_scan(
                        out=t1, data0=rr, data1=t5, initial=gi0[:, d, 0, gg:gg + 1], op0=ALU.mult, op1=ALU.add),
                        r=TK(4) + [self.a8mag, gi0], w=TK(0))
                    P.op("dve", lambda e, rr=rr, t6=t6, t2=t2, d=d, gg=gg: e.tensor_tensor_scan(
                        out=t2, data0=rr, data1=t6, initial=gi0[:, d, 1, gg:gg + 1], op0=ALU.mult, op1=ALU.add),
                        r=TK(5) + [self.a8mag, gi0], w=TK(1))
                    vop(t3, t1, Ec, ALU.mult, TK(0) + [Et], TK(2))
                    vop(t4, t2, Es, ALU.mult, TK(1) + [Et], TK(3))
                    vop(t5, t3, t4, ALU.subtract, TK(2) + TK(3), TK(4))
                    vop(t3, t2, Ec, ALU.mult, TK(1) + [Et], TK(2))
                    vop(t4, t1, Es, ALU.mult, TK(0) + [Et], TK(3))
                    vop(t6, t3, t4, ALU.add, TK(2) + TK(3), TK(5))
                    for comp, tH in ((0, t5), (1, t6)):
                        base = Hb[:, d, comp, gg, :]
                        if d == 0:
                            dstv = base[:, sq * S1 + 1: sq * S1 + 1 + Cq]
                            init_slot = base[:, sq * S1: sq * S1 + 1]
                        else:
                            dstv = mk(base[:, sq * S1 + Cq - 1: sq * S1 + Cq], [[-1, Cq]])
                            init_slot = base[:, sq * S1 + Cq: sq * S1 + Cq + 1]
                        P.op("act", lambda e, dstv=dstv, tH=tH: e.copy(out=dstv, in_=tH), r=[tt_[4 + comp]], w=[(Hb, d, gg)])
                        P.op("act", lambda e, init_slot=init_slot, d=d, comp=comp, gg=gg: e.copy(out=init_slot, in_=h0[:, d, comp, gg:gg + 1]),
                             r=[h0], w=[(Hb, d, gg)])
                        if not U["lat"]:
                            P.op("act", lambda e, sq=sq, d=d, comp=comp, gg=gg, tH=tH: e.copy(
                                out=fin[:, sq, d, comp, gg:gg + 1], in_=tH[:, Cq - 1:Cq]), r=[tt_[4 + comp]], w=[fin])
        if not U["lat"]:
            for sq in range(NSEQ):
                for d in range(2):
                    for comp, nm in enumerate(("nsr", "nsi")):
                        dst = self.O[nm][sq, l, d]
                        P.dma("sp", bass.AP(dst.tensor, dst.offset, [[1, 128], [128, 8]]), fin[:, sq, d, comp, :], r=[fin],
                              w=[("out", nm, sq, l, d)], allow_slow_non_contiguous=True)
        P.release(mk2)
        HK = [(Hb, d, gg) for d in range(2) for gg in range(8)]
        NB = T // 512
        zT = uT
        yexp = [P.alloc(f"yexp{i}", [T], BF16) for i in range(2)]
        wglu = P.alloc("wglu", [2, 256], BF16)
        sg = [P.alloc(f"sg{i}", [512], BF16) for i in range(2)]
        P.dma("sp", wglu[:], self.W["wglu"][l], r=self.wkeys("wglu", l, 0, 256), w=[wglu])
        acc = []
        for tb in range(NB):
            b = P.next_bank(); P.reserved_banks.add(b); acc.append(b)
        for chunk in range(2):
            for gi in range(8):
                g = chunk * 8 + gi
                h, gg = g % 2, g // 2
                bk = P.next_bank(); ps = P.bank(bk)
                P.op("pe", lambda e, g=g, ps=ps: e.matmul(ps[:, 0:CT], lhsT=kt[:, g, :], rhs=X[:, g, :], start=True, stop=False),
                     r=[kt, (X, g)], w=[("ps", bk)])
                k = 0
                for d in range(2):
                    for comp in range(2):
                        off = 0 if d == 0 else 1
                        rhs = mk(Hb[64 * h:64 * h + 64, d, comp, gg, off:off + 1], [[S1, NSEQ], [1, Cq]])
                        outv = ps[:, 0:CT].rearrange("p (a b) -> p a b", b=Cq)
                        k += 1
                        P.op("pe", lambda e, h=h, d=d, comp=comp, gg=gg, rhs=rhs, outv=outv, k=k: e.matmul(
                            outv, lhsT=wo[64 * h:64 * h + 64, d, gg, comp, :], rhs=rhs, start=False, stop=(k == 4)),
                            r=[wo] + HK, w=[("ps", bk)])
                ye = yexp[g % 2]
                in0 = mk(ps[:, 0:1], [[1, CT], [0, 8]])
                in1 = mk(self.maskj[:, 0:1], [[0, CT], [1, 8]])
                P.op("dve", lambda e, ye=ye, in0=in0, in1=in1: e.tensor_tensor(
                    out=ye[:].rearrange("p (a b) -> p a b", b=8), in0=in0, in1=in1, op=ALU.mult),
                    r=[("ps", bk), self.maskj], w=[ye])
                for tb in range(NB):
                    P.op("pe", lambda e, gi=gi, tb=tb, ye=ye: e.matmul(
                        P.bank(acc[tb])[:, :], lhsT=ysel[:, gi, :], rhs=ye[:, tb * 512:(tb + 1) * 512], start=(gi == 0), stop=(gi == 7)),
                        r=[ysel, ye], w=[("ps", acc[tb])])
            for tb in range(NB):
                P.op("act", lambda e, chunk=chunk, tb=tb: e.activation(
                    out=zT[:, chunk, tb * 512:(tb + 1) * 512], in_=P.bank(acc[tb])[:, :], func=AF.Gelu),
                    r=[("ps", acc[tb])], w=[(uT, chunk, tb)])
        for b in acc:
            P.reserved_banks.discard(b)
        for tb in range(NB):
            for oc in range(2):
                bk = P.next_bank(); ps = P.bank(bk)
                for kc in range(2):
                    P.op("pe", lambda e, kc=kc, oc=oc, tb=tb, ps=ps: e.matmul(
                        ps[:, :], lhsT=wglu[:, kc, oc * 128:(oc + 1) * 128], rhs=zT[:, kc, tb * 512:(tb + 1) * 512],
                        start=(kc == 0), stop=(kc == 1)), r=[wglu, (uT, kc, tb)], w=[("ps", bk)])
                s_ = sg[(tb * 2 + oc) % 2]
                P.op("act", lambda e, s_=s_, ps=ps: e.activation(out=s_[:], in_=ps[:, :], func=AF.Sigmoid), r=[("ps", bk)], w=[s_])
                P.op("dve", lambda e, s_=s_, oc=oc, tb=tb: e.tensor_tensor(
                    out=ssmT[:, oc, tb * 512:(tb + 1) * 512], in0=zT[:, oc, tb * 512:(tb + 1) * 512], in1=s_[:], op=ALU.mult),
                    r=[s_, (uT, oc, tb)], w=[(ssmT, oc, tb)])
        P.release(mk0)

    def kv_pass(self, U, l, krT, vaug, gbT, pT):
        P, C = self.P, self.C
        T, NSEQ, L, lat = U["T"], U["NSEQ"], U["L"], U["lat"]
        mk0 = P.mark()
        win = self.W["win"][l]
        hT = P.alloc("hT", [8, 512], BF16)
        wkd = P.alloc("wkd", [8, 2, 128], BF16)
        wkv = P.alloc("wkv", [8, 256], BF16)
        wg = P.alloc("wg", [8, 768], BF16)
        for kv in range(2):
            for hh in range(2):
                P.dma("sp", wkd[:, :, kv, hh * 64:(hh + 1) * 64], win[:, :, 512 + kv * 64:512 + (kv + 1) * 64],
                      r=self.wkeys("win", l, 512, 640), w=[wkd])
        P.dma("sp", wkv[:], win[:, :, 512:768], r=self.wkeys("win", l, 512, 768), w=[wkv])
        P.dma("sp", wg[:], win[:, :, 768:1536], r=self.wkeys("win", l, 768, 1536), w=[wg])
        if lat:
            wkp = P.alloc("wkp", [8, 2, 128], BF16)
            rope = P.alloc("rope", [2, 512], F32)
            r1 = P.alloc("r1", [512], F32); r2 = P.alloc("r2", [512], F32)
            for b_ in range(2):
                srcv = mk(wkd[:, 0, 0, 0:1], [[64, 32], [32, 2], [1, 16]], off=(1 - b_) * 16)
                dstv = mk(wkp[:, 0, 0, 0:1], [[64, 32], [32, 2], [1, 16]], off=b_ * 16)
                P.op("pool", lambda e, srcv=srcv, dstv=dstv: e.tensor_copy(out=dstv, in_=srcv), r=[wkd], w=[wkp])
        else:
            kvst = [P.alloc(f"kvst{i}", [256], F32) for i in range(2)]
        gct = P.alloc("gct", [512], F32)
        P.op("dve", lambda e: e.memset(vaug[:, :, :, 64:128], 1.0), w=[vaug])
        for blk in range(T // 512):
            t0 = blk * 512
            self.norm_mod(U, l, 1, t0, 512, lambda c: (hT[:, c, :], [hT]))
            if lat:
                P.dma("sp", rope[:], C["rope"][:, :, t0:t0 + 512], w=[rope])
            for kv in range(2):
                bk, ps = self.proj(wkd[:, :, kv, :], (0, 128), hT, 512, wk=[wkd])
                if lat:
                    bk2, ps2 = self.proj(wkp[:, :, kv, :], (0, 128), hT, 512, wk=[wkp])
                    P.op("dve", lambda e, ps=ps: e.tensor_tensor(out=r1[:], in0=ps[:, :], in1=rope[:, 0, :], op=ALU.mult), r=[("ps", bk), rope], w=[r1])
                    P.op("dve", lambda e, ps2=ps2: e.tensor_tensor(out=r2[:], in0=ps2[:, :], in1=rope[:, 1, :], op=ALU.mult), r=[("ps", bk2), rope], w=[r2])
                    P.op("dve", lambda e, kv=kv, t0=t0: e.tensor_tensor(out=krT[:, kv, t0:t0 + 512], in0=r1[:], in1=r2[:], op=ALU.add),
                         r=[r1, r2], w=[(krT, kv, blk)])
                else:
                    P.op("act", lambda e, kv=kv, t0=t0, ps=ps: e.copy(out=krT[:, kv, t0:t0 + 512], in_=ps[:, :]), r=[("ps", bk)], w=[(krT, kv, blk)])
            for i in range(4):
                if self.cut == 12:
                    break
                tile_i = blk * 4 + i
                bk = P.next_bank(); ps = P.bank(bk)
                c0 = 128 if lat else 0
                for kc in range(8):
                    P.op("pe", lambda e, kc=kc, i=i, ps=ps, c0=c0: e.matmul(ps[:, c0:256], lhsT=hT[:, kc, i * 128:(i + 1) * 128], rhs=wkv[:, kc, c0:256],
                                                                              start=(kc == 0), stop=(kc == 7)), r=[hT, wkv], w=[("ps", bk)])
                P.op("dve", lambda e, tile_i=tile_i, ps=ps: e.tensor_copy(out=vaug[:, tile_i, :, 0:64], in_=ps[:, 128:256].rearrange("p (a b) -> p a b", b=64)),
                     r=[("ps", bk)], w=[(vaug, tile_i)])
                if not lat and self.cut != 15:
                    st = kvst[tile_i % 2]
                    P.op("act", lambda e, st=st, ps=ps: e.copy(out=st[:], in_=ps[:, 0:256]), r=[("ps", bk)], w=[st])
                    sq, tl = divmod(tile_i, L // 128)
                    P.dma("sp", self.O["nk"][sq, l, tl * 128:(tl + 1) * 128, :], st[:, 0:128], r=[st], w=[("out", "nk", tile_i, l)])
                    P.dma("sp", self.O["nv"][sq, l, tl * 128:(tl + 1) * 128, :], st[:, 128:256], r=[st], w=[("out", "nv", tile_i, l)])
            for oc in range(6):
                if self.cut in (12, 13):
                    break
                bk, ps = self.proj(wg, (oc * 128, oc * 128 + 128), hT, 512)
                which, c = divmod(oc, 2)
                if which == 0:
                    P.op("act", lambda e, c=c, t0=t0, ps=ps: e.copy(out=gbT[:, c, t0:t0 + 512], in_=ps[:, :]), r=[("ps", bk)], w=[(gbT, c, blk)])
                elif which == 1:
                    P.op("act", lambda e, c=c, t0=t0, ps=ps: e.copy(out=pT[:, c, t0:t0 + 512], in_=ps[:, :]), r=[("ps", bk)], w=[(pT, c, blk)])
                else:
                    P.op("dve", lambda e, c=c, t0=t0, ps=ps: e.tensor_tensor(out=pT[:, c, t0:t0 + 512], in0=ps[:, :], in1=pT[:, c, t0:t0 + 512], op=ALU.mult),
                         r=[("ps", bk), (pT, c, blk)], w=[(pT, c, blk)])
        P.release(mk0)

    def mix_pass(self, U, l, krT, vaug, gbT, pT, ssmT):
        P, C, I = self.P, self.C, self.I
        T, NSEQ, L, lat, v = U["T"], U["NSEQ"], U["L"], U["lat"], U["v"]
        mk0 = P.mark()
        win = self.W["win"][l]
        hT = P.alloc("hT", [8, 512], BF16)
        wq = P.alloc("wq", [8, 512], BF16)
        wo_ = P.alloc("wo_", [8, D], BF16)
        qT = P.alloc("qT", [4, 512], BF16)
        atT = P.alloc("atT", [4, 512], BF16)
        cvT = P.alloc("cvT", [2, 512], BF16)
        cacc = P.alloc("cacc", [512], F32)
        PT = [P.alloc(f"PT{i}", [512], BF16) for i in range(3)]
        Rt = P.alloc("Rt", [512], F32)
        P.dma("sp", wq[:], win[:, :, 0:512], r=self.wkeys("win", l, 0, 512), w=[wq])
        P.dma("sp", wo_[:], self.W["wout"][l], r=self.wkeys("wout", l, 0, D), w=[wo_])
        if lat:
            wqp = P.alloc("wqp", [8, 512], BF16)
            rope = P.alloc("rope", [2, 512], F32)
            r1 = P.alloc("r1", [512], F32); r2 = P.alloc("r2", [512], F32)
            maskb = P.alloc("maskb", [2, 512], BF16)
            ckd = P.alloc("ckd", [2, PAST], BF16)
            cva = P.alloc("cva", [4, 2, 128], BF16)
            cst = P.alloc("cst", [4, 2, 64], F32)
            for b_ in range(2):
                srcv = mk(wq[:, 0, 0:1], [[64, 64], [32, 2], [1, 16]], off=(1 - b_) * 16)
                dstv = mk(wqp[:, 0, 0:1], [[64, 64], [32, 2], [1, 16]], off=b_ * 16)
                P.op("pool", lambda e, srcv=srcv, dstv=dstv: e.tensor_copy(out=dstv, in_=srcv), r=[wq], w=[wqp])
            P.dma("sp", maskb[:], C["maskb"], w=[maskb])
            m01 = P.alloc("m01", [2, 256], BF16)
            P.op("dve", lambda e: e.tensor_single_scalar(out=m01[:], in_=maskb[:, :, 0:256], scalar=0.0, op=ALU.is_equal), r=[maskb], w=[m01])
            P.op("dve", lambda e: e.memset(cva[:, :, :, 64:128], 1.0), w=[cva])
            P.dma("sp", cst[:], I["cv"][l].rearrange("(i p) (k d) -> p i k d", p=128, d=64), w=[cst])
            P.op("dve", lambda e: e.tensor_copy(out=cva[:, :, :, 0:64], in_=cst[:]), r=[cst], w=[cva])
            for kv in range(2):
                for hh in range(2):
                    P.dma("sp", cst[:, :, hh, :], I["ck"][l][:, kv * 64:(kv + 1) * 64].rearrange("(i p) d -> p i d", p=128), r=[cva], w=[cst])
                bk = P.next_bank(); ps = P.bank(bk)
                for i in range(4):
                    P.op("pe", lambda e, i=i, ps=ps: e.transpose(out=ps[:, i * 128:(i + 1) * 128], in_=cst[:, i, :, :].rearrange("p a b -> p (a b)"),
                                                                 identity=self.ident[:]), r=[cst, self.ident], w=[("ps", bk)])
                P.op("act", lambda e, kv=kv, ps=ps: e.copy(out=ckd[:, kv, :], in_=ps[:, :]), r=[("ps", bk)], w=[ckd])
        po_banks = []
        for i in range(2):
            b = P.next_bank(); P.reserved_banks.add(b); po_banks.append(b)
        npo = 0
        TPS = L // 128
        for blk in range(T // 512):
            t0 = blk * 512
            self.norm_mod(U, l, 1, t0, 512, lambda c: (hT[:, c, :], [hT]))
            if lat:
                P.dma("sp", rope[:], C["rope"][:, :, t0:t0 + 512], w=[rope])
            for hc in range(4):
                bk, ps = self.proj(wq, (hc * 128, hc * 128 + 128), hT, 512)
                if lat:
                    bk2, ps2 = self.proj(wqp, (hc * 128, hc * 128 + 128), hT, 512)
                    P.op("dve", lambda e, ps=ps: e.tensor_tensor(out=r1[:], in0=ps[:, :], in1=rope[:, 0, :], op=ALU.mult), r=[("ps", bk), rope], w=[r1])
                    P.op("dve", lambda e, ps2=ps2: e.tensor_tensor(out=r2[:], in0=ps2[:, :], in1=rope[:, 1, :], op=ALU.mult), r=[("ps", bk2), rope], w=[r2])
                    P.op("dve", lambda e, hc=hc: e.tensor_tensor(out=qT[:, hc, :], in0=r1[:], in1=r2[:], op=ALU.add), r=[r1, r2], w=[(qT, hc)])
                else:
                    P.op("act", lambda e, hc=hc, ps=ps: e.copy(out=qT[:, hc, :], in_=ps[:, :]), r=[("ps", bk)], w=[(qT, hc)])
            if lat:
                pieces = [(t0, t0 + 512, t0 > 0, t0 + 512 < T)]
            else:
                pieces = [(t0 + i * L, t0 + (i + 1) * L, False, False) for i in range(512 // L)]
            for c in range(2):
                for (a0, a1, hl, hr) in pieces:
                    n = a1 - a0
                    o0 = a0 - t0
                    pk = [(pT, c, b) for b in range(max(0, blk - 1), min(T // 512, blk + 2))]
                    P.op("dve", lambda e, c=c, a0=a0, a1=a1, o0=o0, n=n: e.tensor_scalar_mul(
                        out=cacc[:, o0:o0 + n], in0=pT[:, c, a0:a1], scalar1=self.scw[:, l, c, 1:2]), r=pk + [self.scw], w=[cacc])
                    a = 0 if hl else 1
                    P.op("dve", lambda e, c=c, a0=a0, a1=a1, o0=o0, n=n, a=a: e.scalar_tensor_tensor(
                        out=cacc[:, o0 + a:o0 + n], in0=pT[:, c, a0 + a - 1:a1 - 1], scalar=self.scw[:, l, c, 0:1],
                        in1=cacc[:, o0 + a:o0 + n], op0=ALU.mult, op1=ALU.add), r=pk + [self.scw, cacc], w=[cacc])
                    b_ = 0 if hr else 1
                    P.op("dve", lambda e, c=c, a0=a0, a1=a1, o0=o0, n=n, b_=b_: e.scalar_tensor_tensor(
                        out=cacc[:, o0:o0 + n - b_], in0=pT[:, c, a0 + 1:a1 + 1 - b_], scalar=self.scw[:, l, c, 2:3],
                        in1=cacc[:, o0:o0 + n - b_], op0=ALU.mult, op1=ALU.add), r=pk + [self.scw, cacc], w=[cacc])
                P.op("dve", lambda e, c=c, t0=t0: e.tensor_tensor(out=cvT[:, c, :], in0=cacc[:], in1=gbT[:, c, t0:t0 + 512], op=ALU.mult),
                     r=[cacc, (gbT, c, blk)], w=[(cvT, c)])
            for qi in range(4):
                qt = blk * 4 + qi
                sq, ql = divmod(qt, TPS)
                for kv in range(2):
                    srcs = []
                    if lat:
                        for kt_, m in ((ql - 1, 0), (ql, None), (ql + 1, 1)):
                            if 0 <= kt_ < TPS:
                                srcs.append((krT[:, kv, kt_ * 128:(kt_ + 1) * 128], [(krT, kv, kt_ // 4)], vaug[:, kt_, kv, :], [(vaug, kt_)], m))
                        for i in range(4):
                            srcs.append((ckd[:, kv, i * 128:(i + 1) * 128], [ckd], cva[:, i, kv, :], [cva], None))
                    else:
                        for kt_ in range(TPS):
                            gt = sq * TPS + kt_
                            srcs.append((krT[:, kv, gt * 128:(gt + 1) * 128], [(krT, kv, gt // 4)], vaug[:, gt, kv, :], [(vaug, gt)], None))
                    pob = po_banks[npo % 2]; npo += 1
                    po = P.bank(pob)
                    ns = len(srcs)

                    def S(i):
                        kT_ap, kkeys, _, _, m = srcs[i]
                        pt = PT[i % 3]
                        for hh in range(2):
                            bk = P.next_bank(); ps = P.bank(bk)
                            for j in range(2):
                                hq = 2 * j + hh
                                h = kv * 4 + hq
                                P.op("pe", lambda e, j=j, h=h, hh=hh, ps=ps, kT_ap=kT_ap: e.matmul(
                                    ps[:, j * 128:(j + 1) * 128], lhsT=kT_ap[64 * hh:64 * hh + 64, :],
                                    rhs=qT[64 * hh:64 * hh + 64, h // 2, qi * 128:(qi + 1) * 128], start=True, stop=True),
                                    r=kkeys + [(qT, h // 2)], w=[("ps", bk)])
                            P.op("act", lambda e, pt=pt, ps=ps, hh=hh: e.activation(out=pt[:, hh * 256:(hh + 1) * 256], in_=ps[:, 0:256], func=AF.Exp, scale=0.125),
                                 r=[("ps", bk)], w=[pt])
                            if m is not None:
                                P.op("pool", lambda e, pt=pt, hh=hh, m=m: e.tensor_tensor(out=pt[:, hh * 256:(hh + 1) * 256], in0=pt[:, hh * 256:(hh + 1) * 256],
                                                                                           in1=m01[:, m, :], op=ALU.mult), r=[pt, m01], w=[pt])

                    def PV(i):
                        _, _, v_ap, vkeys, _ = srcs[i]
                        pt = PT[i % 3]
                        P.op("pe", lambda e, v_ap=v_ap, pt=pt, i=i: e.matmul(po[:, :], lhsT=v_ap, rhs=pt[:], start=(i == 0), stop=False),
                             r=vkeys + [pt], w=[("ps", pob)])
                    S(0)
                    for i in range(ns):
                        if i + 1 < ns:
                            S(i + 1)
                        PV(i)
                    es_rhs = mk(self.esrow[0:1, l, kv * 4, 0:1], [[128, 2], [256, 2], [1, 128]])
                    P.op("pe", lambda e, es_rhs=es_rhs: e.matmul(po[:, :].rearrange("p (a b c) -> p a b c", a=2, b=2), lhsT=self.vsink[0:1, :], rhs=es_rhs,
                                                                  start=False, stop=True), r=[self.vsink, self.esrow], w=[("ps", pob)])
                    P.op("dve", lambda e: e.reciprocal(out=Rt[0:64, :], in_=po[64:128, :]), r=[("ps", pob)], w=[Rt])
                    for par in range(2):
                        in0 = po[0:64, par * 256:(par + 1) * 256].rearrange("p (a b) -> p a b", b=128)
                        in1 = Rt[0:64, par * 256:(par + 1) * 256].rearrange("p (a b) -> p a b", b=128)
                        outv = atT[64 * par:64 * par + 64, kv * 2:kv * 2 + 2, qi * 128:(qi + 1) * 128]
                        P.op("dve", lambda e, in0=in0, in1=in1, outv=outv: e.tensor_tensor(out=outv, in0=in0, in1=in1, op=ALU.mult),
                             r=[("ps", pob), Rt], w=[(atT, kv)])
            rhs_list = [(atT[:, i, :], [(atT, 0), (atT, 1)]) for i in range(4)] + [(cvT[:, i, :], [(cvT, i)]) for i in range(2)] + \
                       [(ssmT[:, i, t0:t0 + 512], [(ssmT, i, blk)]) for i in range(2)]
            for oc in range(8):
                bk = P.next_bank(); ps = P.bank(bk)
                for kc, (rap, rkeys) in enumerate(rhs_list):
                    P.op("pe", lambda e, kc=kc, oc=oc, rap=rap, ps=ps: e.matmul(ps[:, :], lhsT=wo_[:, kc, oc * 128:(oc + 1) * 128], rhs=rap,
                                                                                  start=(kc == 0), stop=(kc == 7)), r=[wo_] + rkeys, w=[("ps", bk)])
                P.op("dve", lambda e, oc=oc, t0=t0, ps=ps: e.scalar_tensor_tensor(
                    out=self.xT[:, oc, t0:t0 + 512], in0=ps[:, :], scalar=self.modT[:, l, 16 + oc, v:v + 1], in1=self.xT[:, oc, t0:t0 + 512],
                    op0=ALU.mult, op1=ALU.add), r=[("ps", bk), (self.modT, l), (self.xT, oc, blk)], w=[(self.xT, oc, blk)])
        for b in po_banks:
            P.reserved_banks.discard(b)
        P.release(mk0)

    def ffn_pass(self, U, l):
        P = self.P
        T, NSEQ, L, lat, v = U["T"], U["NSEQ"], U["L"], U["lat"], U["v"]
        mk0 = P.mark()
        wdn = P.alloc("wdn", [22, D], BF16)
        for c in range(22):
            P.dma("sp", wdn[:, c, :], self.W["wdn"][l][:, c, :], r=self.wkeys("wdn", l, 0, D, [c]), w=[(wdn, c)])
        h2T = P.alloc("h2T", [8, 2, 258], BF16)
        halo = P.alloc("halo", [8, 2], BF16)
        gated = P.alloc("gated", [22, 2, 256], BF16)
        wus = [P.alloc(f"wus{i}", [8, 256], BF16) for i in range(3)]
        ca = [P.alloc(f"ca{i}", [256], F32) for i in range(2)]
        cg = [P.alloc(f"cg{i}", [256], F32) for i in range(2)]
        sgt = [P.alloc(f"sgt{i}", [256], F32) for i in range(2)]
        pieces = []
        for t0 in range(0, T, 256):
            sq_start = (t0 % L) == 0
            sq_end = ((t0 + 256) % L) == 0
            pieces.append((t0, t0 + 256, not sq_start, not sq_end))
        nwu = 0
        for sb in range(len(pieces) // 2):
            pcs = pieces[2 * sb:2 * sb + 2]
            for pi, (a0, a1, hl, hr) in enumerate(pcs):
                if pi == 0 and hl:
                    n = (a1 - a0) + int(hr)
                    P.op("pool", lambda e: e.tensor_copy(out=h2T[:, :, 0, 0:1], in_=halo[:, :, 0:1]), r=[halo], w=[(h2T, 0)])
                    self.norm_mod(U, l, 2, a0, n, lambda c, n=n: (h2T[:, c, 0, 1:1 + n], [(h2T, 0)]))
                else:
                    n = (a1 - a0) + int(hl) + int(hr)
                    self.norm_mod(U, l, 2, a0 - int(hl), n, lambda c, pi=pi, n=n: (h2T[:, c, pi, 0:n], [(h2T, pi)]))
            a0_, a1_, hl_l, hr_l = pcs[1]
            lastcol = int(hl_l) + (a1_ - a0_) - 1
            P.op("pool", lambda e, lastcol=lastcol: e.tensor_copy(out=halo[:, :, 0:1], in_=h2T[:, :, 1, lastcol:lastcol + 1]), r=[(h2T, 1)], w=[halo])
            for c in range(22):
                wu = wus[nwu % 3]; nwu += 1
                P.dma("sp", wu[:, :, 0:128], self.W["wup"][l][:, :, c * 128:(c + 1) * 128], r=self.wkeys("wup", l, c * 128, (c + 1) * 128), w=[wu])
                P.dma("sp", wu[:, :, 128:256], self.W["wup"][l][:, :, DFF + c * 128:DFF + (c + 1) * 128],
                      r=self.wkeys("wup", l, DFF + c * 128, DFF + (c + 1) * 128), w=[wu])
                for pi, (a0, a1, hl, hr) in enumerate(pcs):
                    m = a1 - a0
                    hl_, hr_ = int(hl), int(hr)
                    n = m + hl_ + hr_
                    res = []
                    for half, (acc_t, ch) in enumerate(((ca[pi], c), (cg[pi], 22 + c))):
                        bk = P.next_bank(); ps = P.bank(bk)
                        for kc in range(8):
                            P.op("pe", lambda e, kc=kc, half=half, ps=ps, pi=pi, n=n: e.matmul(
                                ps[:, 0:n], lhsT=wu[:, kc, half * 128:(half + 1) * 128], rhs=h2T[:, kc, pi, 0:n], start=(kc == 0), stop=(kc == 7)),
                                r=[wu, (h2T, pi)], w=[("ps", bk)])
                        w_ = self.fcw[:, l, ch, :]
                        P.op("act", lambda e, acc_t=acc_t, ps=ps, w_=w_: e.activation(
                            out=acc_t[:, 0:m], in_=ps[:, hl_:hl_ + m], func=AF.Identity, scale=w_[:, 1:2]), r=[("ps", bk), self.fcw], w=[acc_t])
                        a = 0 if hl else 1
                        P.op("dve", lambda e, acc_t=acc_t, ps=ps, w_=w_, a=a: e.scalar_tensor_tensor(
                            out=acc_t[:, a:m], in0=ps[:, hl_ + a - 1:hl_ + m - 1], scalar=w_[:, 0:1], in1=acc_t[:, a:m], op0=ALU.mult, op1=ALU.add),
                            r=[("ps", bk), self.fcw, acc_t], w=[acc_t])
                        b_ = 0 if hr else 1
                        P.op("dve", lambda e, acc_t=acc_t, ps=ps, w_=w_, b_=b_: e.scalar_tensor_tensor(
                            out=acc_t[:, 0:m - b_], in0=ps[:, hl_ + 1:hl_ + m + 1 - b_], scalar=w_[:, 2:3], in1=acc_t[:, 0:m - b_], op0=ALU.mult, op1=ALU.add),
                            r=[("ps", bk), self.fcw, acc_t], w=[acc_t])
                    sg_ = sgt[pi]
                    P.op("act", lambda e, sg_=sg_, pi=pi: e.activation(out=sg_[:, 0:m], in_=cg[pi][:, 0:m], func=AF.Silu), r=[cg[pi]], w=[sg_])
                    P.op("dve", lambda e, sg_=sg_, pi=pi, c=c: e.tensor_tensor(out=gated[:, c, pi, 0:m], in0=ca[pi][:, 0:m], in1=sg_[:, 0:m], op=ALU.mult),
                         r=[ca[pi], sg_], w=[(gated, c, pi)])
            for pi, (a0, a1, hl, hr) in enumerate(pcs):
                m = a1 - a0
                for oc in range(8):
                    bk = P.next_bank(); ps = P.bank(bk)
                    for c in range(22):
                        P.op("pe", lambda e, c=c, oc=oc, pi=pi, ps=ps: e.matmul(ps[:, 0:m], lhsT=wdn[:, c, oc * 128:(oc + 1) * 128], rhs=gated[:, c, pi, 0:m],
                                                                                  start=(c == 0), stop=(c == 21)), r=[(wdn, c), (gated, c, pi)], w=[("ps", bk)])
                    P.op("dve", lambda e, oc=oc, a0=a0, a1=a1, ps=ps: e.scalar_tensor_tensor(
                        out=self.xT[:, oc, a0:a1], in0=ps[:, 0:m], scalar=self.modT[:, l, 40 + oc, v:v + 1], in1=self.xT[:, oc, a0:a1],
                        op0=ALU.mult, op1=ALU.add), r=[("ps", bk), (self.modT, l)] + self.xkeys(a0, m, [oc]), w=self.xkeys(a0, m, [oc]))
        P.release(mk0)

    def final_pass(self, U):
        P = self.P
        T = U["T"]
        mk0 = P.mark()
        yT = P.alloc("yT", [8, 512], F32)
        yst = [P.alloc(f"yst{i}", [D], F32) for i in range(2)]
        nst = 0
        for blk in range(T // 512):
            t0 = blk * 512
            rstd = self.norm_cols(t0, 512)
            for c in range(8):
                tmp = self.n_tmp[c % 2]
                P.op("dve", lambda e, c=c, tmp=tmp, t0=t0: e.tensor_tensor(out=tmp[:], in0=self.xT[:, c, t0:t0 + 512], in1=rstd[:], op=ALU.mult),
                     r=self.xkeys(t0, 512, [c]) + [rstd], w=[tmp])
                P.op("act", lambda e, c=c, tmp=tmp: e.activation(out=yT[:, c, :], in_=tmp[:], func=AF.Identity, scale=self.nfT[:, c:c + 1]),
                     r=[tmp, self.nfT], w=[(yT, c)])
            for i in range(4):
                st = yst[nst % 2]; nst += 1
                for hf in range(2):
                    bk = P.next_bank(); ps = P.bank(bk)
                    for cc in range(4):
                        c = hf * 4 + cc
                        P.op("pe", lambda e, c=c, cc=cc, i=i, ps=ps: e.transpose(out=ps[:, cc * 128:(cc + 1) * 128], in_=yT[:, c, i * 128:(i + 1) * 128],
                                                                                 identity=self.ident[:]), r=[(yT, c), self.ident], w=[("ps", bk)])
                    if hf == 0:
                        P.op("act", lambda e, st=st, ps=ps: e.copy(out=st[:, 0:512], in_=ps[:, :]), r=[("ps", bk)], w=[(st, 0)])
                    else:
                        P.op("dve", lambda e, st=st, ps=ps: e.tensor_copy(out=st[:, 512:1024], in_=ps[:, :]), r=[("ps", bk)], w=[(st, 1)])
                P.dma("sp", U["y"][t0 + i * 128:t0 + (i + 1) * 128, :], st[:], r=[(st, 0), (st, 1)], w=[("out", "y", U["name"], blk, i)])
        P.release(mk0)

    def run_unit(self, U, layers=(0, 1), passes=("ssm", "kv", "mix", "ffn", "final")):
        P = self.P
        T = U["T"]
        mk0 = P.mark()
        self.n_sqb = [P.alloc(f"n_sqb{i}", [512], BF16) for i in range(2)]
        self.n_rt = P.alloc("n_rt", [512], F32)
        self.n_rstd = P.alloc("n_rstd", [512], F32)
        self.n_tmp = [P.alloc(f"n_tmp{i}", [512], F32) for i in range(2)]
        self.load_x(U)
        for l in layers:
            self.prep_adaln(l)
            mk1 = P.mark()
            ssmT = P.alloc("ssmT", [2, T], BF16)
            if "ssm" in passes:
                self.ssm_pass(U, l, ssmT)
            if "kv" in passes:
                krT = P.alloc("krT", [2, T], BF16)
                vaug = P.alloc("vaug", [T // 128, 2, 128], BF16)
                gbT = P.alloc("gbT", [2, T], BF16)
                pT = P.alloc("pT", [2, T], F32 if False else BF16)
                self.kv_pass(U, l, krT, vaug, gbT, pT)
                if "mix" in passes:
                    self.mix_pass(U, l, krT, vaug, gbT, pT, ssmT)
            P.release(mk1)
            if "ffn" in passes:
                self.ffn_pass(U, l)
        if "final" in passes:
            self.final_pass(U)
        P.release(mk0)

    def build(self):
        P = self.P
        self.xT = P.alloc("xT", [8, 2048], F32)
        self.epsT = P.alloc("epsT", [1], F32)
        P.op("dve", lambda e: e.memset(self.epsT[:], EPS), w=[self.epsT])
        self.persistent()
        self.prep_adaln_setup()
        if "prep" in self.stages or "adaln" in self.stages:
            self.prep_adaln(0)
        self.cast_mark = P.mark()
        bufs_pool = self.alloc_cast_bufs("p")
        mk_a = P.mark()
        bufs_act = self.alloc_cast_bufs("a")
        if "prep" in self.stages or "ssmt" in self.stages:
            self.prep_ssm()
        if "prep" in self.stages or "casts" in self.stages:
            self.prep_casts([0], "act", bufs_act)
        P.release(mk_a)
        if "prep" in self.stages or "casts" in self.stages:
            self.prep_casts([1], "pool", bufs_pool, first_dep=["ssm_done"] if ("prep" in self.stages or "ssmt" in self.stages) else [])
        UP = dict(name="P", T=512, NSEQ=2, L=256, v=0, lat=False, x=self.I["xp"], y=self.O["yp"])
        US = dict(name="S", T=2048, NSEQ=1, L=2048, v=1, lat=True, x=self.I["xs"], y=self.O["ys"])
        if "P" in self.stages:
            self.run_unit(UP, **self.unit_kw.get("P", {}))
        if self.cast_mark is not None:
            P.release(self.cast_mark)
        if "S" in self.stages:
            self.run_unit(US, **self.unit_kw.get("S", {}))
        if self.post is not None:
            self.post(self)
        P.emit()
        self.es.close()
        return self.nc

    unit_kw = {}
    cast_mark = None
    cast_only = None
    post = None
    cut = 0


def make_in_maps(inputs, consts, B=None):
    f = lambda a: np.ascontiguousarray(np.asarray(a, dtype=np.float32))
    xp = f(inputs["x_prompt"]); xs = f(inputs["x_sample"])
    maps = []
    for c in range(8):
        b = c // 4
        m = {
            "xp": xp[2 * c:2 * c + 2].reshape(512, D),
            "xs": xs[b],
            "ck": f(inputs["cache_k"])[b].reshape(2, PAST, 128),
            "cv": f(inputs["cache_v"])[b].reshape(2, PAST, 128),
            "sre": f(inputs["state_ssm_re"])[b],
            "sim": f(inputs["state_ssm_im"])[b],
            "cvec": np.stack([f(inputs["c_ctx"]), f(inputs["c"])[b]], 0),
        }
        for name, _ in IN_SPECS[7:]:
            m[name] = f(inputs[name])
        for k, a in consts.items():
            m["c_" + k] = a
        if B is not None:
            used = set(B.I.keys()) | set("c_" + k for k in B.C.keys())
            m = {k: a for k, a in m.items() if k in used}
        maps.append({k: np.ascontiguousarray(a) for k, a in m.items()})
    return maps


_CACHE = {}


def kernel(**inputs):
    consts = make_consts()
    if "nc" not in _CACHE:
        B = Builder()
        _CACHE["nc"] = B.build()
        _CACHE["B"] = B
    nc = _CACHE["nc"]
    in_maps = make_in_maps(inputs, consts, _CACHE["B"])
    res = run_bass_kernel_spmd(nc, in_maps, core_ids=list(range(8)))
    R = res.results
    y_prompt = np.concatenate([R[c]["yp"].reshape(2, 256, D) for c in range(8)], 0)
    y_sample = np.stack([R[0]["ys"], R[4]["ys"]], 0)
    nk = np.concatenate([R[c]["nk"].reshape(2, 2, 256, 2, 64) for c in range(8)], 0)
    nv = np.concatenate([R[c]["nv"].reshape(2, 2, 256, 2, 64) for c in range(8)], 0)
    nsr = np.concatenate([R[c]["nsr"] for c in range(8)], 0)
    nsi = np.concatenate([R[c]["nsi"] for c in range(8)], 0)
    return (y_prompt.astype(np.float32), y_sample.astype(np.float32), nk.astype(np.float32), nv.astype(np.float32),
            nsr.astype(np.float32), nsi.astype(np.float32))
```

```python
import math
from contextlib import ExitStack

import numpy as np
import ml_dtypes

import concourse.bass as bass
import concourse.mybir as mybir
from concourse.bass_utils import run_bass_kernel_spmd

F32 = mybir.dt.float32
BF16 = mybir.dt.bfloat16
U8 = mybir.dt.uint8
I32 = mybir.dt.int32
ALU = mybir.AluOpType
AF = mybir.ActivationFunctionType
AX = mybir.AxisListType

D = 1024
DEPTH = 2
NQH = 8
HD = 64
DFF = 2816
IN_DIM = 1792
PAST = 512
EPS = 1e-6
NEG = -30000.0
TWO_PI = 2.0 * math.pi

ARENA_BYTES = 207 * 1024
CASTW = 1024


class Tile:
    def __init__(self, name, ap, lo, hi):
        self.name, self.ap, self.lo, self.hi = name, ap, lo, hi

    def __getitem__(self, k):
        return self.ap[k]


class Op:
    __slots__ = ("eng", "calls", "waits", "sem", "val", "isdma")


class _Rec:
    def __init__(self):
        self.calls = []

    def __getattr__(self, name):
        def f(*a, **k):
            self.calls.append((name, a, k))
            return None
        return f


class Prog:
    ENG = ("pe", "act", "dve", "pool", "sp")
    CAP_C = 30000
    CAP_D = 1800

    def __init__(self, nc, es):
        self.nc, self.es = nc, es
        self.ops = {e: [] for e in self.ENG}
        self.state = {}
        self.seen = {e: {} for e in self.ENG}
        self.sems = {}
        self.cnt = {}
        self.tile_init = {}
        self.tiles_live = []
        self.freed = []
        self.tile_keys = {}
        self.arena = es.enter_context(nc.sbuf_tensor("arena", [128, ARENA_BYTES], U8))
        self.top = 0
        self.nsem = 0
        self.bank_rr = 0
        self.reserved_banks = set()
        self.psum = es.enter_context(nc.psum_tensor("psum", [128, 8 * 512], F32))
        self.n_ops = 0
        self.final = {}

    def alloc(self, name, shape, dtype):
        esz = 4 if dtype in (F32, I32) else 2
        n = int(np.prod(shape)) * esz
        n = (n + 63) // 64 * 64
        lo = self.top
        hi = lo + n
        assert hi <= ARENA_BYTES, f"arena overflow allocating {name}: {hi}"
        self.top = hi
        ap = self.arena[:, lo:hi].bitcast(dtype)
        used = int(np.prod(shape))
        ap = ap[:, 0:used]
        if len(shape) == 2:
            ap = ap.rearrange("p (a b) -> p a b", b=shape[1])
        elif len(shape) == 3:
            ap = ap.rearrange("p (a b c) -> p a b c", b=shape[1], c=shape[2])
        elif len(shape) == 4:
            ap = ap.rearrange("p (a b c d) -> p a b c d", b=shape[1], c=shape[2], d=shape[3])
        name = f"{name}#{len(self.tile_keys)}"
        t = Tile(name, ap, lo, hi)
        inh = []
        for (flo, fhi, toks) in self.freed:
            if flo < hi and lo < fhi:
                inh.extend(toks)
        self.tile_init[name] = inh
        self.tile_keys[name] = set()
        self.tiles_live.append(t)
        return t

    def mark(self):
        return (self.top, len(self.tiles_live))

    def release(self, mk):
        top, nlive = mk
        for t in self.tiles_live[nlive:]:
            toks = list(self.tile_init[t.name])
            for k in self.tile_keys[t.name]:
                st = self.state.get(k)
                if st:
                    toks.extend(st[0].items()); toks.extend(st[1].items())
            best = {}
            for (s, v) in toks:
                if v > best.get(s, -1):
                    best[s] = v
            self.freed.append((t.lo, t.hi, list(best.items())))
        del self.tiles_live[nlive:]
        self.top = top

    def bank(self, i):
        return self.psum[:, i * 512:(i + 1) * 512]

    def next_bank(self):
        while True:
            b = self.bank_rr % 8
            self.bank_rr += 1
            if b not in self.reserved_banks:
                return b

    def _key(self, k):
        assert isinstance(k, (Tile, tuple, str)), f"bad dependency key {type(k)}"
        if isinstance(k, Tile):
            k = (k.name, None)
        elif isinstance(k, tuple) and isinstance(k[0], Tile):
            k = (k[0].name,) + tuple(k[1:])
        if isinstance(k, tuple) and k[0] in self.tile_keys:
            self.tile_keys[k[0]].add(k)
        return k

    def _getstate(self, k):
        st = self.state.get(k)
        if st is None:
            inh = self.tile_init.get(k[0], []) if isinstance(k, tuple) else []
            d = {}
            for (s_, v_) in inh:
                if v_ > d.get(s_, -1):
                    d[s_] = v_
            st = [d, {}]
            self.state[k] = st
        return st

    DMA_K = {"sp": 16, "pool": 12, "act": 8, "dve": 2, "pe": 2}

    def _token(self, eng, isdma):
        if isdma:
            K = self.DMA_K[eng]
            c = self.cnt.get((eng, "d"), 0)
            self.cnt[(eng, "d")] = c + 1
            j, m = c % K, c // K
            sk = (eng, "d", j)
            if sk not in self.sems:
                self.sems[sk] = self.es.enter_context(self.nc.semaphore(f"s_{eng}_d{j}"))
                self.nsem += 1
            forced = (sk, 16 * m) if m > 0 else None
            self.final[sk] = 16 * (m + 1)
            return (sk, 16 * (m + 1)), forced
        c = self.cnt.get((eng, "c"), 0)
        epoch, idx = divmod(c, self.CAP_C)
        self.cnt[(eng, "c")] = c + 1
        sk = (eng, "c", epoch)
        if sk not in self.sems:
            self.sems[sk] = self.es.enter_context(self.nc.semaphore(f"s_{eng}_c{epoch}"))
            self.nsem += 1
        self.final[sk] = idx + 1
        return (sk, idx + 1), None

    def op(self, eng, fn, r=(), w=(), dma=False):
        o = Op()
        rec = _Rec()
        fn(rec)
        assert len(rec.calls) >= 1
        o.eng, o.calls, o.isdma = eng, rec.calls, dma
        need = {}
        rk = [self._key(k) for k in r]
        wk = [self._key(k) for k in w]
        for k in rk:
            st = self._getstate(k)
            for (s, v) in st[0].items():
                if v > need.get(s, -1):
                    need[s] = v
            if isinstance(k, tuple) and k[0] == "ps":
                for (s, v) in st[1].items():
                    if s[0] != eng and v > need.get(s, -1):
                        need[s] = v
        for k in wk:
            st = self._getstate(k)
            for (s, v) in list(st[0].items()) + list(st[1].items()):
                if s[0] == eng and s[1] == "c":
                    continue
                if v > need.get(s, -1):
                    need[s] = v
        tok, forced = self._token(eng, dma)
        if forced is not None and forced[1] > need.get(forced[0], -1):
            need[forced[0]] = forced[1]
        waits = []
        seen = self.seen[eng]
        for s, v in need.items():
            if s[0] == "pe" and eng == "pe" and s[1] == "c":
                continue
            if seen.get(s, -1) >= v:
                continue
            seen[s] = v
            waits.append((s, v))
        o.waits = waits
        o.sem, o.val = tok
        for k in rk:
            self.state[k][1][tok[0]] = tok[1]
        for k in wk:
            self.state[k] = [{tok[0]: tok[1]}, {}]
        self.ops[eng].append(o)
        self.n_ops += 1
        return o

    def dma(self, eng, out, in_, r=(), w=(), **kw):
        return self.op(eng, lambda e: e.dma_start(out=out, in_=in_, **kw), r=r, w=w, dma=True)

    def emit(self):
        nc = self.nc
        with nc.Block() as block:
            def run(engname):
                def f(e):
                    for o in self.ops[engname]:
                        for (s, v) in o.waits:
                            e.wait_ge(self.sems[s], v)
                        ins = None
                        for (nm_, a_, k_) in o.calls:
                            ins = getattr(e, nm_)(*a_, **k_)
                        ins.then_inc(self.sems[o.sem], 16 if o.isdma else 1)
                    if engname == "sp":
                        for sk, v in self.final.items():
                            e.wait_ge(self.sems[sk], v)
                return f
            block.tensor(run("pe"))
            block.scalar(run("act"))
            block.vector(run("dve"))
            block.gpsimd(run("pool"))
            block.sync(run("sp"))


def mk(ap, dims, off=0):
    return bass.AP(ap.tensor, ap.offset + off, [list(ap.ap[0])] + [list(d) for d in dims])


def make_consts():
    c = {}
    c["ident"] = np.eye(128, dtype=np.float32)
    c["identbf"] = np.eye(128, dtype=np.float32).astype(ml_dtypes.bfloat16)
    t = np.arange(2048)
    row = (t // 64).astype(np.float32)
    col = (t % 64).astype(np.float32)
    freqs = (np.float32(10000.0) ** (-np.arange(16, dtype=np.float32) / np.float32(16))).astype(np.float32)
    rope = np.zeros((128, 2, 2048), np.float32)
    for p in range(128):
        d = p % 64
        blk, i = divmod(d, 16)
        pos = row if blk < 2 else col
        ang = (pos * freqs[i]).astype(np.float32)
        rope[p, 0] = np.cos(ang)
        rope[p, 1] = -np.sin(ang) if blk in (0, 2) else np.sin(ang)
    c["rope"] = rope
    kl = np.arange(128)[:, None]
    ql = np.arange(128)[None, :]
    mb = np.zeros((128, 2, 512), np.float32)
    lo = np.where(kl >= ql, 0.0, NEG)
    hi = np.where(kl <= ql, 0.0, NEG)
    for hq in range(4):
        mb[:, 0, hq * 128:(hq + 1) * 128] = lo
        mb[:, 1, hq * 128:(hq + 1) * 128] = hi
    c["maskb"] = mb.astype(ml_dtypes.bfloat16)
    xs = np.zeros((128, 8, 240), np.float32)
    for g in range(8):
        for ci in range(16):
            xs[g * 16 + ci, g, 112 + ci] = 1.0
    c["xsel"] = xs.astype(ml_dtypes.bfloat16)
    ys = np.zeros((128, 8, 128), np.float32)
    for g in range(8):
        for j in range(8):
            for co in range(16):
                ys[j * 16 + co, g, g * 16 + co] = 1.0
    c["ysel"] = ys.astype(ml_dtypes.bfloat16)
    mj = np.zeros((128, 8), np.float32)
    for j in range(8):
        mj[j * 16:(j + 1) * 16, j] = 1.0
    c["maskj"] = mj
    tm = np.zeros((128, 2, 128), np.float32)
    s_idx = (np.arange(128) // 16)[:, None]
    j_idx = (np.arange(128) // 16)[None, :]
    tm[:, 0, :] = (j_idx >= s_idx)
    tm[:, 1, :] = (j_idx <= s_idx)
    c["toepm"] = tm
    vs = np.zeros((1, 128), np.float32)
    vs[0, 64:] = 1.0
    c["vsink"] = vs.astype(ml_dtypes.bfloat16)
    return c


CONST_SPECS = [("ident", [128, 128], F32), ("identbf", [128, 128], BF16), ("rope", [128, 2, 2048], F32),
               ("maskb", [128, 2, 512], BF16), ("xsel", [128, 8, 240], BF16), ("ysel", [128, 8, 128], BF16),
               ("maskj", [128, 8], F32), ("toepm", [128, 2, 128], F32), ("vsink", [1, 128], BF16)]

IN_SPECS = [("xp", [512, D]), ("xs", [2048, D]), ("ck", [2, PAST, 128]), ("cv", [2, PAST, 128]),
            ("sre", [2, 2, 16, 64]), ("sim", [2, 2, 16, 64]), ("cvec", [2, D]),
            ("norm_mix", [2, D]), ("norm_ffn", [2, D]), ("norm_final", [D]),
            ("w_ada", [2, D, 6 * D]), ("b_ada", [2, 6 * D]), ("w_in", [2, D, IN_DIM]), ("w_out", [2, D, D]),
            ("attn_sink", [2, 8]), ("sc_conv", [2, 256, 3]),
            ("ssm_lam_re", [2, 2, 16, 64]), ("ssm_lam_im", [2, 2, 16, 64]), ("ssm_log_dt", [2, 2, 16]),
            ("ssm_b_re", [2, 2, 16, 64, 16]), ("ssm_b_im", [2, 2, 16, 64, 16]),
            ("ssm_c_re", [2, 2, 16, 16, 64]), ("ssm_c_im", [2, 2, 16, 16, 64]),
            ("ssm_d", [2, 16, 16]), ("ssm_w_glu", [2, 256, 256]),
            ("ffn_w_up", [2, D, 2 * DFF]), ("ffn_conv", [2, 2 * DFF, 3]), ("ffn_w_down", [2, DFF, D])]

OUT_SPECS = [("yp", [512, D]), ("ys", [2048, D]), ("nk", [2, 2, 256, 128]), ("nv", [2, 2, 256, 128]),
             ("nsr", [2, 2, 2, 16, 64]), ("nsi", [2, 2, 2, 16, 64])]


class Builder:
    def __init__(self, stages=("prep", "P", "S"), dbg=()):
        self.stages = stages
        self.dbg_specs = list(dbg)
        self.nc = nc = bass.Bass("TRN2", target_bir_lowering=False)
        self.es = ExitStack()
        class _Lazy(dict):
            def __init__(s_, specs, prefix):
                super().__init__()
                s_.specs, s_.prefix = specs, prefix

            def __missing__(s_, name):
                shape, dt = s_.specs[name]
                ap = nc.dram_tensor(s_.prefix + name, shape, dt, kind="ExternalInput").ap()
                s_[name] = ap
                return ap
        self.I = _Lazy({n: (sh, F32) for n, sh in IN_SPECS}, "")
        self.C = _Lazy({n: (sh, dt) for n, sh, dt in CONST_SPECS}, "c_")
        self.O = {}
        for name, shape in OUT_SPECS:
            self.O[name] = nc.dram_tensor(name, shape, F32, kind="ExternalOutput").ap()
        for name, shape, dt_ in self.dbg_specs:
            self.O[name] = nc.dram_tensor(name, shape, dt_, kind="ExternalOutput").ap()
        self.W = {}
        for name, kc, n in [("win", 8, IN_DIM), ("wout", 8, D), ("wup", 8, 2 * DFF), ("wdn", 22, D), ("wglu", 2, 256)]:
            self.W[name] = [nc.dram_tensor(f"s_{name}{l}", [128, kc, n], BF16, kind="Internal").ap() for l in range(2)]
        self.S_kt = [nc.dram_tensor(f"s_kt{l}", [128, 16, 128], BF16, kind="Internal").ap() for l in range(2)]
        self.S_ws = [nc.dram_tensor(f"s_ws{l}", [128, 2, 16, 128], BF16, kind="Internal").ap() for l in range(2)]
        self.S_wo = [nc.dram_tensor(f"s_wo{l}", [128, 2, 8, 2, 128], BF16, kind="Internal").ap() for l in range(2)]
        self.S_e = [nc.dram_tensor(f"s_e{l}", [128, 2, 2, 8, 256], F32, kind="Internal").ap() for l in range(2)]
        self.P = Prog(nc, self.es)

    def wkeys(self, name, l, c0, c1, kcs=None):
        nkc = {"win": 8, "wout": 8, "wup": 8, "wdn": 22, "wglu": 2}[name]
        ks = []
        for kc in (range(nkc) if kcs is None else kcs):
            for b in range(c0 // CASTW, (c1 - 1) // CASTW + 1):
                ks.append(("W", name, l, kc, b))
        return ks

    def persistent(self):
        P = self.P
        self.ident = P.alloc("ident", [128], F32)
        self.identbf = P.alloc("identbf", [128], BF16)
        self.onesbf = P.alloc("onesbf", [128], BF16)
        self.modT = P.alloc("modT", [2, 48, 2], F32)
        self.gsc1 = P.alloc("gsc1", [2, 8, 2], F32)
        self.gsc2 = P.alloc("gsc2", [2, 8, 2], F32)
        self.nfT = P.alloc("nfT", [8], F32)
        self.a8mag = P.alloc("a8mag", [4, 8], F32)
        self.e1 = P.alloc("e1", [2, 4, 8], F32)
        self.esrow = P.alloc("esrow", [2, 8, 128], BF16)
        self.vsink = P.alloc("vsink", [128], BF16)
        self.scw = P.alloc("scw", [2, 2, 3], F32)
        self.fcw = P.alloc("fcw", [2, 44, 3], F32)
        self.maskj = P.alloc("maskj", [8], F32)
        P.dma("sp", self.ident[:], self.C["ident"][:, :], w=[self.ident])
        P.dma("sp", self.identbf[:], self.C["identbf"][:, :], w=[self.identbf])
        P.dma("sp", self.maskj[:], self.C["maskj"][:, :], w=[self.maskj])
        P.dma("sp", self.vsink[0:1, :], self.C["vsink"][:, :], w=[self.vsink])
        P.op("dve", lambda e: e.memset(self.onesbf[:], 1.0), w=[self.onesbf])
        I = self.I
        P.dma("sp", self.nfT[:], I["norm_final"].rearrange("(c p) -> p c", p=128), w=[self.nfT],
              allow_slow_non_contiguous=True)
        mk_s = P.mark()
        sk = P.alloc("sk", [16], F32); ske = P.alloc("ske", [16], F32)
        P.dma("sp", sk[0:1, :], I["attn_sink"].rearrange("l h -> (l h)").rearrange("(o n) -> o n", o=1), w=[sk])
        P.op("act", lambda e: e.activation(out=ske[0:1, :], in_=sk[0:1, :], func=AF.Exp), r=[sk], w=[ske])
        P.op("dve", lambda e: e.tensor_copy(out=self.esrow[0:1, :, :, :].rearrange("p l h q -> p (l h) q"),
                                            in_=mk(ske[0:1, :], [[1, 16], [0, 128]])), r=[ske], w=[self.esrow])
        P.release(mk_s)
        for l in range(2):
            P.dma("sp", self.scw[:, l], I["sc_conv"][l].rearrange("(c p) k -> p c k", p=128), w=[self.scw])
            P.dma("sp", self.fcw[:, l], I["ffn_conv"][l].rearrange("(c p) k -> p c k", p=128), w=[self.fcw])

    def alloc_cast_bufs(self, tag, NB=3):
        P = self.P
        return ([P.alloc(f"cst{tag}{i}", [CASTW], F32) for i in range(NB)], [P.alloc(f"cob{tag}{i}", [CASTW], BF16) for i in range(NB)])

    def prep_casts(self, layers, eng, bufs, first_dep=()):
        P, I = self.P, self.I
        stage, outb = bufs
        NB = len(stage)
        pieces = []
        for l in layers:
            for name, src, K, N in [("win", "w_in", D, IN_DIM), ("wglu", "ssm_w_glu", 256, 256), ("wout", "w_out", D, D),
                                    ("wup", "ffn_w_up", D, 2 * DFF), ("wdn", "ffn_w_down", DFF, D)]:
                if self.cast_only is not None and name not in self.cast_only:
                    continue
                for kc in range(K // 128):
                    for b, c0 in enumerate(range(0, N, CASTW)):
                        pieces.append((l, name, src, kc, b, c0, min(CASTW, N - c0)))

        def load(i):
            l, name, src, kc, b, c0, cw = pieces[i]
            st = stage[i % NB]
            P.dma(eng, st[:, 0:cw], I[src][l, kc * 128:(kc + 1) * 128, c0:c0 + cw], r=list(first_dep) if i == 0 else [], w=[st])
        PRE = NB - 1
        for i in range(min(PRE, len(pieces))):
            load(i)
        for i, (l, name, src, kc, b, c0, cw) in enumerate(pieces):
            st, ob = stage[i % NB], outb[i % NB]
            if eng == "act":
                P.op("act", lambda e, st=st, ob=ob, cw=cw: e.copy(out=ob[:, 0:cw], in_=st[:, 0:cw]), r=[st], w=[ob])
            else:
                P.op(eng, lambda e, st=st, ob=ob, cw=cw: e.tensor_copy(out=ob[:, 0:cw], in_=st[:, 0:cw]), r=[st], w=[ob])
            P.dma(eng, self.W[name][l][:, kc, c0:c0 + cw], ob[:, 0:cw], r=[ob], w=[("W", name, l, kc, b)])
            if i + PRE < len(pieces):
                load(i + PRE)

    def prep_adaln_setup(self):
        P, I = self.P, self.I
        self.scT = P.alloc("scT", [8, 2], F32)
        self.bT = P.alloc("bT", [2, 48], F32)
        self.nmT = P.alloc("nmT", [2, 2, 8], F32)
        mk0 = P.mark()
        craw = P.alloc("craw", [8, 2], F32)
        for v in range(2):
            P.dma("sp", craw[:, :, v], I["cvec"][v].rearrange("(c p) -> p c", p=128), w=[craw], allow_slow_non_contiguous=True)
        P.op("act", lambda e: e.activation(out=self.scT[:], in_=craw[:], func=AF.Silu), r=[craw], w=[self.scT])
        for l in range(2):
            P.dma("sp", self.bT[:, l], I["b_ada"][l].rearrange("(c p) -> p c", p=128), w=[self.bT], allow_slow_non_contiguous=True)
            P.dma("sp", self.nmT[:, 0, l], I["norm_mix"][l].rearrange("(c p) -> p c", p=128), w=[self.nmT], allow_slow_non_contiguous=True)
            P.dma("sp", self.nmT[:, 1, l], I["norm_ffn"][l].rearrange("(c p) -> p c", p=128), w=[self.nmT], allow_slow_non_contiguous=True)
        P.release(mk0)
        self.adaln_done = set()

    def prep_adaln(self, l):
        if l in self.adaln_done:
            return
        self.adaln_done.add(l)
        P, I = self.P, self.I
        scT, bT, nmT = self.scT, self.bT, self.nmT
        mk0 = P.mark()
        wa = [P.alloc(f"wa{i}", [8, 512], F32) for i in range(2)]
        bk = P.next_bank()
        P.reserved_banks.add(bk)
        ps = P.bank(bk)
        for j in range(12):
            w = wa[j % 2]
            P.dma("sp", w[:], I["w_ada"][l, :, j * 512:(j + 1) * 512].rearrange("(kc p) n -> p kc n", p=128), w=[w])
            for oc in range(4):
                col = (j * 4 + oc) * 2
                for kc in range(8):
                    P.op("pe", lambda e, w=w, oc=oc, kc=kc, col=col, ps=ps: e.matmul(
                        ps[:, col:col + 2], lhsT=w[:, kc, oc * 128:(oc + 1) * 128], rhs=scT[:, kc, :],
                        start=(kc == 0), stop=(kc == 7)), r=[w, scT], w=[("ps", bk)])
        P.op("dve", lambda e, l=l, ps=ps: e.tensor_tensor(
            out=self.modT[:, l], in0=ps[:, 0:96].rearrange("p (a b) -> p a b", b=2),
            in1=mk(bT[:, l], [[1, 48], [0, 2]]), op=ALU.add), r=[("ps", bk), bT], w=[(self.modT, l)])
        P.op("dve", lambda e, l=l: e.scalar_tensor_tensor(
            out=self.gsc1[:, l], in0=self.modT[:, l, 8:16, :], scalar=1.0,
            in1=mk(nmT[:, 0, l], [[1, 8], [0, 2]]), op0=ALU.add, op1=ALU.mult), r=[(self.modT, l), nmT], w=[(self.gsc1, l)])
        P.op("dve", lambda e, l=l: e.scalar_tensor_tensor(
            out=self.gsc2[:, l], in0=self.modT[:, l, 32:40, :], scalar=1.0,
            in1=mk(nmT[:, 1, l], [[1, 8], [0, 2]]), op0=ALU.add, op1=ALU.mult), r=[(self.modT, l), nmT], w=[(self.gsc2, l)])
        P.reserved_banks.discard(bk)
        P.release(mk0)

    def prep_ssm(self):
        P, I, C = self.P, self.I, self.C
        mk0 = P.mark()
        V = lambda fn, r, w: P.op("dve", fn, r=r, w=w)
        A = lambda fn, r, w: P.op("act", fn, r=r, w=w)
        LD = [(l, d) for l in range(2) for d in range(2)]
        lamr = P.alloc("lamr", [4, 8], F32); lami = P.alloc("lami", [4, 8], F32); ldt = P.alloc("ldt", [4, 8], F32)
        for i, (l, d) in enumerate(LD):
            for h in range(2):
                for tl, nm in ((lamr, "ssm_lam_re"), (lami, "ssm_lam_im")):
                    src = I[nm][l, d]
                    P.dma("sp", tl[64 * h:64 * h + 64, i, :], bass.AP(src.tensor, src.offset + h * 64, [[1, 64], [128, 8]]),
                          w=[tl], allow_slow_non_contiguous=True)
                src = I["ssm_log_dt"][l, d]
                P.dma("sp", ldt[64 * h:64 * h + 64, i, :], bass.AP(src.tensor, src.offset + h, [[0, 64], [2, 8]]),
                      w=[ldt], allow_slow_non_contiguous=True)
        names = ["dt", "lrdt", "mag", "ang", "rs", "rc", "sn", "cs", "ar", "ai", "den", "rden", "am1", "t1", "t2", "t3", "t4",
                 "fr", "fi", "mag2", "rm", "ivr", "ivi", "inv8"]
        T = {n: P.alloc(n, [4, 8], F32) for n in names}
        negpi = P.alloc("negpi", [1], F32)
        V(lambda e: e.memset(negpi[:], -math.pi), [], [negpi])
        qi = P.alloc("qi", [4, 8], I32)

        def taylor_exp(out_t, x_t, deg, tmp):
            V(lambda e: e.tensor_scalar(out=out_t[:], in0=x_t[:], scalar1=1.0 / deg, scalar2=1.0, op0=ALU.mult, op1=ALU.add), [x_t], [out_t])
            for k in range(deg - 1, 0, -1):
                V(lambda e: e.tensor_tensor(out=tmp[:], in0=out_t[:], in1=x_t[:], op=ALU.mult), [out_t, x_t], [tmp])
                V(lambda e, k=k: e.tensor_scalar(out=out_t[:], in0=tmp[:], scalar1=1.0 / k, scalar2=1.0, op0=ALU.mult, op1=ALU.add), [tmp], [out_t])
        V(lambda e: e.tensor_copy(out=qi[:], in_=ldt[:]), [ldt], [qi])
        V(lambda e: e.tensor_copy(out=T["t1"][:], in_=qi[:]), [qi], [T["t1"]])
        V(lambda e: e.tensor_tensor(out=T["t2"][:], in0=ldt[:], in1=T["t1"][:], op=ALU.subtract), [ldt, T["t1"]], [T["t2"]])
        taylor_exp(T["t3"], T["t2"], 12, T["t4"])
        V(lambda e: e.memset(T["dt"][:], 0.0), [], [T["dt"]])
        for j in range(-10, 1):
            V(lambda e, j=j: e.tensor_scalar(out=T["t4"][:], in0=T["t1"][:], scalar1=float(j), scalar2=math.exp(j), op0=ALU.is_equal, op1=ALU.mult),
              [T["t1"]], [T["t4"]])
            V(lambda e: e.tensor_tensor(out=T["dt"][:], in0=T["dt"][:], in1=T["t4"][:], op=ALU.add), [T["dt"], T["t4"]], [T["dt"]])
        V(lambda e: e.tensor_tensor(out=T["dt"][:], in0=T["dt"][:], in1=T["t3"][:], op=ALU.mult), [T["dt"], T["t3"]], [T["dt"]])
        V(lambda e: e.tensor_tensor(out=T["lrdt"][:], in0=lamr[:], in1=T["dt"][:], op=ALU.mult), [lamr, T["dt"]], [T["lrdt"]])
        taylor_exp(T["mag"], T["lrdt"], 7, T["t4"])
        V(lambda e: e.tensor_tensor(out=T["t1"][:], in0=T["mag"][:], in1=T["mag"][:], op=ALU.mult), [T["mag"]], [T["t1"]])
        V(lambda e: e.tensor_tensor(out=T["t2"][:], in0=T["t1"][:], in1=T["t1"][:], op=ALU.mult), [T["t1"]], [T["t2"]])
        V(lambda e: e.tensor_tensor(out=self.a8mag[:], in0=T["t2"][:], in1=T["t2"][:], op=ALU.mult), [T["t2"]], [self.a8mag])
        V(lambda e: e.reciprocal(out=T["inv8"][:], in_=self.a8mag[:]), [self.a8mag], [T["inv8"]])
        V(lambda e: e.tensor_tensor(out=T["ang"][:], in0=lami[:], in1=T["dt"][:], op=ALU.mult), [lami, T["dt"]], [T["ang"]])
        def range_reduce(out_t, add):
            V(lambda e: e.tensor_scalar(out=T["t1"][:], in0=T["ang"][:], scalar1=add, scalar2=1.0 / TWO_PI, op0=ALU.add, op1=ALU.mult),
              [T["ang"]], [T["t1"]])
            V(lambda e: e.tensor_copy(out=qi[:], in_=T["t1"][:]), [T["t1"]], [qi])
            V(lambda e: e.tensor_copy(out=T["t2"][:], in_=qi[:]), [qi], [T["t2"]])
            V(lambda e: e.scalar_tensor_tensor(out=T["t3"][:], in0=T["t2"][:], scalar=-TWO_PI, in1=T["ang"][:], op0=ALU.mult, op1=ALU.add),
              [T["t2"], T["ang"]], [T["t3"]])
            V(lambda e: e.tensor_scalar_add(out=T["t3"][:], in0=T["t3"][:], scalar1=add), [T["t3"]], [T["t3"]])
            V(lambda e: e.tensor_scalar(out=T["t4"][:], in0=T["t3"][:], scalar1=math.pi, scalar2=-TWO_PI, op0=ALU.is_gt, op1=ALU.mult),
              [T["t3"]], [T["t4"]])
            V(lambda e: e.tensor_tensor(out=T["t3"][:], in0=T["t3"][:], in1=T["t4"][:], op=ALU.add), [T["t3"], T["t4"]], [T["t3"]])
            V(lambda e: e.tensor_scalar(out=T["t4"][:], in0=T["t3"][:], scalar1=-math.pi, scalar2=TWO_PI, op0=ALU.is_lt, op1=ALU.mult),
              [T["t3"]], [T["t4"]])
            V(lambda e: e.tensor_tensor(out=out_t[:], in0=T["t3"][:], in1=T["t4"][:], op=ALU.add), [T["t3"], T["t4"]], [out_t])
        range_reduce(T["rs"], 0.0)
        xx, x2, ps_, pc_ = T["t1"], T["t2"], T["t3"], T["t4"]
        V(lambda e: e.tensor_scalar_mul(out=xx[:], in0=T["rs"][:], scalar1=0.25), [T["rs"]], [xx])
        V(lambda e: e.tensor_tensor(out=x2[:], in0=xx[:], in1=xx[:], op=ALU.mult), [xx], [x2])

        def horner(p, coefs):
            V(lambda e: e.tensor_scalar(out=p[:], in0=x2[:], scalar1=coefs[0], scalar2=1.0, op0=ALU.mult, op1=ALU.add), [x2], [p])
            for cf in coefs[1:]:
                V(lambda e: e.tensor_tensor(out=p[:], in0=p[:], in1=x2[:], op=ALU.mult), [p, x2], [p])
                V(lambda e, cf=cf: e.tensor_scalar(out=p[:], in0=p[:], scalar1=cf, scalar2=1.0, op0=ALU.mult, op1=ALU.add), [p], [p])
        horner(ps_, [-1.0 / 110.0, -1.0 / 72.0, -1.0 / 42.0, -1.0 / 20.0, -1.0 / 6.0])
        V(lambda e: e.tensor_tensor(out=ps_[:], in0=ps_[:], in1=xx[:], op=ALU.mult), [ps_, xx], [ps_])
        horner(pc_, [-1.0 / 90.0, -1.0 / 56.0, -1.0 / 30.0, -1.0 / 12.0, -1.0 / 2.0])
        sA, cA = T["sn"], T["cs"]
        for it in range(2):
            V(lambda e: e.scalar_tensor_tensor(out=sA[:], in0=ps_[:], scalar=2.0, in1=pc_[:], op0=ALU.mult, op1=ALU.mult), [ps_, pc_], [sA])
            V(lambda e: e.tensor_tensor(out=cA[:], in0=ps_[:], in1=ps_[:], op=ALU.mult), [ps_], [cA])
            V(lambda e: e.tensor_scalar(out=cA[:], in0=cA[:], scalar1=-2.0, scalar2=1.0, op0=ALU.mult, op1=ALU.add), [cA], [cA])
            if it == 0:
                V(lambda e: e.tensor_copy(out=ps_[:], in_=sA[:]), [sA], [ps_])
                V(lambda e: e.tensor_copy(out=pc_[:], in_=cA[:]), [cA], [pc_])

        def tt(o, a, b, op):
            V(lambda e: e.tensor_tensor(out=o[:], in0=a[:], in1=b[:], op=op), [a, b], [o])
        tt(T["ar"], T["mag"], T["cs"], ALU.mult)
        tt(T["ai"], T["mag"], T["sn"], ALU.mult)
        tt(T["t1"], lamr, lamr, ALU.mult)
        tt(T["t2"], lami, lami, ALU.mult)
        tt(T["den"], T["t1"], T["t2"], ALU.add)
        V(lambda e: e.reciprocal(out=T["rden"][:], in_=T["den"][:]), [T["den"]], [T["rden"]])
        V(lambda e: e.tensor_scalar_add(out=T["am1"][:], in0=T["ar"][:], scalar1=-1.0), [T["ar"]], [T["am1"]])
        tt(T["t1"], T["am1"], lamr, ALU.mult)
        tt(T["t2"], T["ai"], lami, ALU.mult)
        tt(T["t3"], T["t1"], T["t2"], ALU.add)
        tt(T["fr"], T["t3"], T["rden"], ALU.mult)
        tt(T["t1"], T["ai"], lamr, ALU.mult)
        tt(T["t2"], T["am1"], lami, ALU.mult)
        tt(T["t3"], T["t1"], T["t2"], ALU.subtract)
        tt(T["fi"], T["t3"], T["rden"], ALU.mult)
        tt(T["t1"], T["ar"], T["ar"], ALU.mult)
        tt(T["t2"], T["ai"], T["ai"], ALU.mult)
        tt(T["mag2"], T["t1"], T["t2"], ALU.add)
        V(lambda e: e.reciprocal(out=T["rm"][:], in_=T["mag2"][:]), [T["mag2"]], [T["rm"]])
        tt(T["ivr"], T["ar"], T["rm"], ALU.mult)
        V(lambda e: e.scalar_tensor_tensor(out=T["ivi"][:], in0=T["ai"][:], scalar=-1.0, in1=T["rm"][:], op0=ALU.mult, op1=ALU.mult),
          [T["ai"], T["rm"]], [T["ivi"]])
        if self.cut == 1:
            P.release(mk0); return
        apr = P.alloc("apr", [4, 17, 8], F32); api = P.alloc("api", [4, 17, 8], F32)
        V(lambda e: e.memset(apr[:, :, 8, :], 1.0), [], [(apr, 8)])
        V(lambda e: e.memset(api[:, :, 8, :], 0.0), [], [(api, 8)])
        V(lambda e: e.tensor_copy(out=apr[:, :, 9, :], in_=T["ar"][:]), [T["ar"]], [(apr, 9)])
        V(lambda e: e.tensor_copy(out=api[:, :, 9, :], in_=T["ai"][:]), [T["ai"]], [(api, 9)])
        V(lambda e: e.tensor_copy(out=apr[:, :, 7, :], in_=T["ivr"][:]), [T["ivr"]], [(apr, 7)])
        V(lambda e: e.tensor_copy(out=api[:, :, 7, :], in_=T["ivi"][:]), [T["ivi"]], [(api, 7)])

        def cmul_small(k_out, k_in, br, bi):
            xr, xi = apr[:, :, k_in, :], api[:, :, k_in, :]
            V(lambda e: e.tensor_tensor(out=T["t1"][:], in0=xr, in1=br[:], op=ALU.mult), [(apr, k_in), br], [T["t1"]])
            V(lambda e: e.tensor_tensor(out=T["t2"][:], in0=xi, in1=bi[:], op=ALU.mult), [(api, k_in), bi], [T["t2"]])
            V(lambda e: e.tensor_tensor(out=apr[:, :, k_out, :], in0=T["t1"][:], in1=T["t2"][:], op=ALU.subtract), [T["t1"], T["t2"]], [(apr, k_out)])
            V(lambda e: e.tensor_tensor(out=T["t3"][:], in0=xr, in1=bi[:], op=ALU.mult), [(apr, k_in), bi], [T["t3"]])
            V(lambda e: e.tensor_tensor(out=T["t4"][:], in0=xi, in1=br[:], op=ALU.mult), [(api, k_in), br], [T["t4"]])
            V(lambda e: e.tensor_tensor(out=api[:, :, k_out, :], in0=T["t3"][:], in1=T["t4"][:], op=ALU.add), [T["t3"], T["t4"]], [(api, k_out)])
        for k in range(9, 16):
            cmul_small(k + 1, k, T["ar"], T["ai"])
        for k in range(7, 0, -1):
            cmul_small(k - 1, k, T["ivr"], T["ivi"])
        APW_R = [(apr, k) for k in range(17)]
        APW_I = [(api, k) for k in range(17)]
        V(lambda e: e.tensor_tensor(out=self.e1[:, 0], in0=apr[:, :, 16, :], in1=T["inv8"][:], op=ALU.mult), [(apr, 16), T["inv8"]], [self.e1])
        V(lambda e: e.tensor_tensor(out=self.e1[:, 1], in0=api[:, :, 16, :], in1=T["inv8"][:], op=ALU.mult), [(api, 16), T["inv8"]], [self.e1])

        if self.cut == 2:
            P.release(mk0); return
        Et = P.alloc("Et", [2, 8, 256], F32)
        wk = P.alloc("wk", [2, 2, 8], F32)
        et1 = P.alloc("et1", [8, 128], F32); et2 = P.alloc("et2", [8, 128], F32)
        Br = P.alloc("Br", [8, 16], F32); Bi = P.alloc("Bi", [8, 16], F32)
        bbr = P.alloc("bbr", [8, 16], F32); bbi = P.alloc("bbi", [8, 16], F32)
        Cr = P.alloc("Cr", [8, 16], F32); Ci = P.alloc("Ci", [8, 16], F32)
        cn = P.alloc("cn", [2, 64], F32)
        pbr = P.alloc("pbr", [8, 8, 16], F32); pbi = P.alloc("pbi", [8, 8, 16], F32)
        pcr = P.alloc("pcr", [8, 8, 16], F32); pci = P.alloc("pci", [8, 8, 16], F32)
        q1 = P.alloc("q1", [8, 8, 16], F32); q2 = P.alloc("q2", [8, 8, 16], F32)
        wsb = P.alloc("wsb", [16, 128], BF16)
        wob = P.alloc("wob", [8, 2, 128], BF16)
        ktacc = P.alloc("ktacc", [16, 128], F32)
        ktb = P.alloc("ktb", [16, 128], BF16)
        toep = P.alloc("toep", [2, 128], F32)
        dtab = P.alloc("dtab", [16], F32)
        ktmp = P.alloc("ktmp", [128], F32)
        P.dma("sp", toep[:], C["toepm"][:, :, :], w=[toep])

        def bc_last(ap2, n):
            return mk(ap2, [list(ap2.ap[1]), [0, n]])

        def cprod(outr, outi, kstart, kstep, Xr, Xi, neg_im, xkeys):
            a_r = apr[:, ld, kstart, :]
            a_i = api[:, ld, kstart, :]
            AR = mk(a_r, [[1, 8], [8 * kstep, 8], [0, 16]])
            AI = mk(a_i, [[1, 8], [8 * kstep, 8], [0, 16]])
            XR = mk(Xr[:], [[16, 8], [0, 8], [1, 16]])
            XI = mk(Xi[:], [[16, 8], [0, 8], [1, 16]])
            V(lambda e: e.tensor_tensor(out=q1[:], in0=AR, in1=XR, op=ALU.mult), APW_R + xkeys, [q1])
            V(lambda e: e.tensor_tensor(out=q2[:], in0=AI, in1=XI, op=ALU.mult), APW_I + xkeys, [q2])
            V(lambda e: e.tensor_tensor(out=outr[:], in0=q1[:], in1=q2[:], op=ALU.subtract), [q1, q2], [outr])
            V(lambda e: e.tensor_tensor(out=q1[:], in0=AR, in1=XI, op=ALU.mult), APW_R + xkeys, [q1])
            V(lambda e: e.tensor_tensor(out=q2[:], in0=AI, in1=XR, op=ALU.mult), APW_I + xkeys, [q2])
            if neg_im:
                V(lambda e: e.scalar_tensor_tensor(out=outi[:], in0=q1[:], scalar=-1.0, in1=q2[:], op0=ALU.mult, op1=ALU.subtract),
                  [q1, q2], [outi])
            else:
                V(lambda e: e.tensor_tensor(out=outi[:], in0=q1[:], in1=q2[:], op=ALU.add), [q1, q2], [outi])

        for ld, (l, d) in enumerate(LD):
            V(lambda e: e.memset(Et[:, 0, :, 0:1], 1.0), [], [Et])
            V(lambda e: e.memset(Et[:, 1, :, 0:1], 0.0), [], [Et])
            V(lambda e, ld=ld: e.tensor_copy(out=wk[:, 0, 0, :], in_=self.e1[:, 0, ld, :]), [self.e1], [wk])
            V(lambda e, ld=ld: e.tensor_copy(out=wk[:, 0, 1, :], in_=self.e1[:, 1, ld, :]), [self.e1], [wk])
            for k in range(8):
                n = 1 << k
                pp, qq = k % 2, (k + 1) % 2
                wr = mk(wk[:, pp, 0, :], [[1, 8], [0, n]])
                wi = mk(wk[:, pp, 1, :], [[1, 8], [0, n]])
                t1v = et1[:, :, 0:n]; t2v = et2[:, :, 0:n]
                V(lambda e, n=n, wr=wr, t1v=t1v: e.tensor_tensor(out=t1v, in0=Et[:, 0, :, 0:n], in1=wr, op=ALU.mult), [Et, wk], [et1])
                V(lambda e, n=n, wi=wi, t2v=t2v: e.tensor_tensor(out=t2v, in0=Et[:, 1, :, 0:n], in1=wi, op=ALU.mult), [Et, wk], [et2])
                V(lambda e, n=n, t1v=t1v, t2v=t2v: e.tensor_tensor(out=Et[:, 0, :, n:2 * n], in0=t1v, in1=t2v, op=ALU.subtract), [et1, et2], [Et])
                V(lambda e, n=n, wi=wi, t1v=t1v: e.tensor_tensor(out=t1v, in0=Et[:, 0, :, 0:n], in1=wi, op=ALU.mult), [Et, wk], [et1])
                V(lambda e, n=n, wr=wr, t2v=t2v: e.tensor_tensor(out=t2v, in0=Et[:, 1, :, 0:n], in1=wr, op=ALU.mult), [Et, wk], [et2])
                V(lambda e, n=n, t1v=t1v, t2v=t2v: e.tensor_tensor(out=Et[:, 1, :, n:2 * n], in0=t1v, in1=t2v, op=ALU.add), [et1, et2], [Et])
                if k < 7:
                    a_r, a_i = wk[:, pp, 0, :], wk[:, pp, 1, :]
                    s1 = et1[:, :, 0]; s2 = et2[:, :, 0]
                    V(lambda e, a_r=a_r, s1=s1: e.tensor_tensor(out=s1, in0=a_r, in1=a_r, op=ALU.mult), [wk], [et1])
                    V(lambda e, a_i=a_i, s2=s2: e.tensor_tensor(out=s2, in0=a_i, in1=a_i, op=ALU.mult), [wk], [et2])
                    V(lambda e, qq=qq, s1=s1, s2=s2: e.tensor_tensor(out=wk[:, qq, 0, :], in0=s1, in1=s2, op=ALU.subtract), [et1, et2], [wk])
                    V(lambda e, a_r=a_r, a_i=a_i, s1=s1: e.tensor_tensor(out=s1, in0=a_r, in1=a_i, op=ALU.mult), [wk], [et1])
                    V(lambda e, qq=qq, s1=s1: e.tensor_scalar_mul(out=wk[:, qq, 1, :], in0=s1, scalar1=2.0), [et1], [wk])
            P.dma("sp", self.S_e[l][:, d], Et[:], r=[Et], w=[("S_e", l, d)])
            if self.cut == 3:
                continue
            for h in range(2):
                for tl, nm in ((Br, "ssm_b_re"), (Bi, "ssm_b_im")):
                    src = I[nm][l, d]
                    P.dma("sp", tl[64 * h:64 * h + 64, :, :], bass.AP(src.tensor, src.offset + h * 1024, [[16, 64], [2048, 8], [1, 16]]), w=[tl])
            FR = bc_last(T["fr"][:, ld, :], 16); FI = bc_last(T["fi"][:, ld, :], 16)
            V(lambda e, FR=FR: e.tensor_tensor(out=q1[:, :, 0, :], in0=Br[:], in1=FR, op=ALU.mult), [Br, T["fr"]], [q1])
            V(lambda e, FI=FI: e.tensor_tensor(out=q2[:, :, 0, :], in0=Bi[:], in1=FI, op=ALU.mult), [Bi, T["fi"]], [q2])
            V(lambda e: e.tensor_tensor(out=bbr[:], in0=q1[:, :, 0, :], in1=q2[:, :, 0, :], op=ALU.subtract), [q1, q2], [bbr])
            V(lambda e, FR=FR: e.tensor_tensor(out=q1[:, :, 0, :], in0=Bi[:], in1=FR, op=ALU.mult), [Bi, T["fr"]], [q1])
            V(lambda e, FI=FI: e.tensor_tensor(out=q2[:, :, 0, :], in0=Br[:], in1=FI, op=ALU.mult), [Br, T["fi"]], [q2])
            V(lambda e: e.tensor_tensor(out=bbi[:], in0=q1[:, :, 0, :], in1=q2[:, :, 0, :], op=ALU.add), [q1, q2], [bbi])
            for Cx, nm in ((Cr, "ssm_c_re"), (Ci, "ssm_c_im")):
                P.dma("sp", cn[:], I[nm][l, d].rearrange("(t g) c n -> (g c) t n", t=2), w=[cn])
                for t in range(2):
                    bk = P.next_bank(); ps = P.bank(bk)
                    P.op("pe", lambda e, t=t, ps=ps: e.transpose(out=ps[0:64, 0:128], in_=cn[:, t, :], identity=self.ident[:]),
                         r=[cn, self.ident], w=[("ps", bk)])
                    for par in range(2):
                        src_ = mk(ps[0:64, 0:128], [[32, 4], [1, 16]], off=par * 16)
                        V(lambda e, Cx=Cx, t=t, par=par, src_=src_: e.tensor_copy(out=Cx[64 * par:64 * par + 64, 4 * t:4 * t + 4, :], in_=src_),
                          [("ps", bk)], [Cx])
            if self.cut == 4:
                continue
            if d == 0:
                cprod(pbr, pbi, 15, -1, bbr, bbi, False, [bbr, bbi])
                cprod(pcr, pci, 1, 1, Cr, Ci, True, [Cr, Ci])
            else:
                cprod(pbr, pbi, 8, 1, bbr, bbi, False, [bbr, bbi])
                cprod(pcr, pci, 8, -1, Cr, Ci, True, [Cr, Ci])
            if self.cut == 5:
                continue
            for ggp in range(4):
                bk = P.next_bank(); ps = P.bank(bk)
                for ggl in range(2):
                    gg = 2 * ggp + ggl
                    for comp, pb in enumerate((pbr, pbi)):
                        col = (ggl * 2 + comp) * 128
                        P.op("pe", lambda e, pb=pb, gg=gg, col=col, ps=ps: e.transpose(
                            out=ps[:, col:col + 128], in_=pb[:, gg, :, :].rearrange("p s c -> p (s c)"),
                            identity=self.ident[:]), r=[pb, self.ident], w=[("ps", bk)])
                for comp in range(2):
                    src_ = mk(ps[:, comp * 128:comp * 128 + 1], [[256, 2], [64, 2], [1, 64]])
                    dst_ = mk(wsb[:, 4 * ggp, comp * 64:comp * 64 + 1], [[256, 2], [128, 2], [1, 64]])
                    V(lambda e, src_=src_, dst_=dst_: e.tensor_copy(out=dst_, in_=src_), [("ps", bk)], [wsb])
            P.dma("sp", self.S_ws[l][:, d], wsb[:], r=[wsb], w=[("S_ws", l, d)])
            if self.cut == 6:
                continue
            if d == 0:
                for s8 in range(8):
                    src = I["ssm_d"][l]
                    P.dma("sp", dtab[16 * s8:16 * s8 + 16, :], bass.AP(src.tensor, src.offset, [[1, 16], [16, 16]]), w=[dtab],
                          allow_slow_non_contiguous=True)
            for g in range(16):
                h, gg = g % 2, g // 2
                bk = P.next_bank(); ps = P.bank(bk)
                P.op("pe", lambda e, h=h, gg=gg, ps=ps: e.matmul(
                    ps[:, 0:128], lhsT=pbr[64 * h:64 * h + 64, gg, :, :].rearrange("p s c -> p (s c)"),
                    rhs=pcr[64 * h:64 * h + 64, gg, :, :].rearrange("p s c -> p (s c)"), start=True, stop=False),
                    r=[pbr, pcr], w=[("ps", bk)])
                P.op("pe", lambda e, h=h, gg=gg, ps=ps: e.matmul(
                    ps[:, 0:128], lhsT=pbi[64 * h:64 * h + 64, gg, :, :].rearrange("p s c -> p (s c)"),
                    rhs=pci[64 * h:64 * h + 64, gg, :, :].rearrange("p s c -> p (s c)"), start=False, stop=True),
                    r=[pbi, pci], w=[("ps", bk)])
                if d == 0:
                    V(lambda e, g=g, ps=ps: e.tensor_tensor(out=ktacc[:, g, :], in0=ps[:, 0:128], in1=toep[:, 0, :], op=ALU.mult),
                      [("ps", bk), toep], [(ktacc, g)])
                else:
                    V(lambda e, ps=ps: e.tensor_tensor(out=ktmp[:], in0=ps[:, 0:128], in1=toep[:, 1, :], op=ALU.mult),
                      [("ps", bk), toep], [ktmp])
                    V(lambda e, g=g: e.tensor_tensor(out=ktacc[:, g, :], in0=ktacc[:, g, :], in1=ktmp[:], op=ALU.add),
                      [ktmp, (ktacc, g)], [(ktacc, g)])
                    V(lambda e, g=g: e.scalar_tensor_tensor(out=ktb[:, g, :], in0=self.ident[:], scalar=dtab[:, g:g + 1],
                                                              in1=ktacc[:, g, :], op0=ALU.mult, op1=ALU.add),
                      [self.ident, dtab, (ktacc, g)], [ktb])
            if d == 1:
                P.dma("sp", self.S_kt[l], ktb[:], r=[ktb], w=[("S_kt", l)])
            if self.cut == 7:
                continue
            if d == 0:
                cprod(pcr, pci, 9, 1, Cr, Ci, True, [Cr, Ci])
            else:
                cprod(pcr, pci, 16, -1, Cr, Ci, True, [Cr, Ci])
            V(lambda e: e.tensor_copy(out=wob[:, :, 0, :], in_=pcr[:].rearrange("p g j c -> p g (j c)")), [pcr], [wob])
            V(lambda e: e.tensor_copy(out=wob[:, :, 1, :], in_=pci[:].rearrange("p g j c -> p g (j c)")), [pci], [wob])
            P.dma("sp", self.S_wo[l][:, d], wob[:], r=[wob], w=[("S_wo", l, d)])
        V(lambda e: e.memset(self.a8mag[:, 0, 0:1], 0.0) if False else e.tensor_copy(out=self.a8mag[:, 0, 0:1], in_=self.a8mag[:, 0, 0:1]), [self.a8mag], [self.a8mag, "ssm_done"])
        P.release(mk0)

    def dbg_out(self, name, tile_ap, keys):
        if name in self.O:
            self.P.dma("sp", self.O[name], tile_ap, r=keys, w=[("dbg", name)])

    def load_x(self, U):
        P = self.P
        T = U["T"]
        mk0 = P.mark()
        xst = [P.alloc(f"xst{i}", [4, D], F32) for i in range(2)]
        for blk in range(T // 512):
            st = xst[blk % 2]
            P.dma("sp", st[:], U["x"][blk * 512:(blk + 1) * 512, :].rearrange("(i p) f -> p i f", p=128), w=[st])
            for c in range(8):
                bk = P.next_bank(); ps = P.bank(bk)
                for i in range(4):
                    P.op("pe", lambda e, st=st, i=i, c=c, ps=ps: e.transpose(
                        out=ps[:, i * 128:(i + 1) * 128], in_=st[:, i, c * 128:(c + 1) * 128], identity=self.ident[:]),
                        r=[st, self.ident], w=[("ps", bk)])
                eng = "act" if c % 2 else "dve"
                dst = self.xT[:, c, blk * 512:(blk + 1) * 512]
                if eng == "act":
                    P.op("act", lambda e, dst=dst, ps=ps: e.copy(out=dst, in_=ps[:, :]), r=[("ps", bk)], w=[(self.xT, c, blk)])
                else:
                    P.op("dve", lambda e, dst=dst, ps=ps: e.tensor_copy(out=dst, in_=ps[:, :]), r=[("ps", bk)], w=[(self.xT, c, blk)])
        P.release(mk0)

    def xkeys(self, t0, n, cs=range(8)):
        return [(self.xT, c, b) for c in cs for b in range(t0 // 512, (t0 + n - 1) // 512 + 1)]

    def norm_cols(self, t0, n):
        P = self.P
        rt, rstd = self.n_rt, self.n_rstd
        bk = P.next_bank(); ps = P.bank(bk)
        for c in range(8):
            sqb = self.n_sqb[c % 2]
            P.op("act", lambda e, c=c, sqb=sqb: e.activation(out=sqb[:, 0:n], in_=self.xT[:, c, t0:t0 + n], func=AF.Square),
                 r=self.xkeys(t0, n, [c]), w=[sqb])
            P.op("pe", lambda e, c=c, ps=ps, sqb=sqb: e.matmul(ps[:, 0:n], lhsT=self.onesbf[:], rhs=sqb[:, 0:n], start=(c == 0), stop=(c == 7)),
                 r=[sqb, self.onesbf], w=[("ps", bk)])
        P.op("act", lambda e, ps=ps: e.activation(out=rt[:, 0:n], in_=ps[:, 0:n], func=AF.Sqrt, scale=1.0 / D, bias=self.epsT[:, 0:1]),
             r=[("ps", bk), self.epsT], w=[rt])
        P.op("dve", lambda e: e.reciprocal(out=rstd[:, 0:n], in_=rt[:, 0:n]), r=[rt], w=[rstd])
        return rstd

    def norm_mod(self, U, l, which, t0, n, dst):
        P = self.P
        v = U["v"]
        rstd = self.norm_cols(t0, n)
        gsc = self.gsc1 if which == 1 else self.gsc2
        shb = 0 if which == 1 else 24
        for c in range(8):
            tmp = self.n_tmp[c % 2]
            P.op("dve", lambda e, c=c, tmp=tmp: e.tensor_tensor(out=tmp[:, 0:n], in0=self.xT[:, c, t0:t0 + n], in1=rstd[:, 0:n], op=ALU.mult),
                 r=self.xkeys(t0, n, [c]) + [rstd], w=[tmp])
            d_ap, d_keys = dst(c)
            P.op("act", lambda e, c=c, tmp=tmp, d_ap=d_ap: e.activation(
                out=d_ap, in_=tmp[:, 0:n], func=AF.Identity, scale=gsc[:, l, c, v:v + 1], bias=self.modT[:, l, shb + c, v:v + 1]),
                r=[tmp, (gsc, l), (self.modT, l)], w=d_keys)

    def proj(self, wt, wcols, hT, n, kc_n=8, wk=None):
        P = self.P
        bk = P.next_bank(); ps = P.bank(bk)
        w0, w1 = wcols
        for kc in range(kc_n):
            P.op("pe", lambda e, kc=kc, ps=ps: e.matmul(ps[0:(w1 - w0), 0:n], lhsT=wt[:, kc, w0:w1], rhs=hT[:, kc, 0:n],
                                                          start=(kc == 0), stop=(kc == kc_n - 1)),
                 r=(wk if wk is not None else [wt]) + [hT], w=[("ps", bk)])
        return bk, ps

    def ssm_pass(self, U, l, ssmT):
        P, C, I = self.P, self.C, self.I
        T, NSEQ, L = U["T"], U["NSEQ"], U["L"]
        CT, Cq = T // 8, L // 8
        mk0 = P.mark()
        uT = P.alloc("uT", [2, T], BF16)
        mk1 = P.mark()
        wu = P.alloc("wu", [8, 256], BF16)
        hT = P.alloc("hT", [8, 512], BF16)
        P.dma("sp", wu[:], self.W["win"][l][:, :, 1536:1792], r=self.wkeys("win", l, 1536, 1792), w=[wu])
        for blk in range(T // 512):
            self.norm_mod(U, l, 1, blk * 512, 512, lambda c: (hT[:, c, :], [hT]))
            for oc in range(2):
                bk, ps = self.proj(wu, (oc * 128, oc * 128 + 128), hT, 512)
                P.op("dve", lambda e, oc=oc, blk=blk, ps=ps: e.tensor_copy(out=uT[:, oc, blk * 512:(blk + 1) * 512], in_=ps[:, :]),
                     r=[("ps", bk)], w=[(uT, oc, blk)])
        P.release(mk1)
        UK = [(uT, oc, b) for oc in range(2) for b in range(T // 512)]
        kt = P.alloc("kt", [16, 128], BF16); ws = P.alloc("ws", [2, 16, 128], BF16); wo = P.alloc("wo", [2, 8, 2, 128], BF16)
        xsel = P.alloc("xsel", [8, 240], BF16); ysel = P.alloc("ysel", [8, 128], BF16)
        P.dma("sp", kt[:], self.S_kt[l], r=[("S_kt", l)], w=[kt])
        P.dma("sp", ws[:], self.S_ws[l], r=[("S_ws", l, 0), ("S_ws", l, 1)], w=[ws])
        P.dma("sp", wo[:], self.S_wo[l], r=[("S_wo", l, 0), ("S_wo", l, 1)], w=[wo])
        P.dma("sp", xsel[:], C["xsel"], w=[xsel])
        P.dma("sp", ysel[:], C["ysel"], w=[ysel])
        X = P.alloc("X", [16, CT], BF16)
        for g in range(16):
            bk = P.next_bank(); ps = P.bank(bk)
            for s in range(8):
                rhs = mk(uT[:, g // 8, s:s + 1], [[8, CT]])
                P.op("pe", lambda e, g=g, s=s, rhs=rhs, ps=ps: e.matmul(
                    ps[:, 0:CT], lhsT=xsel[:, g % 8, (7 - s) * 16:(7 - s) * 16 + 128], rhs=rhs, start=(s == 0), stop=(s == 7)),
                    r=UK + [xsel], w=[("ps", bk)])
            P.op("act", lambda e, g=g, ps=ps: e.copy(out=X[:, g, :], in_=ps[:, 0:CT]), r=[("ps", bk)], w=[(X, g)])
        S1 = Cq + 1
        Hb = P.alloc("Hb", [2, 2, 8, NSEQ * S1], BF16)
        fin = P.alloc("fin", [NSEQ, 2, 2, 8], F32)
        mk2 = P.mark()
        Et = P.alloc("Et2", [2, 8, 256], F32)
        tt_ = [P.alloc(f"l2t{i}", [Cq], F32) for i in range(6)]
        h0 = P.alloc("h0", [2, 2, 8], F32)
        gi0 = P.alloc("gi0", [2, 2, 8], F32)
        hq = P.alloc("hq", [4, 8], F32)
        if U["lat"]:
            for d in range(2):
                for comp, nm in enumerate(("sre", "sim")):
                    for h in range(2):
                        src = I[nm][l, d]
                        P.dma("sp", h0[64 * h:64 * h + 64, d, comp, :], bass.AP(src.tensor, src.offset + h * 64, [[1, 64], [128, 8]]),
                              w=[h0], allow_slow_non_contiguous=True)
            for d in range(2):
                ld = l * 2 + d
                er, ei = self.e1[:, 0, ld, :], self.e1[:, 1, ld, :]
                hr, hi = h0[:, d, 0, :], h0[:, d, 1, :]
                ops = [(hq[:, 0, :], er, hr, ALU.mult), (hq[:, 1, :], ei, hi, ALU.mult), (gi0[:, d, 0, :], hq[:, 0, :], hq[:, 1, :], ALU.subtract),
                       (hq[:, 2, :], er, hi, ALU.mult), (hq[:, 3, :], ei, hr, ALU.mult), (gi0[:, d, 1, :], hq[:, 2, :], hq[:, 3, :], ALU.add)]
                for (o_, a_, b_, op_) in ops:
                    P.op("dve", lambda e, o_=o_, a_=a_, b_=b_, op_=op_: e.tensor_tensor(out=o_, in0=a_, in1=b_, op=op_),
                         r=[h0, hq, self.e1, gi0], w=[hq, gi0])
        else:
            P.op("dve", lambda e: e.memset(h0[:], 0.0), w=[h0])
            P.op("dve", lambda e: e.memset(gi0[:], 0.0), w=[gi0])
        for d in range(2):
            ld = l * 2 + d
            P.dma("sp", Et[:], self.S_e[l][:, d], r=[("S_e", l, d)], w=[Et])
            for gg in range(8):
                bk = P.next_bank(); ps = P.bank(bk)
                for h in range(2):
                    g = 2 * gg + h
                    for comp in range(2):
                        P.op("pe", lambda e, g=g, h=h, comp=comp, d=d, ps=ps: e.matmul(
                            ps[64 * h:64 * h + 64, comp * CT:(comp + 1) * CT], lhsT=ws[:, d, g, comp * 64:(comp + 1) * 64], rhs=X[:, g, :],
                            start=True, stop=True, tile_position=(0, 64 * h)), r=[ws, (X, g)], w=[("ps", bk)])
                for sq in range(NSEQ):
                    def seqview(base):
                        a = ps[:, base + sq * Cq: base + (sq + 1) * Cq]
                        if d == 0:
                            return a
                        return mk(ps[:, base + (sq + 1) * Cq - 1: base + (sq + 1) * Cq], [[-1, Cq]])
                    Sr, Si = seqview(0), seqview(CT)
                    Ec, Es = Et[:, 0, gg, 0:Cq], Et[:, 1, gg, 0:Cq]
                    t1, t2, t3, t4, t5, t6 = [t[:, 0:Cq] for t in tt_]
                    TK = lambda i: [tt_[i]]
                    def vop(o_, a_, b_, op_, r, w):
                        P.op("dve", lambda e: e.tensor_tensor(out=o_, in0=a_, in1=b_, op=op_), r=r, w=w)
                    vop(t1, Sr, Ec, ALU.mult, [("ps", bk), Et], TK(0))
                    vop(t2, Si, Es, ALU.mult, [("ps", bk), Et], TK(1))
                    vop(t5, t1, t2, ALU.add, TK(0) + TK(1), TK(4))
                    vop(t3, Si, Ec, ALU.mult, [("ps", bk), Et], TK(2))
                    vop(t4, Sr, Es, ALU.mult, [("ps", bk), Et], TK(3))
                    vop(t6, t3, t4, ALU.subtract, TK(2) + TK(3), TK(5))
                    rr = mk(self.a8mag[:, ld, gg:gg + 1], [[0, Cq]])
                    P.op("dve", lambda e, rr=rr, t5=t5, t1=t1, d=d, gg=gg: e.tensor_tensor_scan(
                        out=t1, data0=rr, data1=t5, initial=gi0[:, d, 0, gg:gg + 1], op0=ALU.mult, op1=ALU.add),
                        r=TK(4) + [self.a8mag, gi0], w=TK(0))
                    P.op("dve", lambda e, rr=rr, t6=t6, t2=t2, d=d, gg=gg: e.tensor_tensor_scan(
                        out=t2, data0=rr, data1=t6, initial=gi0[:, d, 1, gg:gg + 1], op0=ALU.mult, op1=ALU.add),
                        r=TK(5) + [self.a8mag, gi0], w=TK(1))
                    vop(t3, t1, Ec, ALU.mult, TK(0) + [Et], TK(2))
                    vop(t4, t2, Es, ALU.mult, TK(1) + [Et], TK(3))
                    vop(t5, t3, t4, ALU.subtract, TK(2) + TK(3), TK(4))
                    vop(t3, t2, Ec, ALU.mult, TK(1) + [Et], TK(2))
                    vop(t4, t1, Es, ALU.mult, TK(0) + [Et], TK(3))
                    vop(t6, t3, t4, ALU.add, TK(2) + TK(3), TK(5))
                    for comp, tH in ((0, t5), (1, t6)):
                        base = Hb[:, d, comp, gg, :]
                        if d == 0:
                            dstv = base[:, sq * S1 + 1: sq * S1 + 1 + Cq]
                            init_slot = base[:, sq * S1: sq * S1 + 1]
                        else:
                            dstv = mk(base[:, sq * S1 + Cq - 1: sq * S1 + Cq], [[-1, Cq]])
                            init_slot = base[:, sq * S1 + Cq: sq * S1 + Cq + 1]
                        P.op("act", lambda e, dstv=dstv, tH=tH: e.copy(out=dstv, in_=tH), r=[tt_[4 + comp]], w=[(Hb, d, gg)])
                        P.op("act", lambda e, init_slot=init_slot, d=d, comp=comp, gg=gg: e.copy(out=init_slot, in_=h0[:, d, comp, gg:gg + 1]),
                             r=[h0], w=[(Hb, d, gg)])
                        if not U["lat"]:
                            P.op("act", lambda e, sq=sq, d=d, comp=comp, gg=gg, tH=tH: e.copy(
                                out=fin[:, sq, d, comp, gg:gg + 1], in_=tH[:, Cq - 1:Cq]), r=[tt_[4 + comp]], w=[fin])
        if not U["lat"]:
            for sq in range(NSEQ):
                for d in range(2):
                    for comp, nm in enumerate(("nsr", "nsi")):
                        dst = self.O[nm][sq, l, d]
                        P.dma("sp", bass.AP(dst.tensor, dst.offset, [[1, 128], [128, 8]]), fin[:, sq, d, comp, :], r=[fin],
                              w=[("out", nm, sq, l, d)], allow_slow_non_contiguous=True)
        P.release(mk2)
        HK = [(Hb, d, gg) for d in range(2) for gg in range(8)]
        NB = T // 512
        zT = uT
        yexp = [P.alloc(f"yexp{i}", [T], BF16) for i in range(2)]
        wglu = P.alloc("wglu", [2, 256], BF16)
        sg = [P.alloc(f"sg{i}", [512], BF16) for i in range(2)]
        P.dma("sp", wglu[:], self.W["wglu"][l], r=self.wkeys("wglu", l, 0, 256), w=[wglu])
        acc = []
        for tb in range(NB):
            b = P.next_bank(); P.reserved_banks.add(b); acc.append(b)
        for chunk in range(2):
            for gi in range(8):
                g = chunk * 8 + gi
                h, gg = g % 2, g // 2
                bk = P.next_bank(); ps = P.bank(bk)
                P.op("pe", lambda e, g=g, ps=ps: e.matmul(ps[:, 0:CT], lhsT=kt[:, g, :], rhs=X[:, g, :], start=True, stop=False),
                     r=[kt, (X, g)], w=[("ps", bk)])
                k = 0
                for d in range(2):
                    for comp in range(2):
                        off = 0 if d == 0 else 1
                        rhs = mk(Hb[64 * h:64 * h + 64, d, comp, gg, off:off + 1], [[S1, NSEQ], [1, Cq]])
                        outv = ps[:, 0:CT].rearrange("p (a b) -> p a b", b=Cq)
                        k += 1
                        P.op("pe", lambda e, h=h, d=d, comp=comp, gg=gg, rhs=rhs, outv=outv, k=k: e.matmul(
                            outv, lhsT=wo[64 * h:64 * h + 64, d, gg, comp, :], rhs=rhs, start=False, stop=(k == 4)),
                            r=[wo] + HK, w=[("ps", bk)])
                ye = yexp[g % 2]
                in0 = mk(ps[:, 0:1], [[1, CT], [0, 8]])
                in1 = mk(self.maskj[:, 0:1], [[0, CT], [1, 8]])
                P.op("dve", lambda e, ye=ye, in0=in0, in1=in1: e.tensor_tensor(
                    out=ye[:].rearrange("p (a b) -> p a b", b=8), in0=in0, in1=in1, op=ALU.mult),
                    r=[("ps", bk), self.maskj], w=[ye])
                for tb in range(NB):
                    P.op("pe", lambda e, gi=gi, tb=tb, ye=ye: e.matmul(
                        P.bank(acc[tb])[:, :], lhsT=ysel[:, gi, :], rhs=ye[:, tb * 512:(tb + 1) * 512], start=(gi == 0), stop=(gi == 7)),
                        r=[ysel, ye], w=[("ps", acc[tb])])
            for tb in range(NB):
                P.op("act", lambda e, chunk=chunk, tb=tb: e.activation(
                    out=zT[:, chunk, tb * 512:(tb + 1) * 512], in_=P.bank(acc[tb])[:, :], func=AF.Gelu),
                    r=[("ps", acc[tb])], w=[(uT, chunk, tb)])
        for b in acc:
            P.reserved_banks.discard(b)
        for tb in range(NB):
            for oc in range(2):
                bk = P.next_bank(); ps = P.bank(bk)
                for kc in range(2):
                    P.op("pe", lambda e, kc=kc, oc=oc, tb=tb, ps=ps: e.matmul(
                        ps[:, :], lhsT=wglu[:, kc, oc * 128:(oc + 1) * 128], rhs=zT[:, kc, tb * 512:(tb + 1) * 512],
                        start=(kc == 0), stop=(kc == 1)), r=[wglu, (uT, kc, tb)], w=[("ps", bk)])
                s_ = sg[(tb * 2 + oc) % 2]
                P.op("act", lambda e, s_=s_, ps=ps: e.activation(out=s_[:], in_=ps[:, :], func=AF.Sigmoid), r=[("ps", bk)], w=[s_])
                P.op("dve", lambda e, s_=s_, oc=oc, tb=tb: e.tensor_tensor(
                    out=ssmT[:, oc, tb * 512:(tb + 1) * 512], in0=zT[:, oc, tb * 512:(tb + 1) * 512], in1=s_[:], op=ALU.mult),
                    r=[s_, (uT, oc, tb)], w=[(ssmT, oc, tb)])
        P.release(mk0)

    def kv_pass(self, U, l, krT, vaug, gbT, pT):
        P, C = self.P, self.C
        T, NSEQ, L, lat = U["T"], U["NSEQ"], U["L"], U["lat"]
        mk0 = P.mark()
        win = self.W["win"][l]
        hT = P.alloc("hT", [8, 512], BF16)
        wkd = P.alloc("wkd", [8, 2, 128], BF16)
        wkv = P.alloc("wkv", [8, 256], BF16)
        wg = P.alloc("wg", [8, 768], BF16)
        for kv in range(2):
            for hh in range(2):
                P.dma("sp", wkd[:, :, kv, hh * 64:(hh + 1) * 64], win[:, :, 512 + kv * 64:512 + (kv + 1) * 64],
                      r=self.wkeys("win", l, 512, 640), w=[wkd])
        P.dma("sp", wkv[:], win[:, :, 512:768], r=self.wkeys("win", l, 512, 768), w=[wkv])
        P.dma("sp", wg[:], win[:, :, 768:1536], r=self.wkeys("win", l, 768, 1536), w=[wg])
        if lat:
            wkp = P.alloc("wkp", [8, 2, 128], BF16)
            rope = P.alloc("rope", [2, 512], F32)
            r1 = P.alloc("r1", [512], F32); r2 = P.alloc("r2", [512], F32)
            for b_ in range(2):
                srcv = mk(wkd[:, 0, 0, 0:1], [[64, 32], [32, 2], [1, 16]], off=(1 - b_) * 16)
                dstv = mk(wkp[:, 0, 0, 0:1], [[64, 32], [32, 2], [1, 16]], off=b_ * 16)
                P.op("pool", lambda e, srcv=srcv, dstv=dstv: e.tensor_copy(out=dstv, in_=srcv), r=[wkd], w=[wkp])
        else:
            kvst = [P.alloc(f"kvst{i}", [256], F32) for i in range(2)]
        gct = P.alloc("gct", [512], F32)
        P.op("dve", lambda e: e.memset(vaug[:, :, :, 64:128], 1.0), w=[vaug])
        for blk in range(T // 512):
            t0 = blk * 512
            self.norm_mod(U, l, 1, t0, 512, lambda c: (hT[:, c, :], [hT]))
            if lat:
                P.dma("sp", rope[:], C["rope"][:, :, t0:t0 + 512], w=[rope])
            for kv in range(2):
                bk, ps = self.proj(wkd[:, :, kv, :], (0, 128), hT, 512, wk=[wkd])
                if lat:
                    bk2, ps2 = self.proj(wkp[:, :, kv, :], (0, 128), hT, 512, wk=[wkp])
                    P.op("dve", lambda e, ps=ps: e.tensor_tensor(out=r1[:], in0=ps[:, :], in1=rope[:, 0, :], op=ALU.mult), r=[("ps", bk), rope], w=[r1])
                    P.op("dve", lambda e, ps2=ps2: e.tensor_tensor(out=r2[:], in0=ps2[:, :], in1=rope[:, 1, :], op=ALU.mult), r=[("ps", bk2), rope], w=[r2])
                    P.op("dve", lambda e, kv=kv, t0=t0: e.tensor_tensor(out=krT[:, kv, t0:t0 + 512], in0=r1[:], in1=r2[:], op=ALU.add),
                         r=[r1, r2], w=[(krT, kv, blk)])
                else:
                    P.op("act", lambda e, kv=kv, t0=t0, ps=ps: e.copy(out=krT[:, kv, t0:t0 + 512], in_=ps[:, :]), r=[("ps", bk)], w=[(krT, kv, blk)])
            for i in range(4):
                if self.cut == 12:
                    break
                tile_i = blk * 4 + i
                bk = P.next_bank(); ps = P.bank(bk)
                c0 = 128 if lat else 0
                for kc in range(8):
                    P.op("pe", lambda e, kc=kc, i=i, ps=ps, c0=c0: e.matmul(ps[:, c0:256], lhsT=hT[:, kc, i * 128:(i + 1) * 128], rhs=wkv[:, kc, c0:256],
                                                                              start=(kc == 0), stop=(kc == 7)), r=[hT, wkv], w=[("ps", bk)])
                P.op("dve", lambda e, tile_i=tile_i, ps=ps: e.tensor_copy(out=vaug[:, tile_i, :, 0:64], in_=ps[:, 128:256].rearrange("p (a b) -> p a b", b=64)),
                     r=[("ps", bk)], w=[(vaug, tile_i)])
                if not lat and self.cut != 15:
                    st = kvst[tile_i % 2]
                    P.op("act", lambda e, st=st, ps=ps: e.copy(out=st[:], in_=ps[:, 0:256]), r=[("ps", bk)], w=[st])
                    sq, tl = divmod(tile_i, L // 128)
                    P.dma("sp", self.O["nk"][sq, l, tl * 128:(tl + 1) * 128, :], st[:, 0:128], r=[st], w=[("out", "nk", tile_i, l)])
                    P.dma("sp", self.O["nv"][sq, l, tl * 128:(tl + 1) * 128, :], st[:, 128:256], r=[st], w=[("out", "nv", tile_i, l)])
            for oc in range(6):
                if self.cut in (12, 13):
                    break
                bk, ps = self.proj(wg, (oc * 128, oc * 128 + 128), hT, 512)
                which, c = divmod(oc, 2)
                if which == 0:
                    P.op("act", lambda e, c=c, t0=t0, ps=ps: e.copy(out=gbT[:, c, t0:t0 + 512], in_=ps[:, :]), r=[("ps", bk)], w=[(gbT, c, blk)])
                elif which == 1:
                    P.op("act", lambda e, c=c, t0=t0, ps=ps: e.copy(out=pT[:, c, t0:t0 + 512], in_=ps[:, :]), r=[("ps", bk)], w=[(pT, c, blk)])
                else:
                    P.op("dve", lambda e, c=c, t0=t0, ps=ps: e.tensor_tensor(out=pT[:, c, t0:t0 + 512], in0=ps[:, :], in1=pT[:, c, t0:t0 + 512], op=ALU.mult),
                         r=[("ps", bk), (pT, c, blk)], w=[(pT, c, blk)])
        P.release(mk0)

    def mix_pass(self, U, l, krT, vaug, gbT, pT, ssmT):
        P, C, I = self.P, self.C, self.I
        T, NSEQ, L, lat, v = U["T"], U["NSEQ"], U["L"], U["lat"], U["v"]
        mk0 = P.mark()
        win = self.W["win"][l]
        hT = P.alloc("hT", [8, 512], BF16)
        wq = P.alloc("wq", [8, 512], BF16)
        wo_ = P.alloc("wo_", [8, D], BF16)
        qT = P.alloc("qT", [4, 512], BF16)
        atT = P.alloc("atT", [4, 512], BF16)
        cvT = P.alloc("cvT", [2, 512], BF16)
        cacc = P.alloc("cacc", [512], F32)
        PT = [P.alloc(f"PT{i}", [512], BF16) for i in range(3)]
        Rt = P.alloc("Rt", [512], F32)
        P.dma("sp", wq[:], win[:, :, 0:512], r=self.wkeys("win", l, 0, 512), w=[wq])
        P.dma("sp", wo_[:], self.W["wout"][l], r=self.wkeys("wout", l, 0, D), w=[wo_])
        if lat:
            wqp = P.alloc("wqp", [8, 512], BF16)
            rope = P.alloc("rope", [2, 512], F32)
            r1 = P.alloc("r1", [512], F32); r2 = P.alloc("r2", [512], F32)
            maskb = P.alloc("maskb", [2, 512], BF16)
            ckd = P.alloc("ckd", [2, PAST], BF16)
            cva = P.alloc("cva", [4, 2, 128], BF16)
            cst = P.alloc("cst", [4, 2, 64], F32)
            for b_ in range(2):
                srcv = mk(wq[:, 0, 0:1], [[64, 64], [32, 2], [1, 16]], off=(1 - b_) * 16)
                dstv = mk(wqp[:, 0, 0:1], [[64, 64], [32, 2], [1, 16]], off=b_ * 16)
                P.op("pool", lambda e, srcv=srcv, dstv=dstv: e.tensor_copy(out=dstv, in_=srcv), r=[wq], w=[wqp])
            P.dma("sp", maskb[:], C["maskb"], w=[maskb])
            m01 = P.alloc("m01", [2, 256], BF16)
            P.op("dve", lambda e: e.tensor_single_scalar(out=m01[:], in_=maskb[:, :, 0:256], scalar=0.0, op=ALU.is_equal), r=[maskb], w=[m01])
            P.op("dve", lambda e: e.memset(cva[:, :, :, 64:128], 1.0), w=[cva])
            P.dma("sp", cst[:], I["cv"][l].rearrange("(i p) (k d) -> p i k d", p=128, d=64), w=[cst])
            P.op("dve", lambda e: e.tensor_copy(out=cva[:, :, :, 0:64], in_=cst[:]), r=[cst], w=[cva])
            for kv in range(2):
                for hh in range(2):
                    P.dma("sp", cst[:, :, hh, :], I["ck"][l][:, kv * 64:(kv + 1) * 64].rearrange("(i p) d -> p i d", p=128), r=[cva], w=[cst])
                bk = P.next_bank(); ps = P.bank(bk)
                for i in range(4):
                    P.op("pe", lambda e, i=i, ps=ps: e.transpose(out=ps[:, i * 128:(i + 1) * 128], in_=cst[:, i, :, :].rearrange("p a b -> p (a b)"),
                                                                 identity=self.ident[:]), r=[cst, self.ident], w=[("ps", bk)])
                P.op("act", lambda e, kv=kv, ps=ps: e.copy(out=ckd[:, kv, :], in_=ps[:, :]), r=[("ps", bk)], w=[ckd])
        po_banks = []
        for i in range(2):
            b = P.next_bank(); P.reserved_banks.add(b); po_banks.append(b)
        npo = 0
        TPS = L // 128
        for blk in range(T // 512):
            t0 = blk * 512
            self.norm_mod(U, l, 1, t0, 512, lambda c: (hT[:, c, :], [hT]))
            if lat:
                P.dma("sp", rope[:], C["rope"][:, :, t0:t0 + 512], w=[rope])
            for hc in range(4):
                bk, ps = self.proj(wq, (hc * 128, hc * 128 + 128), hT, 512)
                if lat:
                    bk2, ps2 = self.proj(wqp, (hc * 128, hc * 128 + 128), hT, 512)
                    P.op("dve", lambda e, ps=ps: e.tensor_tensor(out=r1[:], in0=ps[:, :], in1=rope[:, 0, :], op=ALU.mult), r=[("ps", bk), rope], w=[r1])
                    P.op("dve", lambda e, ps2=ps2: e.tensor_tensor(out=r2[:], in0=ps2[:, :], in1=rope[:, 1, :], op=ALU.mult), r=[("ps", bk2), rope], w=[r2])
                    P.op("dve", lambda e, hc=hc: e.tensor_tensor(out=qT[:, hc, :], in0=r1[:], in1=r2[:], op=ALU.add), r=[r1, r2], w=[(qT, hc)])
                else:
                    P.op("act", lambda e, hc=hc, ps=ps: e.copy(out=qT[:, hc, :], in_=ps[:, :]), r=[("ps", bk)], w=[(qT, hc)])
            if lat:
                pieces = [(t0, t0 + 512, t0 > 0, t0 + 512 < T)]
            else:
                pieces = [(t0 + i * L, t0 + (i + 1) * L, False, False) for i in range(512 // L)]
            for c in range(2):
                for (a0, a1, hl, hr) in pieces:
                    n = a1 - a0
                    o0 = a0 - t0
                    pk = [(pT, c, b) for b in range(max(0, blk - 1), min(T // 512, blk + 2))]
                    P.op("dve", lambda e, c=c, a0=a0, a1=a1, o0=o0, n=n: e.tensor_scalar_mul(
                        out=cacc[:, o0:o0 + n], in0=pT[:, c, a0:a1], scalar1=self.scw[:, l, c, 1:2]), r=pk + [self.scw], w=[cacc])
                    a = 0 if hl else 1
                    P.op("dve", lambda e, c=c, a0=a0, a1=a1, o0=o0, n=n, a=a: e.scalar_tensor_tensor(
                        out=cacc[:, o0 + a:o0 + n], in0=pT[:, c, a0 + a - 1:a1 - 1], scalar=self.scw[:, l, c, 0:1],
                        in1=cacc[:, o0 + a:o0 + n], op0=ALU.mult, op1=ALU.add), r=pk + [self.scw, cacc], w=[cacc])
                    b_ = 0 if hr else 1
                    P.op("dve", lambda e, c=c, a0=a0, a1=a1, o0=o0, n=n, b_=b_: e.scalar_tensor_tensor(
                        out=cacc[:, o0:o0 + n - b_], in0=pT[:, c, a0 + 1:a1 + 1 - b_], scalar=self.scw[:, l, c, 2:3],
                        in1=cacc[:, o0:o0 + n - b_], op0=ALU.mult, op1=ALU.add), r=pk + [self.scw, cacc], w=[cacc])
                P.op("dve", lambda e, c=c, t0=t0: e.tensor_tensor(out=cvT[:, c, :], in0=cacc[:], in1=gbT[:, c, t0:t0 + 512], op=ALU.mult),
                     r=[cacc, (gbT, c, blk)], w=[(cvT, c)])
            for qi in range(4):
                qt = blk * 4 + qi
                sq, ql = divmod(qt, TPS)
                for kv in range(2):
                    srcs = []
                    if lat:
                        for kt_, m in ((ql - 1, 0), (ql, None), (ql + 1, 1)):
                            if 0 <= kt_ < TPS:
                                srcs.append((krT[:, kv, kt_ * 128:(kt_ + 1) * 128], [(krT, kv, kt_ // 4)], vaug[:, kt_, kv, :], [(vaug, kt_)], m))
                        for i in range(4):
                            srcs.append((ckd[:, kv, i * 128:(i + 1) * 128], [ckd], cva[:, i, kv, :], [cva], None))
                    else:
                        for kt_ in range(TPS):
                            gt = sq * TPS + kt_
                            srcs.append((krT[:, kv, gt * 128:(gt + 1) * 128], [(krT, kv, gt // 4)], vaug[:, gt, kv, :], [(vaug, gt)], None))
                    pob = po_banks[npo % 2]; npo += 1
                    po = P.bank(pob)
                    ns = len(srcs)

                    def S(i):
                        kT_ap, kkeys, _, _, m = srcs[i]
                        pt = PT[i % 3]
                        for hh in range(2):
                            bk = P.next_bank(); ps = P.bank(bk)
                            for j in range(2):
                                hq = 2 * j + hh
                                h = kv * 4 + hq
                                P.op("pe", lambda e, j=j, h=h, hh=hh, ps=ps, kT_ap=kT_ap: e.matmul(
                                    ps[:, j * 128:(j + 1) * 128], lhsT=kT_ap[64 * hh:64 * hh + 64, :],
                                    rhs=qT[64 * hh:64 * hh + 64, h // 2, qi * 128:(qi + 1) * 128], start=True, stop=True),
                                    r=kkeys + [(qT, h // 2)], w=[("ps", bk)])
                            P.op("act", lambda e, pt=pt, ps=ps, hh=hh: e.activation(out=pt[:, hh * 256:(hh + 1) * 256], in_=ps[:, 0:256], func=AF.Exp, scale=0.125),
                                 r=[("ps", bk)], w=[pt])
                            if m is not None:
                                P.op("pool", lambda e, pt=pt, hh=hh, m=m: e.tensor_tensor(out=pt[:, hh * 256:(hh + 1) * 256], in0=pt[:, hh * 256:(hh + 1) * 256],
                                                                                           in1=m01[:, m, :], op=ALU.mult), r=[pt, m01], w=[pt])

                    def PV(i):
                        _, _, v_ap, vkeys, _ = srcs[i]
                        pt = PT[i % 3]
                        P.op("pe", lambda e, v_ap=v_ap, pt=pt, i=i: e.matmul(po[:, :], lhsT=v_ap, rhs=pt[:], start=(i == 0), stop=False),
                             r=vkeys + [pt], w=[("ps", pob)])
                    S(0)
                    for i in range(ns):
                        if i + 1 < ns:
                            S(i + 1)
                        PV(i)
                    es_rhs = mk(self.esrow[0:1, l, kv * 4, 0:1], [[128, 2], [256, 2], [1, 128]])
                    P.op("pe", lambda e, es_rhs=es_rhs: e.matmul(po[:, :].rearrange("p (a b c) -> p a b c", a=2, b=2), lhsT=self.vsink[0:1, :], rhs=es_rhs,
                                                                  start=False, stop=True), r=[self.vsink, self.esrow], w=[("ps", pob)])
                    P.op("dve", lambda e: e.reciprocal(out=Rt[0:64, :], in_=po[64:128, :]), r=[("ps", pob)], w=[Rt])
                    for par in range(2):
                        in0 = po[0:64, par * 256:(par + 1) * 256].rearrange("p (a b) -> p a b", b=128)
                        in1 = Rt[0:64, par * 256:(par + 1) * 256].rearrange("p (a b) -> p a b", b=128)
                        outv = atT[64 * par:64 * par + 64, kv * 2:kv * 2 + 2, qi * 128:(qi + 1) * 128]
                        P.op("dve", lambda e, in0=in0, in1=in1, outv=outv: e.tensor_tensor(out=outv, in0=in0, in1=in1, op=ALU.mult),
                             r=[("ps", pob), Rt], w=[(atT, kv)])
            rhs_list = [(atT[:, i, :], [(atT, 0), (atT, 1)]) for i in range(4)] + [(cvT[:, i, :], [(cvT, i)]) for i in range(2)] + \
                       [(ssmT[:, i, t0:t0 + 512], [(ssmT, i, blk)]) for i in range(2)]
            for oc in range(8):
                bk = P.next_bank(); ps = P.bank(bk)
                for kc, (rap, rkeys) in enumerate(rhs_list):
                    P.op("pe", lambda e, kc=kc, oc=oc, rap=rap, ps=ps: e.matmul(ps[:, :], lhsT=wo_[:, kc, oc * 128:(oc + 1) * 128], rhs=rap,
                                                                                  start=(kc == 0), stop=(kc == 7)), r=[wo_] + rkeys, w=[("ps", bk)])
                P.op("dve", lambda e, oc=oc, t0=t0, ps=ps: e.scalar_tensor_tensor(
                    out=self.xT[:, oc, t0:t0 + 512], in0=ps[:, :], scalar=self.modT[:, l, 16 + oc, v:v + 1], in1=self.xT[:, oc, t0:t0 + 512],
                    op0=ALU.mult, op1=ALU.add), r=[("ps", bk), (self.modT, l), (self.xT, oc, blk)], w=[(self.xT, oc, blk)])
        for b in po_banks:
            P.reserved_banks.discard(b)
        P.release(mk0)

    def ffn_pass(self, U, l):
        P = self.P
        T, NSEQ, L, lat, v = U["T"], U["NSEQ"], U["L"], U["lat"], U["v"]
        mk0 = P.mark()
        wdn = P.alloc("wdn", [22, D], BF16)
        for c in range(22):
            P.dma("sp", wdn[:, c, :], self.W["wdn"][l][:, c, :], r=self.wkeys("wdn", l, 0, D, [c]), w=[(wdn, c)])
        h2T = P.alloc("h2T", [8, 2, 258], BF16)
        halo = P.alloc("halo", [8, 2], BF16)
        gated = P.alloc("gated", [22, 2, 256], BF16)
        wus = [P.alloc(f"wus{i}", [8, 256], BF16) for i in range(3)]
        ca = [P.alloc(f"ca{i}", [256], F32) for i in range(2)]
        cg = [P.alloc(f"cg{i}", [256], F32) for i in range(2)]
        sgt = [P.alloc(f"sgt{i}", [256], F32) for i in range(2)]
        pieces = []
        for t0 in range(0, T, 256):
            sq_start = (t0 % L) == 0
            sq_end = ((t0 + 256) % L) == 0
            pieces.append((t0, t0 + 256, not sq_start, not sq_end))
        nwu = 0
        for sb in range(len(pieces) // 2):
            pcs = pieces[2 * sb:2 * sb + 2]
            for pi, (a0, a1, hl, hr) in enumerate(pcs):
                if pi == 0 and hl:
                    n = (a1 - a0) + int(hr)
                    P.op("pool", lambda e: e.tensor_copy(out=h2T[:, :, 0, 0:1], in_=halo[:, :, 0:1]), r=[halo], w=[(h2T, 0)])
                    self.norm_mod(U, l, 2, a0, n, lambda c, n=n: (h2T[:, c, 0, 1:1 + n], [(h2T, 0)]))
                else:
                    n = (a1 - a0) + int(hl) + int(hr)
                    self.norm_mod(U, l, 2, a0 - int(hl), n, lambda c, pi=pi, n=n: (h2T[:, c, pi, 0:n], [(h2T, pi)]))
            a0_, a1_, hl_l, hr_l = pcs[1]
            lastcol = int(hl_l) + (a1_ - a0_) - 1
            P.op("pool", lambda e, lastcol=lastcol: e.tensor_copy(out=halo[:, :, 0:1], in_=h2T[:, :, 1, lastcol:lastcol + 1]), r=[(h2T, 1)], w=[halo])
            for c in range(22):
                wu = wus[nwu % 3]; nwu += 1
                P.dma("sp", wu[:, :, 0:128], self.W["wup"][l][:, :, c * 128:(c + 1) * 128], r=self.wkeys("wup", l, c * 128, (c + 1) * 128), w=[wu])
                P.dma("sp", wu[:, :, 128:256], self.W["wup"][l][:, :, DFF + c * 128:DFF + (c + 1) * 128],
                      r=self.wkeys("wup", l, DFF + c * 128, DFF + (c + 1) * 128), w=[wu])
                for pi, (a0, a1, hl, hr) in enumerate(pcs):
                    m = a1 - a0
                    hl_, hr_ = int(hl), int(hr)
                    n = m + hl_ + hr_
                    res = []
                    for half, (acc_t, ch) in enumerate(((ca[pi], c), (cg[pi], 22 + c))):
                        bk = P.next_bank(); ps = P.bank(bk)
                        for kc in range(8):
                            P.op("pe", lambda e, kc=kc, half=half, ps=ps, pi=pi, n=n: e.matmul(
                                ps[:, 0:n], lhsT=wu[:, kc, half * 128:(half + 1) * 128], rhs=h2T[:, kc, pi, 0:n], start=(kc == 0), stop=(kc == 7)),
                                r=[wu, (h2T, pi)], w=[("ps", bk)])
                        w_ = self.fcw[:, l, ch, :]
                        P.op("act", lambda e, acc_t=acc_t, ps=ps, w_=w_: e.activation(
                            out=acc_t[:, 0:m], in_=ps[:, hl_:hl_ + m], func=AF.Identity, scale=w_[:, 1:2]), r=[("ps", bk), self.fcw], w=[acc_t])
                        a = 0 if hl else 1
                        P.op("dve", lambda e, acc_t=acc_t, ps=ps, w_=w_, a=a: e.scalar_tensor_tensor(
                            out=acc_t[:, a:m], in0=ps[:, hl_ + a - 1:hl_ + m - 1], scalar=w_[:, 0:1], in1=acc_t[:, a:m], op0=ALU.mult, op1=ALU.add),
                            r=[("ps", bk), self.fcw, acc_t], w=[acc_t])
                        b_ = 0 if hr else 1
                        P.op("dve", lambda e, acc_t=acc_t, ps=ps, w_=w_, b_=b_: e.scalar_tensor_tensor(
                            out=acc_t[:, 0:m - b_], in0=ps[:, hl_ + 1:hl_ + m + 1 - b_], scalar=w_[:, 2:3], in1=acc_t[:, 0:m - b_], op0=ALU.mult, op1=ALU.add),
                            r=[("ps", bk), self.fcw, acc_t], w=[acc_t])
                    sg_ = sgt[pi]
                    P.op("act", lambda e, sg_=sg_, pi=pi: e.activation(out=sg_[:, 0:m], in_=cg[pi][:, 0:m], func=AF.Silu), r=[cg[pi]], w=[sg_])
                    P.op("dve", lambda e, sg_=sg_, pi=pi, c=c: e.tensor_tensor(out=gated[:, c, pi, 0:m], in0=ca[pi][:, 0:m], in1=sg_[:, 0:m], op=ALU.mult),
                         r=[ca[pi], sg_], w=[(gated, c, pi)])
            for pi, (a0, a1, hl, hr) in enumerate(pcs):
                m = a1 - a0
                for oc in range(8):
                    bk = P.next_bank(); ps = P.bank(bk)
                    for c in range(22):
                        P.op("pe", lambda e, c=c, oc=oc, pi=pi, ps=ps: e.matmul(ps[:, 0:m], lhsT=wdn[:, c, oc * 128:(oc + 1) * 128], rhs=gated[:, c, pi, 0:m],
                                                                                  start=(c == 0), stop=(c == 21)), r=[(wdn, c), (gated, c, pi)], w=[("ps", bk)])
                    P.op("dve", lambda e, oc=oc, a0=a0, a1=a1, ps=ps: e.scalar_tensor_tensor(
                        out=self.xT[:, oc, a0:a1], in0=ps[:, 0:m], scalar=self.modT[:, l, 40 + oc, v:v + 1], in1=self.xT[:, oc, a0:a1],
                        op0=ALU.mult, op1=ALU.add), r=[("ps", bk), (self.modT, l)] + self.xkeys(a0, m, [oc]), w=self.xkeys(a0, m, [oc]))
        P.release(mk0)

    def final_pass(self, U):
        P = self.P
        T = U["T"]
        mk0 = P.mark()
        yT = P.alloc("yT", [8, 512], F32)
        yst = [P.alloc(f"yst{i}", [D], F32) for i in range(2)]
        nst = 0
        for blk in range(T // 512):
            t0 = blk * 512
            rstd = self.norm_cols(t0, 512)
            for c in range(8):
                tmp = self.n_tmp[c % 2]
                P.op("dve", lambda e, c=c, tmp=tmp, t0=t0: e.tensor_tensor(out=tmp[:], in0=self.xT[:, c, t0:t0 + 512], in1=rstd[:], op=ALU.mult),
                     r=self.xkeys(t0, 512, [c]) + [rstd], w=[tmp])
                P.op("act", lambda e, c=c, tmp=tmp: e.activation(out=yT[:, c, :], in_=tmp[:], func=AF.Identity, scale=self.nfT[:, c:c + 1]),
                     r=[tmp, self.nfT], w=[(yT, c)])
            for i in range(4):
                st = yst[nst % 2]; nst += 1
                for hf in range(2):
                    bk = P.next_bank(); ps = P.bank(bk)
                    for cc in range(4):
                        c = hf * 4 + cc
                        P.op("pe", lambda e, c=c, cc=cc, i=i, ps=ps: e.transpose(out=ps[:, cc * 128:(cc + 1) * 128], in_=yT[:, c, i * 128:(i + 1) * 128],
                                                                                 identity=self.ident[:]), r=[(yT, c), self.ident], w=[("ps", bk)])
                    if hf == 0:
                        P.op("act", lambda e, st=st, ps=ps: e.copy(out=st[:, 0:512], in_=ps[:, :]), r=[("ps", bk)], w=[(st, 0)])
                    else:
                        P.op("dve", lambda e, st=st, ps=ps: e.tensor_copy(out=st[:, 512:1024], in_=ps[:, :]), r=[("ps", bk)], w=[(st, 1)])
                P.dma("sp", U["y"][t0 + i * 128:t0 + (i + 1) * 128, :], st[:], r=[(st, 0), (st, 1)], w=[("out", "y", U["name"], blk, i)])
        P.release(mk0)

    def run_unit(self, U, layers=(0, 1), passes=("ssm", "kv", "mix", "ffn", "final")):
        P = self.P
        T = U["T"]
        mk0 = P.mark()
        self.n_sqb = [P.alloc(f"n_sqb{i}", [512], BF16) for i in range(2)]
        self.n_rt = P.alloc("n_rt", [512], F32)
        self.n_rstd = P.alloc("n_rstd", [512], F32)
        self.n_tmp = [P.alloc(f"n_tmp{i}", [512], F32) for i in range(2)]
        self.load_x(U)
        for l in layers:
            self.prep_adaln(l)
            mk1 = P.mark()
            ssmT = P.alloc("ssmT", [2, T], BF16)
            if "ssm" in passes:
                self.ssm_pass(U, l, ssmT)
            if "kv" in passes:
                krT = P.alloc("krT", [2, T], BF16)
                vaug = P.alloc("vaug", [T // 128, 2, 128], BF16)
                gbT = P.alloc("gbT", [2, T], BF16)
                pT = P.alloc("pT", [2, T], F32 if False else BF16)
                self.kv_pass(U, l, krT, vaug, gbT, pT)
                if "mix" in passes:
                    self.mix_pass(U, l, krT, vaug, gbT, pT, ssmT)
            P.release(mk1)
            if "ffn" in passes:
                self.ffn_pass(U, l)
        if "final" in passes:
            self.final_pass(U)
        P.release(mk0)

    def build(self):
        P = self.P
        self.xT = P.alloc("xT", [8, 2048], F32)
        self.epsT = P.alloc("epsT", [1], F32)
        P.op("dve", lambda e: e.memset(self.epsT[:], EPS), w=[self.epsT])
        self.persistent()
        self.prep_adaln_setup()
        if "prep" in self.stages or "adaln" in self.stages:
            self.prep_adaln(0)
        self.cast_mark = P.mark()
        bufs_pool = self.alloc_cast_bufs("p")
        mk_a = P.mark()
        bufs_act = self.alloc_cast_bufs("a")
        if "prep" in self.stages or "ssmt" in self.stages:
            self.prep_ssm()
        if "prep" in self.stages or "casts" in self.stages:
            self.prep_casts([0], "act", bufs_act)
        P.release(mk_a)
        if "prep" in self.stages or "casts" in self.stages:
            self.prep_casts([1], "pool", bufs_pool, first_dep=["ssm_done"] if ("prep" in self.stages or "ssmt" in self.stages) else [])
        UP = dict(name="P", T=512, NSEQ=2, L=256, v=0, lat=False, x=self.I["xp"], y=self.O["yp"])
        US = dict(name="S", T=2048, NSEQ=1, L=2048, v=1, lat=True, x=self.I["xs"], y=self.O["ys"])
        if "P" in self.stages:
            self.run_unit(UP, **self.unit_kw.get("P", {}))
        if self.cast_mark is not None:
            P.release(self.cast_mark)
        if "S" in self.stages:
            self.run_unit(US, **self.unit_kw.get("S", {}))
        if self.post is not None:
            self.post(self)
        P.emit()
        self.es.close()
        return self.nc

    unit_kw = {}
    cast_mark = None
    cast_only = None
    post = None
    cut = 0


def make_in_maps(inputs, consts, B=None):
    f = lambda a: np.ascontiguousarray(np.asarray(a, dtype=np.float32))
    xp = f(inputs["x_prompt"]); xs = f(inputs["x_sample"])
    maps = []
    for c in range(8):
        b = c // 4
        m = {
            "xp": xp[2 * c:2 * c + 2].reshape(512, D),
            "xs": xs[b],
            "ck": f(inputs["cache_k"])[b].reshape(2, PAST, 128),
            "cv": f(inputs["cache_v"])[b].reshape(2, PAST, 128),
            "sre": f(inputs["state_ssm_re"])[b],
            "sim": f(inputs["state_ssm_im"])[b],
            "cvec": np.stack([f(inputs["c_ctx"]), f(inputs["c"])[b]], 0),
        }
        for name, _ in IN_SPECS[7:]:
            m[name] = f(inputs[name])
        for k, a in consts.items():
            m["c_" + k] = a
        if B is not None:
            used = set(B.I.keys()) | set("c_" + k for k in B.C.keys())
            m = {k: a for k, a in m.items() if k in used}
        maps.append({k: np.ascontiguousarray(a) for k, a in m.items()})
    return maps


_CACHE = {}


def kernel(**inputs):
    consts = make_consts()
    if "nc" not in _CACHE:
        B = Builder()
        _CACHE["nc"] = B.build()
        _CACHE["B"] = B
    nc = _CACHE["nc"]
    in_maps = make_in_maps(inputs, consts, _CACHE["B"])
    res = run_bass_kernel_spmd(nc, in_maps, core_ids=list(range(8)))
    R = res.results
    y_prompt = np.concatenate([R[c]["yp"].reshape(2, 256, D) for c in range(8)], 0)
    y_sample = np.stack([R[0]["ys"], R[4]["ys"]], 0)
    nk = np.concatenate([R[c]["nk"].reshape(2, 2, 256, 2, 64) for c in range(8)], 0)
    nv = np.concatenate([R[c]["nv"].reshape(2, 2, 256, 2, 64) for c in range(8)], 0)
    nsr = np.concatenate([R[c]["nsr"] for c in range(8)], 0)
    nsi = np.concatenate([R[c]["nsi"] for c in range(8)], 0)
    return (y_prompt.astype(np.float32), y_sample.astype(np.float32), nk.astype(np.float32), nv.astype(np.float32),
            nsr.astype(np.float32), nsi.astype(np.float32))
```
